# Optimizing a Trainium2 kernel written in Bass

```python
import math
import jax, jax.numpy as jnp
from jax import lax
import numpy as np

D_MODEL = 1024
BATCH = 4
SEQ = 4096
DEPTH = 4
DEC_BATCH = 32
DEC_SEQ = 32
PAST_LEN = 2048

CHUNK = 64
CONV_W = 4
EPS = 1e-6
MIXERS = ('delta', 'fox', 'ssd')

DN_HEADS = 8
DN_HD = 128
DN_W = DN_HEADS * DN_HD
FOX_HEADS = 16
FOX_HD = 64
FOX_W = FOX_HEADS * FOX_HD
Q_BLOCK = 128
SSD_DINNER = 2 * D_MODEL
SSD_HD = 64
SSD_HEADS = SSD_DINNER // SSD_HD
SSD_GROUPS = 4
SSD_HPG = SSD_HEADS // SSD_GROUPS
SSD_N = 128
SSD_CONV_DIM = SSD_DINNER + 2 * SSD_GROUPS * SSD_N
N_MEM = 256
MEM_HEADS = 4
MEM_HD = 256
MEM_W = MEM_HEADS * MEM_HD
PEER_HEADS = 8
PEER_NKEYS = 128
PEER_N = PEER_NKEYS * PEER_NKEYS
PEER_DQ = 256
PEER_TOPK = 16
PEER_TOK_BLOCK = 256
W_IN_DN = 4 * DN_W + 2 * DN_HEADS + MEM_W
W_IN_FOX = 3 * FOX_W + FOX_HEADS + MEM_W
W_IN_SSD = SSD_DINNER + SSD_CONV_DIM + SSD_HEADS + MEM_W

kernel_name = 'hybrid_stream_encoder_step'


def rmsnorm(x, g):
    xf = x.astype(jnp.float32)
    y = xf * lax.rsqrt(jnp.mean(xf * xf, axis=-1, keepdims=True) + EPS)
    return (y * g.astype(jnp.float32)).astype(x.dtype)


def l2norm(x):
    xf = x.astype(jnp.float32)
    return xf * lax.rsqrt(jnp.sum(xf * xf, axis=-1, keepdims=True) + EPS)


def causal_conv(u, buf, w, b=None):
    L = u.shape[1]
    full = jnp.concatenate([buf.astype(u.dtype), u], axis=1)
    out = full[:, 0:L] * w[0]
    for j in range(1, CONV_W):
        out = out + full[:, j:j + L] * w[j]
    if b is not None:
        out = out + b
    return out, full[:, -(CONV_W - 1):]


def split_chunks(a, axis, c):
    shp = a.shape
    a = a.reshape(shp[:axis] + (shp[axis] // c, c) + shp[axis + 1:])
    return jnp.moveaxis(a, axis, 0)


def merge_chunks(a, axis):
    a = jnp.moveaxis(a, 0, axis)
    shp = a.shape
    return a.reshape(shp[:axis] + (shp[axis] * shp[axis + 1],) + shp[axis + 2:])


def gated_delta_scan(q, k, v, g, beta, s0):
    L = q.shape[2]
    c = min(CHUNK, L)
    tri = jnp.tril(jnp.ones((c, c), dtype=bool))
    strict = jnp.tril(jnp.ones((c, c), dtype=bool), -1)
    eye = jnp.eye(c, dtype=jnp.float32)

    def step(s, xs):
        qc, kc, vc, gc, bc = xs
        gam = jnp.cumsum(gc, axis=-1)
        dec = jnp.exp(jnp.where(tri, gam[..., :, None] - gam[..., None, :], -jnp.inf))
        eg = jnp.exp(gam)[..., None]
        kk = jnp.einsum('bhtd,bhid->bhti', kc, kc)
        a_mat = jnp.where(strict, bc[..., None] * dec * kk, 0.0) + eye
        rhs = bc[..., None] * (vc - eg * jnp.einsum('bhtd,bhde->bhte', kc, s))
        delta = lax.linalg.triangular_solve(a_mat, rhs, left_side=True, lower=True,
                                            unit_diagonal=True)
        qk = dec * jnp.einsum('bhtd,bhid->bhti', qc, kc)
        o = eg * jnp.einsum('bhtd,bhde->bhte', qc, s) + jnp.einsum('bhti,bhie->bhte', qk, delta)
        g_last = gam[..., -1:]
        s = jnp.exp(g_last)[..., None] * s + jnp.einsum(
            'bhid,bhie->bhde', kc * jnp.exp(g_last - gam)[..., None], delta)
        return s, o

    xs = (split_chunks(q, 2, c), split_chunks(k, 2, c), split_chunks(v, 2, c),
          split_chunks(g, 2, c), split_chunks(beta, 2, c))
    s, o = lax.scan(step, s0, xs)
    return merge_chunks(o, 2), s


def delta_mixer(pc, conv_w, a_log, dt_bias, norm_g, s0, conv_buf):
    B, L, _ = pc.shape
    f32 = jnp.float32
    qkv, z, a, b = jnp.split(pc, [3 * DN_W, 4 * DN_W, 4 * DN_W + DN_HEADS], axis=-1)
    qkv, new_buf = causal_conv(qkv, conv_buf, conv_w)
    q, k, v = jnp.split(jax.nn.silu(qkv), 3, axis=-1)
    q = l2norm(jnp.swapaxes(q.reshape(B, L, DN_HEADS, DN_HD), 1, 2)) * DN_HD ** -0.5
    k = l2norm(jnp.swapaxes(k.reshape(B, L, DN_HEADS, DN_HD), 1, 2))
    v = jnp.swapaxes(v.reshape(B, L, DN_HEADS, DN_HD), 1, 2).astype(f32)
    g = -jnp.exp(a_log.astype(f32)) * jax.nn.softplus(a.astype(f32) + dt_bias.astype(f32))
    beta = jax.nn.sigmoid(b.astype(f32))
    o, s_new = gated_delta_scan(q, k, v, jnp.swapaxes(g, 1, 2), jnp.swapaxes(beta, 1, 2),
                                s0.astype(f32))
    o = jnp.swapaxes(o, 1, 2)
    o = rmsnorm(o, norm_g) * jax.nn.silu(z.reshape(B, L, DN_HEADS, DN_HD).astype(f32))
    return o.reshape(B, L, DN_W).astype(pc.dtype), (s_new.astype(pc.dtype), new_buf)


def fox_block(qb, cqb, qpos, k, v, ck, kpos):
    s = jnp.einsum('bqhd,bkhd->bhqk', qb, k).astype(jnp.float32) * FOX_HD ** -0.5
    s = s + jnp.swapaxes(cqb, 1, 2)[..., :, None] - jnp.swapaxes(ck, 1, 2)[..., None, :]
    s = jnp.where(kpos[None, :] <= qpos[:, None], s, -jnp.inf)
    p = jax.nn.softmax(s, axis=-1).astype(v.dtype)
    return jnp.einsum('bhqk,bkhd->bqhd', p, v)


def fox_mixer(pc, bf, past):
    B, L, _ = pc.shape
    q, k, v, f = jnp.split(pc, [FOX_W, 2 * FOX_W, 3 * FOX_W], axis=-1)
    q = q.reshape(B, L, FOX_HEADS, FOX_HD)
    k = k.reshape(B, L, FOX_HEADS, FOX_HD)
    v = v.reshape(B, L, FOX_HEADS, FOX_HD)
    logf = jax.nn.log_sigmoid(f.astype(jnp.float32) + bf.astype(jnp.float32))
    if past is None:
        cum = jnp.cumsum(logf, axis=1)
        pos = jnp.arange(L)

        def block(n):
            s0 = n * Q_BLOCK
            qb = lax.dynamic_slice_in_dim(q, s0, Q_BLOCK, axis=1)
            cqb = lax.dynamic_slice_in_dim(cum, s0, Q_BLOCK, axis=1)
            return fox_block(qb, cqb, s0 + jnp.arange(Q_BLOCK), k, v, cum, pos)

        o = lax.map(block, jnp.arange(L // Q_BLOCK))
        o = jnp.moveaxis(o, 0, 1).reshape(B, L, FOX_W)
    else:
        ck, cv, clf = past
        P = ck.shape[1]
        kall = jnp.concatenate([ck.astype(k.dtype), k], axis=1)
        vall = jnp.concatenate([cv.astype(v.dtype), v], axis=1)
        cum = jnp.cumsum(jnp.concatenate([clf.astype(jnp.float32), logf], axis=1), axis=1)
        o = fox_block(q, cum[:, P:], P + jnp.arange(L), kall, vall, cum,
                      jnp.arange(P + L)).reshape(B, L, FOX_W)
    return o, (k, v, logf.astype(pc.dtype))


def ssd_scan(x, dt, da, bm, cm, h0):
    L = x.shape[3]
    c = min(CHUNK, L)
    tri = jnp.tril(jnp.ones((c, c), dtype=bool))

    def step(h, xs):
        xc, dtc, dac, bc, cc = xs
        gam = jnp.cumsum(dac, axis=-1)
        dec = jnp.exp(jnp.where(tri, gam[..., :, None] - gam[..., None, :], -jnp.inf))
        cb = jnp.einsum('bgtn,bgin->bgti', cc, bc)
        xdt = xc * dtc[..., None]
        y = jnp.einsum('bgjti,bgjip->bgjtp', dec * cb[:, :, None], xdt) + jnp.exp(gam)[..., None] * jnp.einsum('bgtn,bgjpn->bgjtp', cc, h)
        g_last = gam[..., -1:]
        h = jnp.exp(g_last)[..., None] * h + jnp.einsum(
            'bgjip,bgin->bgjpn', xdt * jnp.exp(g_last - gam)[..., None], bc)
        return h, y

    xs = (split_chunks(x, 3, c), split_chunks(dt, 3, c), split_chunks(da, 3, c),
          split_chunks(bm, 2, c), split_chunks(cm, 2, c))
    h, y = lax.scan(step, h0, xs)
    return merge_chunks(y, 3), h


def ssd_mixer(pc, conv_w, conv_b, a_log, dt_bias, d_skip, norm_g, h0, conv_buf):
    B, L, _ = pc.shape
    f32 = jnp.float32
    z, xbc, dt = jnp.split(pc, [SSD_DINNER, SSD_DINNER + SSD_CONV_DIM], axis=-1)
    xbc, new_buf = causal_conv(xbc, conv_buf, conv_w, conv_b)
    xbc = jax.nn.silu(xbc.astype(f32))
    xs, bm, cm = jnp.split(xbc, [SSD_DINNER, SSD_DINNER + SSD_GROUPS * SSD_N], axis=-1)
    x5 = xs.reshape(B, L, SSD_GROUPS, SSD_HPG, SSD_HD).transpose(0, 2, 3, 1, 4)
    bm = bm.reshape(B, L, SSD_GROUPS, SSD_N).transpose(0, 2, 1, 3)
    cm = cm.reshape(B, L, SSD_GROUPS, SSD_N).transpose(0, 2, 1, 3)
    dt = jax.nn.softplus(dt.astype(f32) + dt_bias.astype(f32))
    dt = dt.reshape(B, L, SSD_GROUPS, SSD_HPG).transpose(0, 2, 3, 1)
    a = -jnp.exp(a_log.astype(f32)).reshape(SSD_GROUPS, SSD_HPG, 1)
    h0 = h0.astype(f32).reshape(B, SSD_GROUPS, SSD_HPG, SSD_HD, SSD_N)
    y, h = ssd_scan(x5, dt, dt * a, bm, cm, h0)
    y = y + d_skip.astype(f32).reshape(SSD_GROUPS, SSD_HPG, 1, 1) * x5
    y = y.transpose(0, 3, 1, 2, 4).reshape(B, L, SSD_DINNER)
    y = rmsnorm(y * jax.nn.silu(z.astype(f32)), norm_g)
    return y.astype(pc.dtype), (h.reshape(B, SSD_HEADS, SSD_HD, SSD_N).astype(pc.dtype), new_buf)


def mem_attend(q, mk, mv):
    s = jnp.einsum('bqhd,bkhd->bhqk', q, mk.astype(q.dtype)).astype(jnp.float32) * MEM_HD ** -0.5
    p = jax.nn.softmax(s, axis=-1).astype(q.dtype)
    return jnp.einsum('bhqk,bkhd->bqhd', p, mv.astype(q.dtype))


def peer_ffn(x, wq, k1, k2, u, vt):
    B, L, D = x.shape
    T = B * L
    nblk = -(-T // PEER_TOK_BLOCK)
    xt = jnp.pad(x.reshape(T, D), ((0, nblk * PEER_TOK_BLOCK - T), (0, 0)))
    xt = xt.reshape(nblk, PEER_TOK_BLOCK, D)
    half = PEER_DQ // 2

    def block(xb):
        q = (xb @ wq).reshape(PEER_TOK_BLOCK, PEER_HEADS, PEER_DQ).astype(jnp.float32)
        s1 = jnp.einsum('thd,nd->thn', q[..., :half], k1.astype(jnp.float32))
        s2 = jnp.einsum('thd,nd->thn', q[..., half:], k2.astype(jnp.float32))
        v1, i1 = lax.top_k(s1, PEER_TOPK)
        v2, i2 = lax.top_k(s2, PEER_TOPK)
        cand_s = (v1[..., :, None] + v2[..., None, :]).reshape(PEER_TOK_BLOCK, PEER_HEADS, PEER_TOPK * PEER_TOPK)
        cand_i = (i1[..., :, None] * PEER_NKEYS + i2[..., None, :]).reshape(PEER_TOK_BLOCK, PEER_HEADS, PEER_TOPK * PEER_TOPK)
        top_s, pos = lax.top_k(cand_s, PEER_TOPK)
        eidx = jnp.take_along_axis(cand_i, pos, axis=-1)
        gate = jax.nn.softmax(top_s, axis=-1)
        act = jax.nn.gelu(jnp.einsum('td,thkd->thk', xb, u[eidx]).astype(jnp.float32), approximate=False)
        return jnp.einsum('thk,thkd->td', (gate * act).astype(xb.dtype), vt[eidx])

    y = lax.map(block, xt)
    return y.reshape(nblk * PEER_TOK_BLOCK, D)[:T].reshape(B, L, D)


def trunk_layer(kind, x, mem_k, mem_v, state, mp, g_mix, g_ffn, pq, pk1, pk2, pu, pv):
    B, L, _ = x.shape
    hn = rmsnorm(x, g_mix)
    proj = hn @ mp['w_in']
    pc, mq = proj[..., :-MEM_W], proj[..., -MEM_W:]
    if kind == 'delta':
        o_mix, new_state = delta_mixer(pc, mp['conv'], mp['alog'], mp['dtb'], mp['ng'], state[0], state[1])
    elif kind == 'fox':
        o_mix, new_state = fox_mixer(pc, mp['bf'], state)
    else:
        o_mix, new_state = ssd_mixer(pc, mp['conv'], mp['convb'], mp['alog'], mp['dtb'], mp['d'], mp['ng'], state[0], state[1])
    o_mem = mem_attend(mq.reshape(B, L, MEM_HEADS, MEM_HD), mem_k, mem_v).reshape(B, L, MEM_W)
    x = x + jnp.concatenate([o_mix, o_mem], axis=-1) @ mp['w_out']
    x = x + peer_ffn(rmsnorm(x, g_ffn), pq, pk1, pk2, pu, pv)
    return x, new_state


def setup_inputs(seed: int = 0) -> dict:
    key = jax.random.key(seed)
    ks = iter(jax.random.split(key, 64))
    f32 = jnp.float32

    def nrm(shape, scale=1.0):
        return scale * jax.random.normal(next(ks), shape, f32)

    def gain(shape):
        return 1.0 + 0.02 * jax.random.normal(next(ks), shape, f32)

    def a_log(n):
        return jnp.log(jax.random.uniform(next(ks), (n,), f32, 1.0, 16.0))

    def dt_bias(n):
        dt = jnp.exp(jax.random.uniform(next(ks), (n,), f32, math.log(1e-3), math.log(1e-1)))
        return dt + jnp.log(-jnp.expm1(-dt))

    inv = D_MODEL ** -0.5
    return {
        'x_prompt': nrm((BATCH, SEQ, D_MODEL)),
        'x_sample': nrm((DEC_BATCH, DEC_SEQ, D_MODEL)),
        'cache_mem_k': nrm((DEPTH, DEC_BATCH, N_MEM, MEM_HEADS, MEM_HD)),
        'cache_mem_v': nrm((DEPTH, DEC_BATCH, N_MEM, MEM_HEADS, MEM_HD)),
        'state_l0_delta': nrm((DEC_BATCH, DN_HEADS, DN_HD, DN_HD), DN_HD ** -0.5),
        'state_l0_conv': nrm((DEC_BATCH, CONV_W - 1, 3 * DN_W)),
        'cache_l1_k': nrm((DEC_BATCH, PAST_LEN, FOX_HEADS, FOX_HD)),
        'cache_l1_v': nrm((DEC_BATCH, PAST_LEN, FOX_HEADS, FOX_HD)),
        'cache_l1_logf': jax.nn.log_sigmoid(3.0 + nrm((DEC_BATCH, PAST_LEN, FOX_HEADS))),
        'state_l2_ssm': nrm((DEC_BATCH, SSD_HEADS, SSD_HD, SSD_N), 0.1),
        'state_l2_conv': nrm((DEC_BATCH, CONV_W - 1, SSD_CONV_DIM)),
        'state_l3_delta': nrm((DEC_BATCH, DN_HEADS, DN_HD, DN_HD), DN_HD ** -0.5),
        'state_l3_conv': nrm((DEC_BATCH, CONV_W - 1, 3 * DN_W)),
        'mem_prompt': nrm((BATCH, N_MEM, D_MODEL)),
        'g_mix': gain((DEPTH, D_MODEL)),
        'g_mem': gain((DEPTH, D_MODEL)),
        'w_mem_k': nrm((DEPTH, D_MODEL, MEM_W), inv),
        'w_mem_v': nrm((DEPTH, D_MODEL, MEM_W), inv),
        'g_ffn': gain((DEPTH, D_MODEL)),
        'peer_wq': nrm((DEPTH, D_MODEL, PEER_HEADS * PEER_DQ), inv),
        'peer_k1': nrm((DEPTH, PEER_NKEYS, PEER_DQ // 2), (PEER_DQ // 2) ** -0.5),
        'peer_k2': nrm((DEPTH, PEER_NKEYS, PEER_DQ // 2), (PEER_DQ // 2) ** -0.5),
        'peer_u': nrm((DEPTH, PEER_N, D_MODEL), inv),
        'peer_v': nrm((DEPTH, PEER_N, D_MODEL), (PEER_HEADS * PEER_TOPK) ** -0.5),
        'w_in_l0': nrm((D_MODEL, W_IN_DN), inv),
        'w_out_l0': nrm((DN_W + MEM_W, D_MODEL), (DN_W + MEM_W) ** -0.5),
        'dn_conv_l0': nrm((CONV_W, 3 * DN_W), CONV_W ** -0.5),
        'dn_alog_l0': a_log(DN_HEADS),
        'dn_dtb_l0': dt_bias(DN_HEADS),
        'dn_ng_l0': gain((DN_HD,)),
        'w_in_l1': nrm((D_MODEL, W_IN_FOX), inv),
        'w_out_l1': nrm((FOX_W + MEM_W, D_MODEL), (FOX_W + MEM_W) ** -0.5),
        'fox_bf_l1': 3.0 + nrm((FOX_HEADS,), 0.1),
        'w_in_l2': nrm((D_MODEL, W_IN_SSD), inv),
        'w_out_l2': nrm((SSD_DINNER + MEM_W, D_MODEL), (SSD_DINNER + MEM_W) ** -0.5),
        'ssd_conv_l2': nrm((CONV_W, SSD_CONV_DIM), CONV_W ** -0.5),
        'ssd_convb_l2': nrm((SSD_CONV_DIM,), 0.02),
        'ssd_alog_l2': a_log(SSD_HEADS),
        'ssd_dtb_l2': dt_bias(SSD_HEADS),
        'ssd_d_l2': 1.0 + nrm((SSD_HEADS,), 0.1),
        'ssd_ng_l2': gain((SSD_DINNER,)),
        'w_in_l3': nrm((D_MODEL, W_IN_DN), inv),
        'w_out_l3': nrm((DN_W + MEM_W, D_MODEL), (DN_W + MEM_W) ** -0.5),
        'dn_conv_l3': nrm((CONV_W, 3 * DN_W), CONV_W ** -0.5),
        'dn_alog_l3': a_log(DN_HEADS),
        'dn_dtb_l3': dt_bias(DN_HEADS),
        'dn_ng_l3': gain((DN_HD,)),
        'g_final': gain((D_MODEL,)),
    }


def reference(x_prompt, x_sample, cache_mem_k, cache_mem_v,
              state_l0_delta, state_l0_conv, cache_l1_k, cache_l1_v, cache_l1_logf,
              state_l2_ssm, state_l2_conv, state_l3_delta, state_l3_conv,
              mem_prompt,
              g_mix, g_mem, w_mem_k, w_mem_v, g_ffn, peer_wq, peer_k1, peer_k2, peer_u, peer_v,
              w_in_l0, w_out_l0, dn_conv_l0, dn_alog_l0, dn_dtb_l0, dn_ng_l0,
              w_in_l1, w_out_l1, fox_bf_l1,
              w_in_l2, w_out_l2, ssd_conv_l2, ssd_convb_l2, ssd_alog_l2, ssd_dtb_l2, ssd_d_l2, ssd_ng_l2,
              w_in_l3, w_out_l3, dn_conv_l3, dn_alog_l3, dn_dtb_l3, dn_ng_l3,
              g_final):
    mix_params = (
        {'w_in': w_in_l0, 'w_out': w_out_l0, 'conv': dn_conv_l0, 'alog': dn_alog_l0, 'dtb': dn_dtb_l0, 'ng': dn_ng_l0},
        {'w_in': w_in_l1, 'w_out': w_out_l1, 'bf': fox_bf_l1},
        {'w_in': w_in_l2, 'w_out': w_out_l2, 'conv': ssd_conv_l2, 'convb': ssd_convb_l2, 'alog': ssd_alog_l2, 'dtb': ssd_dtb_l2, 'd': ssd_d_l2, 'ng': ssd_ng_l2},
        {'w_in': w_in_l3, 'w_out': w_out_l3, 'conv': dn_conv_l3, 'alog': dn_alog_l3, 'dtb': dn_dtb_l3, 'ng': dn_ng_l3},
    )
    bp = x_prompt.shape[0]
    dtp = x_prompt.dtype
    prompt_states = (
        (jnp.zeros((bp, DN_HEADS, DN_HD, DN_HD), dtp), jnp.zeros((bp, CONV_W - 1, 3 * DN_W), dtp)),
        None,
        (jnp.zeros((bp, SSD_HEADS, SSD_HD, SSD_N), dtp), jnp.zeros((bp, CONV_W - 1, SSD_CONV_DIM), dtp)),
        (jnp.zeros((bp, DN_HEADS, DN_HD, DN_HD), dtp), jnp.zeros((bp, CONV_W - 1, 3 * DN_W), dtp)),
    )
    sample_states = (
        (state_l0_delta, state_l0_conv),
        (cache_l1_k, cache_l1_v, cache_l1_logf),
        (state_l2_ssm, state_l2_conv),
        (state_l3_delta, state_l3_conv),
    )

    x = x_prompt
    p_mk, p_mv, p_new = [], [], []
    for i in range(DEPTH):
        mn = rmsnorm(mem_prompt, g_mem[i])
        mk = (mn @ w_mem_k[i]).reshape(bp, N_MEM, MEM_HEADS, MEM_HD)
        mv = (mn @ w_mem_v[i]).reshape(bp, N_MEM, MEM_HEADS, MEM_HD)
        p_mk.append(mk)
        p_mv.append(mv)
        x, st = trunk_layer(MIXERS[i % len(MIXERS)], x, mk, mv, prompt_states[i], mix_params[i],
                            g_mix[i], g_ffn[i], peer_wq[i], peer_k1[i], peer_k2[i], peer_u[i], peer_v[i])
        p_new.append(st)
    y_prompt = rmsnorm(x, g_final)

    xs = x_sample
    s_new = []
    for i in range(DEPTH):
        xs, st = trunk_layer(MIXERS[i % len(MIXERS)], xs, cache_mem_k[i], cache_mem_v[i], sample_states[i],
                             mix_params[i], g_mix[i], g_ffn[i], peer_wq[i], peer_k1[i], peer_k2[i],
                             peer_u[i], peer_v[i])
        s_new.append(st)
    y_sample = rmsnorm(xs, g_final)

    p_mem_k = jnp.stack(p_mk)
    p_mem_v = jnp.stack(p_mv)
    p_l0_delta, p_l0_conv = p_new[0]
    p_l1_k, p_l1_v, p_l1_logf = p_new[1]
    p_l2_ssm, p_l2_conv = p_new[2]
    p_l3_delta, p_l3_conv = p_new[3]
    s_l0_delta, s_l0_conv = s_new[0]
    s_l1_k, s_l1_v, s_l1_logf = s_new[1]
    s_l2_ssm, s_l2_conv = s_new[2]
    s_l3_delta, s_l3_conv = s_new[3]
    return (y_prompt, y_sample, p_mem_k, p_mem_v,
            p_l0_delta, p_l0_conv, p_l1_k, p_l1_v, p_l1_logf, p_l2_ssm, p_l2_conv, p_l3_delta, p_l3_conv,
            s_l0_delta, s_l0_conv, s_l1_k, s_l1_v, s_l1_logf, s_l2_ssm, s_l2_conv, s_l3_delta, s_l3_conv)
```

```python
import numpy as np
from contextlib import ExitStack
import concourse.bass as bass
import concourse.mybir as mybir
from concourse.bass_utils import run_bass_kernel_spmd

F32 = mybir.dt.float32
BF16 = mybir.dt.bfloat16
I32 = mybir.dt.int32
U32 = mybir.dt.uint32
AF = mybir.ActivationFunctionType
ALU = mybir.AluOpType
AX = mybir.AxisListType

D = 1024
EPS = 1e-6
MIXERS = ('delta', 'fox', 'ssd')
W_IN = {'delta': 5136, 'fox': 4112, 'ssd': 6176}
W_MIX = {'delta': 1024, 'fox': 1024, 'ssd': 2048}
WMAX = 6176
NEG = -1.0e30

ENG = ['pe', 'dve', 'act', 'pool', 'sp']
NLANES = {'sp': 12, 'pool': 12, 'act': 6}
ENGOBJ = {'pe': 'tensor', 'dve': 'vector', 'act': 'scalar', 'pool': 'gpsimd', 'sp': 'sync'}


class V:
    __slots__ = ('ap', 'key')

    def __init__(self, ap, key):
        self.ap = ap
        self.key = key

    def __getitem__(self, idx):
        return V(self.ap[idx], self.key)


class Tl:
    def __init__(self, h, name):
        self.h = h
        self.name = name

    def __getitem__(self, idx):
        return V(self.h[idx], self.name)

    def v(self, fn):
        return V(fn(self.h), self.name)


class Sched:
    def __init__(self, nc):
        self.nc = nc
        self.stream = {e: [] for e in ENG}
        self.count = {e: 0 for e in ENG}
        self.known = {e: {} for e in ENG}
        self.res = {}
        self.dma_n = {q: 0 for q in NLANES}
        self.nops = 0
        self.noself = False

    def _deps(self, reads, writes):
        deps = {}
        for r in reads:
            st = self.res.get(r)
            if st and st['w']:
                sk, v = st['w']
                if deps.get(sk, 0) < v:
                    deps[sk] = v
        for w in writes:
            st = self.res.get(w)
            if st:
                if st['w']:
                    sk, v = st['w']
                    if deps.get(sk, 0) < v:
                        deps[sk] = v
                for sk, v in st['r'].items():
                    if deps.get(sk, 0) < v:
                        deps[sk] = v
        return deps

    def _emit_waits(self, eng, deps):
        kn = self.known[eng]
        for sk, v in deps.items():
            if sk == ('c', eng) and (eng == 'pe' or self.noself):
                continue
            if kn.get(sk, 0) >= v:
                continue
            self.stream[eng].append(('wait', sk, v))
            kn[sk] = v

    def _record(self, sk, val, reads, writes):
        for r in reads:
            st = self.res.get(r)
            if st is None:
                st = self.res[r] = {'w': None, 'r': {}}
            if st['r'].get(sk, 0) < val:
                st['r'][sk] = val
        for w in writes:
            self.res[w] = {'w': (sk, val), 'r': {}}

    def op(self, eng, fn, reads=(), writes=()):
        deps = self._deps(reads, writes)
        self._emit_waits(eng, deps)
        self.count[eng] += 1
        sk = ('c', eng)
        self.stream[eng].append(('op', fn, sk, 1))
        self._record(sk, self.count[eng], reads, writes)
        self.nops += 1

    def dma(self, q, fn, reads=(), writes=()):
        n = self.dma_n[q]
        nl = NLANES[q]
        lane = n % nl
        val = 16 * (n // nl + 1)
        sk = ('d', q, lane)
        deps = self._deps(reads, writes)
        if n >= nl and deps.get(sk, 0) < val - 16:
            deps[sk] = val - 16
        self._emit_waits(q, deps)
        self.dma_n[q] = n + 1
        self.stream[q].append(('op', fn, sk, 16))
        self._record(sk, val, reads, writes)
        self.nops += 1

    def _all_outstanding(self):
        fin = {}
        for e in ENG:
            if self.count[e]:
                fin[('c', e)] = self.count[e]
        for q in NLANES:
            n = self.dma_n[q]
            for l in range(NLANES[q]):
                cnt = (n - l + NLANES[q] - 1) // NLANES[q] if n > l else 0
                if cnt:
                    fin[('d', q, l)] = 16 * cnt
        return fin

    def barrier(self):
        fin = self._all_outstanding()
        for e in ENG:
            kn = self.known[e]
            for sk, v in fin.items():
                if sk == ('c', e):
                    continue
                if kn.get(sk, 0) < v:
                    self.stream[e].append(('wait', sk, v))
                    kn[sk] = v
        self.res = {}

    def emit(self):
        nc = self.nc
        self.barrier()
        sks = [('c', e) for e in ENG] + [('d', q, l) for q in NLANES for l in range(NLANES[q])]
        with ExitStack() as es:
            sems = {}
            for sk in sks:
                sems[sk] = es.enter_context(nc.semaphore("s_" + "_".join(str(x) for x in sk)))
            block = es.enter_context(nc.Block())
            for e in ENG:
                items = self.stream[e]

                def body(engine, items=items):
                    for it in items:
                        if it[0] == 'wait':
                            engine.wait_ge(sems[it[1]], it[2])
                        else:
                            it[1](engine).then_inc(sems[it[2]], it[3])
                getattr(block, ENGOBJ[e])(body)


class KB:
    def __init__(self, nc):
        self.nc = nc
        self.S = Sched(nc)
        self.uid = 0

    def _rk(self, *vs):
        return [v.key for v in vs if isinstance(v, V)]

    @staticmethod
    def _a(x):
        return x.ap if isinstance(x, V) else x

    def sb(self, es, name, shape, dt=F32):
        self.uid += 1
        nm = "%s_%d" % (name, self.uid)
        return Tl(es.enter_context(self.nc.sbuf_tensor(nm, list(shape), dt)), nm)

    def ps(self, es, name, shape, dt=F32):
        self.uid += 1
        nm = "%s_%d" % (name, self.uid)
        return Tl(es.enter_context(self.nc.psum_tensor(nm, list(shape), dt)), nm)

    def dma(self, out, in_, q='sp', **kw):
        o, i = out.ap, in_.ap
        self.S.dma(q, lambda e: e.dma_start(out=o, in_=i, **kw), reads=[in_.key], writes=[out.key])

    def gather(self, out, table, idx):
        o, t, ix = out.ap, table.ap, idx.ap
        self.S.dma('pool', lambda e: e.indirect_dma_start(
            out=o, out_offset=None, in_=t,
            in_offset=bass.IndirectOffsetOnAxis(ap=ix, axis=0)),
            reads=[table.key, idx.key], writes=[out.key])

    def mm(self, out, lhsT, rhs, start=True, stop=True):
        o, l, r = out.ap, lhsT.ap, rhs.ap
        self.S.op('pe', lambda e: e.matmul(o, lhsT=l, rhs=r, start=start, stop=stop),
                  reads=[lhsT.key, rhs.key] + ([] if start else [out.key]), writes=[out.key])

    def tr(self, out, in_, ident):
        o, i, d = out.ap, in_.ap, ident.ap
        self.S.op('pe', lambda e: e.transpose(out=o, in_=i, identity=d),
                  reads=[in_.key, ident.key], writes=[out.key])

    def act(self, out, in_, func, bias=None, scale=None, accum=None):
        o, i = out.ap, in_.ap
        kw = {}
        rd = [in_.key]
        if bias is not None:
            kw['bias'] = self._a(bias)
            rd += self._rk(bias)
        if scale is not None:
            kw['scale'] = self._a(scale)
            rd += self._rk(scale)
        wr = [out.key]
        if accum is not None:
            kw['accum_out'] = accum.ap
            wr.append(accum.key)
        self.S.op('act', lambda e: e.activation(out=o, in_=i, func=func, **kw), reads=rd, writes=wr)

    def ts(self, out, in0, s1, op0, s2=None, op1=None, eng='dve', accum=None):
        o, i = out.ap, in0.ap
        a1, a2 = self._a(s1), self._a(s2)
        rd = [in0.key] + self._rk(s1, s2)
        kw = {}
        if op1 is not None:
            kw['op1'] = op1
        wr = [out.key]
        if accum is not None:
            kw['accum_out'] = accum.ap
            wr.append(accum.key)
        self.S.op(eng, lambda e: e.tensor_scalar(out=o, in0=i, scalar1=a1, scalar2=a2, op0=op0, **kw),
                  reads=rd, writes=wr)

    def tt(self, out, in0, in1, op, eng='dve'):
        o, a, b = out.ap, in0.ap, in1.ap
        self.S.op(eng, lambda e: e.tensor_tensor(out=o, in0=a, in1=b, op=op),
                  reads=[in0.key, in1.key], writes=[out.key])

    def stt(self, out, in0, scalar, in1, op0, op1):
        o, a, b = out.ap, in0.ap, in1.ap
        s = self._a(scalar)
        self.S.op('dve', lambda e: e.scalar_tensor_tensor(out=o, in0=a, scalar=s, in1=b, op0=op0, op1=op1),
                  reads=[in0.key, in1.key] + self._rk(scalar), writes=[out.key])

    def ttr(self, out, in0, in1, accum):
        o, a, b, c = out.ap, in0.ap, in1.ap, accum.ap
        self.S.op('dve', lambda e: e.scalar_tensor_tensor(out=o, in0=a, scalar=1.0, in1=b, op0=ALU.mult,
                                                          op1=ALU.mult, accum_out=c),
                  reads=[in0.key, in1.key], writes=[out.key, accum.key])

    def red(self, out, in_, op, axis=AX.X):
        o, i = out.ap, in_.ap
        self.S.op('dve', lambda e: e.tensor_reduce(out=o, in_=i, axis=axis, op=op),
                  reads=[in_.key], writes=[out.key])

    def copy(self, out, in_, eng='dve'):
        o, i = out.ap, in_.ap
        if eng == 'act':
            self.S.op('act', lambda e: e.copy(out=o, in_=i), reads=[in_.key], writes=[out.key])
        else:
            self.S.op(eng, lambda e: e.tensor_copy(out=o, in_=i), reads=[in_.key], writes=[out.key])

    def memset(self, out, val, eng='pool'):
        o = out.ap
        self.S.op(eng, lambda e: e.memset(o, val), writes=[out.key])

    def recip(self, out, in_):
        o, i = out.ap, in_.ap
        self.S.op('dve', lambda e: e.reciprocal(out=o, in_=i), reads=[in_.key], writes=[out.key])

    def aselect(self, out, in_, pattern, cmp, fill, base, cm):
        o, i = out.ap, in_.ap
        self.S.op('pool', lambda e: e.affine_select(out=o, in_=i, pattern=pattern, compare_op=cmp, fill=fill,
                                                    base=base, channel_multiplier=cm),
                  reads=[in_.key], writes=[out.key])

    def iota(self, out, pattern, base=0, cm=0):
        o = out.ap
        self.S.op('pool', lambda e: e.iota(o, pattern=pattern, base=base, channel_multiplier=cm,
                                           allow_small_or_imprecise_dtypes=True), writes=[out.key])

    def max8(self, out, in_):
        o, i = out.ap, in_.ap
        self.S.op('dve', lambda e: e.max(out=o, in_=i), reads=[in_.key], writes=[out.key])

    def maxidx(self, out, in_max, in_values):
        o, m, v = out.ap, in_max.ap, in_values.ap
        self.S.op('dve', lambda e: e.max_index(out=o, in_max=m, in_values=v),
                  reads=[in_max.key, in_values.key], writes=[out.key])

    def mrep(self, out, in_to_replace, in_values, imm):
        o, r, v = out.ap, in_to_replace.ap, in_values.ap
        self.S.op('dve', lambda e: e.match_replace(out=o, in_to_replace=r, in_values=v, imm_value=imm),
                  reads=[in_to_replace.key, in_values.key], writes=[out.key])


class MK:
    def __init__(self, L, PAST, DEPTH, debug=False):
        self.L, self.PAST, self.DEPTH, self.debug = L, PAST, DEPTH, debug
        self.NT = L // 128
        self.NTOK = L + 128
        self.nc = bass.Bass("TRN2", target_bir_lowering=False)
        self.K = KB(self.nc)
        self.kinds = [MIXERS[i % 3] for i in range(DEPTH)]
        self.io = {}

    def din(self, name, shape, dt=F32):
        t = self.nc.dram_tensor(name, list(shape), dt, kind="ExternalInput")
        self.io[name] = Tl(t.ap(), name)
        return self.io[name]

    def dout(self, name, shape, dt=F32):
        t = self.nc.dram_tensor(name, list(shape), dt, kind="ExternalOutput")
        self.io[name] = Tl(t.ap(), name)
        return self.io[name]

    def dscr(self, name, shape, dt=F32):
        kind = "ExternalOutput" if self.debug else "Internal"
        t = self.nc.dram_tensor(name, list(shape), dt, kind=kind)
        self.io[name] = Tl(t.ap(), name)
        return self.io[name]

    def declare(self):
        L, PAST, DP = self.L, self.PAST, self.DEPTH
        d = self.din
        d('xp', [L, D]); d('xs', [128, D]); d('memp', [256, D])
        d('cmk', [DP, 4, 256, 1024]); d('cmv', [DP, 4, 256, 1024])
        d('g_mix', [DP, D]); d('g_mem', [DP, D]); d('g_ffn', [DP, D]); d('g_final', [1, D])
        d('w_mem_k', [DP, D, 1024]); d('w_mem_v', [DP, D, 1024])
        d('peer_wq', [DP, D, 2048]); d('peer_k1', [DP, 128, 128]); d('peer_k2', [DP, 128, 128])
        for i in range(DP):
            d('peer_u_l%d' % i, [16384, D]); d('peer_v_l%d' % i, [16384, D])
        for i, kd in enumerate(self.kinds):
            d('w_in_l%d' % i, [D, W_IN[kd]]); d('w_out_l%d' % i, [W_MIX[kd] + 1024, D])
            if kd == 'delta':
                d('conv_l%d' % i, [4, 3072]); d('alog_l%d' % i, [1, 8]); d('dtb_l%d' % i, [1, 8]); d('ng_l%d' % i, [1, 128])
                d('st_d_l%d' % i, [4, 8, 128, 128]); d('st_c_l%d' % i, [4, 3, 3072])
            elif kd == 'fox':
                d('bf_l%d' % i, [1, 16])
                d('c_k_l%d' % i, [4, PAST, 1024]); d('c_v_l%d' % i, [4, PAST, 1024]); d('c_f_l%d' % i, [4, PAST, 16])
            else:
                d('conv_l%d' % i, [4, 3072]); d('convb_l%d' % i, [1, 3072]); d('alog_l%d' % i, [1, 32]); d('dtb_l%d' % i, [1, 32])
                d('dsk_l%d' % i, [1, 32]); d('ng_l%d' % i, [1, 2048])
                d('st_s_l%d' % i, [4, 32, 64, 128]); d('st_c_l%d' % i, [4, 3, 3072])
        o = self.dout
        o('y_p', [L, D]); o('y_s', [128, D])
        o('o_mk', [DP, 256, 1024]); o('o_mv', [DP, 256, 1024])
        for i, kd in enumerate(self.kinds):
            if kd == 'delta':
                o('p_d_l%d' % i, [8, 128, 128]); o('p_c_l%d' % i, [3, 3072])
                o('s_d_l%d' % i, [4, 8, 128, 128]); o('s_c_l%d' % i, [4, 3, 3072])
            elif kd == 'fox':
                o('p_k_l%d' % i, [L, 1024]); o('p_v_l%d' % i, [L, 1024]); o('p_f_l%d' % i, [L, 16])
                o('s_k_l%d' % i, [128, 1024]); o('s_v_l%d' % i, [128, 1024]); o('s_f_l%d' % i, [128, 16])
            else:
                o('p_s_l%d' % i, [32, 64, 128]); o('p_c_l%d' % i, [3, 3072])
                o('s_s_l%d' % i, [4, 32, 64, 128]); o('s_c_l%d' % i, [4, 3, 3072])
        s = self.dscr
        s('xres', [self.NTOK, D])
        s('proj', [self.NTOK, WMAX])
        s('ocat', [self.NTOK, 3072])
        s('cvin', [3 + L + 4 * 35, 3072])
        s('cvo', [self.NTOK, 3072])
        s('aux', [self.NTOK, 64])
        s('cumT', [16, L])
        s('cumTs', [4, 16, PAST + 32])

    def cv_row(self, seq):
        return 3 if seq < 0 else 3 + self.L + 35 * seq + 3

    def consts(self, es):
        K = self.K
        c = {}
        c['idf'] = K.sb(es, 'idf', [128, 128], F32)
        K.memset(c['idf'][:], 0.0)
        K.aselect(c['idf'][:], c['idf'][:], [[-1, 128]], ALU.not_equal, 1.0, 0, 1)
        c['idb'] = K.sb(es, 'idb', [128, 128], BF16)
        K.copy(c['idb'][:], c['idf'][:])
        c['triu'] = K.sb(es, 'triu', [128, 128], F32)
        K.memset(c['triu'][:], 1.0)
        K.aselect(c['triu'][:], c['triu'][:], [[1, 128]], ALU.is_ge, 0.0, 0, -1)
        c['striu'] = K.sb(es, 'striu', [128, 128], F32)
        K.memset(c['striu'][:], 1.0)
        K.aselect(c['striu'][:], c['striu'][:], [[1, 128]], ALU.is_ge, 0.0, -1, -1)
        c['tril'] = K.sb(es, 'tril', [128, 128], F32)
        K.memset(c['tril'][:], 1.0)
        K.aselect(c['tril'][:], c['tril'][:], [[-1, 128]], ALU.is_ge, 0.0, 0, 1)
        c['stril'] = K.sb(es, 'stril', [128, 128], F32)
        K.memset(c['stril'][:], 1.0)
        K.aselect(c['stril'][:], c['stril'][:], [[-1, 128]], ALU.is_ge, 0.0, -1, 1)
        c['ones'] = K.sb(es, 'ones', [128, 128], F32)
        K.memset(c['ones'][:], 1.0)
        c['zeros'] = K.sb(es, 'zeros', [128, 3072], F32)
        K.memset(c['zeros'][:], 0.0)
        self.c = c
        self.pb = [K.ps(es, 'pb%d' % i, [128, 512], F32) for i in range(8)]
        self.pbi = 0

    def bank(self):
        b = self.pb[self.pbi % 8]
        self.pbi += 1
        return b

    def rmsnorm_rstd(self, es_tmp, x, width, tmp, rstd):
        K = self.K
        K.act(tmp, x, AF.Square, accum=rstd)
        self.rsqrt(rstd, rstd, 1.0 / width, EPS)

    def rsqrt(self, out, in_, scale, eps):
        K = self.K
        K.ts(out, in_, scale, ALU.mult, eps, ALU.add)
        K.act(out, out, AF.Sqrt)
        K.recip(out, out)

    def load_w_bf16(self, wsrc, wdst, nk, ncols, stg):
        K = self.K
        i = 0
        for k in range(nk):
            for c0 in range(0, ncols, 2048):
                c1 = min(ncols, c0 + 2048)
                s = stg[i % len(stg)]
                K.dma(s[:, 0:c1 - c0], wsrc[k * 128:(k + 1) * 128, c0:c1], q='sp' if i % 2 == 0 else 'act')
                K.copy(wdst[:, k, c0:c1], s[:, 0:c1 - c0], eng='dve' if i % 2 == 0 else 'pool')
                i += 1

    def transpose_to(self, dst, src, nblk, ident, dt, evac_engs=('act', 'dve')):
        K = self.K
        per = 4 if dt == F32 else 8
        for b0 in range(0, nblk, per):
            nb = min(per, nblk - b0)
            bk = self.bank()
            if dt == F32:
                pv = bk
                for j in range(nb):
                    K.tr(pv[:, j * 128:(j + 1) * 128], src[:, (b0 + j) * 128:(b0 + j + 1) * 128], ident[:])
                K.copy(dst.v(lambda h: h[:, b0:b0 + nb, :]),
                       pv.v(lambda h: h[:, 0:nb * 128].rearrange("p (k t) -> p k t", k=nb)),
                       eng=evac_engs[(b0 // per) % len(evac_engs)])
            else:
                for j in range(nb):
                    K.tr(bk.v(lambda h: h.bitcast(BF16)[:, j * 128:(j + 1) * 128]),
                         src[:, (b0 + j) * 128:(b0 + j + 1) * 128], ident[:])
                K.copy(dst.v(lambda h: h[:, b0:b0 + nb, :]),
                       bk.v(lambda h: h.bitcast(BF16)[:, 0:nb * 128].rearrange("p (k t) -> p k t", k=nb)),
                       eng=evac_engs[(b0 // per) % len(evac_engs)])

    def phase_A(self, li):
        K, io, kd = self.K, self.io, self.kinds[li]
        W = W_IN[kd]
        with ExitStack() as es:
            wbf = K.sb(es, 'wbf', [128, 8, W], BF16)
            stg = [K.sb(es, 'stg', [128, 2048], F32) for _ in range(2)]
            gB = K.sb(es, 'gB', [128, D], F32)
            K.dma(gB[:], io['g_mix'].v(lambda h: h[li:li + 1, :].partition_broadcast(128)))
            self.load_w_bf16(io['w_in_l%d' % li], wbf, 8, W, stg)
            xt = [K.sb(es, 'xt', [128, D], F32) for _ in range(2)]
            sq = K.sb(es, 'sq', [128, D], F32)
            rstd = [K.sb(es, 'rstd', [128, 1], F32) for _ in range(2)]
            hn = [K.sb(es, 'hn', [128, D], BF16) for _ in range(2)]
            hnT = [K.sb(es, 'hnT', [128, 8, 128], BF16) for _ in range(2)]
            pj = [K.sb(es, 'pj', [128, W], F32) for _ in range(2)]
            for ti in range(self.NT + 1):
                b = ti % 2
                r0 = ti * 128
                if li == 0:
                    src = io['xp'][r0:r0 + 128, :] if ti < self.NT else io['xs'][:, :]
                else:
                    src = io['xres'][r0:r0 + 128, :]
                K.dma(xt[b][:], src)
                self.rmsnorm_rstd(es, xt[b][:], D, sq[:], rstd[b][:])
                K.stt(hn[b][:], xt[b][:], rstd[b][:], gB[:], ALU.mult, ALU.mult)
                self.transpose_to(hnT[b], hn[b], 8, self.c['idb'], BF16)
                g = 0
                for c0 in range(0, W, 512):
                    c1 = min(W, c0 + 512)
                    bk = self.bank()
                    for k in range(8):
                        K.mm(bk[:, 0:c1 - c0], hnT[b][:, k, :], wbf[:, k, c0:c1], start=(k == 0), stop=(k == 7))
                    K.copy(pj[b][:, c0:c1], bk[:, 0:c1 - c0], eng='act' if g % 2 == 0 else 'dve')
                    g += 1
                K.dma(io['proj'][r0:r0 + 128, 0:W], pj[b][:], q='sp')
                if kd in ('delta', 'ssd'):
                    coff = 0 if kd == 'delta' else 2048
                    if ti < self.NT:
                        cr = self.cv_row(-1) + r0
                        K.dma(io['cvin'][cr:cr + 128, :], pj[b][:, coff:coff + 3072], q='act')
                    else:
                        for s in range(4):
                            cr = self.cv_row(s)
                            K.dma(io['cvin'][cr:cr + 32, :], pj[b][s * 32:(s + 1) * 32, coff:coff + 3072], q='act')
        K.S.barrier()

    def phase_B_mem(self, li):
        K, io, kd = self.K, self.io, self.kinds[li]
        W = W_IN[kd]
        WM = W_MIX[kd]
        L = self.L
        idb = self.c['idb']
        with ExitStack() as es:
            mkT = [K.sb(es, 'mkT', [128, 8, 256], BF16) for _ in range(2)]
            mvb = [[K.sb(es, 'mvb', [128, 1024], BF16) for _ in range(2)] for _ in range(2)]
            mraw = [K.sb(es, 'mraw', [128, 1024], F32) for _ in range(2)]
            mkb = K.sb(es, 'mkb', [128, 1024], BF16)
            with ExitStack() as es2:
                wk = K.sb(es2, 'wk', [128, 8, 1024], BF16)
                wv = K.sb(es2, 'wv', [128, 8, 1024], BF16)
                stg = [K.sb(es2, 'stg', [128, 2048], F32) for _ in range(2)]
                self.load_w_bf16(io['w_mem_k'].v(lambda h: h[li]), wk, 8, 1024, stg)
                self.load_w_bf16(io['w_mem_v'].v(lambda h: h[li]), wv, 8, 1024, stg)
                gB = K.sb(es2, 'gB', [128, D], F32)
                K.dma(gB[:], io['g_mem'].v(lambda h: h[li:li + 1, :].partition_broadcast(128)))
                sq = K.sb(es2, 'sq', [128, D], F32)
                rstd = K.sb(es2, 'rstd', [128, 1], F32)
                mnb = K.sb(es2, 'mnb', [128, D], BF16)
                mnT = K.sb(es2, 'mnT', [128, 8, 128], BF16)
                for t in range(2):
                    K.dma(mraw[0][:], io['memp'][t * 128:(t + 1) * 128, :])
                    self.rmsnorm_rstd(es2, mraw[0][:], D, sq[:], rstd[:])
                    K.stt(mnb[:], mraw[0][:], rstd[:], gB[:], ALU.mult, ALU.mult)
                    self.transpose_to(mnT, mnb, 8, idb, BF16)
                    for (wsb, oname, isk) in ((wk, 'o_mk', True), (wv, 'o_mv', False)):
                        for g in range(2):
                            bk = self.bank()
                            for k in range(8):
                                K.mm(bk[:, :], mnT[:, k, :], wsb[:, k, g * 512:(g + 1) * 512], start=(k == 0), stop=(k == 7))
                            K.copy(mraw[1][:, g * 512:(g + 1) * 512], bk[:, :], eng='act' if g == 0 else 'dve')
                        K.dma(io[oname].v(lambda h: h[li, t * 128:(t + 1) * 128, :]), mraw[1][:])
                        if isk:
                            K.copy(mkb[:], mraw[1][:], eng='pool')
                            bk = self.bank()
                            for c8 in range(8):
                                K.tr(bk.v(lambda h: h.bitcast(BF16)[:, c8 * 128:(c8 + 1) * 128]), mkb[:, c8 * 128:(c8 + 1) * 128], idb[:])
                            K.copy(mkT[0].v(lambda h: h[:, :, t * 128:(t + 1) * 128]),
                                   bk.v(lambda h: h.bitcast(BF16)[:, :].rearrange("p (k t) -> p k t", k=8)), eng='act')
                        else:
                            K.copy(mvb[0][t][:], mraw[1][:], eng='pool')
            mq = K.sb(es, 'mq', [128, 1024], F32)
            mqb = K.sb(es, 'mqb', [128, 1024], BF16)
            mqT = K.sb(es, 'mqT', [128, 8, 128], BF16)
            mx = K.sb(es, 'mx', [128, 4], F32)
            rs = K.sb(es, 'rs', [128, 4], F32)
            pb = K.sb(es, 'pb', [128, 4, 256], BF16)
            pT = K.sb(es, 'pT', [128, 4, 2, 128], BF16)
            om = K.sb(es, 'om', [128, 1024], F32)

            def attend(np_, row0, kT, vb, tcol0):
                for hp in range(2):
                    bk = self.bank()
                    for hh in range(2):
                        h = hp * 2 + hh
                        for j in range(2):
                            K.mm(bk[0:np_, hh * 256:(hh + 1) * 256], mqT[:, 2 * h + j, tcol0:tcol0 + np_], kT[:, 2 * h + j, :],
                                 start=(j == 0), stop=(j == 1))
                    K.red(mx[0:np_, hp * 2:hp * 2 + 2], bk.v(lambda h_: h_[0:np_, :].rearrange("p (a b) -> p a b", a=2)), ALU.max)
                    K.ts(mx[0:np_, hp * 2:hp * 2 + 2], mx[0:np_, hp * 2:hp * 2 + 2], -1.0 / 16.0, ALU.mult)
                    for hh in range(2):
                        h = hp * 2 + hh
                        K.act(pb[0:np_, h, :], bk[0:np_, hh * 256:(hh + 1) * 256], AF.Exp, bias=mx[0:np_, h:h + 1], scale=1.0 / 16.0,
                              accum=rs[0:np_, h:h + 1])
                K.recip(rs[0:np_, :], rs[0:np_, :])
                bk = self.bank()
                for h in range(4):
                    for mt in range(2):
                        K.tr(bk.v(lambda h_: h_.bitcast(BF16)[:, (h * 2 + mt) * 128:(h * 2 + mt) * 128 + np_]),
                             pb[0:np_, h, mt * 128:(mt + 1) * 128], idb[0:np_, 0:np_])
                K.copy(pT.v(lambda h_: h_[:, :, :, 0:np_]),
                       bk.v(lambda h_: h_.bitcast(BF16)[:, :].rearrange("p (a b t) -> p a b t", a=4, b=2)[:, :, :, 0:np_]), eng='act')
                for hp in range(2):
                    bk = self.bank()
                    for hh in range(2):
                        h = hp * 2 + hh
                        for mt in range(2):
                            K.mm(bk[0:np_, hh * 256:(hh + 1) * 256], pT[:, h, mt, 0:np_], vb[mt][:, h * 256:(h + 1) * 256],
                                 start=(mt == 0), stop=(mt == 1))
                    K.tt(om.v(lambda h_: h_[0:np_, hp * 512:(hp + 1) * 512].rearrange("p (a b) -> p a b", a=2)),
                         bk.v(lambda h_: h_[0:np_, :].rearrange("p (a b) -> p a b", a=2)),
                         rs.v(lambda h_: h_[0:np_, hp * 2:hp * 2 + 2].unsqueeze(2).to_broadcast([np_, 2, 256])), ALU.mult)
                K.dma(io['ocat'][row0:row0 + np_, WM:WM + 1024], om[0:np_, :])

            for ti in range(self.NT):
                K.dma(mq[:], io['proj'][ti * 128:(ti + 1) * 128, W - 1024:W])
                K.copy(mqb[:], mq[:], eng='pool')
                self.transpose_to(mqT, mqb, 8, idb, BF16)
                attend(128, ti * 128, mkT[0], mvb[0], 0)
            K.dma(mq[:], io['proj'][L:L + 128, W - 1024:W])
            K.copy(mqb[:], mq[:], eng='pool')
            self.transpose_to(mqT, mqb, 8, idb, BF16)
            for s in range(4):
                for t in range(2):
                    K.dma(mraw[0][:], io['cmk'].v(lambda h: h[li, s, t * 128:(t + 1) * 128, :]))
                    K.dma(mraw[1][:], io['cmv'].v(lambda h: h[li, s, t * 128:(t + 1) * 128, :]), q='act')
                    K.copy(mkb[:], mraw[0][:], eng='pool')
                    bk = self.bank()
                    for c8 in range(8):
                        K.tr(bk.v(lambda h: h.bitcast(BF16)[:, c8 * 128:(c8 + 1) * 128]), mkb[:, c8 * 128:(c8 + 1) * 128], idb[:])
                    K.copy(mkT[1].v(lambda h: h[:, :, t * 128:(t + 1) * 128]),
                           bk.v(lambda h: h.bitcast(BF16)[:, :].rearrange("p (k t) -> p k t", k=8)), eng='act')
                    K.copy(mvb[1][t][:], mraw[1][:], eng='pool')
                attend(32, L + 32 * s, mkT[1], mvb[1], 32 * s)
        K.S.barrier()

    def phase_B(self, li):
        kd = self.kinds[li]
        self.phase_B_mem(li)
        getattr(self, 'phase_B_' + kd)(li)

    def conv_tile(self, ti, sh, cw, dst, cbias=None):
        K, io, L = self.K, self.io, self.L
        for j in range(4):
            if ti < self.NT:
                r = self.cv_row(-1) + ti * 128 - 3 + j
                K.dma(sh[j][:], io['cvin'][r:r + 128, :], q='sp' if j % 2 == 0 else 'act')
            else:
                for s_ in range(4):
                    r = self.cv_row(s_) - 3 + j
                    K.dma(sh[j][s_ * 32:(s_ + 1) * 32, :], io['cvin'][r:r + 32, :], q='sp' if j % 2 == 0 else 'act')
        for j in range(4):
            K.tt(sh[j][:], sh[j][:], cw[j][:], ALU.mult, eng='dve' if j % 2 == 0 else 'pool')
        K.tt(sh[0][:], sh[0][:], sh[1][:], ALU.add)
        K.tt(sh[2][:], sh[2][:], sh[3][:], ALU.add, eng='pool')
        K.tt(sh[0][:], sh[0][:], sh[2][:], ALU.add)
        if cbias is not None:
            K.tt(sh[0][:], sh[0][:], cbias[:], ALU.add)
        K.act(dst, sh[0][:], AF.Silu)

    def conv_states(self, li, pname, sname):
        K, io, L = self.K, self.io, self.L
        K.dma(io['cvin'][0:3, :], self.c['zeros'][0:3, :])
        for s_ in range(4):
            r = self.cv_row(s_) - 3
            K.dma(io['cvin'][r:r + 3, :], io['st_c_l%d' % li].v(lambda h: h[s_]))

    def conv_state_out(self, li):
        K, io, L = self.K, self.io, self.L
        r = self.cv_row(-1) + L - 3
        K.dma(io['p_c_l%d' % li][0:3, :], io['cvin'][r:r + 3, :])
        for s_ in range(4):
            r = self.cv_row(s_) + 32 - 3
            K.dma(io['s_c_l%d' % li].v(lambda h: h[s_]), io['cvin'][r:r + 3, :])

    def phase_B_delta(self, li):
        K, io, L = self.K, self.io, self.L
        idf = self.c['idf']
        self.conv_state_out(li)
        with ExitStack() as es:
            cw = [K.sb(es, 'cw', [128, 3072], F32) for _ in range(4)]
            for j in range(4):
                K.dma(cw[j][:], io['conv_l%d' % li].v(lambda h: h[j:j + 1, :].partition_broadcast(128)))
            alB = K.sb(es, 'alB', [128, 8], F32)
            dtB = K.sb(es, 'dtB', [128, 8], F32)
            K.dma(alB[:], io['alog_l%d' % li].v(lambda h: h[0:1, :].partition_broadcast(128)))
            K.dma(dtB[:], io['dtb_l%d' % li].v(lambda h: h[0:1, :].partition_broadcast(128)))
            K.act(alB[:], alB[:], AF.Exp)
            K.ts(alB[:], alB[:], -1.0, ALU.mult)
            sh = [K.sb(es, 'sh', [128, 3072], F32) for _ in range(4)]
            qkv = K.sb(es, 'qkv', [128, 3072], F32)
            sq = K.sb(es, 'sq', [128, 2048], F32)
            ss = K.sb(es, 'ss', [128, 16], F32)
            ab = K.sb(es, 'ab', [128, 16], F32)
            ax = K.sb(es, 'ax', [128, 24], F32)
            for ti in range(self.NT + 1):
                r0 = ti * 128
                self.conv_tile(ti, sh, cw, qkv[:])
                K.act(sq[:], qkv[:, 0:2048], AF.Square)
                K.red(ss[:], sq.v(lambda h: h[:, :].rearrange("p (a b) -> p a b", a=16)), ALU.add)
                self.rsqrt(ss[:], ss[:], 1.0, EPS)
                K.ts(ss[:, 0:8], ss[:, 0:8], 128.0 ** -0.5, ALU.mult)
                K.tt(qkv.v(lambda h: h[:, 0:2048].rearrange("p (a b) -> p a b", a=16)),
                     qkv.v(lambda h: h[:, 0:2048].rearrange("p (a b) -> p a b", a=16)),
                     ss.v(lambda h: h[:, :].unsqueeze(2).to_broadcast([128, 16, 128])), ALU.mult)
                K.dma(io['cvo'][r0:r0 + 128, :], qkv[:])
                K.dma(ab[:], io['proj'][r0:r0 + 128, 4096:4112], q='act')
                K.tt(ax[:, 0:8], ab[:, 0:8], dtB[:], ALU.add)
                K.act(ax[:, 0:8], ax[:, 0:8], AF.Exp)
                K.act(ax[:, 0:8], ax[:, 0:8], AF.Ln, bias=1.0)
                K.tt(ax[:, 0:8], ax[:, 0:8], alB[:], ALU.mult)
                K.act(ax[:, 8:16], ab[:, 8:16], AF.Sigmoid)
                K.ts(ax[:, 16:24], ax[:, 8:16], -1.0, ALU.mult)
                K.dma(io['aux'][r0:r0 + 128, 0:24], ax[:])
        K.S.barrier()
        with ExitStack() as es:
            T = {}
            def t(name, shape, dt=F32):
                T[name] = K.sb(es, name, shape, dt)
            t('qkv', [128, 3072]); t('z', [128, 1024]); t('ax', [128, 24])
            t('gam', [128, 8]); t('eg', [128, 8]); t('beg', [128, 8]); t('w2', [128, 8])
            t('glast', [128, 8]); t('egl', [128, 8])
            t('Qeg4', [128, 4, 128]); t('bV4', [128, 4, 128]); t('bKeg4', [128, 4, 128]); t('Kw4', [128, 4, 128])
            t('QT4', [128, 4, 128]); t('KT4', [128, 4, 128]); t('QegT4', [128, 4, 128])
            t('gbc', [128, 4, 128]); t('d4', [128, 4, 128]); t('m1', [128, 4, 128]); t('m2', [128, 4, 128])
            t('decs4', [128, 4, 128]); t('decT4', [128, 4, 128])
            t('N4', [128, 4, 128]); t('M4', [128, 4, 128]); t('X4', [128, 4, 128])
            t('Pa', [128, 4, 128]); t('Pb', [128, 4, 128]); t('PTa', [128, 4, 128]); t('PTb', [128, 4, 128])
            t('U4', [128, 4, 128]); t('WT4', [128, 4, 128]); t('qkT4', [128, 4, 128]); t('dl4', [128, 4, 128])
            t('S', [128, 8, 128]); t('osb', [128, 8, 128]); t('osq', [128, 8, 128]); t('oss', [128, 8])
            t('ngB', [128, 128]); t('tmp4', [128, 4, 128])
            K.dma(T['ngB'][:], io['ng_l%d' % li].v(lambda h: h[0:1, :].partition_broadcast(128)))

            def V3(nm, c, j0=0, nj=4, w=None):
                w = c if w is None else w
                return T[nm].v(lambda h: h[0:c, j0:j0 + nj, 0:w])

            def chunk(c, row0):
                nl = {128: 7, 32: 5}[c]
                K.dma(T['qkv'][0:c, :], io['cvo'][row0:row0 + c, :])
                K.dma(T['z'][0:c, :], io['proj'][row0:row0 + c, 3072:4096], q='act')
                K.dma(T['ax'][0:c, :], io['aux'][row0:row0 + c, 0:24], q='act')
                bk = self.bank()
                K.mm(bk[0:c, 0:8], self.c['triu'][0:c, 0:c], T['ax'][0:c, 0:8])
                K.copy(T['gam'][0:c, :], bk[0:c, 0:8])
                K.act(T['eg'][0:c, :], T['gam'][0:c, :], AF.Exp)
                K.tt(T['beg'][0:c, :], T['eg'][0:c, :], T['ax'][0:c, 8:16], ALU.mult)
                for hg in range(2):
                    h0 = hg * 4
                    qv = lambda off: T['qkv'].v(lambda h: h[0:c, off + h0 * 128:off + (h0 + 4) * 128].rearrange("p (a b) -> p a b", a=4))
                    bc = lambda nm, lo: T[nm].v(lambda h: h[0:c, lo + h0:lo + h0 + 4].unsqueeze(2).to_broadcast([c, 4, 128]))
                    bcc = lambda nm, lo: T[nm].v(lambda h: h[0:c, lo + h0:lo + h0 + 4].unsqueeze(2).to_broadcast([c, 4, c]))
                    K.tt(V3('Qeg4', c, w=128), qv(0), bc('eg', 0), ALU.mult)
                    K.tt(V3('bV4', c, w=128), qv(2048), bc('ax', 8), ALU.mult, eng='pool')
                    K.tt(V3('bKeg4', c, w=128), qv(1024), bc('beg', 0), ALU.mult)
                    for (src_off, dst, srcT) in ((0, 'QT4', None), (1024, 'KT4', None), (None, 'QegT4', 'Qeg4')):
                        bk = self.bank()
                        for j in range(4):
                            if srcT is None:
                                src = T['qkv'][0:c, src_off + (h0 + j) * 128:src_off + (h0 + j + 1) * 128]
                            else:
                                src = T[srcT][0:c, j, :]
                            K.tr(bk[:, j * c:(j + 1) * c], src, idf[0:c, 0:c])
                        K.copy(T[dst].v(lambda h: h[:, :, 0:c]), bk.v(lambda h: h[:, 0:4 * c].rearrange("p (a b) -> p a b", a=4)),
                               eng='act')
                    bkg = self.bank()
                    for j in range(4):
                        K.ts(T['gbc'][0:c, j, :], self.c['ones'][0:c, :], T['ax'][0:c, h0 + j:h0 + j + 1], ALU.mult)
                        K.mm(bkg[:, j * c:(j + 1) * c], T['gbc'][0:c, j, :], self.c['triu'][0:c, 0:c])
                    gB3 = bkg.v(lambda h: h[0:c, 0:4 * c].rearrange("p (a b) -> p a b", a=4))
                    K.copy(T['glast'].v(lambda h: h[:, h0:h0 + 4]),
                           bkg.v(lambda h: h[:, 0:4 * c].rearrange("p (a b) -> p a b", a=4)[:, :, c - 1]))
                    K.act(T['egl'][:, h0:h0 + 4], T['glast'][:, h0:h0 + 4], AF.Exp)
                    K.tt(T['w2'][0:c, h0:h0 + 4], T['glast'][0:c, h0:h0 + 4], T['gam'][0:c, h0:h0 + 4], ALU.subtract)
                    K.act(T['w2'][0:c, h0:h0 + 4], T['w2'][0:c, h0:h0 + 4], AF.Exp)
                    K.tt(V3('Kw4', c, w=128), qv(1024), bc('w2', 0), ALU.mult, eng='pool')
                    K.tt(V3('d4', c), gB3, bcc('gam', 0), ALU.subtract)
                    K.ts(V3('m1', c), V3('d4', c), 0.0, ALU.max)
                    K.ts(V3('m2', c), V3('d4', c), 0.0, ALU.min, eng='pool')
                    K.act(V3('m1', c), V3('m1', c), AF.Exp, scale=-1.0)
                    K.act(V3('m2', c), V3('m2', c), AF.Exp)
                    K.tt(V3('decs4', c), V3('m1', c), self.c['stril'].v(lambda h: h[0:c, 0:c].unsqueeze(1).to_broadcast([c, 4, c])), ALU.mult)
                    K.tt(V3('decT4', c), V3('m2', c), self.c['triu'].v(lambda h: h[0:c, 0:c].unsqueeze(1).to_broadcast([c, 4, c])), ALU.mult,
                         eng='pool')
                    bk = self.bank()
                    for j in range(4):
                        K.mm(bk[0:c, j * c:(j + 1) * c], T['KT4'][:, j, 0:c], T['KT4'][:, j, 0:c])
                    K.tt(V3('tmp4', c), bk.v(lambda h: h[0:c, 0:4 * c].rearrange("p (a b) -> p a b", a=4)), bcc('ax', 16), ALU.mult)
                    K.tt(V3('N4', c), V3('tmp4', c), V3('decs4', c), ALU.mult)
                    bk = self.bank()
                    for j in range(4):
                        K.tr(bk[0:c, j * c:(j + 1) * c], T['N4'][0:c, j, 0:c], idf[0:c, 0:c])
                    K.copy(V3('M4', c), bk.v(lambda h: h[0:c, 0:4 * c].rearrange("p (a b) -> p a b", a=4)), eng='act')
                    K.tt(V3('X4', c), V3('M4', c), idf.v(lambda h: h[0:c, 0:c].unsqueeze(1).to_broadcast([c, 4, c])), ALU.add)
                    P, PT = 'M4', 'N4'
                    nxt = [('Pa', 'PTa'), ('Pb', 'PTb')]
                    for k in range(1, nl):
                        Pn, PTn = nxt[k % 2]
                        bk = self.bank()
                        for j in range(4):
                            K.mm(bk[0:c, j * c:(j + 1) * c], T[P][0:c, j, 0:c], T[PT][0:c, j, 0:c])
                        K.copy(V3(PTn, c), bk.v(lambda h: h[0:c, 0:4 * c].rearrange("p (a b) -> p a b", a=4)), eng='act')
                        if k < nl - 1:
                            bk2 = self.bank()
                            for j in range(4):
                                K.mm(bk2[0:c, j * c:(j + 1) * c], T[PT][0:c, j, 0:c], T[P][0:c, j, 0:c])
                            K.copy(V3(Pn, c), bk2.v(lambda h: h[0:c, 0:4 * c].rearrange("p (a b) -> p a b", a=4)), eng='pool' if False else 'dve')
                        bk3 = self.bank()
                        for j in range(4):
                            K.mm(bk3[0:c, j * c:(j + 1) * c], T[PTn][0:c, j, 0:c], T['X4'][0:c, j, 0:c])
                        K.tt(V3('X4', c), V3('X4', c), bk3.v(lambda h: h[0:c, 0:4 * c].rearrange("p (a b) -> p a b", a=4)), ALU.add)
                        P, PT = Pn, PTn
                    bk = self.bank()
                    for j in range(4):
                        K.mm(bk[0:c, j * 128:(j + 1) * 128], T['X4'][0:c, j, 0:c], T['bV4'][0:c, j, :])
                    K.copy(V3('U4', c, w=128), bk.v(lambda h: h[0:c, :].rearrange("p (a b) -> p a b", a=4)), eng='act')
                    bk = self.bank()
                    for j in range(4):
                        K.mm(bk[:, j * c:(j + 1) * c], T['bKeg4'][0:c, j, :], T['X4'][0:c, j, 0:c])
                    K.copy(T['WT4'].v(lambda h: h[:, :, 0:c]), bk.v(lambda h: h[:, 0:4 * c].rearrange("p (a b) -> p a b", a=4)), eng='act')
                    bk = self.bank()
                    for j in range(4):
                        K.mm(bk[0:c, j * c:(j + 1) * c], T['KT4'][:, j, 0:c], T['QT4'][:, j, 0:c])
                    K.tt(V3('qkT4', c), bk.v(lambda h: h[0:c, 0:4 * c].rearrange("p (a b) -> p a b", a=4)), V3('decT4', c), ALU.mult)
                    bk = self.bank()
                    for j in range(4):
                        K.mm(bk[0:c, j * 128:(j + 1) * 128], T['WT4'][:, j, 0:c], T['S'][:, h0 + j, :])
                    K.tt(V3('dl4', c, w=128), V3('U4', c, w=128), bk.v(lambda h: h[0:c, :].rearrange("p (a b) -> p a b", a=4)), ALU.subtract)
                    bk = self.bank()
                    for j in range(4):
                        K.mm(bk[0:c, j * 128:(j + 1) * 128], T['QegT4'][:, j, 0:c], T['S'][:, h0 + j, :], start=True, stop=False)
                        K.mm(bk[0:c, j * 128:(j + 1) * 128], T['qkT4'][0:c, j, 0:c], T['dl4'][0:c, j, :], start=False, stop=True)
                    K.copy(T['osb'].v(lambda h: h[0:c, h0:h0 + 4, :]), bk.v(lambda h: h[0:c, :].rearrange("p (a b) -> p a b", a=4)), eng='act')
                    bk = self.bank()
                    for j in range(4):
                        K.mm(bk[:, j * 128:(j + 1) * 128], T['Kw4'][0:c, j, :], T['dl4'][0:c, j, :])
                    Sv = T['S'].v(lambda h: h[:, h0:h0 + 4, :])
                    K.tt(Sv, Sv, T['egl'].v(lambda h: h[:, h0:h0 + 4].unsqueeze(2).to_broadcast([128, 4, 128])), ALU.mult)
                    K.tt(Sv, Sv, bk.v(lambda h: h[:, :].rearrange("p (a b) -> p a b", a=4)), ALU.add)
                ov = T['osb'].v(lambda h: h[0:c, :, :])
                K.act(T['osq'].v(lambda h: h[0:c, :, :]), ov, AF.Square)
                K.red(T['oss'][0:c, :], T['osq'].v(lambda h: h[0:c, :, :]), ALU.add)
                self.rsqrt(T['oss'][0:c, :], T['oss'][0:c, :], 1.0 / 128.0, EPS)
                K.tt(ov, ov, T['oss'].v(lambda h: h[0:c, :].unsqueeze(2).to_broadcast([c, 8, 128])), ALU.mult)
                K.tt(ov, ov, T['ngB'].v(lambda h: h[0:c, :].unsqueeze(1).to_broadcast([c, 8, 128])), ALU.mult, eng='pool')
                K.act(T['z'][0:c, :], T['z'][0:c, :], AF.Silu)
                K.tt(ov, ov, T['z'].v(lambda h: h[0:c, :].rearrange("p (a b) -> p a b", a=8)), ALU.mult)
                K.dma(io['ocat'][row0:row0 + c, 0:1024], T['osb'].v(lambda h: h[0:c, :, :].rearrange("p a b -> p (a b)")))

            K.memset(T['S'][:], 0.0)
            for n in range(self.NT):
                chunk(128, n * 128)
            K.dma(io['p_d_l%d' % li].v(lambda h: h.rearrange("a d e -> d a e")), T['S'][:])
            for s_ in range(4):
                K.dma(T['S'][:], io['st_d_l%d' % li].v(lambda h: h[s_].rearrange("a d e -> d a e")))
                chunk(32, L + 32 * s_)
                K.dma(io['s_d_l%d' % li].v(lambda h: h[s_].rearrange("a d e -> d a e")), T['S'][:])
        K.S.barrier()

    def phase_B_fox(self, li):
        K, io, L, PAST = self.K, self.io, self.L, self.PAST
        idf, idb = self.c['idf'], self.c['idb']
        NT = self.NT
        NP = PAST // 128
        for r0 in range(0, L, 512):
            r1 = min(L, r0 + 512)
            K.dma(io['p_k_l%d' % li][r0:r1, :], io['proj'][r0:r1, 1024:2048])
            K.dma(io['p_v_l%d' % li][r0:r1, :], io['proj'][r0:r1, 2048:3072], q='act')
        K.dma(io['s_k_l%d' % li][:, :], io['proj'][L:L + 128, 1024:2048])
        K.dma(io['s_v_l%d' % li][:, :], io['proj'][L:L + 128, 2048:3072])
        with ExitStack() as es:
            bfB = K.sb(es, 'bfB', [128, 16], F32)
            K.dma(bfB[:], io['bf_l%d' % li].v(lambda h: h[0:1, :].partition_broadcast(128)))
            lf = K.sb(es, 'lf', [128, 16], F32)
            lfT = K.sb(es, 'lfT', [16, max(L, PAST + 32)], F32)
            ncT = K.sb(es, 'ncT', [16, max(L, PAST + 32)], F32)

            def logf_tile(r0, col0, n):
                K.dma(lf[0:n, :], io['proj'][r0:r0 + n, 3072:3088])
                K.tt(lf[0:n, :], lf[0:n, :], bfB[0:n, :], ALU.add)
                K.act(lf[0:n, :], lf[0:n, :], AF.Exp, scale=-1.0)
                K.act(lf[0:n, :], lf[0:n, :], AF.Ln, bias=1.0)
                K.ts(lf[0:n, :], lf[0:n, :], -1.0, ALU.mult)
                K.dma(io['aux'][r0:r0 + n, 0:16], lf[0:n, :])
                bk = self.bank()
                K.tr(bk[0:16, 0:n], lf[0:n, :], idf[0:n, 0:n])
                K.copy(lfT[:, col0:col0 + n], bk[0:16, 0:n])

            def neg_cumsum(n):
                SC = getattr(self, 'scan_chunk', 2048)
                for c0 in range(0, n, SC):
                    c1 = min(n, c0 + SC)
                    init = 0.0 if c0 == 0 else ncT[:, c0 - 1:c0]
                    o, d0, d1 = ncT[:, c0:c1], lfT[:, c0:c1], self.c['zeros'][0:16, 0:c1 - c0]
                    ia = init.ap if isinstance(init, V) else init
                    K.S.op('dve', lambda e, o=o, d0=d0, d1=d1, ia=ia: e.tensor_tensor_scan(
                        out=o.ap, data0=d0.ap, data1=d1.ap, initial=ia, op0=ALU.add, op1=ALU.add),
                        reads=[lfT.name, self.c['zeros'].name, ncT.name], writes=[ncT.name])
                K.ts(lfT[:, 0:n], ncT[:, 0:n], -1.0, ALU.mult)

            for ti in range(NT):
                logf_tile(ti * 128, ti * 128, 128)
            neg_cumsum(L)
            K.dma(io['cumT'][:, 0:L], lfT[:, 0:L])
            for r0 in range(0, L, 512):
                r1 = min(L, r0 + 512)
                K.dma(io['p_f_l%d' % li][r0:r1, :], io['aux'][r0:r1, 0:16])
            for s_ in range(4):
                for t in range(NP):
                    K.dma(lf[:, :], io['c_f_l%d' % li].v(lambda h: h[s_, t * 128:(t + 1) * 128, :]))
                    bk = self.bank()
                    K.tr(bk[0:16, 0:128], lf[:, :], idf[:, :])
                    K.copy(lfT[:, t * 128:(t + 1) * 128], bk[0:16, 0:128])
                logf_tile(L + 32 * s_, PAST, 32)
                neg_cumsum(PAST + 32)
                K.dma(io['cumTs'].v(lambda h: h[s_, :, 0:PAST + 32]), lfT[:, 0:PAST + 32])
            K.dma(io['s_f_l%d' % li][:, :], io['aux'][L:L + 128, 0:16])
        K.S.barrier()
        with ExitStack() as es:
            KMAX = max(L, PAST + 32)
            NKT = max(NT, NP)
            raw = K.sb(es, 'raw', [128, NKT, 128], F32)
            rawb = K.sb(es, 'rawb', [128, NKT, 128], BF16)
            QT = K.sb(es, 'QT', [128, L], BF16)
            KT = K.sb(es, 'KT', [128, KMAX], BF16)
            Vp = K.sb(es, 'Vp', [128, NKT, 128], BF16)
            vnew = K.sb(es, 'vnew', [32, 128], BF16)
            ncr = [K.sb(es, 'ncr', [1, KMAX], F32) for _ in range(2)]
            sc = K.sb(es, 'sc', [128, KMAX], F32)
            pbf = K.sb(es, 'pbf', [128, KMAX], BF16)
            pT = K.sb(es, 'pT', [128, NKT, 128], BF16)
            pTt = K.sb(es, 'pTt', [32, 32], BF16)
            mx = K.sb(es, 'mx', [128, 1], F32)
            rs = K.sb(es, 'rs', [128, 1], F32)
            osb = K.sb(es, 'osb', [128, 128], F32)
            q32 = K.sb(es, 'q32', [32, 128], F32)
            q32b = K.sb(es, 'q32b', [32, 128], BF16)
            cmask = K.sb(es, 'cmask', [128, 128], F32)
            K.memset(cmask[:], 0.0)
            K.aselect(cmask[:], cmask[:], [[-1, 128]], ALU.is_ge, NEG, 0, 1)
            ones1 = self.c['ones']

            def load_cols(dst, src2d, ntile):
                for t0 in range(0, ntile, 8):
                    nb = min(8, ntile - t0)
                    K.dma(dst[:, t0:t0 + nb, :],
                          V(src2d.ap[t0 * 128:(t0 + nb) * 128, :].rearrange("(n p) c -> p n c", p=128), src2d.key),
                          q='sp' if (t0 // 8) % 2 == 0 else 'act')

            def to_T(dst, src_b, ntile, npart=128):
                for t0 in range(0, ntile, 8):
                    nb = min(8, ntile - t0)
                    bk = self.bank()
                    for j in range(nb):
                        K.tr(bk.v(lambda h: h.bitcast(BF16)[:, j * 128:j * 128 + npart]), src_b[0:npart, t0 + j, :], idb[0:npart, 0:npart])
                    K.copy(dst.v(lambda h: h[:, t0 * 128:(t0 + nb) * 128].rearrange("p (a b) -> p a b", a=nb)[:, :, 0:npart]),
                           bk.v(lambda h: h.bitcast(BF16)[:, 0:nb * 128].rearrange("p (a b) -> p a b", a=nb)[:, :, 0:npart]),
                           eng='act' if (t0 // 8) % 2 == 0 else 'dve')

            def attend(nq, qTv, p0, nfull, ntail, ncrow, hh, out_v):
                keys = nfull * 128 + ntail
                dw = ntail if ntail else 128
                d0 = keys - dw
                g = 0
                for c0 in range(0, keys, 512):
                    c1 = min(keys, c0 + 512)
                    bk = self.bank()
                    K.mm(bk[0:nq, 0:c1 - c0], qTv, KT[p0:p0 + 64, c0:c1], start=True, stop=False)
                    K.mm(bk[0:nq, 0:c1 - c0], ones1[0:1, 0:nq], ncrow[0:1, c0:c1], start=False, stop=True)
                    e1 = min(c1, d0)
                    if e1 > c0:
                        K.copy(sc[0:nq, c0:e1], bk[0:nq, 0:e1 - c0], eng='act' if g % 2 == 0 else 'dve')
                    if c1 > d0:
                        K.tt(sc[0:nq, d0:keys], bk[0:nq, d0 - c0:keys - c0], cmask[0:nq, 0:dw], ALU.add)
                    g += 1
                K.red(mx[0:nq, :], sc[0:nq, 0:keys], ALU.max)
                K.ts(mx[0:nq, :], mx[0:nq, :], -1.0, ALU.mult)
                K.act(pbf[0:nq, 0:keys], sc[0:nq, 0:keys], AF.Exp, bias=mx[0:nq, :], accum=rs[0:nq, :])
                K.recip(rs[0:nq, :], rs[0:nq, :])
                for t0 in range(0, nfull, 8):
                    nb = min(8, nfull - t0)
                    bk = self.bank()
                    for j in range(nb):
                        K.tr(bk.v(lambda h: h.bitcast(BF16)[:, j * 128:j * 128 + nq]), pbf[0:nq, (t0 + j) * 128:(t0 + j + 1) * 128], idb[0:nq, 0:nq])
                    K.copy(pT.v(lambda h: h[:, t0:t0 + nb, 0:nq]),
                           bk.v(lambda h: h.bitcast(BF16)[:, 0:nb * 128].rearrange("p (a b) -> p a b", a=nb)[:, :, 0:nq]),
                           eng='act' if (t0 // 8) % 2 == 0 else 'dve')
                if ntail:
                    bk = self.bank()
                    K.tr(bk.v(lambda h: h.bitcast(BF16)[0:ntail, 0:nq]), pbf[0:nq, nfull * 128:keys], idb[0:nq, 0:nq])
                    K.copy(pTt[0:ntail, 0:nq], bk.v(lambda h: h.bitcast(BF16)[0:ntail, 0:nq]))
                bk = self.bank()
                for kt in range(nfull):
                    K.mm(bk[0:nq, 0:64], pT[:, kt, 0:nq], Vp[:, kt, hh * 64:(hh + 1) * 64], start=(kt == 0), stop=(kt == nfull - 1 and not ntail))
                if ntail:
                    K.mm(bk[0:nq, 0:64], pTt[0:ntail, 0:nq], vnew[0:ntail, hh * 64:(hh + 1) * 64], start=(nfull == 0), stop=True)
                K.ts(out_v, bk[0:nq, 0:64], rs[0:nq, :], ALU.mult)

            for m in range(8):
                cq, ck, cv = m * 128, 1024 + m * 128, 2048 + m * 128
                load_cols(raw, io['proj'][0:L, cq:cq + 128], NT)
                K.ts(rawb[:, 0:NT, :], raw[:, 0:NT, :], 0.125, ALU.mult)
                to_T(QT, rawb, NT)
                load_cols(raw, io['proj'][0:L, ck:ck + 128], NT)
                K.copy(rawb[:, 0:NT, :], raw[:, 0:NT, :], eng='pool')
                to_T(KT, rawb, NT)
                load_cols(raw, io['proj'][0:L, cv:cv + 128], NT)
                K.copy(Vp[:, 0:NT, :], raw[:, 0:NT, :], eng='pool')
                for hh in range(2):
                    K.dma(ncr[hh][0:1, 0:L], io['cumT'][2 * m + hh:2 * m + hh + 1, 0:L], q='act')
                for qt in range(NT):
                    for hh in range(2):
                        p0 = 64 * hh
                        attend(128, QT[p0:p0 + 64, qt * 128:(qt + 1) * 128], p0, qt + 1, 0, ncr[hh], hh, osb[:, hh * 64:(hh + 1) * 64])
                    K.dma(io['ocat'][qt * 128:(qt + 1) * 128, cq:cq + 128], osb[:, :])
                for s_ in range(4):
                    r0 = L + 32 * s_
                    K.dma(q32[:, :], io['proj'][r0:r0 + 32, cq:cq + 128])
                    K.ts(rawb[0:32, 0, :], q32[:, :], 0.125, ALU.mult)
                    to_T(QT, rawb, 1, npart=32)
                    load_cols(raw, io['c_k_l%d' % li].v(lambda h: h[s_, :, m * 128:(m + 1) * 128]), NP)
                    K.copy(rawb[:, 0:NP, :], raw[:, 0:NP, :], eng='pool')
                    to_T(KT, rawb, NP)
                    K.dma(q32[:, :], io['proj'][r0:r0 + 32, ck:ck + 128])
                    K.copy(q32b[:, :], q32[:, :])
                    bk = self.bank()
                    K.tr(bk.v(lambda h: h.bitcast(BF16)[:, 0:32]), q32b[:, :], idb[0:32, 0:32])
                    K.copy(KT[:, PAST:PAST + 32], bk.v(lambda h: h.bitcast(BF16)[:, 0:32]))
                    load_cols(raw, io['c_v_l%d' % li].v(lambda h: h[s_, :, m * 128:(m + 1) * 128]), NP)
                    K.copy(Vp[:, 0:NP, :], raw[:, 0:NP, :], eng='pool')
                    K.dma(q32[:, :], io['proj'][r0:r0 + 32, cv:cv + 128])
                    K.copy(vnew[:, :], q32[:, :])
                    for hh in range(2):
                        K.dma(ncr[hh][0:1, 0:PAST + 32], io['cumTs'].v(lambda h: h[s_, 2 * m + hh:2 * m + hh + 1, 0:PAST + 32]), q='act')
                    for hh in range(2):
                        p0 = 64 * hh
                        attend(32, QT[p0:p0 + 64, 0:32], p0, NP, 32, ncr[hh], hh, osb[0:32, hh * 64:(hh + 1) * 64])
                    K.dma(io['ocat'][r0:r0 + 32, cq:cq + 128], osb[0:32, :])
        K.S.barrier()

    def phase_B_ssd(self, li):
        K, io, L = self.K, self.io, self.L
        idf = self.c['idf']
        self.conv_state_out(li)
        with ExitStack() as es:
            cw = [K.sb(es, 'cw', [128, 3072], F32) for _ in range(4)]
            for j in range(4):
                K.dma(cw[j][:], io['conv_l%d' % li].v(lambda h: h[j:j + 1, :].partition_broadcast(128)))
            cb = K.sb(es, 'cb', [128, 3072], F32)
            K.dma(cb[:], io['convb_l%d' % li].v(lambda h: h[0:1, :].partition_broadcast(128)))
            alB = K.sb(es, 'alB', [128, 32], F32)
            dtB = K.sb(es, 'dtB', [128, 32], F32)
            K.dma(alB[:], io['alog_l%d' % li].v(lambda h: h[0:1, :].partition_broadcast(128)))
            K.dma(dtB[:], io['dtb_l%d' % li].v(lambda h: h[0:1, :].partition_broadcast(128)))
            K.act(alB[:], alB[:], AF.Exp)
            K.ts(alB[:], alB[:], -1.0, ALU.mult)
            sh = [K.sb(es, 'sh', [128, 3072], F32) for _ in range(4)]
            xbc = K.sb(es, 'xbc', [128, 3072], F32)
            ab = K.sb(es, 'ab', [128, 32], F32)
            ax = K.sb(es, 'ax', [128, 64], F32)
            for ti in range(self.NT + 1):
                r0 = ti * 128
                self.conv_tile(ti, sh, cw, xbc[:], cbias=cb)
                K.dma(io['cvo'][r0:r0 + 128, :], xbc[:])
                K.dma(ab[:], io['proj'][r0:r0 + 128, 5120:5152], q='act')
                K.tt(ax[:, 0:32], ab[:], dtB[:], ALU.add)
                K.act(ax[:, 0:32], ax[:, 0:32], AF.Exp)
                K.act(ax[:, 0:32], ax[:, 0:32], AF.Ln, bias=1.0)
                K.tt(ax[:, 32:64], ax[:, 0:32], alB[:], ALU.mult)
                K.dma(io['aux'][r0:r0 + 128, 0:64], ax[:])
        K.S.barrier()
        with ExitStack() as es:
            T = {}
            def t(name, shape, dt=F32):
                T[name] = K.sb(es, name, shape, dt)
            t('xbc', [128, 3072]); t('z', [128, 2048]); t('ax', [128, 64])
            t('gam', [128, 32]); t('w2', [128, 32]); t('glast', [128, 32]); t('egl', [128, 32])
            t('xdt', [128, 2048]); t('xdtw', [128, 2048]); t('xd', [128, 2048]); t('ysb', [128, 2048])
            t('BT', [128, 4, 128]); t('CT', [128, 4, 128]); t('cbT', [128, 4, 128])
            t('gbc', [128, 4, 128]); t('d4', [128, 4, 128]); t('MT4', [128, 4, 128]); t('egB4', [128, 4, 128]); t('CegT4', [128, 4, 128])
            t('hT', [128, 32, 64]); t('hO', [64, 32, 128])
            t('dB', [128, 32]); t('ngB', [128, 2048]); t('rstd', [128, 1])
            K.dma(T['dB'][:], io['dsk_l%d' % li].v(lambda h: h[0:1, :].partition_broadcast(128)))
            K.dma(T['ngB'][:], io['ng_l%d' % li].v(lambda h: h[0:1, :].partition_broadcast(128)))

            def chunk(c, row0):
                K.dma(T['xbc'][0:c, :], io['cvo'][row0:row0 + c, :])
                K.dma(T['z'][0:c, :], io['proj'][row0:row0 + c, 0:2048], q='act')
                K.dma(T['ax'][0:c, :], io['aux'][row0:row0 + c, 0:64], q='act')
                bk = self.bank()
                K.mm(bk[0:c, 0:32], self.c['triu'][0:c, 0:c], T['ax'][0:c, 32:64])
                K.copy(T['gam'][0:c, :], bk[0:c, 0:32])
                x3 = T['xbc'].v(lambda h: h[0:c, 0:2048].rearrange("p (a b) -> p a b", a=32))
                K.tt(T['xdt'].v(lambda h: h[0:c, :].rearrange("p (a b) -> p a b", a=32)), x3,
                     T['ax'].v(lambda h: h[0:c, 0:32].unsqueeze(2).to_broadcast([c, 32, 64])), ALU.mult)
                K.tt(T['xd'].v(lambda h: h[0:c, :].rearrange("p (a b) -> p a b", a=32)), x3,
                     T['dB'].v(lambda h: h[0:c, :].unsqueeze(2).to_broadcast([c, 32, 64])), ALU.mult, eng='pool')
                sub = getattr(self, 'sub', 99)
                if sub <= 1:
                    return
                for g in range(4):
                    Bg = T['xbc'][0:c, 2048 + g * 128:2048 + (g + 1) * 128]
                    Cg = T['xbc'][0:c, 2560 + g * 128:2560 + (g + 1) * 128]
                    bk = self.bank()
                    K.tr(bk[:, 0:c], Bg, idf[0:c, 0:c])
                    K.tr(bk[:, c:2 * c], Cg, idf[0:c, 0:c])
                    ea = 'act' if 'A' in getattr(self, 'var', '') else 'dve'
                    K.copy(T['BT'][:, g, 0:c], bk[:, 0:c], eng=ea)
                    K.copy(T['CT'][:, g, 0:c], bk[:, c:2 * c])
                    bk = self.bank()
                    K.mm(bk[0:c, 0:c], T['BT'][:, g, 0:c], T['CT'][:, g, 0:c])
                    K.copy(T['cbT'][0:c, g, 0:c], bk[0:c, 0:c], eng=ea)
                    if sub <= 2:
                        continue
                    bky = self.bank()
                    for half in range(2):
                        h0 = 8 * g + 4 * half
                        bkg = self.bank()
                        for j in range(4):
                            K.ts(T['gbc'][0:c, j, :], self.c['ones'][0:c, :], T['ax'][0:c, 32 + h0 + j:32 + h0 + j + 1], ALU.mult)
                            K.mm(bkg[:, j * c:(j + 1) * c], T['gbc'][0:c, j, :], self.c['triu'][0:c, 0:c])
                        g3 = lambda np_: bkg.v(lambda h: h[0:np_, 0:4 * c].rearrange("p (a b) -> p a b", a=4))
                        K.copy(T['glast'][:, h0:h0 + 4], bkg.v(lambda h: h[:, 0:4 * c].rearrange("p (a b) -> p a b", a=4)[:, :, c - 1]))
                        K.copy(T['egB4'].v(lambda h: h[:, :, 0:c]), g3(128))
                        K.act(T['egB4'].v(lambda h: h[:, :, 0:c]), T['egB4'].v(lambda h: h[:, :, 0:c]), AF.Exp)
                        K.tt(T['CegT4'].v(lambda h: h[:, :, 0:c]), T['egB4'].v(lambda h: h[:, :, 0:c]),
                             T['CT'].v(lambda h: h[:, g, 0:c].unsqueeze(1).to_broadcast([128, 4, c])), ALU.mult, eng='pool')
                        d4 = T['d4'].v(lambda h: h[0:c, :, 0:c])
                        M4 = T['MT4'].v(lambda h: h[0:c, :, 0:c])
                        K.tt(d4, g3(c), T['gam'].v(lambda h: h[0:c, h0:h0 + 4].unsqueeze(2).to_broadcast([c, 4, c])), ALU.subtract)
                        K.ts(d4, d4, 0.0, ALU.min)
                        K.act(d4, d4, AF.Exp)
                        K.tt(M4, d4, self.c['triu'].v(lambda h: h[0:c, 0:c].unsqueeze(1).to_broadcast([c, 4, c])), ALU.mult)
                        K.tt(M4, M4, T['cbT'].v(lambda h: h[0:c, g, 0:c].unsqueeze(1).to_broadcast([c, 4, c])), ALU.mult)
                        for j in range(4):
                            hh = h0 + j
                            oc = (4 * half + j) * 64
                            K.mm(bky[0:c, oc:oc + 64], T['MT4'][0:c, j, 0:c], T['xdt'][0:c, hh * 64:(hh + 1) * 64], start=True, stop=False)
                            K.mm(bky[0:c, oc:oc + 64], T['CegT4'][:, j, 0:c], T['hT'][:, hh, :], start=False, stop=True)
                    K.tt(T['ysb'][0:c, g * 512:(g + 1) * 512], bky[0:c, :], T['xd'][0:c, g * 512:(g + 1) * 512], ALU.add)
                    if sub <= 3:
                        continue
                    gs = slice(8 * g, 8 * g + 8)
                    K.act(T['egl'][:, gs], T['glast'][:, gs], AF.Exp)
                    K.tt(T['w2'][0:c, gs], T['glast'][0:c, gs], T['gam'][0:c, gs], ALU.subtract)
                    K.act(T['w2'][0:c, gs], T['w2'][0:c, gs], AF.Exp)
                    K.tt(T['xdtw'].v(lambda h: h[0:c, g * 512:(g + 1) * 512].rearrange("p (a b) -> p a b", a=8)),
                         T['xdt'].v(lambda h: h[0:c, g * 512:(g + 1) * 512].rearrange("p (a b) -> p a b", a=8)),
                         T['w2'].v(lambda h: h[0:c, gs].unsqueeze(2).to_broadcast([c, 8, 64])), ALU.mult, eng='pool')
                    bkh = self.bank()
                    for j8 in range(8):
                        hh = 8 * g + j8
                        K.mm(bkh[:, j8 * 64:(j8 + 1) * 64], Bg, T['xdtw'][0:c, hh * 64:(hh + 1) * 64])
                    hv = T['hT'].v(lambda h: h[:, gs, :])
                    K.tt(hv, hv, T['egl'].v(lambda h: h[:, gs].unsqueeze(2).to_broadcast([128, 8, 64])), ALU.mult)
                    K.tt(hv, hv, bkh.v(lambda h: h[:, :].rearrange("p (a b) -> p a b", a=8)), ALU.add)
                if sub <= 4:
                    return
                K.act(T['z'][0:c, :], T['z'][0:c, :], AF.Silu)
                K.tt(T['ysb'][0:c, :], T['ysb'][0:c, :], T['z'][0:c, :], ALU.mult)
                self.rmsnorm_rstd(es, T['ysb'][0:c, :], 2048, T['z'][0:c, :], T['rstd'][0:c, :])
                K.stt(T['ysb'][0:c, :], T['ysb'][0:c, :], T['rstd'][0:c, :], T['ngB'][0:c, :], ALU.mult, ALU.mult)
                K.dma(io['ocat'][row0:row0 + c, 0:2048], T['ysb'][0:c, :])

            def state_out(dst):
                for b8 in range(8):
                    bk = self.bank()
                    for j in range(4):
                        K.tr(bk[0:64, j * 128:(j + 1) * 128], T['hT'][:, b8 * 4 + j, :], idf[:, :])
                    K.copy(T['hO'].v(lambda h: h[:, b8 * 4:b8 * 4 + 4, :]), bk.v(lambda h: h[0:64, :].rearrange("p (a b) -> p a b", a=4)))
                for q4 in range(4):
                    K.dma(dst[:, q4 * 8:(q4 + 1) * 8, :], T['hO'][:, q4 * 8:(q4 + 1) * 8, :], q='sp' if q4 % 2 == 0 else 'act')

            def state_in(src):
                for q4 in range(4):
                    K.dma(T['hO'][:, q4 * 8:(q4 + 1) * 8, :], src[:, q4 * 8:(q4 + 1) * 8, :], q='sp' if q4 % 2 == 0 else 'act')
                for b8 in range(4):
                    bk = self.bank()
                    for j in range(8):
                        K.tr(bk[:, j * 64:(j + 1) * 64], T['hO'][:, b8 * 8 + j, :], idf[0:64, 0:64])
                    K.copy(T['hT'].v(lambda h: h[:, b8 * 8:b8 * 8 + 8, :]), bk.v(lambda h: h[:, :].rearrange("p (a b) -> p a b", a=8)))

            cut = getattr(self, 'cut', 99)
            K.memset(T['hT'][:], 0.0)
            if cut >= 2:
                for n in range(self.NT):
                    chunk(128, n * 128)
            if cut >= 3:
                state_out(io['p_s_l%d' % li].v(lambda h: h.rearrange("a p n -> p a n")))
            for s_ in range(4):
                if cut >= 4:
                    state_in(io['st_s_l%d' % li].v(lambda h: h[s_].rearrange("a p n -> p a n")))
                if cut >= 5:
                    chunk(32, L + 32 * s_)
                if cut >= 6:
                    state_out(io['s_s_l%d' % li].v(lambda h: h[s_].rearrange("a p n -> p a n")))
        K.S.barrier()

    def phase_C1(self, li):
        K, io, kd = self.K, self.io, self.kinds[li]
        WO = W_MIX[kd] + 1024
        KO = WO // 128
        with ExitStack() as es:
            wo = K.sb(es, 'wo', [128, KO, D], BF16)
            stg = [K.sb(es, 'stg', [128, 2048], F32) for _ in range(2)]
            self.load_w_bf16(io['w_out_l%d' % li], wo, KO, D, stg)
            oc = [K.sb(es, 'oc', [128, WO], F32) for _ in range(2)]
            ocb = [K.sb(es, 'ocb', [128, WO], BF16) for _ in range(2)]
            ocT = [K.sb(es, 'ocT', [128, KO, 128], BF16) for _ in range(2)]
            xt = [K.sb(es, 'xt', [128, D], F32) for _ in range(2)]
            x1 = [K.sb(es, 'x1', [128, D], F32) for _ in range(2)]
            for ti in range(self.NT + 1):
                b = ti % 2
                r0 = ti * 128
                K.dma(oc[b][:], io['ocat'][r0:r0 + 128, 0:WO])
                if li == 0:
                    src = io['xp'][r0:r0 + 128, :] if ti < self.NT else io['xs'][:, :]
                else:
                    src = io['xres'][r0:r0 + 128, :]
                K.dma(xt[b][:], src, q='act')
                K.copy(ocb[b][:], oc[b][:], eng='pool')
                self.transpose_to(ocT[b], ocb[b], KO, self.c['idb'], BF16)
                for g in range(2):
                    bk = self.bank()
                    for k in range(KO):
                        K.mm(bk[:, :], ocT[b][:, k, :], wo[:, k, g * 512:(g + 1) * 512], start=(k == 0), stop=(k == KO - 1))
                    K.tt(x1[b][:, g * 512:(g + 1) * 512], xt[b][:, g * 512:(g + 1) * 512], bk[:, :], ALU.add)
                K.dma(io['xres'][r0:r0 + 128, :], x1[b][:])
        K.S.barrier()

    def phase_C2(self, li):
        K, io = self.K, self.io
        last = (li == self.DEPTH - 1)
        G = 6
        with ExitStack() as es:
            wq = K.sb(es, 'wq', [128, 8, 2048], BF16)
            stg = [K.sb(es, 'stg', [128, 2048], F32) for _ in range(2)]
            self.load_w_bf16(io['peer_wq'].v(lambda h: h[li]), wq, 8, 2048, stg)
            gB = K.sb(es, 'gB', [128, D], F32)
            K.dma(gB[:], io['g_ffn'].v(lambda h: h[li:li + 1, :].partition_broadcast(128)))
            if last:
                gF = K.sb(es, 'gF', [128, D], F32)
                K.dma(gF[:], io['g_final'].v(lambda h: h[0:1, :].partition_broadcast(128)))
            kraw = K.sb(es, 'kraw', [128, 2, 128], F32)
            kT = K.sb(es, 'kT', [128, 2, 128], F32)
            K.dma(kraw[:, 0, :], io['peer_k1'].v(lambda h: h[li]))
            K.dma(kraw[:, 1, :], io['peer_k2'].v(lambda h: h[li]))
            bk = self.bank()
            for hf in range(2):
                K.tr(bk[:, hf * 128:(hf + 1) * 128], kraw[:, hf, :], self.c['idf'][:])
            K.copy(kT.v(lambda h: h[:, :, :]), bk.v(lambda h: h[:, 0:256].rearrange("p (k t) -> p k t", k=2)))
            io16i = K.sb(es, 'io16i', [128, 16], I32)
            io16 = K.sb(es, 'io16', [128, 16], F32)
            K.iota(io16i[:], [[1, 16]], 0, 0)
            K.copy(io16[:], io16i[:])

            xt = K.sb(es, 'xt', [128, D], F32)
            sq = K.sb(es, 'sq', [128, D], F32)
            rstd = K.sb(es, 'rstd', [128, 1], F32)
            xn = K.sb(es, 'xn', [128, D], F32)
            xnb = K.sb(es, 'xnb', [128, D], BF16)
            xnT = K.sb(es, 'xnT', [128, 8, 128], BF16)
            qT = K.sb(es, 'qT', [128, 16, 128], F32)
            sc = K.sb(es, 'sc', [128, 16, 128], F32)
            sc2 = K.sb(es, 'sc2', [128, 16, 128], F32)
            v12 = K.sb(es, 'v12', [128, 16, 16], F32)
            i12 = K.sb(es, 'i12', [128, 16, 16], U32)
            i12f = K.sb(es, 'i12f', [128, 16, 16], F32)
            cand = K.sb(es, 'cand', [128, 8, 256], F32)
            cand2 = K.sb(es, 'cand2', [128, 8, 256], F32)
            tsv = K.sb(es, 'tsv', [128, 8, 16], F32)
            pos = K.sb(es, 'pos', [128, 8, 16], U32)
            ai = K.sb(es, 'ai', [128, 8, 16], U32)
            bi = K.sb(es, 'bi', [128, 8, 16], U32)
            af = K.sb(es, 'af', [128, 8, 16], F32)
            bf = K.sb(es, 'bf', [128, 8, 16], F32)
            oh = K.sb(es, 'oh', [128, 16, 16], F32)
            oh2 = K.sb(es, 'oh2', [128, 16, 16], F32)
            i1s = K.sb(es, 'i1s', [128, 8, 16], F32)
            i2s = K.sb(es, 'i2s', [128, 8, 16], F32)
            eif = K.sb(es, 'eif', [128, 128], F32)
            eidx = K.sb(es, 'eidx', [128, 128], I32)
            gate = K.sb(es, 'gate', [128, 8, 16], F32)
            gsum = K.sb(es, 'gsum', [128, 8], F32)
            dots = K.sb(es, 'dots', [128, 128], F32)
            actw = K.sb(es, 'actw', [128, 128], F32)
            gb = [K.sb(es, 'gbuf', [128, D], F32) for _ in range(G)]
            scr = K.sb(es, 'scr', [128, D], F32)
            acc = [K.sb(es, 'acc', [128, D], F32) for _ in range(2)]
            ut = io['peer_u_l%d' % li]
            vt = io['peer_v_l%d' % li]
            for ti in range(self.NT + 1):
                r0 = ti * 128
                K.dma(xt[:], io['xres'][r0:r0 + 128, :])
                self.rmsnorm_rstd(es, xt[:], D, sq[:], rstd[:])
                K.stt(xn[:], xt[:], rstd[:], gB[:], ALU.mult, ALU.mult)
                K.copy(xnb[:], xn[:], eng='pool')
                self.transpose_to(xnT, xnb, 8, self.c['idb'], BF16)
                for c4 in range(4):
                    bk = self.bank()
                    for j in range(4):
                        cc = c4 * 4 + j
                        for k in range(8):
                            K.mm(bk[:, j * 128:(j + 1) * 128], wq[:, k, cc * 128:(cc + 1) * 128], xnT[:, k, :],
                                 start=(k == 0), stop=(k == 7))
                    K.copy(qT.v(lambda h: h[:, c4 * 4:c4 * 4 + 4, :]),
                           bk.v(lambda h: h[:, :].rearrange("p (k t) -> p k t", k=4)), eng='act' if c4 % 2 == 0 else 'dve')
                for c4 in range(4):
                    bk = self.bank()
                    for j in range(4):
                        cc = c4 * 4 + j
                        K.mm(bk[:, j * 128:(j + 1) * 128], qT[:, cc, :], kT[:, cc % 2, :])
                    K.copy(sc.v(lambda h: h[:, c4 * 4:c4 * 4 + 4, :]),
                           bk.v(lambda h: h[:, :].rearrange("p (k t) -> p k t", k=4)), eng='act' if c4 % 2 == 0 else 'dve')
                for cc in range(16):
                    K.max8(v12[:, cc, 0:8], sc[:, cc, :])
                    K.maxidx(i12[:, cc, 0:8], v12[:, cc, 0:8], sc[:, cc, :])
                    K.mrep(sc2[:, cc, :], v12[:, cc, 0:8], sc[:, cc, :], NEG)
                    K.max8(v12[:, cc, 8:16], sc2[:, cc, :])
                    K.maxidx(i12[:, cc, 8:16], v12[:, cc, 8:16], sc2[:, cc, :])
                K.copy(i12f[:], i12[:])
                for h in range(8):
                    K.tt(cand.v(lambda hh: hh[:, h, :].rearrange("p (a b) -> p a b", a=16)),
                         v12.v(lambda hh: hh[:, 2 * h, :].unsqueeze(2).to_broadcast([128, 16, 16])),
                         v12.v(lambda hh: hh[:, 2 * h + 1, :].unsqueeze(1).to_broadcast([128, 16, 16])), ALU.add)
                    K.max8(tsv[:, h, 0:8], cand[:, h, :])
                    K.maxidx(pos[:, h, 0:8], tsv[:, h, 0:8], cand[:, h, :])
                    K.mrep(cand2[:, h, :], tsv[:, h, 0:8], cand[:, h, :], NEG)
                    K.max8(tsv[:, h, 8:16], cand2[:, h, :])
                    K.maxidx(pos[:, h, 8:16], tsv[:, h, 8:16], cand2[:, h, :])
                K.ts(ai[:], pos[:], 4, ALU.logical_shift_right)
                K.ts(bi[:], pos[:], 15, ALU.bitwise_and)
                K.copy(af[:], ai[:])
                K.copy(bf[:], bi[:])
                for h in range(8):
                    for (xf, half, dst) in ((af, 0, i1s), (bf, 1, i2s)):
                        K.tt(oh[:], xf.v(lambda hh: hh[:, h, :].unsqueeze(2).to_broadcast([128, 16, 16])),
                             io16.v(lambda hh: hh[:, :].unsqueeze(1).to_broadcast([128, 16, 16])), ALU.is_equal)
                        K.tt(oh2[:], oh[:], i12f.v(lambda hh: hh[:, 2 * h + half, :].unsqueeze(1).to_broadcast([128, 16, 16])), ALU.mult)
                        K.red(dst[:, h, :], oh2[:], ALU.add)
                K.stt(eif.v(lambda hh: hh[:, :].rearrange("p (a b) -> p a b", a=8)), i1s[:], 128.0, i2s[:], ALU.mult, ALU.add)
                K.ts(eif[:], eif[:], 0.0, ALU.max, 16383.0, ALU.min)
                K.copy(eidx[:], eif[:])
                K.tt(gate[:], tsv[:], tsv.v(lambda hh: hh[:, :, 0:1].to_broadcast([128, 8, 16])), ALU.subtract)
                K.act(gate[:], gate[:], AF.Exp)
                K.red(gsum[:], gate[:], ALU.add)
                K.recip(gsum[:], gsum[:])
                K.tt(gate[:], gate[:], gsum.v(lambda hh: hh[:, :].unsqueeze(2).to_broadcast([128, 8, 16])), ALU.mult)
                for j in range(128):
                    g = gb[j % G]
                    K.gather(g[:], ut[:, :], eidx[:, j:j + 1])
                    K.ttr(scr[:], g[:], xn[:], dots[:, j:j + 1])
                K.act(actw[:], dots[:], AF.Gelu)
                K.tt(actw[:], actw[:], gate.v(lambda hh: hh[:, :, :].rearrange("p a b -> p (a b)")), ALU.mult)
                for j in range(128):
                    g = gb[j % G]
                    K.gather(g[:], vt[:, :], eidx[:, j:j + 1])
                    a = acc[j % 2]
                    if j < 2:
                        K.ts(a[:], g[:], actw[:, j:j + 1], ALU.mult)
                    else:
                        K.stt(a[:], g[:], actw[:, j:j + 1], a[:], ALU.mult, ALU.add)
                K.tt(acc[0][:], acc[0][:], acc[1][:], ALU.add)
                K.tt(acc[0][:], acc[0][:], xt[:], ALU.add)
                if not last:
                    K.dma(io['xres'][r0:r0 + 128, :], acc[0][:])
                else:
                    if self.debug:
                        K.dma(io['xres'][r0:r0 + 128, :], acc[0][:])
                    self.rmsnorm_rstd(es, acc[0][:], D, sq[:], rstd[:])
                    K.stt(acc[1][:], acc[0][:], rstd[:], gF[:], ALU.mult, ALU.mult)
                    dst = io['y_p'][r0:r0 + 128, :] if ti < self.NT else io['y_s'][:, :]
                    K.dma(dst, acc[1][:])
        K.S.barrier()

    def build(self, stop=None):
        self.declare()
        with ExitStack() as es:
            self.consts(es)
            self.K.S.barrier()
            for li in range(self.DEPTH):
                if self.kinds[li] in ('delta', 'ssd'):
                    self.conv_states(li, None, None)
                self.phase_A(li)
                if stop == ('A', li):
                    break
                if stop != ('skipB', li):
                    self.phase_B(li)
                else:
                    WO = W_MIX[self.kinds[li]] + 1024
                    for ti in range(self.NT + 1):
                        self.K.dma(self.io['ocat'][ti * 128:(ti + 1) * 128, 0:WO], self.io['proj'][ti * 128:(ti + 1) * 128, 0:WO])
                    self.K.S.barrier()
                if stop == ('B', li):
                    break
                self.phase_C1(li)
                if stop == ('C1', li):
                    break
                self.phase_C2(li)
                if stop in (('C2', li), ('skipB', li)):
                    break
            self.K.S.emit()
        return self.nc


def make_in_maps(inp, L, PAST, DEPTH):
    kinds = [MIXERS[i % 3] for i in range(DEPTH)]
    f = lambda a: np.ascontiguousarray(np.asarray(a, dtype=np.float32))
    maps = []
    shared = {}
    for nm in ['g_mix', 'g_mem', 'g_ffn', 'w_mem_k', 'w_mem_v', 'peer_wq', 'peer_k1', 'peer_k2']:
        shared[nm] = f(inp[nm])
    for i in range(DEPTH):
        shared['peer_u_l%d' % i] = f(inp['peer_u'][i])
        shared['peer_v_l%d' % i] = f(inp['peer_v'][i])
    shared['g_final'] = f(inp['g_final']).reshape(1, D)
    for i, kd in enumerate(kinds):
        shared['w_in_l%d' % i] = f(inp['w_in_l%d' % i])
        shared['w_out_l%d' % i] = f(inp['w_out_l%d' % i])
        if kd == 'delta':
            shared['conv_l%d' % i] = f(inp['dn_conv_l%d' % i])
            shared['alog_l%d' % i] = f(inp['dn_alog_l%d' % i]).reshape(1, 8)
            shared['dtb_l%d' % i] = f(inp['dn_dtb_l%d' % i]).reshape(1, 8)
            shared['ng_l%d' % i] = f(inp['dn_ng_l%d' % i]).reshape(1, 128)
        elif kd == 'fox':
            shared['bf_l%d' % i] = f(inp['fox_bf_l%d' % i]).reshape(1, 16)
        else:
            shared['conv_l%d' % i] = f(inp['ssd_conv_l%d' % i])
            shared['convb_l%d' % i] = f(inp['ssd_convb_l%d' % i]).reshape(1, 3072)
            shared['alog_l%d' % i] = f(inp['ssd_alog_l%d' % i]).reshape(1, 32)
            shared['dtb_l%d' % i] = f(inp['ssd_dtb_l%d' % i]).reshape(1, 32)
            shared['dsk_l%d' % i] = f(inp['ssd_d_l%d' % i]).reshape(1, 32)
            shared['ng_l%d' % i] = f(inp['ssd_ng_l%d' % i]).reshape(1, 2048)
    for c in range(8):
        b = c % 4
        sl = slice(4 * c, 4 * c + 4)
        m = dict(shared)
        m['xp'] = f(inp['x_prompt'][b])
        m['xs'] = f(inp['x_sample'][sl]).reshape(128, D)
        m['memp'] = f(inp['mem_prompt'][b])
        m['cmk'] = f(inp['cache_mem_k'][:DEPTH, sl]).reshape(DEPTH, 4, 256, 1024)
        m['cmv'] = f(inp['cache_mem_v'][:DEPTH, sl]).reshape(DEPTH, 4, 256, 1024)
        for i, kd in enumerate(kinds):
            if kd == 'delta':
                m['st_d_l%d' % i] = f(inp['state_l%d_delta' % i][sl])
                m['st_c_l%d' % i] = f(inp['state_l%d_conv' % i][sl])
            elif kd == 'fox':
                m['c_k_l%d' % i] = f(inp['cache_l%d_k' % i][sl]).reshape(4, PAST, 1024)
                m['c_v_l%d' % i] = f(inp['cache_l%d_v' % i][sl]).reshape(4, PAST, 1024)
                m['c_f_l%d' % i] = f(inp['cache_l%d_logf' % i][sl])
            else:
                m['st_s_l%d' % i] = f(inp['state_l%d_ssm' % i][sl])
                m['st_c_l%d' % i] = f(inp['state_l%d_conv' % i][sl])
        maps.append(m)
    return maps


def run_config(inp, L, PAST, DEPTH, debug=False):
    kinds = [MIXERS[i % 3] for i in range(DEPTH)]
    mk = MK(L, PAST, DEPTH, debug=debug)
    nc = mk.build()
    maps = make_in_maps(inp, L, PAST, DEPTH)
    res = run_bass_kernel_spmd(nc, maps, core_ids=list(range(8)))
    R = res.results
    NB = 4
    f32 = np.float32

    def pstack(name, shape):
        return np.stack([np.asarray(R[b][name], dtype=f32).reshape(shape) for b in range(NB)], axis=0)

    def scat(name, shape):
        return np.concatenate([np.asarray(R[c][name], dtype=f32).reshape((4,) + tuple(shape)) for c in range(8)], axis=0)

    y_prompt = pstack('y_p', (L, D))
    y_sample = scat('y_s', (32, D))
    p_mem_k = np.stack([np.asarray(R[b]['o_mk'], dtype=f32).reshape(DEPTH, 256, 4, 256) for b in range(NB)], axis=1)
    p_mem_v = np.stack([np.asarray(R[b]['o_mv'], dtype=f32).reshape(DEPTH, 256, 4, 256) for b in range(NB)], axis=1)
    outs = {}
    for i, kd in enumerate(kinds):
        if kd == 'delta':
            outs['p_l%d_delta' % i] = pstack('p_d_l%d' % i, (8, 128, 128))
            outs['p_l%d_conv' % i] = pstack('p_c_l%d' % i, (3, 3072))
            outs['s_l%d_delta' % i] = scat('s_d_l%d' % i, (8, 128, 128))
            outs['s_l%d_conv' % i] = scat('s_c_l%d' % i, (3, 3072))
        elif kd == 'fox':
            outs['p_l%d_k' % i] = pstack('p_k_l%d' % i, (L, 16, 64))
            outs['p_l%d_v' % i] = pstack('p_v_l%d' % i, (L, 16, 64))
            outs['p_l%d_logf' % i] = pstack('p_f_l%d' % i, (L, 16))
            outs['s_l%d_k' % i] = scat('s_k_l%d' % i, (32, 16, 64))
            outs['s_l%d_v' % i] = scat('s_v_l%d' % i, (32, 16, 64))
            outs['s_l%d_logf' % i] = scat('s_f_l%d' % i, (32, 16))
        else:
            outs['p_l%d_ssm' % i] = pstack('p_s_l%d' % i, (32, 64, 128))
            outs['p_l%d_conv' % i] = pstack('p_c_l%d' % i, (3, 3072))
            outs['s_l%d_ssm' % i] = scat('s_s_l%d' % i, (32, 64, 128))
            outs['s_l%d_conv' % i] = scat('s_c_l%d' % i, (3, 3072))
    order = ['p_l0_delta', 'p_l0_conv', 'p_l1_k', 'p_l1_v', 'p_l1_logf', 'p_l2_ssm', 'p_l2_conv', 'p_l3_delta', 'p_l3_conv',
             's_l0_delta', 's_l0_conv', 's_l1_k', 's_l1_v', 's_l1_logf', 's_l2_ssm', 's_l2_conv', 's_l3_delta', 's_l3_conv']
    return (y_prompt, y_sample, p_mem_k, p_mem_v) + tuple(outs.get(k) for k in order)


def kernel(**inputs):
    L = int(np.shape(inputs['x_prompt'])[1])
    PAST = int(np.shape(inputs['cache_l1_k'])[1])
    return run_config(inputs, L, PAST, 4)
```

```python
import numpy as np
from contextlib import ExitStack
import concourse.bass as bass
import concourse.mybir as mybir
from concourse.bass_utils import run_bass_kernel_spmd

F32 = mybir.dt.float32
BF16 = mybir.dt.bfloat16
I32 = mybir.dt.int32
U32 = mybir.dt.uint32
AF = mybir.ActivationFunctionType
ALU = mybir.AluOpType
AX = mybir.AxisListType

D = 1024
EPS = 1e-6
MIXERS = ('delta', 'fox', 'ssd')
W_IN = {'delta': 5136, 'fox': 4112, 'ssd': 6176}
W_MIX = {'delta': 1024, 'fox': 1024, 'ssd': 2048}
WMAX = 6176
NEG = -1.0e30

ENG = ['pe', 'dve', 'act', 'pool', 'sp']
NLANES = {'sp': 12, 'pool': 12, 'act': 6}
ENGOBJ = {'pe': 'tensor', 'dve': 'vector', 'act': 'scalar', 'pool': 'gpsimd', 'sp': 'sync'}


class V:
    __slots__ = ('ap', 'key')

    def __init__(self, ap, key):
        self.ap = ap
        self.key = key

    def __getitem__(self, idx):
        return V(self.ap[idx], self.key)


class Tl:
    def __init__(self, h, name):
        self.h = h
        self.name = name

    def __getitem__(self, idx):
        return V(self.h[idx], self.name)

    def v(self, fn):
        return V(fn(self.h), self.name)


class Sched:
    def __init__(self, nc):
        self.nc = nc
        self.stream = {e: [] for e in ENG}
        self.count = {e: 0 for e in ENG}
        self.known = {e: {} for e in ENG}
        self.res = {}
        self.dma_n = {q: 0 for q in NLANES}
        self.nops = 0
        self.noself = False

    def _deps(self, reads, writes):
        deps = {}
        for r in reads:
            st = self.res.get(r)
            if st and st['w']:
                sk, v = st['w']
                if deps.get(sk, 0) < v:
                    deps[sk] = v
        for w in writes:
            st = self.res.get(w)
            if st:
                if st['w']:
                    sk, v = st['w']
                    if deps.get(sk, 0) < v:
                        deps[sk] = v
                for sk, v in st['r'].items():
                    if deps.get(sk, 0) < v:
                        deps[sk] = v
        return deps

    def _emit_waits(self, eng, deps):
        kn = self.known[eng]
        for sk, v in deps.items():
            if sk == ('c', eng) and (eng == 'pe' or self.noself):
                continue
            if kn.get(sk, 0) >= v:
                continue
            self.stream[eng].append(('wait', sk, v))
            kn[sk] = v

    def _record(self, sk, val, reads, writes):
        for r in reads:
            st = self.res.get(r)
            if st is None:
                st = self.res[r] = {'w': None, 'r': {}}
            if st['r'].get(sk, 0) < val:
                st['r'][sk] = val
        for w in writes:
            self.res[w] = {'w': (sk, val), 'r': {}}

    def op(self, eng, fn, reads=(), writes=()):
        deps = self._deps(reads, writes)
        self._emit_waits(eng, deps)
        self.count[eng] += 1
        sk = ('c', eng)
        self.stream[eng].append(('op', fn, sk, 1))
        self._record(sk, self.count[eng], reads, writes)
        self.nops += 1

    def dma(self, q, fn, reads=(), writes=()):
        n = self.dma_n[q]
        nl = NLANES[q]
        lane = n % nl
        val = 16 * (n // nl + 1)
        sk = ('d', q, lane)
        deps = self._deps(reads, writes)
        if n >= nl and deps.get(sk, 0) < val - 16:
            deps[sk] = val - 16
        self._emit_waits(q, deps)
        self.dma_n[q] = n + 1
        self.stream[q].append(('op', fn, sk, 16))
        self._record(sk, val, reads, writes)
        self.nops += 1

    def _all_outstanding(self):
        fin = {}
        for e in ENG:
            if self.count[e]:
                fin[('c', e)] = self.count[e]
        for q in NLANES:
            n = self.dma_n[q]
            for l in range(NLANES[q]):
                cnt = (n - l + NLANES[q] - 1) // NLANES[q] if n > l else 0
                if cnt:
                    fin[('d', q, l)] = 16 * cnt
        return fin

    def barrier(self):
        fin = self._all_outstanding()
        for e in ENG:
            kn = self.known[e]
            for sk, v in fin.items():
                if sk == ('c', e):
                    continue
                if kn.get(sk, 0) < v:
                    self.stream[e].append(('wait', sk, v))
                    kn[sk] = v
        self.res = {}

    def emit(self):
        nc = self.nc
        self.barrier()
        sks = [('c', e) for e in ENG] + [('d', q, l) for q in NLANES for l in range(NLANES[q])]
        with ExitStack() as es:
            sems = {}
            for sk in sks:
                sems[sk] = es.enter_context(nc.semaphore("s_" + "_".join(str(x) for x in sk)))
            block = es.enter_context(nc.Block())
            for e in ENG:
                items = self.stream[e]

                def body(engine, items=items):
                    for it in items:
                        if it[0] == 'wait':
                            engine.wait_ge(sems[it[1]], it[2])
                        else:
                            it[1](engine).then_inc(sems[it[2]], it[3])
                getattr(block, ENGOBJ[e])(body)


class KB:
    def __init__(self, nc):
        self.nc = nc
        self.S = Sched(nc)
        self.uid = 0

    def _rk(self, *vs):
        return [v.key for v in vs if isinstance(v, V)]

    @staticmethod
    def _a(x):
        return x.ap if isinstance(x, V) else x

    def sb(self, es, name, shape, dt=F32):
        self.uid += 1
        nm = "%s_%d" % (name, self.uid)
        return Tl(es.enter_context(self.nc.sbuf_tensor(nm, list(shape), dt)), nm)

    def ps(self, es, name, shape, dt=F32):
        self.uid += 1
        nm = "%s_%d" % (name, self.uid)
        return Tl(es.enter_context(self.nc.psum_tensor(nm, list(shape), dt)), nm)

    def dma(self, out, in_, q='sp', **kw):
        o, i = out.ap, in_.ap
        self.S.dma(q, lambda e: e.dma_start(out=o, in_=i, **kw), reads=[in_.key], writes=[out.key])

    def gather(self, out, table, idx):
        o, t, ix = out.ap, table.ap, idx.ap
        self.S.dma('pool', lambda e: e.indirect_dma_start(
            out=o, out_offset=None, in_=t,
            in_offset=bass.IndirectOffsetOnAxis(ap=ix, axis=0)),
            reads=[table.key, idx.key], writes=[out.key])

    def mm(self, out, lhsT, rhs, start=True, stop=True):
        o, l, r = out.ap, lhsT.ap, rhs.ap
        self.S.op('pe', lambda e: e.matmul(o, lhsT=l, rhs=r, start=start, stop=stop),
                  reads=[lhsT.key, rhs.key] + ([] if start else [out.key]), writes=[out.key])

    def tr(self, out, in_, ident):
        o, i, d = out.ap, in_.ap, ident.ap
        self.S.op('pe', lambda e: e.transpose(out=o, in_=i, identity=d),
                  reads=[in_.key, ident.key], writes=[out.key])

    def act(self, out, in_, func, bias=None, scale=None, accum=None):
        o, i = out.ap, in_.ap
        kw = {}
        rd = [in_.key]
        if bias is not None:
            kw['bias'] = self._a(bias)
            rd += self._rk(bias)
        if scale is not None:
            kw['scale'] = self._a(scale)
            rd += self._rk(scale)
        wr = [out.key]
        if accum is not None:
            kw['accum_out'] = accum.ap
            wr.append(accum.key)
        self.S.op('act', lambda e: e.activation(out=o, in_=i, func=func, **kw), reads=rd, writes=wr)

    def ts(self, out, in0, s1, op0, s2=None, op1=None, eng='dve', accum=None):
        o, i = out.ap, in0.ap
        a1, a2 = self._a(s1), self._a(s2)
        rd = [in0.key] + self._rk(s1, s2)
        kw = {}
        if op1 is not None:
            kw['op1'] = op1
        wr = [out.key]
        if accum is not None:
            kw['accum_out'] = accum.ap
            wr.append(accum.key)
        self.S.op(eng, lambda e: e.tensor_scalar(out=o, in0=i, scalar1=a1, scalar2=a2, op0=op0, **kw),
                  reads=rd, writes=wr)

    def tt(self, out, in0, in1, op, eng='dve'):
        o, a, b = out.ap, in0.ap, in1.ap
        self.S.op(eng, lambda e: e.tensor_tensor(out=o, in0=a, in1=b, op=op),
                  reads=[in0.key, in1.key], writes=[out.key])

    def stt(self, out, in0, scalar, in1, op0, op1):
        o, a, b = out.ap, in0.ap, in1.ap
        s = self._a(scalar)
        self.S.op('dve', lambda e: e.scalar_tensor_tensor(out=o, in0=a, scalar=s, in1=b, op0=op0, op1=op1),
                  reads=[in0.key, in1.key] + self._rk(scalar), writes=[out.key])

    def ttr(self, out, in0, in1, accum):
        o, a, b, c = out.ap, in0.ap, in1.ap, accum.ap
        self.S.op('dve', lambda e: e.scalar_tensor_tensor(out=o, in0=a, scalar=1.0, in1=b, op0=ALU.mult,
                                                          op1=ALU.mult, accum_out=c),
                  reads=[in0.key, in1.key], writes=[out.key, accum.key])

    def red(self, out, in_, op, axis=AX.X):
        o, i = out.ap, in_.ap
        self.S.op('dve', lambda e: e.tensor_reduce(out=o, in_=i, axis=axis, op=op),
                  reads=[in_.key], writes=[out.key])

    def copy(self, out, in_, eng='dve'):
        o, i = out.ap, in_.ap
        if eng == 'act':
            self.S.op('act', lambda e: e.copy(out=o, in_=i), reads=[in_.key], writes=[out.key])
        else:
            self.S.op(eng, lambda e: e.tensor_copy(out=o, in_=i), reads=[in_.key], writes=[out.key])

    def memset(self, out, val, eng='pool'):
        o = out.ap
        self.S.op(eng, lambda e: e.memset(o, val), writes=[out.key])

    def recip(self, out, in_):
        o, i = out.ap, in_.ap
        self.S.op('dve', lambda e: e.reciprocal(out=o, in_=i), reads=[in_.key], writes=[out.key])

    def aselect(self, out, in_, pattern, cmp, fill, base, cm):
        o, i = out.ap, in_.ap
        self.S.op('pool', lambda e: e.affine_select(out=o, in_=i, pattern=pattern, compare_op=cmp, fill=fill,
                                                    base=base, channel_multiplier=cm),
                  reads=[in_.key], writes=[out.key])

    def iota(self, out, pattern, base=0, cm=0):
        o = out.ap
        self.S.op('pool', lambda e: e.iota(o, pattern=pattern, base=base, channel_multiplier=cm,
                                           allow_small_or_imprecise_dtypes=True), writes=[out.key])

    def max8(self, out, in_):
        o, i = out.ap, in_.ap
        self.S.op('dve', lambda e: e.max(out=o, in_=i), reads=[in_.key], writes=[out.key])

    def maxidx(self, out, in_max, in_values):
        o, m, v = out.ap, in_max.ap, in_values.ap
        self.S.op('dve', lambda e: e.max_index(out=o, in_max=m, in_values=v),
                  reads=[in_max.key, in_values.key], writes=[out.key])

    def mrep(self, out, in_to_replace, in_values, imm):
        o, r, v = out.ap, in_to_replace.ap, in_values.ap
        self.S.op('dve', lambda e: e.match_replace(out=o, in_to_replace=r, in_values=v, imm_value=imm),
                  reads=[in_to_replace.key, in_values.key], writes=[out.key])


class MK:
    def __init__(self, L, PAST, DEPTH, debug=False):
        self.L, self.PAST, self.DEPTH, self.debug = L, PAST, DEPTH, debug
        self.NT = L // 128
        self.NTOK = L + 128
        self.nc = bass.Bass("TRN2", target_bir_lowering=False)
        self.K = KB(self.nc)
        self.kinds = [MIXERS[i % 3] for i in range(DEPTH)]
        self.io = {}
        self.in_shapes = {}

    def din(self, name, shape, dt=F32):
        t = self.nc.dram_tensor(name, list(shape), dt, kind="ExternalInput")
        self.io[name] = Tl(t.ap(), name)
        self.in_shapes[name] = list(shape)
        return self.io[name]

    def dout(self, name, shape, dt=F32):
        t = self.nc.dram_tensor(name, list(shape), dt, kind="ExternalOutput")
        self.io[name] = Tl(t.ap(), name)
        return self.io[name]

    def dscr(self, name, shape, dt=F32):
        kind = "ExternalOutput" if self.debug else "Internal"
        t = self.nc.dram_tensor(name, list(shape), dt, kind=kind)
        self.io[name] = Tl(t.ap(), name)
        return self.io[name]

    def declare(self):
        L, PAST, DP = self.L, self.PAST, self.DEPTH
        d = self.din
        d('xp', [L, D]); d('xs', [128, D]); d('memp', [256, D])
        d('cmk', [DP, 4, 256, 1024]); d('cmv', [DP, 4, 256, 1024])
        d('g_mix', [DP, D]); d('g_mem', [DP, D]); d('g_ffn', [DP, D]); d('g_final', [1, D])
        d('w_mem_k', [DP, D, 1024]); d('w_mem_v', [DP, D, 1024])
        d('peer_wq', [DP, D, 2048]); d('peer_k1', [DP, 128, 128]); d('peer_k2', [DP, 128, 128])
        for i in range(DP):
            d('peer_u_l%d' % i, [16384, D]); d('peer_v_l%d' % i, [16384, D])
        for i, kd in enumerate(self.kinds):
            d('w_in_l%d' % i, [D, W_IN[kd]]); d('w_out_l%d' % i, [W_MIX[kd] + 1024, D])
            if kd == 'delta':
                d('conv_l%d' % i, [4, 3072]); d('alog_l%d' % i, [1, 8]); d('dtb_l%d' % i, [1, 8]); d('ng_l%d' % i, [1, 128])
                d('st_d_l%d' % i, [4, 8, 128, 128]); d('st_c_l%d' % i, [4, 3, 3072])
            elif kd == 'fox':
                d('bf_l%d' % i, [1, 16])
                d('c_k_l%d' % i, [4, PAST, 1024]); d('c_v_l%d' % i, [4, PAST, 1024]); d('c_f_l%d' % i, [4, PAST, 16])
            else:
                d('conv_l%d' % i, [4, 3072]); d('convb_l%d' % i, [1, 3072]); d('alog_l%d' % i, [1, 32]); d('dtb_l%d' % i, [1, 32])
                d('dsk_l%d' % i, [1, 32]); d('ng_l%d' % i, [1, 2048])
                d('st_s_l%d' % i, [4, 32, 64, 128]); d('st_c_l%d' % i, [4, 3, 3072])
        o = self.dout
        o('y_p', [L, D]); o('y_s', [128, D])
        o('o_mk', [DP, 256, 1024]); o('o_mv', [DP, 256, 1024])
        for i, kd in enumerate(self.kinds):
            if kd == 'delta':
                o('p_d_l%d' % i, [8, 128, 128]); o('p_c_l%d' % i, [3, 3072])
                o('s_d_l%d' % i, [4, 8, 128, 128]); o('s_c_l%d' % i, [4, 3, 3072])
            elif kd == 'fox':
                o('p_k_l%d' % i, [L, 1024]); o('p_v_l%d' % i, [L, 1024]); o('p_f_l%d' % i, [L, 16])
                o('s_k_l%d' % i, [128, 1024]); o('s_v_l%d' % i, [128, 1024]); o('s_f_l%d' % i, [128, 16])
            else:
                o('p_s_l%d' % i, [32, 64, 128]); o('p_c_l%d' % i, [3, 3072])
                o('s_s_l%d' % i, [4, 32, 64, 128]); o('s_c_l%d' % i, [4, 3, 3072])
        s = self.dscr
        s('xres', [self.NTOK, D])
        s('proj', [self.NTOK, WMAX])
        s('ocat', [self.NTOK, 3072])
        s('cvin', [3 + L + 4 * 35, 3072])
        s('cvo', [self.NTOK, 3072])
        s('aux', [self.NTOK, 64])
        s('cumT', [16, L])
        for i in range(DP):
            t = self.nc.dram_tensor('uvb_l%d' % i, [16384, 2048], BF16, kind="Internal")
            self.io['uvb_l%d' % i] = Tl(t.ap(), 'uvb_l%d' % i)
        s('cumTs', [4, 16, PAST + 32])

    def cv_row(self, seq):
        return 3 if seq < 0 else 3 + self.L + 35 * seq + 3

    def consts(self, es):
        K = self.K
        c = {}
        c['idf'] = K.sb(es, 'idf', [128, 128], F32)
        K.memset(c['idf'][:], 0.0)
        K.aselect(c['idf'][:], c['idf'][:], [[-1, 128]], ALU.not_equal, 1.0, 0, 1)
        c['idb'] = K.sb(es, 'idb', [128, 128], BF16)
        K.copy(c['idb'][:], c['idf'][:])
        c['triu'] = K.sb(es, 'triu', [128, 128], F32)
        K.memset(c['triu'][:], 1.0)
        K.aselect(c['triu'][:], c['triu'][:], [[1, 128]], ALU.is_ge, 0.0, 0, -1)
        c['striu'] = K.sb(es, 'striu', [128, 128], F32)
        K.memset(c['striu'][:], 1.0)
        K.aselect(c['striu'][:], c['striu'][:], [[1, 128]], ALU.is_ge, 0.0, -1, -1)
        c['tril'] = K.sb(es, 'tril', [128, 128], F32)
        K.memset(c['tril'][:], 1.0)
        K.aselect(c['tril'][:], c['tril'][:], [[-1, 128]], ALU.is_ge, 0.0, 0, 1)
        c['stril'] = K.sb(es, 'stril', [128, 128], F32)
        K.memset(c['stril'][:], 1.0)
        K.aselect(c['stril'][:], c['stril'][:], [[-1, 128]], ALU.is_ge, 0.0, -1, 1)
        c['ones'] = K.sb(es, 'ones', [128, 128], F32)
        K.memset(c['ones'][:], 1.0)
        c['zeros'] = K.sb(es, 'zeros', [128, 3072], F32)
        K.memset(c['zeros'][:], 0.0)
        self.c = c
        self.pb = [K.ps(es, 'pb%d' % i, [128, 512], F32) for i in range(8)]
        self.pbi = 0
        self.nrot = 8

    def bank(self):
        b = self.pb[self.pbi % self.nrot]
        self.pbi += 1
        return b

    def rmsnorm_rstd(self, es_tmp, x, width, tmp, rstd):
        K = self.K
        K.act(tmp, x, AF.Square, accum=rstd)
        self.rsqrt(rstd, rstd, 1.0 / width, EPS)

    def rsqrt(self, out, in_, scale, eps):
        K = self.K
        K.ts(out, in_, scale, ALU.mult, eps, ALU.add)
        K.act(out, out, AF.Sqrt)
        K.recip(out, out)

    def load_w_bf16(self, wsrc, wdst, nk, ncols, stg):
        K = self.K
        i = 0
        for k in range(nk):
            for c0 in range(0, ncols, 2048):
                c1 = min(ncols, c0 + 2048)
                s = stg[i % len(stg)]
                K.dma(s[:, 0:c1 - c0], wsrc[k * 128:(k + 1) * 128, c0:c1], q='sp' if i % 2 == 0 else 'act')
                K.copy(wdst[:, k, c0:c1], s[:, 0:c1 - c0], eng='dve' if i % 2 == 0 else 'pool')
                i += 1

    def transpose_to(self, dst, src, nblk, ident, dt, evac_engs=('act', 'dve')):
        K = self.K
        per = 4 if dt == F32 else 8
        for b0 in range(0, nblk, per):
            nb = min(per, nblk - b0)
            bk = self.bank()
            if dt == F32:
                pv = bk
                for j in range(nb):
                    K.tr(pv[:, j * 128:(j + 1) * 128], src[:, (b0 + j) * 128:(b0 + j + 1) * 128], ident[:])
                K.copy(dst.v(lambda h: h[:, b0:b0 + nb, :]),
                       pv.v(lambda h: h[:, 0:nb * 128].rearrange("p (k t) -> p k t", k=nb)),
                       eng=evac_engs[(b0 // per) % len(evac_engs)])
            else:
                for j in range(nb):
                    K.tr(bk.v(lambda h: h.bitcast(BF16)[:, j * 128:(j + 1) * 128]),
                         src[:, (b0 + j) * 128:(b0 + j + 1) * 128], ident[:])
                K.copy(dst.v(lambda h: h[:, b0:b0 + nb, :]),
                       bk.v(lambda h: h.bitcast(BF16)[:, 0:nb * 128].rearrange("p (k t) -> p k t", k=nb)),
                       eng=evac_engs[(b0 // per) % len(evac_engs)])

    def build_uv(self):
        K, io = self.K, self.io
        RB = 4
        with ExitStack() as es:
            ld = [K.sb(es, 'uvld', [128, RB, 1024], F32) for _ in range(3)]
            cv = [K.sb(es, 'uvcv', [128, RB, 1024], BF16) for _ in range(3)]
            n = 0
            for li in range(self.DEPTH):
                dst = io['uvb_l%d' % li]
                for half, nm in ((0, 'peer_u_l%d' % li), (1, 'peer_v_l%d' % li)):
                    src = io[nm]
                    for r0 in range(0, 128, RB):
                        b = n % 3
                        q = 'sp' if n % 2 == 0 else 'act'
                        K.dma(ld[b][:], src.v(lambda h: h.rearrange("(p r) d -> p r d", p=128)[:, r0:r0 + RB, :]), q=q)
                        K.copy(cv[b][:], ld[b][:], eng=('act', 'pool', 'dve')[n % 3])
                        K.dma(dst.v(lambda h: h.rearrange("(p r) d -> p r d", p=128)[:, r0:r0 + RB, half * 1024:(half + 1) * 1024]),
                              cv[b][:], q=q)
                        n += 1
        K.S.barrier()

    def phase_A(self, li):
        K, io, kd = self.K, self.io, self.kinds[li]
        W = W_IN[kd]
        with ExitStack() as es:
            wbf = K.sb(es, 'wbf', [128, 8, W], BF16)
            stg = [K.sb(es, 'stg', [128, 2048], F32) for _ in range(2)]
            gB = K.sb(es, 'gB', [128, D], F32)
            K.dma(gB[:], io['g_mix'].v(lambda h: h[li:li + 1, :].partition_broadcast(128)))
            self.load_w_bf16(io['w_in_l%d' % li], wbf, 8, W, stg)
            xt = [K.sb(es, 'xt', [128, D], F32) for _ in range(2)]
            sq = K.sb(es, 'sq', [128, D], F32)
            rstd = [K.sb(es, 'rstd', [128, 1], F32) for _ in range(2)]
            hn = [K.sb(es, 'hn', [128, D], BF16) for _ in range(2)]
            hnT = [K.sb(es, 'hnT', [128, 8, 128], BF16) for _ in range(2)]
            pj = [K.sb(es, 'pj', [128, W], F32) for _ in range(2)]
            for ti in range(self.NT + 1):
                b = ti % 2
                r0 = ti * 128
                if li == 0:
                    src = io['xp'][r0:r0 + 128, :] if ti < self.NT else io['xs'][:, :]
                else:
                    src = io['xres'][r0:r0 + 128, :]
                K.dma(xt[b][:], src)
                self.rmsnorm_rstd(es, xt[b][:], D, sq[:], rstd[b][:])
                K.stt(hn[b][:], xt[b][:], rstd[b][:], gB[:], ALU.mult, ALU.mult)
                self.transpose_to(hnT[b], hn[b], 8, self.c['idb'], BF16)
                g = 0
                for c0 in range(0, W, 512):
                    c1 = min(W, c0 + 512)
                    bk = self.bank()
                    for k in range(8):
                        K.mm(bk[:, 0:c1 - c0], hnT[b][:, k, :], wbf[:, k, c0:c1], start=(k == 0), stop=(k == 7))
                    K.copy(pj[b][:, c0:c1], bk[:, 0:c1 - c0], eng='act' if g % 2 == 0 else 'dve')
                    g += 1
                K.dma(io['proj'][r0:r0 + 128, 0:W], pj[b][:], q='sp')
                if kd in ('delta', 'ssd'):
                    coff = 0 if kd == 'delta' else 2048
                    if ti < self.NT:
                        cr = self.cv_row(-1) + r0
                        K.dma(io['cvin'][cr:cr + 128, :], pj[b][:, coff:coff + 3072], q='act')
                    else:
                        for s in range(4):
                            cr = self.cv_row(s)
                            K.dma(io['cvin'][cr:cr + 32, :], pj[b][s * 32:(s + 1) * 32, coff:coff + 3072], q='act')
        K.S.barrier()

    def phase_B_mem(self, li):
        K, io, kd = self.K, self.io, self.kinds[li]
        W = W_IN[kd]
        WM = W_MIX[kd]
        L = self.L
        idb = self.c['idb']
        with ExitStack() as es:
            mkT = [K.sb(es, 'mkT', [128, 8, 256], BF16) for _ in range(2)]
            mvb = [[K.sb(es, 'mvb', [128, 1024], BF16) for _ in range(2)] for _ in range(2)]
            mraw = [K.sb(es, 'mraw', [128, 1024], F32) for _ in range(2)]
            mkb = K.sb(es, 'mkb', [128, 1024], BF16)
            with ExitStack() as es2:
                wk = K.sb(es2, 'wk', [128, 8, 1024], BF16)
                wv = K.sb(es2, 'wv', [128, 8, 1024], BF16)
                stg = [K.sb(es2, 'stg', [128, 2048], F32) for _ in range(2)]
                self.load_w_bf16(io['w_mem_k'].v(lambda h: h[li]), wk, 8, 1024, stg)
                self.load_w_bf16(io['w_mem_v'].v(lambda h: h[li]), wv, 8, 1024, stg)
                gB = K.sb(es2, 'gB', [128, D], F32)
                K.dma(gB[:], io['g_mem'].v(lambda h: h[li:li + 1, :].partition_broadcast(128)))
                sq = K.sb(es2, 'sq', [128, D], F32)
                rstd = K.sb(es2, 'rstd', [128, 1], F32)
                mnb = K.sb(es2, 'mnb', [128, D], BF16)
                mnT = K.sb(es2, 'mnT', [128, 8, 128], BF16)
                for t in range(2):
                    K.dma(mraw[0][:], io['memp'][t * 128:(t + 1) * 128, :])
                    self.rmsnorm_rstd(es2, mraw[0][:], D, sq[:], rstd[:])
                    K.stt(mnb[:], mraw[0][:], rstd[:], gB[:], ALU.mult, ALU.mult)
                    self.transpose_to(mnT, mnb, 8, idb, BF16)
                    for (wsb, oname, isk) in ((wk, 'o_mk', True), (wv, 'o_mv', False)):
                        for g in range(2):
                            bk = self.bank()
                            for k in range(8):
                                K.mm(bk[:, :], mnT[:, k, :], wsb[:, k, g * 512:(g + 1) * 512], start=(k == 0), stop=(k == 7))
                            K.copy(mraw[1][:, g * 512:(g + 1) * 512], bk[:, :], eng='act' if g == 0 else 'dve')
                        K.dma(io[oname].v(lambda h: h[li, t * 128:(t + 1) * 128, :]), mraw[1][:])
                        if isk:
                            K.copy(mkb[:], mraw[1][:], eng='pool')
                            bk = self.bank()
                            for c8 in range(8):
                                K.tr(bk.v(lambda h: h.bitcast(BF16)[:, c8 * 128:(c8 + 1) * 128]), mkb[:, c8 * 128:(c8 + 1) * 128], idb[:])
                            K.copy(mkT[0].v(lambda h: h[:, :, t * 128:(t + 1) * 128]),
                                   bk.v(lambda h: h.bitcast(BF16)[:, :].rearrange("p (k t) -> p k t", k=8)), eng='act')
                        else:
                            K.copy(mvb[0][t][:], mraw[1][:], eng='pool')
            mq = K.sb(es, 'mq', [128, 1024], F32)
            mqb = K.sb(es, 'mqb', [128, 1024], BF16)
            mqT = K.sb(es, 'mqT', [128, 8, 128], BF16)
            mx = K.sb(es, 'mx', [128, 4], F32)
            rs = K.sb(es, 'rs', [128, 4], F32)
            pb = K.sb(es, 'pb', [128, 4, 256], BF16)
            pT = K.sb(es, 'pT', [128, 4, 2, 128], BF16)
            om = K.sb(es, 'om', [128, 1024], F32)

            def attend(np_, row0, kT, vb, tcol0):
                for hp in range(2):
                    bk = self.bank()
                    for hh in range(2):
                        h = hp * 2 + hh
                        for j in range(2):
                            K.mm(bk[0:np_, hh * 256:(hh + 1) * 256], mqT[:, 2 * h + j, tcol0:tcol0 + np_], kT[:, 2 * h + j, :],
                                 start=(j == 0), stop=(j == 1))
                    K.red(mx[0:np_, hp * 2:hp * 2 + 2], bk.v(lambda h_: h_[0:np_, :].rearrange("p (a b) -> p a b", a=2)), ALU.max)
                    K.ts(mx[0:np_, hp * 2:hp * 2 + 2], mx[0:np_, hp * 2:hp * 2 + 2], -1.0 / 16.0, ALU.mult)
                    for hh in range(2):
                        h = hp * 2 + hh
                        K.act(pb[0:np_, h, :], bk[0:np_, hh * 256:(hh + 1) * 256], AF.Exp, bias=mx[0:np_, h:h + 1], scale=1.0 / 16.0,
                              accum=rs[0:np_, h:h + 1])
                K.recip(rs[0:np_, :], rs[0:np_, :])
                bk = self.bank()
                for h in range(4):
                    for mt in range(2):
                        K.tr(bk.v(lambda h_: h_.bitcast(BF16)[:, (h * 2 + mt) * 128:(h * 2 + mt) * 128 + np_]),
                             pb[0:np_, h, mt * 128:(mt + 1) * 128], idb[0:np_, 0:np_])
                K.copy(pT.v(lambda h_: h_[:, :, :, 0:np_]),
                       bk.v(lambda h_: h_.bitcast(BF16)[:, :].rearrange("p (a b t) -> p a b t", a=4, b=2)[:, :, :, 0:np_]), eng='act')
                for hp in range(2):
                    bk = self.bank()
                    for hh in range(2):
                        h = hp * 2 + hh
                        for mt in range(2):
                            K.mm(bk[0:np_, hh * 256:(hh + 1) * 256], pT[:, h, mt, 0:np_], vb[mt][:, h * 256:(h + 1) * 256],
                                 start=(mt == 0), stop=(mt == 1))
                    K.tt(om.v(lambda h_: h_[0:np_, hp * 512:(hp + 1) * 512].rearrange("p (a b) -> p a b", a=2)),
                         bk.v(lambda h_: h_[0:np_, :].rearrange("p (a b) -> p a b", a=2)),
                         rs.v(lambda h_: h_[0:np_, hp * 2:hp * 2 + 2].unsqueeze(2).to_broadcast([np_, 2, 256])), ALU.mult)
                K.dma(io['ocat'][row0:row0 + np_, WM:WM + 1024], om[0:np_, :])

            for ti in range(self.NT):
                K.dma(mq[:], io['proj'][ti * 128:(ti + 1) * 128, W - 1024:W])
                K.copy(mqb[:], mq[:], eng='pool')
                self.transpose_to(mqT, mqb, 8, idb, BF16)
                attend(128, ti * 128, mkT[0], mvb[0], 0)
            K.dma(mq[:], io['proj'][L:L + 128, W - 1024:W])
            K.copy(mqb[:], mq[:], eng='pool')
            self.transpose_to(mqT, mqb, 8, idb, BF16)
            for s in range(4):
                for t in range(2):
                    K.dma(mraw[0][:], io['cmk'].v(lambda h: h[li, s, t * 128:(t + 1) * 128, :]))
                    K.dma(mraw[1][:], io['cmv'].v(lambda h: h[li, s, t * 128:(t + 1) * 128, :]), q='act')
                    K.copy(mkb[:], mraw[0][:], eng='pool')
                    bk = self.bank()
                    for c8 in range(8):
                        K.tr(bk.v(lambda h: h.bitcast(BF16)[:, c8 * 128:(c8 + 1) * 128]), mkb[:, c8 * 128:(c8 + 1) * 128], idb[:])
                    K.copy(mkT[1].v(lambda h: h[:, :, t * 128:(t + 1) * 128]),
                           bk.v(lambda h: h.bitcast(BF16)[:, :].rearrange("p (k t) -> p k t", k=8)), eng='act')
                    K.copy(mvb[1][t][:], mraw[1][:], eng='pool')
                attend(32, L + 32 * s, mkT[1], mvb[1], 32 * s)
        K.S.barrier()

    def phase_B(self, li):
        kd = self.kinds[li]
        self.phase_B_mem(li)
        getattr(self, 'phase_B_' + kd)(li)

    def conv_tile(self, ti, sh, cw, dst, cbias=None):
        K, io, L = self.K, self.io, self.L
        for j in range(4):
            if ti < self.NT:
                r = self.cv_row(-1) + ti * 128 - 3 + j
                K.dma(sh[j][:], io['cvin'][r:r + 128, :], q='sp' if j % 2 == 0 else 'act')
            else:
                for s_ in range(4):
                    r = self.cv_row(s_) - 3 + j
                    K.dma(sh[j][s_ * 32:(s_ + 1) * 32, :], io['cvin'][r:r + 32, :], q='sp' if j % 2 == 0 else 'act')
        for j in range(4):
            K.tt(sh[j][:], sh[j][:], cw[j][:], ALU.mult, eng='dve' if j % 2 == 0 else 'pool')
        K.tt(sh[0][:], sh[0][:], sh[1][:], ALU.add)
        K.tt(sh[2][:], sh[2][:], sh[3][:], ALU.add, eng='pool')
        K.tt(sh[0][:], sh[0][:], sh[2][:], ALU.add)
        if cbias is not None:
            K.tt(sh[0][:], sh[0][:], cbias[:], ALU.add)
        K.act(dst, sh[0][:], AF.Silu)

    def conv_states(self, li, pname, sname):
        K, io, L = self.K, self.io, self.L
        K.dma(io['cvin'][0:3, :], self.c['zeros'][0:3, :])
        for s_ in range(4):
            r = self.cv_row(s_) - 3
            K.dma(io['cvin'][r:r + 3, :], io['st_c_l%d' % li].v(lambda h: h[s_]))

    def conv_state_out(self, li):
        K, io, L = self.K, self.io, self.L
        r = self.cv_row(-1) + L - 3
        K.dma(io['p_c_l%d' % li][0:3, :], io['cvin'][r:r + 3, :])
        for s_ in range(4):
            r = self.cv_row(s_) + 32 - 3
            K.dma(io['s_c_l%d' % li].v(lambda h: h[s_]), io['cvin'][r:r + 3, :])

    def phase_B_delta(self, li):
        K, io, L = self.K, self.io, self.L
        idf = self.c['idf']
        self.conv_state_out(li)
        with ExitStack() as es:
            cw = [K.sb(es, 'cw', [128, 3072], F32) for _ in range(4)]
            for j in range(4):
                K.dma(cw[j][:], io['conv_l%d' % li].v(lambda h: h[j:j + 1, :].partition_broadcast(128)))
            alB = K.sb(es, 'alB', [128, 8], F32)
            dtB = K.sb(es, 'dtB', [128, 8], F32)
            K.dma(alB[:], io['alog_l%d' % li].v(lambda h: h[0:1, :].partition_broadcast(128)))
            K.dma(dtB[:], io['dtb_l%d' % li].v(lambda h: h[0:1, :].partition_broadcast(128)))
            K.act(alB[:], alB[:], AF.Exp)
            K.ts(alB[:], alB[:], -1.0, ALU.mult)
            sh = [K.sb(es, 'sh', [128, 3072], F32) for _ in range(4)]
            qkv = K.sb(es, 'qkv', [128, 3072], F32)
            sq = K.sb(es, 'sq', [128, 2048], F32)
            ss = K.sb(es, 'ss', [128, 16], F32)
            ab = K.sb(es, 'ab', [128, 16], F32)
            ax = K.sb(es, 'ax', [128, 24], F32)
            for ti in range(self.NT + 1):
                r0 = ti * 128
                self.conv_tile(ti, sh, cw, qkv[:])
                K.act(sq[:], qkv[:, 0:2048], AF.Square)
                K.red(ss[:], sq.v(lambda h: h[:, :].rearrange("p (a b) -> p a b", a=16)), ALU.add)
                self.rsqrt(ss[:], ss[:], 1.0, EPS)
                K.ts(ss[:, 0:8], ss[:, 0:8], 128.0 ** -0.5, ALU.mult)
                K.tt(qkv.v(lambda h: h[:, 0:2048].rearrange("p (a b) -> p a b", a=16)),
                     qkv.v(lambda h: h[:, 0:2048].rearrange("p (a b) -> p a b", a=16)),
                     ss.v(lambda h: h[:, :].unsqueeze(2).to_broadcast([128, 16, 128])), ALU.mult)
                K.dma(io['cvo'][r0:r0 + 128, :], qkv[:])
                K.dma(ab[:], io['proj'][r0:r0 + 128, 4096:4112], q='act')
                K.tt(ax[:, 0:8], ab[:, 0:8], dtB[:], ALU.add)
                K.act(ax[:, 0:8], ax[:, 0:8], AF.Exp)
                K.act(ax[:, 0:8], ax[:, 0:8], AF.Ln, bias=1.0)
                K.tt(ax[:, 0:8], ax[:, 0:8], alB[:], ALU.mult)
                K.act(ax[:, 8:16], ab[:, 8:16], AF.Sigmoid)
                K.ts(ax[:, 16:24], ax[:, 8:16], -1.0, ALU.mult)
                K.dma(io['aux'][r0:r0 + 128, 0:24], ax[:])
        K.S.barrier()
        with ExitStack() as es:
            T = {}
            def t(name, shape, dt=F32):
                T[name] = K.sb(es, name, shape, dt)
            t('qkv', [128, 3072]); t('z', [128, 1024]); t('ax', [128, 24])
            t('gam', [128, 8]); t('eg', [128, 8]); t('beg', [128, 8]); t('w2', [128, 8])
            t('glast', [128, 8]); t('egl', [128, 8])
            t('Qeg4', [128, 4, 128]); t('bV4', [128, 4, 128]); t('bKeg4', [128, 4, 128]); t('Kw4', [128, 4, 128])
            t('QT4', [128, 4, 128]); t('KT4', [128, 4, 128]); t('QegT4', [128, 4, 128])
            t('gbc', [128, 4, 128]); t('d4', [128, 4, 128]); t('m1', [128, 4, 128]); t('m2', [128, 4, 128])
            t('decs4', [128, 4, 128]); t('decT4', [128, 4, 128])
            t('N4', [128, 4, 128]); t('M4', [128, 4, 128]); t('X4', [128, 4, 128])
            t('Pa', [128, 4, 128]); t('Pb', [128, 4, 128]); t('PTa', [128, 4, 128]); t('PTb', [128, 4, 128])
            t('U4', [128, 4, 128]); t('WT4', [128, 4, 128]); t('qkT4', [128, 4, 128]); t('dl4', [128, 4, 128])
            t('S', [128, 8, 128]); t('osb', [128, 8, 128]); t('osq', [128, 8, 128]); t('oss', [128, 8])
            t('ngB', [128, 128]); t('tmp4', [128, 4, 128])
            K.dma(T['ngB'][:], io['ng_l%d' % li].v(lambda h: h[0:1, :].partition_broadcast(128)))

            def V3(nm, c, j0=0, nj=4, w=None):
                w = c if w is None else w
                return T[nm].v(lambda h: h[0:c, j0:j0 + nj, 0:w])

            def chunk(c, row0):
                nl = {128: 7, 32: 5}[c]
                K.dma(T['qkv'][0:c, :], io['cvo'][row0:row0 + c, :])
                K.dma(T['z'][0:c, :], io['proj'][row0:row0 + c, 3072:4096], q='act')
                K.dma(T['ax'][0:c, :], io['aux'][row0:row0 + c, 0:24], q='act')
                bk = self.bank()
                K.mm(bk[0:c, 0:8], self.c['triu'][0:c, 0:c], T['ax'][0:c, 0:8])
                K.copy(T['gam'][0:c, :], bk[0:c, 0:8])
                K.act(T['eg'][0:c, :], T['gam'][0:c, :], AF.Exp)
                K.tt(T['beg'][0:c, :], T['eg'][0:c, :], T['ax'][0:c, 8:16], ALU.mult)
                for hg in range(2):
                    h0 = hg * 4
                    qv = lambda off: T['qkv'].v(lambda h: h[0:c, off + h0 * 128:off + (h0 + 4) * 128].rearrange("p (a b) -> p a b", a=4))
                    bc = lambda nm, lo: T[nm].v(lambda h: h[0:c, lo + h0:lo + h0 + 4].unsqueeze(2).to_broadcast([c, 4, 128]))
                    bcc = lambda nm, lo: T[nm].v(lambda h: h[0:c, lo + h0:lo + h0 + 4].unsqueeze(2).to_broadcast([c, 4, c]))
                    K.tt(V3('Qeg4', c, w=128), qv(0), bc('eg', 0), ALU.mult)
                    K.tt(V3('bV4', c, w=128), qv(2048), bc('ax', 8), ALU.mult, eng='pool')
                    K.tt(V3('bKeg4', c, w=128), qv(1024), bc('beg', 0), ALU.mult)
                    for (src_off, dst, srcT) in ((0, 'QT4', None), (1024, 'KT4', None), (None, 'QegT4', 'Qeg4')):
                        bk = self.bank()
                        for j in range(4):
                            if srcT is None:
                                src = T['qkv'][0:c, src_off + (h0 + j) * 128:src_off + (h0 + j + 1) * 128]
                            else:
                                src = T[srcT][0:c, j, :]
                            K.tr(bk[:, j * c:(j + 1) * c], src, idf[0:c, 0:c])
                        K.copy(T[dst].v(lambda h: h[:, :, 0:c]), bk.v(lambda h: h[:, 0:4 * c].rearrange("p (a b) -> p a b", a=4)),
                               eng='act')
                    bkg = self.bank()
                    for j in range(4):
                        K.ts(T['gbc'][0:c, j, :], self.c['ones'][0:c, :], T['ax'][0:c, h0 + j:h0 + j + 1], ALU.mult)
                        K.mm(bkg[:, j * c:(j + 1) * c], T['gbc'][0:c, j, :], self.c['triu'][0:c, 0:c])
                    gB3 = bkg.v(lambda h: h[0:c, 0:4 * c].rearrange("p (a b) -> p a b", a=4))
                    K.copy(T['glast'].v(lambda h: h[:, h0:h0 + 4]),
                           bkg.v(lambda h: h[:, 0:4 * c].rearrange("p (a b) -> p a b", a=4)[:, :, c - 1]))
                    K.act(T['egl'][:, h0:h0 + 4], T['glast'][:, h0:h0 + 4], AF.Exp)
                    K.tt(T['w2'][0:c, h0:h0 + 4], T['glast'][0:c, h0:h0 + 4], T['gam'][0:c, h0:h0 + 4], ALU.subtract)
                    K.act(T['w2'][0:c, h0:h0 + 4], T['w2'][0:c, h0:h0 + 4], AF.Exp)
                    K.tt(V3('Kw4', c, w=128), qv(1024), bc('w2', 0), ALU.mult, eng='pool')
                    K.tt(V3('d4', c), gB3, bcc('gam', 0), ALU.subtract)
                    K.ts(V3('m1', c), V3('d4', c), 0.0, ALU.max)
                    K.ts(V3('m2', c), V3('d4', c), 0.0, ALU.min, eng='pool')
                    K.act(V3('m1', c), V3('m1', c), AF.Exp, scale=-1.0)
                    K.act(V3('m2', c), V3('m2', c), AF.Exp)
                    K.tt(V3('decs4', c), V3('m1', c), self.c['stril'].v(lambda h: h[0:c, 0:c].unsqueeze(1).to_broadcast([c, 4, c])), ALU.mult)
                    K.tt(V3('decT4', c), V3('m2', c), self.c['triu'].v(lambda h: h[0:c, 0:c].unsqueeze(1).to_broadcast([c, 4, c])), ALU.mult,
                         eng='pool')
                    bk = self.bank()
                    for j in range(4):
                        K.mm(bk[0:c, j * c:(j + 1) * c], T['KT4'][:, j, 0:c], T['KT4'][:, j, 0:c])
                    K.tt(V3('tmp4', c), bk.v(lambda h: h[0:c, 0:4 * c].rearrange("p (a b) -> p a b", a=4)), bcc('ax', 16), ALU.mult)
                    K.tt(V3('N4', c), V3('tmp4', c), V3('decs4', c), ALU.mult)
                    bk = self.bank()
                    for j in range(4):
                        K.tr(bk[0:c, j * c:(j + 1) * c], T['N4'][0:c, j, 0:c], idf[0:c, 0:c])
                    K.copy(V3('M4', c), bk.v(lambda h: h[0:c, 0:4 * c].rearrange("p (a b) -> p a b", a=4)), eng='act')
                    K.tt(V3('X4', c), V3('M4', c), idf.v(lambda h: h[0:c, 0:c].unsqueeze(1).to_broadcast([c, 4, c])), ALU.add)
                    P, PT = 'M4', 'N4'
                    nxt = [('Pa', 'PTa'), ('Pb', 'PTb')]
                    for k in range(1, nl):
                        Pn, PTn = nxt[k % 2]
                        bk = self.bank()
                        for j in range(4):
                            K.mm(bk[0:c, j * c:(j + 1) * c], T[P][0:c, j, 0:c], T[PT][0:c, j, 0:c])
                        K.copy(V3(PTn, c), bk.v(lambda h: h[0:c, 0:4 * c].rearrange("p (a b) -> p a b", a=4)), eng='act')
                        if k < nl - 1:
                            bk2 = self.bank()
                            for j in range(4):
                                K.mm(bk2[0:c, j * c:(j + 1) * c], T[PT][0:c, j, 0:c], T[P][0:c, j, 0:c])
                            K.copy(V3(Pn, c), bk2.v(lambda h: h[0:c, 0:4 * c].rearrange("p (a b) -> p a b", a=4)), eng='pool' if False else 'dve')
                        bk3 = self.bank()
                        for j in range(4):
                            K.mm(bk3[0:c, j * c:(j + 1) * c], T[PTn][0:c, j, 0:c], T['X4'][0:c, j, 0:c])
                        K.tt(V3('X4', c), V3('X4', c), bk3.v(lambda h: h[0:c, 0:4 * c].rearrange("p (a b) -> p a b", a=4)), ALU.add)
                        P, PT = Pn, PTn
                    bk = self.bank()
                    for j in range(4):
                        K.mm(bk[0:c, j * 128:(j + 1) * 128], T['X4'][0:c, j, 0:c], T['bV4'][0:c, j, :])
                    K.copy(V3('U4', c, w=128), bk.v(lambda h: h[0:c, :].rearrange("p (a b) -> p a b", a=4)), eng='act')
                    bk = self.bank()
                    for j in range(4):
                        K.mm(bk[:, j * c:(j + 1) * c], T['bKeg4'][0:c, j, :], T['X4'][0:c, j, 0:c])
                    K.copy(T['WT4'].v(lambda h: h[:, :, 0:c]), bk.v(lambda h: h[:, 0:4 * c].rearrange("p (a b) -> p a b", a=4)), eng='act')
                    bk = self.bank()
                    for j in range(4):
                        K.mm(bk[0:c, j * c:(j + 1) * c], T['KT4'][:, j, 0:c], T['QT4'][:, j, 0:c])
                    K.tt(V3('qkT4', c), bk.v(lambda h: h[0:c, 0:4 * c].rearrange("p (a b) -> p a b", a=4)), V3('decT4', c), ALU.mult)
                    bk = self.bank()
                    for j in range(4):
                        K.mm(bk[0:c, j * 128:(j + 1) * 128], T['WT4'][:, j, 0:c], T['S'][:, h0 + j, :])
                    K.tt(V3('dl4', c, w=128), V3('U4', c, w=128), bk.v(lambda h: h[0:c, :].rearrange("p (a b) -> p a b", a=4)), ALU.subtract)
                    bk = self.bank()
                    for j in range(4):
                        K.mm(bk[0:c, j * 128:(j + 1) * 128], T['QegT4'][:, j, 0:c], T['S'][:, h0 + j, :], start=True, stop=False)
                        K.mm(bk[0:c, j * 128:(j + 1) * 128], T['qkT4'][0:c, j, 0:c], T['dl4'][0:c, j, :], start=False, stop=True)
                    K.copy(T['osb'].v(lambda h: h[0:c, h0:h0 + 4, :]), bk.v(lambda h: h[0:c, :].rearrange("p (a b) -> p a b", a=4)), eng='act')
                    bk = self.bank()
                    for j in range(4):
                        K.mm(bk[:, j * 128:(j + 1) * 128], T['Kw4'][0:c, j, :], T['dl4'][0:c, j, :])
                    Sv = T['S'].v(lambda h: h[:, h0:h0 + 4, :])
                    K.tt(Sv, Sv, T['egl'].v(lambda h: h[:, h0:h0 + 4].unsqueeze(2).to_broadcast([128, 4, 128])), ALU.mult)
                    K.tt(Sv, Sv, bk.v(lambda h: h[:, :].rearrange("p (a b) -> p a b", a=4)), ALU.add)
                ov = T['osb'].v(lambda h: h[0:c, :, :])
                K.act(T['osq'].v(lambda h: h[0:c, :, :]), ov, AF.Square)
                K.red(T['oss'][0:c, :], T['osq'].v(lambda h: h[0:c, :, :]), ALU.add)
                self.rsqrt(T['oss'][0:c, :], T['oss'][0:c, :], 1.0 / 128.0, EPS)
                K.tt(ov, ov, T['oss'].v(lambda h: h[0:c, :].unsqueeze(2).to_broadcast([c, 8, 128])), ALU.mult)
                K.tt(ov, ov, T['ngB'].v(lambda h: h[0:c, :].unsqueeze(1).to_broadcast([c, 8, 128])), ALU.mult, eng='pool')
                K.act(T['z'][0:c, :], T['z'][0:c, :], AF.Silu)
                K.tt(ov, ov, T['z'].v(lambda h: h[0:c, :].rearrange("p (a b) -> p a b", a=8)), ALU.mult)
                K.dma(io['ocat'][row0:row0 + c, 0:1024], T['osb'].v(lambda h: h[0:c, :, :].rearrange("p a b -> p (a b)")))

            K.memset(T['S'][:], 0.0)
            for n in range(self.NT):
                chunk(128, n * 128)
            K.dma(io['p_d_l%d' % li].v(lambda h: h.rearrange("a d e -> d a e")), T['S'][:])
            for s_ in range(4):
                K.dma(T['S'][:], io['st_d_l%d' % li].v(lambda h: h[s_].rearrange("a d e -> d a e")))
                chunk(32, L + 32 * s_)
                K.dma(io['s_d_l%d' % li].v(lambda h: h[s_].rearrange("a d e -> d a e")), T['S'][:])
        K.S.barrier()

    def phase_B_fox(self, li):
        K, io, L, PAST = self.K, self.io, self.L, self.PAST
        idf, idb = self.c['idf'], self.c['idb']
        NT = self.NT
        NP = PAST // 128
        for r0 in range(0, L, 512):
            r1 = min(L, r0 + 512)
            K.dma(io['p_k_l%d' % li][r0:r1, :], io['proj'][r0:r1, 1024:2048])
            K.dma(io['p_v_l%d' % li][r0:r1, :], io['proj'][r0:r1, 2048:3072], q='act')
        K.dma(io['s_k_l%d' % li][:, :], io['proj'][L:L + 128, 1024:2048])
        K.dma(io['s_v_l%d' % li][:, :], io['proj'][L:L + 128, 2048:3072])
        with ExitStack() as es:
            bfB = K.sb(es, 'bfB', [128, 16], F32)
            K.dma(bfB[:], io['bf_l%d' % li].v(lambda h: h[0:1, :].partition_broadcast(128)))
            lf = K.sb(es, 'lf', [128, 16], F32)
            lfT = K.sb(es, 'lfT', [16, max(L, PAST + 32)], F32)
            ncT = K.sb(es, 'ncT', [16, max(L, PAST + 32)], F32)

            def logf_tile(r0, col0, n):
                K.dma(lf[0:n, :], io['proj'][r0:r0 + n, 3072:3088])
                K.tt(lf[0:n, :], lf[0:n, :], bfB[0:n, :], ALU.add)
                K.act(lf[0:n, :], lf[0:n, :], AF.Exp, scale=-1.0)
                K.act(lf[0:n, :], lf[0:n, :], AF.Ln, bias=1.0)
                K.ts(lf[0:n, :], lf[0:n, :], -1.0, ALU.mult)
                K.dma(io['aux'][r0:r0 + n, 0:16], lf[0:n, :])
                bk = self.bank()
                K.tr(bk[0:16, 0:n], lf[0:n, :], idf[0:n, 0:n])
                K.copy(lfT[:, col0:col0 + n], bk[0:16, 0:n])

            def neg_cumsum(n):
                SC = getattr(self, 'scan_chunk', 2048)
                for c0 in range(0, n, SC):
                    c1 = min(n, c0 + SC)
                    init = 0.0 if c0 == 0 else ncT[:, c0 - 1:c0]
                    o, d0, d1 = ncT[:, c0:c1], lfT[:, c0:c1], self.c['zeros'][0:16, 0:c1 - c0]
                    ia = init.ap if isinstance(init, V) else init
                    K.S.op('dve', lambda e, o=o, d0=d0, d1=d1, ia=ia: e.tensor_tensor_scan(
                        out=o.ap, data0=d0.ap, data1=d1.ap, initial=ia, op0=ALU.add, op1=ALU.add),
                        reads=[lfT.name, self.c['zeros'].name, ncT.name], writes=[ncT.name])
                K.ts(lfT[:, 0:n], ncT[:, 0:n], -1.0, ALU.mult)

            for ti in range(NT):
                logf_tile(ti * 128, ti * 128, 128)
            neg_cumsum(L)
            K.dma(io['cumT'][:, 0:L], lfT[:, 0:L])
            for r0 in range(0, L, 512):
                r1 = min(L, r0 + 512)
                K.dma(io['p_f_l%d' % li][r0:r1, :], io['aux'][r0:r1, 0:16])
            for s_ in range(4):
                for t in range(NP):
                    K.dma(lf[:, :], io['c_f_l%d' % li].v(lambda h: h[s_, t * 128:(t + 1) * 128, :]))
                    bk = self.bank()
                    K.tr(bk[0:16, 0:128], lf[:, :], idf[:, :])
                    K.copy(lfT[:, t * 128:(t + 1) * 128], bk[0:16, 0:128])
                logf_tile(L + 32 * s_, PAST, 32)
                neg_cumsum(PAST + 32)
                K.dma(io['cumTs'].v(lambda h: h[s_, :, 0:PAST + 32]), lfT[:, 0:PAST + 32])
            K.dma(io['s_f_l%d' % li][:, :], io['aux'][L:L + 128, 0:16])
        K.S.barrier()
        with ExitStack() as es:
            KMAX = max(L, PAST + 32)
            NKT = max(NT, NP)
            raw = K.sb(es, 'raw', [128, NKT, 128], F32)
            rawb = K.sb(es, 'rawb', [128, NKT, 128], BF16)
            QT = K.sb(es, 'QT', [128, L], BF16)
            KT = K.sb(es, 'KT', [128, KMAX], BF16)
            Vp = K.sb(es, 'Vp', [128, NKT, 128], BF16)
            vnew = K.sb(es, 'vnew', [32, 128], BF16)
            ncr = [K.sb(es, 'ncr', [1, KMAX], F32) for _ in range(2)]
            sc = K.sb(es, 'sc', [128, KMAX], F32)
            pbf = K.sb(es, 'pbf', [128, KMAX], BF16)
            pT = K.sb(es, 'pT', [128, NKT, 128], BF16)
            pTt = K.sb(es, 'pTt', [32, 32], BF16)
            mx = K.sb(es, 'mx', [128, 1], F32)
            rs = K.sb(es, 'rs', [128, 1], F32)
            osb = K.sb(es, 'osb', [128, 128], F32)
            q32 = K.sb(es, 'q32', [32, 128], F32)
            q32b = K.sb(es, 'q32b', [32, 128], BF16)
            cmask = K.sb(es, 'cmask', [128, 128], F32)
            K.memset(cmask[:], 0.0)
            K.aselect(cmask[:], cmask[:], [[-1, 128]], ALU.is_ge, NEG, 0, 1)
            ones1 = self.c['ones']

            def load_cols(dst, src2d, ntile):
                for t0 in range(0, ntile, 8):
                    nb = min(8, ntile - t0)
                    K.dma(dst[:, t0:t0 + nb, :],
                          V(src2d.ap[t0 * 128:(t0 + nb) * 128, :].rearrange("(n p) c -> p n c", p=128), src2d.key),
                          q='sp' if (t0 // 8) % 2 == 0 else 'act')

            def to_T(dst, src_b, ntile, npart=128):
                for t0 in range(0, ntile, 8):
                    nb = min(8, ntile - t0)
                    bk = self.bank()
                    for j in range(nb):
                        K.tr(bk.v(lambda h: h.bitcast(BF16)[:, j * 128:j * 128 + npart]), src_b[0:npart, t0 + j, :], idb[0:npart, 0:npart])
                    K.copy(dst.v(lambda h: h[:, t0 * 128:(t0 + nb) * 128].rearrange("p (a b) -> p a b", a=nb)[:, :, 0:npart]),
                           bk.v(lambda h: h.bitcast(BF16)[:, 0:nb * 128].rearrange("p (a b) -> p a b", a=nb)[:, :, 0:npart]),
                           eng='act' if (t0 // 8) % 2 == 0 else 'dve')

            def attend(nq, qTv, p0, nfull, ntail, ncrow, hh, out_v):
                keys = nfull * 128 + ntail
                dw = ntail if ntail else 128
                d0 = keys - dw
                g = 0
                for c0 in range(0, keys, 512):
                    c1 = min(keys, c0 + 512)
                    bk = self.bank()
                    K.mm(bk[0:nq, 0:c1 - c0], qTv, KT[p0:p0 + 64, c0:c1], start=True, stop=False)
                    K.mm(bk[0:nq, 0:c1 - c0], ones1[0:1, 0:nq], ncrow[0:1, c0:c1], start=False, stop=True)
                    e1 = min(c1, d0)
                    if e1 > c0:
                        K.copy(sc[0:nq, c0:e1], bk[0:nq, 0:e1 - c0], eng='act' if g % 2 == 0 else 'dve')
                    if c1 > d0:
                        K.tt(sc[0:nq, d0:keys], bk[0:nq, d0 - c0:keys - c0], cmask[0:nq, 0:dw], ALU.add)
                    g += 1
                K.red(mx[0:nq, :], sc[0:nq, 0:keys], ALU.max)
                K.ts(mx[0:nq, :], mx[0:nq, :], -1.0, ALU.mult)
                K.act(pbf[0:nq, 0:keys], sc[0:nq, 0:keys], AF.Exp, bias=mx[0:nq, :], accum=rs[0:nq, :])
                K.recip(rs[0:nq, :], rs[0:nq, :])
                for t0 in range(0, nfull, 8):
                    nb = min(8, nfull - t0)
                    bk = self.bank()
                    for j in range(nb):
                        K.tr(bk.v(lambda h: h.bitcast(BF16)[:, j * 128:j * 128 + nq]), pbf[0:nq, (t0 + j) * 128:(t0 + j + 1) * 128], idb[0:nq, 0:nq])
                    K.copy(pT.v(lambda h: h[:, t0:t0 + nb, 0:nq]),
                           bk.v(lambda h: h.bitcast(BF16)[:, 0:nb * 128].rearrange("p (a b) -> p a b", a=nb)[:, :, 0:nq]),
                           eng='act' if (t0 // 8) % 2 == 0 else 'dve')
                if ntail:
                    bk = self.bank()
                    K.tr(bk.v(lambda h: h.bitcast(BF16)[0:ntail, 0:nq]), pbf[0:nq, nfull * 128:keys], idb[0:nq, 0:nq])
                    K.copy(pTt[0:ntail, 0:nq], bk.v(lambda h: h.bitcast(BF16)[0:ntail, 0:nq]))
                bk = self.bank()
                for kt in range(nfull):
                    K.mm(bk[0:nq, 0:64], pT[:, kt, 0:nq], Vp[:, kt, hh * 64:(hh + 1) * 64], start=(kt == 0), stop=(kt == nfull - 1 and not ntail))
                if ntail:
                    K.mm(bk[0:nq, 0:64], pTt[0:ntail, 0:nq], vnew[0:ntail, hh * 64:(hh + 1) * 64], start=(nfull == 0), stop=True)
                K.ts(out_v, bk[0:nq, 0:64], rs[0:nq, :], ALU.mult)

            for m in range(8):
                cq, ck, cv = m * 128, 1024 + m * 128, 2048 + m * 128
                load_cols(raw, io['proj'][0:L, cq:cq + 128], NT)
                K.ts(rawb[:, 0:NT, :], raw[:, 0:NT, :], 0.125, ALU.mult)
                to_T(QT, rawb, NT)
                load_cols(raw, io['proj'][0:L, ck:ck + 128], NT)
                K.copy(rawb[:, 0:NT, :], raw[:, 0:NT, :], eng='pool')
                to_T(KT, rawb, NT)
                load_cols(raw, io['proj'][0:L, cv:cv + 128], NT)
                K.copy(Vp[:, 0:NT, :], raw[:, 0:NT, :], eng='pool')
                for hh in range(2):
                    K.dma(ncr[hh][0:1, 0:L], io['cumT'][2 * m + hh:2 * m + hh + 1, 0:L], q='act')
                for qt in range(NT):
                    for hh in range(2):
                        p0 = 64 * hh
                        attend(128, QT[p0:p0 + 64, qt * 128:(qt + 1) * 128], p0, qt + 1, 0, ncr[hh], hh, osb[:, hh * 64:(hh + 1) * 64])
                    K.dma(io['ocat'][qt * 128:(qt + 1) * 128, cq:cq + 128], osb[:, :])
                for s_ in range(4):
                    r0 = L + 32 * s_
                    K.dma(q32[:, :], io['proj'][r0:r0 + 32, cq:cq + 128])
                    K.ts(rawb[0:32, 0, :], q32[:, :], 0.125, ALU.mult)
                    to_T(QT, rawb, 1, npart=32)
                    load_cols(raw, io['c_k_l%d' % li].v(lambda h: h[s_, :, m * 128:(m + 1) * 128]), NP)
                    K.copy(rawb[:, 0:NP, :], raw[:, 0:NP, :], eng='pool')
                    to_T(KT, rawb, NP)
                    K.dma(q32[:, :], io['proj'][r0:r0 + 32, ck:ck + 128])
                    K.copy(q32b[:, :], q32[:, :])
                    bk = self.bank()
                    K.tr(bk.v(lambda h: h.bitcast(BF16)[:, 0:32]), q32b[:, :], idb[0:32, 0:32])
                    K.copy(KT[:, PAST:PAST + 32], bk.v(lambda h: h.bitcast(BF16)[:, 0:32]))
                    load_cols(raw, io['c_v_l%d' % li].v(lambda h: h[s_, :, m * 128:(m + 1) * 128]), NP)
                    K.copy(Vp[:, 0:NP, :], raw[:, 0:NP, :], eng='pool')
                    K.dma(q32[:, :], io['proj'][r0:r0 + 32, cv:cv + 128])
                    K.copy(vnew[:, :], q32[:, :])
                    for hh in range(2):
                        K.dma(ncr[hh][0:1, 0:PAST + 32], io['cumTs'].v(lambda h: h[s_, 2 * m + hh:2 * m + hh + 1, 0:PAST + 32]), q='act')
                    for hh in range(2):
                        p0 = 64 * hh
                        attend(32, QT[p0:p0 + 64, 0:32], p0, NP, 32, ncr[hh], hh, osb[0:32, hh * 64:(hh + 1) * 64])
                    K.dma(io['ocat'][r0:r0 + 32, cq:cq + 128], osb[0:32, :])
        K.S.barrier()

    def phase_B_ssd(self, li):
        K, io, L = self.K, self.io, self.L
        idf = self.c['idf']
        self.conv_state_out(li)
        with ExitStack() as es:
            cw = [K.sb(es, 'cw', [128, 3072], F32) for _ in range(4)]
            for j in range(4):
                K.dma(cw[j][:], io['conv_l%d' % li].v(lambda h: h[j:j + 1, :].partition_broadcast(128)))
            cb = K.sb(es, 'cb', [128, 3072], F32)
            K.dma(cb[:], io['convb_l%d' % li].v(lambda h: h[0:1, :].partition_broadcast(128)))
            alB = K.sb(es, 'alB', [128, 32], F32)
            dtB = K.sb(es, 'dtB', [128, 32], F32)
            K.dma(alB[:], io['alog_l%d' % li].v(lambda h: h[0:1, :].partition_broadcast(128)))
            K.dma(dtB[:], io['dtb_l%d' % li].v(lambda h: h[0:1, :].partition_broadcast(128)))
            K.act(alB[:], alB[:], AF.Exp)
            K.ts(alB[:], alB[:], -1.0, ALU.mult)
            sh = [K.sb(es, 'sh', [128, 3072], F32) for _ in range(4)]
            xbc = K.sb(es, 'xbc', [128, 3072], F32)
            ab = K.sb(es, 'ab', [128, 32], F32)
            ax = K.sb(es, 'ax', [128, 64], F32)
            for ti in range(self.NT + 1):
                r0 = ti * 128
                self.conv_tile(ti, sh, cw, xbc[:], cbias=cb)
                K.dma(io['cvo'][r0:r0 + 128, :], xbc[:])
                K.dma(ab[:], io['proj'][r0:r0 + 128, 5120:5152], q='act')
                K.tt(ax[:, 0:32], ab[:], dtB[:], ALU.add)
                K.act(ax[:, 0:32], ax[:, 0:32], AF.Exp)
                K.act(ax[:, 0:32], ax[:, 0:32], AF.Ln, bias=1.0)
                K.tt(ax[:, 32:64], ax[:, 0:32], alB[:], ALU.mult)
                K.dma(io['aux'][r0:r0 + 128, 0:64], ax[:])
        K.S.barrier()
        with ExitStack() as es:
            T = {}
            def t(name, shape, dt=F32):
                T[name] = K.sb(es, name, shape, dt)
            t('xbc', [128, 3072]); t('z', [128, 2048]); t('ax', [128, 64])
            t('gam', [128, 32]); t('w2', [128, 32]); t('glast', [128, 32]); t('egl', [128, 32])
            t('xdt', [128, 2048]); t('xdtw', [128, 2048]); t('xd', [128, 2048]); t('ysb', [128, 2048])
            t('BT', [128, 4, 128]); t('CT', [128, 4, 128]); t('cbT', [128, 4, 128])
            t('gbc', [128, 4, 128]); t('d4', [128, 4, 128]); t('MT4', [128, 4, 128]); t('egB4', [128, 4, 128]); t('CegT4', [128, 4, 128])
            t('hT', [128, 32, 64]); t('hO', [64, 32, 128])
            t('dB', [128, 32]); t('ngB', [128, 2048]); t('rstd', [128, 1])
            K.dma(T['dB'][:], io['dsk_l%d' % li].v(lambda h: h[0:1, :].partition_broadcast(128)))
            K.dma(T['ngB'][:], io['ng_l%d' % li].v(lambda h: h[0:1, :].partition_broadcast(128)))

            def chunk(c, row0):
                K.dma(T['xbc'][0:c, :], io['cvo'][row0:row0 + c, :])
                K.dma(T['z'][0:c, :], io['proj'][row0:row0 + c, 0:2048], q='act')
                K.dma(T['ax'][0:c, :], io['aux'][row0:row0 + c, 0:64], q='act')
                bk = self.bank()
                K.mm(bk[0:c, 0:32], self.c['triu'][0:c, 0:c], T['ax'][0:c, 32:64])
                K.copy(T['gam'][0:c, :], bk[0:c, 0:32])
                x3 = T['xbc'].v(lambda h: h[0:c, 0:2048].rearrange("p (a b) -> p a b", a=32))
                K.tt(T['xdt'].v(lambda h: h[0:c, :].rearrange("p (a b) -> p a b", a=32)), x3,
                     T['ax'].v(lambda h: h[0:c, 0:32].unsqueeze(2).to_broadcast([c, 32, 64])), ALU.mult)
                K.tt(T['xd'].v(lambda h: h[0:c, :].rearrange("p (a b) -> p a b", a=32)), x3,
                     T['dB'].v(lambda h: h[0:c, :].unsqueeze(2).to_broadcast([c, 32, 64])), ALU.mult, eng='pool')
                sub = getattr(self, 'sub', 99)
                if sub <= 1:
                    return
                for g in range(4):
                    Bg = T['xbc'][0:c, 2048 + g * 128:2048 + (g + 1) * 128]
                    Cg = T['xbc'][0:c, 2560 + g * 128:2560 + (g + 1) * 128]
                    bk = self.bank()
                    K.tr(bk[:, 0:c], Bg, idf[0:c, 0:c])
                    K.tr(bk[:, c:2 * c], Cg, idf[0:c, 0:c])
                    ea = 'act' if 'A' in getattr(self, 'var', '') else 'dve'
                    K.copy(T['BT'][:, g, 0:c], bk[:, 0:c], eng=ea)
                    K.copy(T['CT'][:, g, 0:c], bk[:, c:2 * c])
                    bk = self.bank()
                    K.mm(bk[0:c, 0:c], T['BT'][:, g, 0:c], T['CT'][:, g, 0:c])
                    K.copy(T['cbT'][0:c, g, 0:c], bk[0:c, 0:c], eng=ea)
                    if sub <= 2:
                        continue
                    bky = self.bank()
                    for half in range(2):
                        h0 = 8 * g + 4 * half
                        bkg = self.bank()
                        for j in range(4):
                            K.ts(T['gbc'][0:c, j, :], self.c['ones'][0:c, :], T['ax'][0:c, 32 + h0 + j:32 + h0 + j + 1], ALU.mult)
                            K.mm(bkg[:, j * c:(j + 1) * c], T['gbc'][0:c, j, :], self.c['triu'][0:c, 0:c])
                        g3 = lambda np_: bkg.v(lambda h: h[0:np_, 0:4 * c].rearrange("p (a b) -> p a b", a=4))
                        K.copy(T['glast'][:, h0:h0 + 4], bkg.v(lambda h: h[:, 0:4 * c].rearrange("p (a b) -> p a b", a=4)[:, :, c - 1]))
                        K.copy(T['egB4'].v(lambda h: h[:, :, 0:c]), g3(128))
                        K.act(T['egB4'].v(lambda h: h[:, :, 0:c]), T['egB4'].v(lambda h: h[:, :, 0:c]), AF.Exp)
                        K.tt(T['CegT4'].v(lambda h: h[:, :, 0:c]), T['egB4'].v(lambda h: h[:, :, 0:c]),
                             T['CT'].v(lambda h: h[:, g, 0:c].unsqueeze(1).to_broadcast([128, 4, c])), ALU.mult, eng='pool')
                        d4 = T['d4'].v(lambda h: h[0:c, :, 0:c])
                        M4 = T['MT4'].v(lambda h: h[0:c, :, 0:c])
                        K.tt(d4, g3(c), T['gam'].v(lambda h: h[0:c, h0:h0 + 4].unsqueeze(2).to_broadcast([c, 4, c])), ALU.subtract)
                        K.ts(d4, d4, 0.0, ALU.min)
                        K.act(d4, d4, AF.Exp)
                        K.tt(M4, d4, self.c['triu'].v(lambda h: h[0:c, 0:c].unsqueeze(1).to_broadcast([c, 4, c])), ALU.mult)
                        K.tt(M4, M4, T['cbT'].v(lambda h: h[0:c, g, 0:c].unsqueeze(1).to_broadcast([c, 4, c])), ALU.mult)
                        for j in range(4):
                            hh = h0 + j
                            oc = (4 * half + j) * 64
                            K.mm(bky[0:c, oc:oc + 64], T['MT4'][0:c, j, 0:c], T['xdt'][0:c, hh * 64:(hh + 1) * 64], start=True, stop=False)
                            K.mm(bky[0:c, oc:oc + 64], T['CegT4'][:, j, 0:c], T['hT'][:, hh, :], start=False, stop=True)
                    K.tt(T['ysb'][0:c, g * 512:(g + 1) * 512], bky[0:c, :], T['xd'][0:c, g * 512:(g + 1) * 512], ALU.add)
                    if sub <= 3:
                        continue
                    gs = slice(8 * g, 8 * g + 8)
                    K.act(T['egl'][:, gs], T['glast'][:, gs], AF.Exp)
                    K.tt(T['w2'][0:c, gs], T['glast'][0:c, gs], T['gam'][0:c, gs], ALU.subtract)
                    K.act(T['w2'][0:c, gs], T['w2'][0:c, gs], AF.Exp)
                    K.tt(T['xdtw'].v(lambda h: h[0:c, g * 512:(g + 1) * 512].rearrange("p (a b) -> p a b", a=8)),
                         T['xdt'].v(lambda h: h[0:c, g * 512:(g + 1) * 512].rearrange("p (a b) -> p a b", a=8)),
                         T['w2'].v(lambda h: h[0:c, gs].unsqueeze(2).to_broadcast([c, 8, 64])), ALU.mult, eng='pool')
                    bkh = self.bank()
                    for j8 in range(8):
                        hh = 8 * g + j8
                        K.mm(bkh[:, j8 * 64:(j8 + 1) * 64], Bg, T['xdtw'][0:c, hh * 64:(hh + 1) * 64])
                    hv = T['hT'].v(lambda h: h[:, gs, :])
                    K.tt(hv, hv, T['egl'].v(lambda h: h[:, gs].unsqueeze(2).to_broadcast([128, 8, 64])), ALU.mult)
                    K.tt(hv, hv, bkh.v(lambda h: h[:, :].rearrange("p (a b) -> p a b", a=8)), ALU.add)
                if sub <= 4:
                    return
                K.act(T['z'][0:c, :], T['z'][0:c, :], AF.Silu)
                K.tt(T['ysb'][0:c, :], T['ysb'][0:c, :], T['z'][0:c, :], ALU.mult)
                self.rmsnorm_rstd(es, T['ysb'][0:c, :], 2048, T['z'][0:c, :], T['rstd'][0:c, :])
                K.stt(T['ysb'][0:c, :], T['ysb'][0:c, :], T['rstd'][0:c, :], T['ngB'][0:c, :], ALU.mult, ALU.mult)
                K.dma(io['ocat'][row0:row0 + c, 0:2048], T['ysb'][0:c, :])

            def state_out(dst):
                for b8 in range(8):
                    bk = self.bank()
                    for j in range(4):
                        K.tr(bk[0:64, j * 128:(j + 1) * 128], T['hT'][:, b8 * 4 + j, :], idf[:, :])
                    K.copy(T['hO'].v(lambda h: h[:, b8 * 4:b8 * 4 + 4, :]), bk.v(lambda h: h[0:64, :].rearrange("p (a b) -> p a b", a=4)))
                for q4 in range(4):
                    K.dma(dst[:, q4 * 8:(q4 + 1) * 8, :], T['hO'][:, q4 * 8:(q4 + 1) * 8, :], q='sp' if q4 % 2 == 0 else 'act')

            def state_in(src):
                for q4 in range(4):
                    K.dma(T['hO'][:, q4 * 8:(q4 + 1) * 8, :], src[:, q4 * 8:(q4 + 1) * 8, :], q='sp' if q4 % 2 == 0 else 'act')
                for b8 in range(4):
                    bk = self.bank()
                    for j in range(8):
                        K.tr(bk[:, j * 64:(j + 1) * 64], T['hO'][:, b8 * 8 + j, :], idf[0:64, 0:64])
                    K.copy(T['hT'].v(lambda h: h[:, b8 * 8:b8 * 8 + 8, :]), bk.v(lambda h: h[:, :].rearrange("p (a b) -> p a b", a=8)))

            cut = getattr(self, 'cut', 99)
            K.memset(T['hT'][:], 0.0)
            if cut >= 2:
                for n in range(self.NT):
                    chunk(128, n * 128)
            if cut >= 3:
                state_out(io['p_s_l%d' % li].v(lambda h: h.rearrange("a p n -> p a n")))
            for s_ in range(4):
                if cut >= 4:
                    state_in(io['st_s_l%d' % li].v(lambda h: h[s_].rearrange("a p n -> p a n")))
                if cut >= 5:
                    chunk(32, L + 32 * s_)
                if cut >= 6:
                    state_out(io['s_s_l%d' % li].v(lambda h: h[s_].rearrange("a p n -> p a n")))
        K.S.barrier()

    def phase_C1(self, li):
        K, io, kd = self.K, self.io, self.kinds[li]
        WO = W_MIX[kd] + 1024
        KO = WO // 128
        with ExitStack() as es:
            wo = K.sb(es, 'wo', [128, KO, D], BF16)
            stg = [K.sb(es, 'stg', [128, 2048], F32) for _ in range(2)]
            self.load_w_bf16(io['w_out_l%d' % li], wo, KO, D, stg)
            oc = [K.sb(es, 'oc', [128, WO], F32) for _ in range(2)]
            ocb = [K.sb(es, 'ocb', [128, WO], BF16) for _ in range(2)]
            ocT = [K.sb(es, 'ocT', [128, KO, 128], BF16) for _ in range(2)]
            xt = [K.sb(es, 'xt', [128, D], F32) for _ in range(2)]
            x1 = [K.sb(es, 'x1', [128, D], F32) for _ in range(2)]
            for ti in range(self.NT + 1):
                b = ti % 2
                r0 = ti * 128
                K.dma(oc[b][:], io['ocat'][r0:r0 + 128, 0:WO])
                if li == 0:
                    src = io['xp'][r0:r0 + 128, :] if ti < self.NT else io['xs'][:, :]
                else:
                    src = io['xres'][r0:r0 + 128, :]
                K.dma(xt[b][:], src, q='act')
                K.copy(ocb[b][:], oc[b][:], eng='pool')
                self.transpose_to(ocT[b], ocb[b], KO, self.c['idb'], BF16)
                for g in range(2):
                    bk = self.bank()
                    for k in range(KO):
                        K.mm(bk[:, :], ocT[b][:, k, :], wo[:, k, g * 512:(g + 1) * 512], start=(k == 0), stop=(k == KO - 1))
                    K.tt(x1[b][:, g * 512:(g + 1) * 512], xt[b][:, g * 512:(g + 1) * 512], bk[:, :], ALU.add)
                K.dma(io['xres'][r0:r0 + 128, :], x1[b][:])
        K.S.barrier()

    def phase_C2(self, li):
        K, io = self.K, self.io
        last = (li == self.DEPTH - 1)
        G = 12
        GS = 4
        self.nrot = 6
        yps = [self.pb[6], self.pb[7]]
        idb = self.c['idb']
        with ExitStack() as es:
            wq = K.sb(es, 'wq', [128, 8, 2048], BF16)
            with ExitStack() as es2:
                stg = [K.sb(es2, 'stg', [128, 2048], F32) for _ in range(2)]
                self.load_w_bf16(io['peer_wq'].v(lambda h: h[li]), wq, 8, 2048, stg)
                K.S.barrier()
            gB = K.sb(es, 'gB', [128, D], F32)
            K.dma(gB[:], io['g_ffn'].v(lambda h: h[li:li + 1, :].partition_broadcast(128)))
            if last:
                gF = K.sb(es, 'gF', [128, D], F32)
                K.dma(gF[:], io['g_final'].v(lambda h: h[0:1, :].partition_broadcast(128)))
            kraw = K.sb(es, 'kraw', [128, 2, 128], F32)
            kT = K.sb(es, 'kT', [128, 2, 128], F32)
            K.dma(kraw[:, 0, :], io['peer_k1'].v(lambda h: h[li]))
            K.dma(kraw[:, 1, :], io['peer_k2'].v(lambda h: h[li]))
            bk = self.bank()
            for hf in range(2):
                K.tr(bk[:, hf * 128:(hf + 1) * 128], kraw[:, hf, :], self.c['idf'][:])
            K.copy(kT.v(lambda h: h[:, :, :]), bk.v(lambda h: h[:, 0:256].rearrange("p (k t) -> p k t", k=2)))
            io16i = K.sb(es, 'io16i', [128, 16], I32)
            io16 = K.sb(es, 'io16', [128, 16], F32)
            K.iota(io16i[:], [[1, 16]], 0, 0)
            K.copy(io16[:], io16i[:])

            xt = K.sb(es, 'xt', [128, D], F32)
            sq = K.sb(es, 'sq', [128, D], F32)
            rstd = K.sb(es, 'rstd', [128, 1], F32)
            xnb = K.sb(es, 'xnb', [128, D], BF16)
            xnT = K.sb(es, 'xnT', [128, 8, 128], BF16)
            qT = K.sb(es, 'qT', [128, 16, 128], F32)
            sc = K.sb(es, 'sc', [128, 16, 128], F32)
            sc2 = K.sb(es, 'sc2', [128, 16, 128], F32)
            v12 = K.sb(es, 'v12', [128, 16, 16], F32)
            i12 = K.sb(es, 'i12', [128, 16, 16], U32)
            i12f = K.sb(es, 'i12f', [128, 16, 16], F32)
            cand = K.sb(es, 'cand', [128, 8, 256], F32)
            cand2 = K.sb(es, 'cand2', [128, 8, 256], F32)
            tsv = K.sb(es, 'tsv', [128, 8, 16], F32)
            pos = K.sb(es, 'pos', [128, 8, 16], U32)
            ai = K.sb(es, 'ai', [128, 8, 16], U32)
            bi = K.sb(es, 'bi', [128, 8, 16], U32)
            af = K.sb(es, 'af', [128, 8, 16], F32)
            bf = K.sb(es, 'bf', [128, 8, 16], F32)
            oh = K.sb(es, 'oh', [128, 16, 16], F32)
            oh2 = K.sb(es, 'oh2', [128, 16, 16], F32)
            i1s = K.sb(es, 'i1s', [128, 8, 16], F32)
            i2s = K.sb(es, 'i2s', [128, 8, 16], F32)
            eif = K.sb(es, 'eif', [128, 128], F32)
            eidx = K.sb(es, 'eidx', [128, 128], I32)
            gate = K.sb(es, 'gate', [128, 8, 16], F32)
            gsum = K.sb(es, 'gsum', [128, 8], F32)
            dots = K.sb(es, 'dots', [128, 128], F32)
            actw = K.sb(es, 'actw', [128, 128], F32)
            gb = [K.sb(es, 'gbuf', [128, 2 * D], BF16) for _ in range(G)]
            scr = K.sb(es, 'scr', [128, D], BF16)
            acc = [K.sb(es, 'acc', [128, D], F32) for _ in range(2)]
            dg = [K.sb(es, 'dg', [128, 128], BF16) for _ in range(8)]
            uvt = io['uvb_l%d' % li]
            gflat = gate.v(lambda hh: hh[:, :, :].rearrange("p a b -> p (a b)"))
            for ti in range(self.NT + 1):
                r0 = ti * 128
                K.dma(xt[:], io['xres'][r0:r0 + 128, :])
                self.rmsnorm_rstd(es, xt[:], D, sq[:], rstd[:])
                K.stt(xnb[:], xt[:], rstd[:], gB[:], ALU.mult, ALU.mult)
                self.transpose_to(xnT, xnb, 8, self.c['idb'], BF16)
                for c4 in range(4):
                    bk = self.bank()
                    for j in range(4):
                        cc = c4 * 4 + j
                        for k in range(8):
                            K.mm(bk[:, j * 128:(j + 1) * 128], wq[:, k, cc * 128:(cc + 1) * 128], xnT[:, k, :],
                                 start=(k == 0), stop=(k == 7))
                    K.copy(qT.v(lambda h: h[:, c4 * 4:c4 * 4 + 4, :]),
                           bk.v(lambda h: h[:, :].rearrange("p (k t) -> p k t", k=4)), eng='act' if c4 % 2 == 0 else 'dve')
                for c4 in range(4):
                    bk = self.bank()
                    for j in range(4):
                        cc = c4 * 4 + j
                        K.mm(bk[:, j * 128:(j + 1) * 128], qT[:, cc, :], kT[:, cc % 2, :])
                    K.copy(sc.v(lambda h: h[:, c4 * 4:c4 * 4 + 4, :]),
                           bk.v(lambda h: h[:, :].rearrange("p (k t) -> p k t", k=4)), eng='act' if c4 % 2 == 0 else 'dve')
                for cc in range(16):
                    K.max8(v12[:, cc, 0:8], sc[:, cc, :])
                    K.maxidx(i12[:, cc, 0:8], v12[:, cc, 0:8], sc[:, cc, :])
                    K.mrep(sc2[:, cc, :], v12[:, cc, 0:8], sc[:, cc, :], NEG)
                    K.max8(v12[:, cc, 8:16], sc2[:, cc, :])
                    K.maxidx(i12[:, cc, 8:16], v12[:, cc, 8:16], sc2[:, cc, :])
                K.copy(i12f[:], i12[:])
                for h in range(8):
                    K.tt(cand.v(lambda hh: hh[:, h, :].rearrange("p (a b) -> p a b", a=16)),
                         v12.v(lambda hh: hh[:, 2 * h, :].unsqueeze(2).to_broadcast([128, 16, 16])),
                         v12.v(lambda hh: hh[:, 2 * h + 1, :].unsqueeze(1).to_broadcast([128, 16, 16])), ALU.add)
                    K.max8(tsv[:, h, 0:8], cand[:, h, :])
                    K.maxidx(pos[:, h, 0:8], tsv[:, h, 0:8], cand[:, h, :])
                    K.mrep(cand2[:, h, :], tsv[:, h, 0:8], cand[:, h, :], NEG)
                    K.max8(tsv[:, h, 8:16], cand2[:, h, :])
                    K.maxidx(pos[:, h, 8:16], tsv[:, h, 8:16], cand2[:, h, :])
                K.ts(ai[:], pos[:], 4, ALU.logical_shift_right)
                K.ts(bi[:], pos[:], 15, ALU.bitwise_and)
                K.copy(af[:], ai[:])
                K.copy(bf[:], bi[:])
                for h in range(8):
                    for (xf, half, dst) in ((af, 0, i1s), (bf, 1, i2s)):
                        K.tt(oh[:], xf.v(lambda hh: hh[:, h, :].unsqueeze(2).to_broadcast([128, 16, 16])),
                             io16.v(lambda hh: hh[:, :].unsqueeze(1).to_broadcast([128, 16, 16])), ALU.is_equal)
                        K.tt(oh2[:], oh[:], i12f.v(lambda hh: hh[:, 2 * h + half, :].unsqueeze(1).to_broadcast([128, 16, 16])), ALU.mult)
                        K.red(dst[:, h, :], oh2[:], ALU.add)
                K.stt(eif.v(lambda hh: hh[:, :].rearrange("p (a b) -> p a b", a=8)), i1s[:], 128.0, i2s[:], ALU.mult, ALU.add)
                K.ts(eif[:], eif[:], 0.0, ALU.max, 16383.0, ALU.min)
                K.copy(eidx[:], eif[:])
                K.tt(gate[:], tsv[:], tsv.v(lambda hh: hh[:, :, 0:1].to_broadcast([128, 8, 16])), ALU.subtract)
                K.act(gate[:], gate[:], AF.Exp)
                K.red(gsum[:], gate[:], ALU.add)
                K.recip(gsum[:], gsum[:])
                K.tt(gate[:], gate[:], gsum.v(lambda hh: hh[:, :].unsqueeze(2).to_broadcast([128, 8, 16])), ALU.mult)
                NGRP = 128 // GS

                def head(gi):
                    for jj in range(GS):
                        j = gi * GS + jj
                        g = gb[j % G]
                        K.gather(g[:], uvt[:, :], eidx[:, j:j + 1])
                        K.ttr(scr[:], g[:, 0:D], xnb[:], dots[:, j:j + 1])
                    sl = slice(gi * GS, (gi + 1) * GS)
                    K.act(actw[:, sl], dots[:, sl], AF.Gelu)

                def tail(gi):
                    sl = slice(gi * GS, (gi + 1) * GS)
                    K.tt(actw[:, sl], actw[:, sl], gflat[:, sl], ALU.mult)
                    for jj in range(GS):
                        j = gi * GS + jj
                        g = gb[j % G]
                        d_ = dg[j % 8]
                        K.ts(d_[:], idb[:], actw[:, j:j + 1], ALU.mult)
                        for hf in range(2):
                            K.mm(yps[hf][:, :], d_[:], g[:, D + hf * 512:D + (hf + 1) * 512], start=(j == 0), stop=(j == 127))

                for gi in range(NGRP + 1):
                    if gi < NGRP:
                        head(gi)
                    if gi >= 1:
                        tail(gi - 1)
                for hf in range(2):
                    K.tt(acc[0][:, hf * 512:(hf + 1) * 512], yps[hf][:, :], xt[:, hf * 512:(hf + 1) * 512], ALU.add)
                if not last:
                    K.dma(io['xres'][r0:r0 + 128, :], acc[0][:])
                else:
                    if self.debug:
                        K.dma(io['xres'][r0:r0 + 128, :], acc[0][:])
                    self.rmsnorm_rstd(es, acc[0][:], D, sq[:], rstd[:])
                    K.stt(acc[1][:], acc[0][:], rstd[:], gF[:], ALU.mult, ALU.mult)
                    dst = io['y_p'][r0:r0 + 128, :] if ti < self.NT else io['y_s'][:, :]
                    K.dma(dst, acc[1][:])
        K.S.barrier()
        self.nrot = 8

    def build(self, stop=None):
        self.declare()
        with ExitStack() as es:
            self.consts(es)
            self.K.S.barrier()
            self.build_uv()
            for li in range(self.DEPTH):
                if self.kinds[li] in ('delta', 'ssd'):
                    self.conv_states(li, None, None)
                self.phase_A(li)
                if stop == ('A', li):
                    break
                if stop != ('skipB', li):
                    self.phase_B(li)
                else:
                    WO = W_MIX[self.kinds[li]] + 1024
                    for ti in range(self.NT + 1):
                        self.K.dma(self.io['ocat'][ti * 128:(ti + 1) * 128, 0:WO], self.io['proj'][ti * 128:(ti + 1) * 128, 0:WO])
                    self.K.S.barrier()
                if stop == ('B', li):
                    break
                self.phase_C1(li)
                if stop == ('C1', li):
                    break
                self.phase_C2(li)
                if stop in (('C2', li), ('skipB', li)):
                    break
            self.K.S.emit()
        return self.nc


N_W_IN = {0: 'w_in_l0', 1: 'w_in_l1', 2: 'w_in_l2', 3: 'w_in_l3'}
N_W_OUT = {0: 'w_out_l0', 1: 'w_out_l1', 2: 'w_out_l2', 3: 'w_out_l3'}
N_DN_CONV = {0: 'dn_conv_l0', 3: 'dn_conv_l3'}
N_DN_ALOG = {0: 'dn_alog_l0', 3: 'dn_alog_l3'}
N_DN_DTB = {0: 'dn_dtb_l0', 3: 'dn_dtb_l3'}
N_DN_NG = {0: 'dn_ng_l0', 3: 'dn_ng_l3'}
N_ST_DELTA = {0: 'state_l0_delta', 3: 'state_l3_delta'}
N_ST_CONV = {0: 'state_l0_conv', 2: 'state_l2_conv', 3: 'state_l3_conv'}
N_FOX_BF = {1: 'fox_bf_l1'}
N_C_K = {1: 'cache_l1_k'}
N_C_V = {1: 'cache_l1_v'}
N_C_F = {1: 'cache_l1_logf'}
N_SSD_CONV = {2: 'ssd_conv_l2'}
N_SSD_CONVB = {2: 'ssd_convb_l2'}
N_SSD_ALOG = {2: 'ssd_alog_l2'}
N_SSD_DTB = {2: 'ssd_dtb_l2'}
N_SSD_D = {2: 'ssd_d_l2'}
N_SSD_NG = {2: 'ssd_ng_l2'}
N_ST_SSM = {2: 'state_l2_ssm'}


def make_in_maps(inp, L, PAST, DEPTH):
    kinds = [MIXERS[i % 3] for i in range(DEPTH)]
    f = lambda a: np.ascontiguousarray(np.asarray(a, dtype=np.float32))
    maps = []
    shared = {}
    shared['g_mix'] = f(inp['g_mix'])[:DEPTH]
    shared['g_mem'] = f(inp['g_mem'])[:DEPTH]
    shared['g_ffn'] = f(inp['g_ffn'])[:DEPTH]
    shared['w_mem_k'] = f(inp['w_mem_k'])[:DEPTH]
    shared['w_mem_v'] = f(inp['w_mem_v'])[:DEPTH]
    shared['peer_wq'] = f(inp['peer_wq'])[:DEPTH]
    shared['peer_k1'] = f(inp['peer_k1'])[:DEPTH]
    shared['peer_k2'] = f(inp['peer_k2'])[:DEPTH]
    peer_u, peer_v = inp['peer_u'], inp['peer_v']
    for i in range(DEPTH):
        shared['peer_u_l%d' % i] = f(peer_u[i])
        shared['peer_v_l%d' % i] = f(peer_v[i])
    shared['g_final'] = f(inp['g_final']).reshape(1, D)
    for i, kd in enumerate(kinds):
        shared['w_in_l%d' % i] = f(inp[N_W_IN[i]])
        shared['w_out_l%d' % i] = f(inp[N_W_OUT[i]])
        if kd == 'delta':
            shared['conv_l%d' % i] = f(inp[N_DN_CONV[i]])
            shared['alog_l%d' % i] = f(inp[N_DN_ALOG[i]]).reshape(1, 8)
            shared['dtb_l%d' % i] = f(inp[N_DN_DTB[i]]).reshape(1, 8)
            shared['ng_l%d' % i] = f(inp[N_DN_NG[i]]).reshape(1, 128)
        elif kd == 'fox':
            shared['bf_l%d' % i] = f(inp[N_FOX_BF[i]]).reshape(1, 16)
        else:
            shared['conv_l%d' % i] = f(inp[N_SSD_CONV[i]])
            shared['convb_l%d' % i] = f(inp[N_SSD_CONVB[i]]).reshape(1, 3072)
            shared['alog_l%d' % i] = f(inp[N_SSD_ALOG[i]]).reshape(1, 32)
            shared['dtb_l%d' % i] = f(inp[N_SSD_DTB[i]]).reshape(1, 32)
            shared['dsk_l%d' % i] = f(inp[N_SSD_D[i]]).reshape(1, 32)
            shared['ng_l%d' % i] = f(inp[N_SSD_NG[i]]).reshape(1, 2048)
    x_prompt, x_sample, mem_prompt = inp['x_prompt'], inp['x_sample'], inp['mem_prompt']
    cache_mem_k, cache_mem_v = inp['cache_mem_k'], inp['cache_mem_v']
    for c in range(8):
        b = c % 4
        sl = slice(4 * c, 4 * c + 4)
        m = dict(shared)
        m['xp'] = f(x_prompt[b])
        m['xs'] = f(x_sample[sl]).reshape(128, D)
        m['memp'] = f(mem_prompt[b])
        m['cmk'] = f(cache_mem_k[:DEPTH, sl]).reshape(DEPTH, 4, 256, 1024)
        m['cmv'] = f(cache_mem_v[:DEPTH, sl]).reshape(DEPTH, 4, 256, 1024)
        for i, kd in enumerate(kinds):
            if kd == 'delta':
                m['st_d_l%d' % i] = f(inp[N_ST_DELTA[i]][sl])
                m['st_c_l%d' % i] = f(inp[N_ST_CONV[i]][sl])
            elif kd == 'fox':
                m['c_k_l%d' % i] = f(inp[N_C_K[i]][sl]).reshape(4, PAST, 1024)
                m['c_v_l%d' % i] = f(inp[N_C_V[i]][sl]).reshape(4, PAST, 1024)
                m['c_f_l%d' % i] = f(inp[N_C_F[i]][sl])
            else:
                m['st_s_l%d' % i] = f(inp[N_ST_SSM[i]][sl])
                m['st_c_l%d' % i] = f(inp[N_ST_CONV[i]][sl])
        maps.append(m)
    return maps


def run_config(inp, L, PAST, DEPTH, debug=False):
    kinds = [MIXERS[i % 3] for i in range(DEPTH)]
    mk = MK(L, PAST, DEPTH, debug=debug)
    nc = mk.build()
    maps = make_in_maps(inp, L, PAST, DEPTH)
    res = run_bass_kernel_spmd(nc, maps, core_ids=list(range(8)))
    R = res.results
    NB = 4
    f32 = np.float32

    def pstack(name, shape):
        return np.stack([np.asarray(R[b][name], dtype=f32).reshape(shape) for b in range(NB)], axis=0)

    def scat(name, shape):
        return np.concatenate([np.asarray(R[c][name], dtype=f32).reshape((4,) + tuple(shape)) for c in range(8)], axis=0)

    y_prompt = pstack('y_p', (L, D))
    y_sample = scat('y_s', (32, D))
    p_mem_k = np.stack([np.asarray(R[b]['o_mk'], dtype=f32).reshape(DEPTH, 256, 4, 256) for b in range(NB)], axis=1)
    p_mem_v = np.stack([np.asarray(R[b]['o_mv'], dtype=f32).reshape(DEPTH, 256, 4, 256) for b in range(NB)], axis=1)
    outs = {}
    for i, kd in enumerate(kinds):
        if kd == 'delta':
            outs['p_l%d_delta' % i] = pstack('p_d_l%d' % i, (8, 128, 128))
            outs['p_l%d_conv' % i] = pstack('p_c_l%d' % i, (3, 3072))
            outs['s_l%d_delta' % i] = scat('s_d_l%d' % i, (8, 128, 128))
            outs['s_l%d_conv' % i] = scat('s_c_l%d' % i, (3, 3072))
        elif kd == 'fox':
            outs['p_l%d_k' % i] = pstack('p_k_l%d' % i, (L, 16, 64))
            outs['p_l%d_v' % i] = pstack('p_v_l%d' % i, (L, 16, 64))
            outs['p_l%d_logf' % i] = pstack('p_f_l%d' % i, (L, 16))
            outs['s_l%d_k' % i] = scat('s_k_l%d' % i, (32, 16, 64))
            outs['s_l%d_v' % i] = scat('s_v_l%d' % i, (32, 16, 64))
            outs['s_l%d_logf' % i] = scat('s_f_l%d' % i, (32, 16))
        else:
            outs['p_l%d_ssm' % i] = pstack('p_s_l%d' % i, (32, 64, 128))
            outs['p_l%d_conv' % i] = pstack('p_c_l%d' % i, (3, 3072))
            outs['s_l%d_ssm' % i] = scat('s_s_l%d' % i, (32, 64, 128))
            outs['s_l%d_conv' % i] = scat('s_c_l%d' % i, (3, 3072))
    order = ['p_l0_delta', 'p_l0_conv', 'p_l1_k', 'p_l1_v', 'p_l1_logf', 'p_l2_ssm', 'p_l2_conv', 'p_l3_delta', 'p_l3_conv',
             's_l0_delta', 's_l0_conv', 's_l1_k', 's_l1_v', 's_l1_logf', 's_l2_ssm', 's_l2_conv', 's_l3_delta', 's_l3_conv']
    return (y_prompt, y_sample, p_mem_k, p_mem_v) + tuple(outs.get(k) for k in order)


def kernel(**inputs):
    L = int(np.shape(inputs['x_prompt'])[1])
    PAST = int(np.shape(inputs['cache_l1_k'])[1])
    return run_config(inputs, L, PAST, 4)
```

```python
import numpy as np
from contextlib import ExitStack
import concourse.bass as bass
import concourse.mybir as mybir
from concourse.bass_utils import run_bass_kernel_spmd

F32 = mybir.dt.float32
BF16 = mybir.dt.bfloat16
I32 = mybir.dt.int32
U32 = mybir.dt.uint32
AF = mybir.ActivationFunctionType
ALU = mybir.AluOpType
AX = mybir.AxisListType

D = 1024
EPS = 1e-6
MIXERS = ('delta', 'fox', 'ssd')
W_IN = {'delta': 5136, 'fox': 4112, 'ssd': 6176}
W_MIX = {'delta': 1024, 'fox': 1024, 'ssd': 2048}
WMAX = 6176
NEG = -1.0e30

ENG = ['pe', 'dve', 'act', 'pool', 'sp']
NLANES = {'sp': 12, 'pool': 12, 'act': 6}
ENGOBJ = {'pe': 'tensor', 'dve': 'vector', 'act': 'scalar', 'pool': 'gpsimd', 'sp': 'sync'}


class V:
    __slots__ = ('ap', 'key')

    def __init__(self, ap, key):
        self.ap = ap
        self.key = key

    def __getitem__(self, idx):
        return V(self.ap[idx], self.key)


class Tl:
    def __init__(self, h, name):
        self.h = h
        self.name = name

    def __getitem__(self, idx):
        return V(self.h[idx], self.name)

    def v(self, fn):
        return V(fn(self.h), self.name)


class Sched:
    def __init__(self, nc):
        self.nc = nc
        self.stream = {e: [] for e in ENG}
        self.count = {e: 0 for e in ENG}
        self.known = {e: {} for e in ENG}
        self.res = {}
        self.dma_n = {q: 0 for q in NLANES}
        self.nops = 0
        self.noself = False

    def _deps(self, reads, writes):
        deps = {}
        for r in reads:
            st = self.res.get(r)
            if st and st['w']:
                sk, v = st['w']
                if deps.get(sk, 0) < v:
                    deps[sk] = v
        for w in writes:
            st = self.res.get(w)
            if st:
                if st['w']:
                    sk, v = st['w']
                    if deps.get(sk, 0) < v:
                        deps[sk] = v
                for sk, v in st['r'].items():
                    if deps.get(sk, 0) < v:
                        deps[sk] = v
        return deps

    def _emit_waits(self, eng, deps):
        kn = self.known[eng]
        for sk, v in deps.items():
            if sk == ('c', eng) and (eng == 'pe' or self.noself):
                continue
            if kn.get(sk, 0) >= v:
                continue
            self.stream[eng].append(('wait', sk, v))
            kn[sk] = v

    def _record(self, sk, val, reads, writes):
        for r in reads:
            st = self.res.get(r)
            if st is None:
                st = self.res[r] = {'w': None, 'r': {}}
            if st['r'].get(sk, 0) < val:
                st['r'][sk] = val
        for w in writes:
            self.res[w] = {'w': (sk, val), 'r': {}}

    def op(self, eng, fn, reads=(), writes=()):
        deps = self._deps(reads, writes)
        self._emit_waits(eng, deps)
        self.count[eng] += 1
        sk = ('c', eng)
        self.stream[eng].append(('op', fn, sk, 1))
        self._record(sk, self.count[eng], reads, writes)
        self.nops += 1

    def dma(self, q, fn, reads=(), writes=()):
        n = self.dma_n[q]
        nl = NLANES[q]
        lane = n % nl
        val = 16 * (n // nl + 1)
        sk = ('d', q, lane)
        deps = self._deps(reads, writes)
        if n >= nl and deps.get(sk, 0) < val - 16:
            deps[sk] = val - 16
        self._emit_waits(q, deps)
        self.dma_n[q] = n + 1
        self.stream[q].append(('op', fn, sk, 16))
        self._record(sk, val, reads, writes)
        self.nops += 1

    def _all_outstanding(self):
        fin = {}
        for e in ENG:
            if self.count[e]:
                fin[('c', e)] = self.count[e]
        for q in NLANES:
            n = self.dma_n[q]
            for l in range(NLANES[q]):
                cnt = (n - l + NLANES[q] - 1) // NLANES[q] if n > l else 0
                if cnt:
                    fin[('d', q, l)] = 16 * cnt
        return fin

    def barrier(self):
        fin = self._all_outstanding()
        for e in ENG:
            kn = self.known[e]
            for sk, v in fin.items():
                if sk == ('c', e):
                    continue
                if kn.get(sk, 0) < v:
                    self.stream[e].append(('wait', sk, v))
                    kn[sk] = v
        self.res = {}

    def emit(self):
        nc = self.nc
        self.barrier()
        sks = [('c', e) for e in ENG] + [('d', q, l) for q in NLANES for l in range(NLANES[q])]
        with ExitStack() as es:
            sems = {}
            for sk in sks:
                sems[sk] = es.enter_context(nc.semaphore("s_" + "_".join(str(x) for x in sk)))
            block = es.enter_context(nc.Block())
            for e in ENG:
                items = self.stream[e]

                def body(engine, items=items):
                    for it in items:
                        if it[0] == 'wait':
                            engine.wait_ge(sems[it[1]], it[2])
                        else:
                            it[1](engine).then_inc(sems[it[2]], it[3])
                getattr(block, ENGOBJ[e])(body)


class KB:
    def __init__(self, nc):
        self.nc = nc
        self.S = Sched(nc)
        self.uid = 0

    def _rk(self, *vs):
        return [v.key for v in vs if isinstance(v, V)]

    @staticmethod
    def _a(x):
        return x.ap if isinstance(x, V) else x

    def sb(self, es, name, shape, dt=F32):
        self.uid += 1
        nm = "%s_%d" % (name, self.uid)
        return Tl(es.enter_context(self.nc.sbuf_tensor(nm, list(shape), dt)), nm)

    def ps(self, es, name, shape, dt=F32):
        self.uid += 1
        nm = "%s_%d" % (name, self.uid)
        return Tl(es.enter_context(self.nc.psum_tensor(nm, list(shape), dt)), nm)

    def dma(self, out, in_, q='sp', **kw):
        o, i = out.ap, in_.ap
        self.S.dma(q, lambda e: e.dma_start(out=o, in_=i, **kw), reads=[in_.key], writes=[out.key])

    def gather(self, out, table, idx):
        o, t, ix = out.ap, table.ap, idx.ap
        self.S.dma('pool', lambda e: e.indirect_dma_start(
            out=o, out_offset=None, in_=t,
            in_offset=bass.IndirectOffsetOnAxis(ap=ix, axis=0)),
            reads=[table.key, idx.key], writes=[out.key])

    def mm(self, out, lhsT, rhs, start=True, stop=True):
        o, l, r = out.ap, lhsT.ap, rhs.ap
        self.S.op('pe', lambda e: e.matmul(o, lhsT=l, rhs=r, start=start, stop=stop),
                  reads=[lhsT.key, rhs.key] + ([] if start else [out.key]), writes=[out.key])

    def tr(self, out, in_, ident):
        o, i, d = out.ap, in_.ap, ident.ap
        self.S.op('pe', lambda e: e.transpose(out=o, in_=i, identity=d),
                  reads=[in_.key, ident.key], writes=[out.key])

    def act(self, out, in_, func, bias=None, scale=None, accum=None):
        o, i = out.ap, in_.ap
        kw = {}
        rd = [in_.key]
        if bias is not None:
            kw['bias'] = self._a(bias)
            rd += self._rk(bias)
        if scale is not None:
            kw['scale'] = self._a(scale)
            rd += self._rk(scale)
        wr = [out.key]
        if accum is not None:
            kw['accum_out'] = accum.ap
            wr.append(accum.key)
        self.S.op('act', lambda e: e.activation(out=o, in_=i, func=func, **kw), reads=rd, writes=wr)

    def ts(self, out, in0, s1, op0, s2=None, op1=None, eng='dve', accum=None):
        o, i = out.ap, in0.ap
        a1, a2 = self._a(s1), self._a(s2)
        rd = [in0.key] + self._rk(s1, s2)
        kw = {}
        if op1 is not None:
            kw['op1'] = op1
        wr = [out.key]
        if accum is not None:
            kw['accum_out'] = accum.ap
            wr.append(accum.key)
        self.S.op(eng, lambda e: e.tensor_scalar(out=o, in0=i, scalar1=a1, scalar2=a2, op0=op0, **kw),
                  reads=rd, writes=wr)

    def tt(self, out, in0, in1, op, eng='dve'):
        o, a, b = out.ap, in0.ap, in1.ap
        self.S.op(eng, lambda e: e.tensor_tensor(out=o, in0=a, in1=b, op=op),
                  reads=[in0.key, in1.key], writes=[out.key])

    def stt(self, out, in0, scalar, in1, op0, op1):
        o, a, b = out.ap, in0.ap, in1.ap
        s = self._a(scalar)
        self.S.op('dve', lambda e: e.scalar_tensor_tensor(out=o, in0=a, scalar=s, in1=b, op0=op0, op1=op1),
                  reads=[in0.key, in1.key] + self._rk(scalar), writes=[out.key])

    def ttr(self, out, in0, in1, accum):
        o, a, b, c = out.ap, in0.ap, in1.ap, accum.ap
        self.S.op('dve', lambda e: e.scalar_tensor_tensor(out=o, in0=a, scalar=1.0, in1=b, op0=ALU.mult,
                                                          op1=ALU.mult, accum_out=c),
                  reads=[in0.key, in1.key], writes=[out.key, accum.key])

    def red(self, out, in_, op, axis=AX.X):
        o, i = out.ap, in_.ap
        self.S.op('dve', lambda e: e.tensor_reduce(out=o, in_=i, axis=axis, op=op),
                  reads=[in_.key], writes=[out.key])

    def copy(self, out, in_, eng='dve'):
        o, i = out.ap, in_.ap
        if eng == 'act':
            self.S.op('act', lambda e: e.copy(out=o, in_=i), reads=[in_.key], writes=[out.key])
        else:
            self.S.op(eng, lambda e: e.tensor_copy(out=o, in_=i), reads=[in_.key], writes=[out.key])

    def memset(self, out, val, eng='pool'):
        o = out.ap
        self.S.op(eng, lambda e: e.memset(o, val), writes=[out.key])

    def recip(self, out, in_):
        o, i = out.ap, in_.ap
        self.S.op('dve', lambda e: e.reciprocal(out=o, in_=i), reads=[in_.key], writes=[out.key])

    def aselect(self, out, in_, pattern, cmp, fill, base, cm):
        o, i = out.ap, in_.ap
        self.S.op('pool', lambda e: e.affine_select(out=o, in_=i, pattern=pattern, compare_op=cmp, fill=fill,
                                                    base=base, channel_multiplier=cm),
                  reads=[in_.key], writes=[out.key])

    def iota(self, out, pattern, base=0, cm=0):
        o = out.ap
        self.S.op('pool', lambda e: e.iota(o, pattern=pattern, base=base, channel_multiplier=cm,
                                           allow_small_or_imprecise_dtypes=True), writes=[out.key])

    def max8(self, out, in_):
        o, i = out.ap, in_.ap
        self.S.op('dve', lambda e: e.max(out=o, in_=i), reads=[in_.key], writes=[out.key])

    def maxidx(self, out, in_max, in_values):
        o, m, v = out.ap, in_max.ap, in_values.ap
        self.S.op('dve', lambda e: e.max_index(out=o, in_max=m, in_values=v),
                  reads=[in_max.key, in_values.key], writes=[out.key])

    def mrep(self, out, in_to_replace, in_values, imm):
        o, r, v = out.ap, in_to_replace.ap, in_values.ap
        self.S.op('dve', lambda e: e.match_replace(out=o, in_to_replace=r, in_values=v, imm_value=imm),
                  reads=[in_to_replace.key, in_values.key], writes=[out.key])


class MK:
    def __init__(self, L, PAST, DEPTH, debug=False):
        self.L, self.PAST, self.DEPTH, self.debug = L, PAST, DEPTH, debug
        self.NT = L // 128
        self.NTOK = L + 128
        self.nc = bass.Bass("TRN2", target_bir_lowering=False)
        self.K = KB(self.nc)
        self.kinds = [MIXERS[i % 3] for i in range(DEPTH)]
        self.io = {}
        self.in_shapes = {}

    def din(self, name, shape, dt=F32):
        t = self.nc.dram_tensor(name, list(shape), dt, kind="ExternalInput")
        self.io[name] = Tl(t.ap(), name)
        self.in_shapes[name] = list(shape)
        return self.io[name]

    def dout(self, name, shape, dt=F32):
        t = self.nc.dram_tensor(name, list(shape), dt, kind="ExternalOutput")
        self.io[name] = Tl(t.ap(), name)
        return self.io[name]

    def dscr(self, name, shape, dt=F32):
        kind = "ExternalOutput" if self.debug else "Internal"
        t = self.nc.dram_tensor(name, list(shape), dt, kind=kind)
        self.io[name] = Tl(t.ap(), name)
        return self.io[name]

    def declare(self):
        L, PAST, DP = self.L, self.PAST, self.DEPTH
        d = self.din
        d('xp', [L, D]); d('xs', [128, D]); d('memp', [256, D])
        d('cmk', [DP, 4, 256, 1024]); d('cmv', [DP, 4, 256, 1024])
        d('g_mix', [DP, D]); d('g_mem', [DP, D]); d('g_ffn', [DP, D]); d('g_final', [1, D])
        d('w_mem_k', [DP, D, 1024]); d('w_mem_v', [DP, D, 1024])
        d('peer_wq', [DP, D, 2048]); d('peer_k1', [DP, 128, 128]); d('peer_k2', [DP, 128, 128])
        for i in range(DP):
            d('peer_u_l%d' % i, [16384, D]); d('peer_v_l%d' % i, [16384, D])
        for i, kd in enumerate(self.kinds):
            d('w_in_l%d' % i, [D, W_IN[kd]]); d('w_out_l%d' % i, [W_MIX[kd] + 1024, D])
            if kd == 'delta':
                d('conv_l%d' % i, [4, 3072]); d('alog_l%d' % i, [1, 8]); d('dtb_l%d' % i, [1, 8]); d('ng_l%d' % i, [1, 128])
                d('st_d_l%d' % i, [4, 8, 128, 128]); d('st_c_l%d' % i, [4, 3, 3072])
            elif kd == 'fox':
                d('bf_l%d' % i, [1, 16])
                d('c_k_l%d' % i, [4, PAST, 1024]); d('c_v_l%d' % i, [4, PAST, 1024]); d('c_f_l%d' % i, [4, PAST, 16])
            else:
                d('conv_l%d' % i, [4, 3072]); d('convb_l%d' % i, [1, 3072]); d('alog_l%d' % i, [1, 32]); d('dtb_l%d' % i, [1, 32])
                d('dsk_l%d' % i, [1, 32]); d('ng_l%d' % i, [1, 2048])
                d('st_s_l%d' % i, [4, 32, 64, 128]); d('st_c_l%d' % i, [4, 3, 3072])
        o = self.dout
        o('y_p', [L, D]); o('y_s', [128, D])
        o('o_mk', [DP, 256, 1024]); o('o_mv', [DP, 256, 1024])
        for i, kd in enumerate(self.kinds):
            if kd == 'delta':
                o('p_d_l%d' % i, [8, 128, 128]); o('p_c_l%d' % i, [3, 3072])
                o('s_d_l%d' % i, [4, 8, 128, 128]); o('s_c_l%d' % i, [4, 3, 3072])
            elif kd == 'fox':
                o('p_k_l%d' % i, [L, 1024]); o('p_v_l%d' % i, [L, 1024]); o('p_f_l%d' % i, [L, 16])
                o('s_k_l%d' % i, [128, 1024]); o('s_v_l%d' % i, [128, 1024]); o('s_f_l%d' % i, [128, 16])
            else:
                o('p_s_l%d' % i, [32, 64, 128]); o('p_c_l%d' % i, [3, 3072])
                o('s_s_l%d' % i, [4, 32, 64, 128]); o('s_c_l%d' % i, [4, 3, 3072])
        s = self.dscr
        s('xres', [self.NTOK, D])
        s('proj', [self.NTOK, WMAX])
        s('ocat', [self.NTOK, 3072])
        s('cvin', [3 + L + 4 * 35, 3072])
        s('cvo', [self.NTOK, 3072])
        s('aux', [self.NTOK, 64])
        s('cumT', [16, L])
        for i in range(DP):
            t = self.nc.dram_tensor('uvb_l%d' % i, [16384, 2048], BF16, kind="Internal")
            self.io['uvb_l%d' % i] = Tl(t.ap(), 'uvb_l%d' % i)
        s('cumTs', [4, 16, PAST + 32])

    def cv_row(self, seq):
        return 3 if seq < 0 else 3 + self.L + 35 * seq + 3

    def consts(self, es):
        K = self.K
        c = {}
        c['idf'] = K.sb(es, 'idf', [128, 128], F32)
        K.memset(c['idf'][:], 0.0)
        K.aselect(c['idf'][:], c['idf'][:], [[-1, 128]], ALU.not_equal, 1.0, 0, 1)
        c['idb'] = K.sb(es, 'idb', [128, 128], BF16)
        K.copy(c['idb'][:], c['idf'][:])
        c['triu'] = K.sb(es, 'triu', [128, 128], F32)
        K.memset(c['triu'][:], 1.0)
        K.aselect(c['triu'][:], c['triu'][:], [[1, 128]], ALU.is_ge, 0.0, 0, -1)
        c['striu'] = K.sb(es, 'striu', [128, 128], F32)
        K.memset(c['striu'][:], 1.0)
        K.aselect(c['striu'][:], c['striu'][:], [[1, 128]], ALU.is_ge, 0.0, -1, -1)
        c['tril'] = K.sb(es, 'tril', [128, 128], F32)
        K.memset(c['tril'][:], 1.0)
        K.aselect(c['tril'][:], c['tril'][:], [[-1, 128]], ALU.is_ge, 0.0, 0, 1)
        c['stril'] = K.sb(es, 'stril', [128, 128], F32)
        K.memset(c['stril'][:], 1.0)
        K.aselect(c['stril'][:], c['stril'][:], [[-1, 128]], ALU.is_ge, 0.0, -1, 1)
        c['ones'] = K.sb(es, 'ones', [128, 128], F32)
        K.memset(c['ones'][:], 1.0)
        c['zeros'] = K.sb(es, 'zeros', [128, 3072], F32)
        K.memset(c['zeros'][:], 0.0)
        self.c = c
        self.pb = [K.ps(es, 'pb%d' % i, [128, 512], F32) for i in range(8)]
        self.pbi = 0
        self.nrot = 8

    def bank(self):
        b = self.pb[self.pbi % self.nrot]
        self.pbi += 1
        return b

    def rmsnorm_rstd(self, es_tmp, x, width, tmp, rstd):
        K = self.K
        K.act(tmp, x, AF.Square, accum=rstd)
        self.rsqrt(rstd, rstd, 1.0 / width, EPS)

    def rsqrt(self, out, in_, scale, eps):
        K = self.K
        K.ts(out, in_, scale, ALU.mult, eps, ALU.add)
        K.act(out, out, AF.Sqrt)
        K.recip(out, out)

    def load_w_bf16(self, wsrc, wdst, nk, ncols, stg):
        K = self.K
        i = 0
        for k in range(nk):
            for c0 in range(0, ncols, 2048):
                c1 = min(ncols, c0 + 2048)
                s = stg[i % len(stg)]
                K.dma(s[:, 0:c1 - c0], wsrc[k * 128:(k + 1) * 128, c0:c1], q='sp' if i % 2 == 0 else 'act')
                K.copy(wdst[:, k, c0:c1], s[:, 0:c1 - c0], eng='dve' if i % 2 == 0 else 'pool')
                i += 1

    def transpose_to(self, dst, src, nblk, ident, dt, evac_engs=('act', 'dve')):
        K = self.K
        per = 4 if dt == F32 else 8
        for b0 in range(0, nblk, per):
            nb = min(per, nblk - b0)
            bk = self.bank()
            if dt == F32:
                pv = bk
                for j in range(nb):
                    K.tr(pv[:, j * 128:(j + 1) * 128], src[:, (b0 + j) * 128:(b0 + j + 1) * 128], ident[:])
                K.copy(dst.v(lambda h: h[:, b0:b0 + nb, :]),
                       pv.v(lambda h: h[:, 0:nb * 128].rearrange("p (k t) -> p k t", k=nb)),
                       eng=evac_engs[(b0 // per) % len(evac_engs)])
            else:
                for j in range(nb):
                    K.tr(bk.v(lambda h: h.bitcast(BF16)[:, j * 128:(j + 1) * 128]),
                         src[:, (b0 + j) * 128:(b0 + j + 1) * 128], ident[:])
                K.copy(dst.v(lambda h: h[:, b0:b0 + nb, :]),
                       bk.v(lambda h: h.bitcast(BF16)[:, 0:nb * 128].rearrange("p (k t) -> p k t", k=nb)),
                       eng=evac_engs[(b0 // per) % len(evac_engs)])

    def build_uv(self):
        K, io = self.K, self.io
        RB = 4
        with ExitStack() as es:
            ld = [K.sb(es, 'uvld', [128, RB, 1024], F32) for _ in range(3)]
            cv = [K.sb(es, 'uvcv', [128, RB, 1024], BF16) for _ in range(3)]
            n = 0
            for li in range(self.DEPTH):
                dst = io['uvb_l%d' % li]
                for half, nm in ((0, 'peer_u_l%d' % li), (1, 'peer_v_l%d' % li)):
                    src = io[nm]
                    for r0 in range(0, 128, RB):
                        b = n % 3
                        q = 'sp' if n % 2 == 0 else 'act'
                        K.dma(ld[b][:], src.v(lambda h: h.rearrange("(p r) d -> p r d", p=128)[:, r0:r0 + RB, :]), q=q)
                        K.copy(cv[b][:], ld[b][:], eng=('act', 'pool', 'dve')[n % 3])
                        K.dma(dst.v(lambda h: h.rearrange("(p r) d -> p r d", p=128)[:, r0:r0 + RB, half * 1024:(half + 1) * 1024]),
                              cv[b][:], q=q)
                        n += 1
        K.S.barrier()

    def phase_A(self, li):
        K, io, kd = self.K, self.io, self.kinds[li]
        W = W_IN[kd]
        with ExitStack() as es:
            wbf = K.sb(es, 'wbf', [128, 8, W], BF16)
            stg = [K.sb(es, 'stg', [128, 2048], F32) for _ in range(2)]
            gB = K.sb(es, 'gB', [128, D], F32)
            K.dma(gB[:], io['g_mix'].v(lambda h: h[li:li + 1, :].partition_broadcast(128)))
            self.load_w_bf16(io['w_in_l%d' % li], wbf, 8, W, stg)
            xt = [K.sb(es, 'xt', [128, D], F32) for _ in range(2)]
            sq = K.sb(es, 'sq', [128, D], F32)
            rstd = [K.sb(es, 'rstd', [128, 1], F32) for _ in range(2)]
            hn = [K.sb(es, 'hn', [128, D], BF16) for _ in range(2)]
            hnT = [K.sb(es, 'hnT', [128, 8, 128], BF16) for _ in range(2)]
            pj = [K.sb(es, 'pj', [128, W], F32) for _ in range(2)]
            for ti in range(self.NT + 1):
                b = ti % 2
                r0 = ti * 128
                if li == 0:
                    src = io['xp'][r0:r0 + 128, :] if ti < self.NT else io['xs'][:, :]
                else:
                    src = io['xres'][r0:r0 + 128, :]
                K.dma(xt[b][:], src)
                self.rmsnorm_rstd(es, xt[b][:], D, sq[:], rstd[b][:])
                K.stt(hn[b][:], xt[b][:], rstd[b][:], gB[:], ALU.mult, ALU.mult)
                self.transpose_to(hnT[b], hn[b], 8, self.c['idb'], BF16)
                g = 0
                for c0 in range(0, W, 512):
                    c1 = min(W, c0 + 512)
                    bk = self.bank()
                    for k in range(8):
                        K.mm(bk[:, 0:c1 - c0], hnT[b][:, k, :], wbf[:, k, c0:c1], start=(k == 0), stop=(k == 7))
                    K.copy(pj[b][:, c0:c1], bk[:, 0:c1 - c0], eng='act' if g % 2 == 0 else 'dve')
                    g += 1
                K.dma(io['proj'][r0:r0 + 128, 0:W], pj[b][:], q='sp')
                if kd in ('delta', 'ssd'):
                    coff = 0 if kd == 'delta' else 2048
                    if ti < self.NT:
                        cr = self.cv_row(-1) + r0
                        K.dma(io['cvin'][cr:cr + 128, :], pj[b][:, coff:coff + 3072], q='act')
                    else:
                        for s in range(4):
                            cr = self.cv_row(s)
                            K.dma(io['cvin'][cr:cr + 32, :], pj[b][s * 32:(s + 1) * 32, coff:coff + 3072], q='act')
        K.S.barrier()

    def phase_B_mem(self, li):
        K, io, kd = self.K, self.io, self.kinds[li]
        W = W_IN[kd]
        WM = W_MIX[kd]
        L = self.L
        idb = self.c['idb']
        with ExitStack() as es:
            mkT = [K.sb(es, 'mkT', [128, 8, 256], BF16) for _ in range(2)]
            mvb = [[K.sb(es, 'mvb', [128, 1024], BF16) for _ in range(2)] for _ in range(2)]
            mraw = [K.sb(es, 'mraw', [128, 1024], F32) for _ in range(2)]
            mkb = K.sb(es, 'mkb', [128, 1024], BF16)
            with ExitStack() as es2:
                wk = K.sb(es2, 'wk', [128, 8, 1024], BF16)
                wv = K.sb(es2, 'wv', [128, 8, 1024], BF16)
                stg = [K.sb(es2, 'stg', [128, 2048], F32) for _ in range(2)]
                self.load_w_bf16(io['w_mem_k'].v(lambda h: h[li]), wk, 8, 1024, stg)
                self.load_w_bf16(io['w_mem_v'].v(lambda h: h[li]), wv, 8, 1024, stg)
                gB = K.sb(es2, 'gB', [128, D], F32)
                K.dma(gB[:], io['g_mem'].v(lambda h: h[li:li + 1, :].partition_broadcast(128)))
                sq = K.sb(es2, 'sq', [128, D], F32)
                rstd = K.sb(es2, 'rstd', [128, 1], F32)
                mnb = K.sb(es2, 'mnb', [128, D], BF16)
                mnT = K.sb(es2, 'mnT', [128, 8, 128], BF16)
                for t in range(2):
                    K.dma(mraw[0][:], io['memp'][t * 128:(t + 1) * 128, :])
                    self.rmsnorm_rstd(es2, mraw[0][:], D, sq[:], rstd[:])
                    K.stt(mnb[:], mraw[0][:], rstd[:], gB[:], ALU.mult, ALU.mult)
                    self.transpose_to(mnT, mnb, 8, idb, BF16)
                    for (wsb, oname, isk) in ((wk, 'o_mk', True), (wv, 'o_mv', False)):
                        for g in range(2):
                            bk = self.bank()
                            for k in range(8):
                                K.mm(bk[:, :], mnT[:, k, :], wsb[:, k, g * 512:(g + 1) * 512], start=(k == 0), stop=(k == 7))
                            K.copy(mraw[1][:, g * 512:(g + 1) * 512], bk[:, :], eng='act' if g == 0 else 'dve')
                        K.dma(io[oname].v(lambda h: h[li, t * 128:(t + 1) * 128, :]), mraw[1][:])
                        if isk:
                            K.copy(mkb[:], mraw[1][:], eng='pool')
                            bk = self.bank()
                            for c8 in range(8):
                                K.tr(bk.v(lambda h: h.bitcast(BF16)[:, c8 * 128:(c8 + 1) * 128]), mkb[:, c8 * 128:(c8 + 1) * 128], idb[:])
                            K.copy(mkT[0].v(lambda h: h[:, :, t * 128:(t + 1) * 128]),
                                   bk.v(lambda h: h.bitcast(BF16)[:, :].rearrange("p (k t) -> p k t", k=8)), eng='act')
                        else:
                            K.copy(mvb[0][t][:], mraw[1][:], eng='pool')
            mq = K.sb(es, 'mq', [128, 1024], F32)
            mqb = K.sb(es, 'mqb', [128, 1024], BF16)
            mqT = K.sb(es, 'mqT', [128, 8, 128], BF16)
            mx = K.sb(es, 'mx', [128, 4], F32)
            rs = K.sb(es, 'rs', [128, 4], F32)
            pb = K.sb(es, 'pb', [128, 4, 256], BF16)
            pT = K.sb(es, 'pT', [128, 4, 2, 128], BF16)
            om = K.sb(es, 'om', [128, 1024], F32)

            def attend(np_, row0, kT, vb, tcol0):
                for hp in range(2):
                    bk = self.bank()
                    for hh in range(2):
                        h = hp * 2 + hh
                        for j in range(2):
                            K.mm(bk[0:np_, hh * 256:(hh + 1) * 256], mqT[:, 2 * h + j, tcol0:tcol0 + np_], kT[:, 2 * h + j, :],
                                 start=(j == 0), stop=(j == 1))
                    K.red(mx[0:np_, hp * 2:hp * 2 + 2], bk.v(lambda h_: h_[0:np_, :].rearrange("p (a b) -> p a b", a=2)), ALU.max)
                    K.ts(mx[0:np_, hp * 2:hp * 2 + 2], mx[0:np_, hp * 2:hp * 2 + 2], -1.0 / 16.0, ALU.mult)
                    for hh in range(2):
                        h = hp * 2 + hh
                        K.act(pb[0:np_, h, :], bk[0:np_, hh * 256:(hh + 1) * 256], AF.Exp, bias=mx[0:np_, h:h + 1], scale=1.0 / 16.0,
                              accum=rs[0:np_, h:h + 1])
                K.recip(rs[0:np_, :], rs[0:np_, :])
                bk = self.bank()
                for h in range(4):
                    for mt in range(2):
                        K.tr(bk.v(lambda h_: h_.bitcast(BF16)[:, (h * 2 + mt) * 128:(h * 2 + mt) * 128 + np_]),
                             pb[0:np_, h, mt * 128:(mt + 1) * 128], idb[0:np_, 0:np_])
                K.copy(pT.v(lambda h_: h_[:, :, :, 0:np_]),
                       bk.v(lambda h_: h_.bitcast(BF16)[:, :].rearrange("p (a b t) -> p a b t", a=4, b=2)[:, :, :, 0:np_]), eng='act')
                for hp in range(2):
                    bk = self.bank()
                    for hh in range(2):
                        h = hp * 2 + hh
                        for mt in range(2):
                            K.mm(bk[0:np_, hh * 256:(hh + 1) * 256], pT[:, h, mt, 0:np_], vb[mt][:, h * 256:(h + 1) * 256],
                                 start=(mt == 0), stop=(mt == 1))
                    K.tt(om.v(lambda h_: h_[0:np_, hp * 512:(hp + 1) * 512].rearrange("p (a b) -> p a b", a=2)),
                         bk.v(lambda h_: h_[0:np_, :].rearrange("p (a b) -> p a b", a=2)),
                         rs.v(lambda h_: h_[0:np_, hp * 2:hp * 2 + 2].unsqueeze(2).to_broadcast([np_, 2, 256])), ALU.mult)
                K.dma(io['ocat'][row0:row0 + np_, WM:WM + 1024], om[0:np_, :])

            for ti in range(self.NT):
                K.dma(mq[:], io['proj'][ti * 128:(ti + 1) * 128, W - 1024:W])
                K.copy(mqb[:], mq[:], eng='pool')
                self.transpose_to(mqT, mqb, 8, idb, BF16)
                attend(128, ti * 128, mkT[0], mvb[0], 0)
            K.dma(mq[:], io['proj'][L:L + 128, W - 1024:W])
            K.copy(mqb[:], mq[:], eng='pool')
            self.transpose_to(mqT, mqb, 8, idb, BF16)
            for s in range(4):
                for t in range(2):
                    K.dma(mraw[0][:], io['cmk'].v(lambda h: h[li, s, t * 128:(t + 1) * 128, :]))
                    K.dma(mraw[1][:], io['cmv'].v(lambda h: h[li, s, t * 128:(t + 1) * 128, :]), q='act')
                    K.copy(mkb[:], mraw[0][:], eng='pool')
                    bk = self.bank()
                    for c8 in range(8):
                        K.tr(bk.v(lambda h: h.bitcast(BF16)[:, c8 * 128:(c8 + 1) * 128]), mkb[:, c8 * 128:(c8 + 1) * 128], idb[:])
                    K.copy(mkT[1].v(lambda h: h[:, :, t * 128:(t + 1) * 128]),
                           bk.v(lambda h: h.bitcast(BF16)[:, :].rearrange("p (k t) -> p k t", k=8)), eng='act')
                    K.copy(mvb[1][t][:], mraw[1][:], eng='pool')
                attend(32, L + 32 * s, mkT[1], mvb[1], 32 * s)
        K.S.barrier()

    def phase_B(self, li):
        kd = self.kinds[li]
        self.phase_B_mem(li)
        getattr(self, 'phase_B_' + kd)(li)

    def conv_tile(self, ti, sh, cw, dst, cbias=None):
        K, io, L = self.K, self.io, self.L
        for j in range(4):
            if ti < self.NT:
                r = self.cv_row(-1) + ti * 128 - 3 + j
                K.dma(sh[j][:], io['cvin'][r:r + 128, :], q='sp' if j % 2 == 0 else 'act')
            else:
                for s_ in range(4):
                    r = self.cv_row(s_) - 3 + j
                    K.dma(sh[j][s_ * 32:(s_ + 1) * 32, :], io['cvin'][r:r + 32, :], q='sp' if j % 2 == 0 else 'act')
        for j in range(4):
            K.tt(sh[j][:], sh[j][:], cw[j][:], ALU.mult, eng='dve' if j % 2 == 0 else 'pool')
        K.tt(sh[0][:], sh[0][:], sh[1][:], ALU.add)
        K.tt(sh[2][:], sh[2][:], sh[3][:], ALU.add, eng='pool')
        K.tt(sh[0][:], sh[0][:], sh[2][:], ALU.add)
        if cbias is not None:
            K.tt(sh[0][:], sh[0][:], cbias[:], ALU.add)
        K.act(dst, sh[0][:], AF.Silu)

    def conv_states(self, li, pname, sname):
        K, io, L = self.K, self.io, self.L
        K.dma(io['cvin'][0:3, :], self.c['zeros'][0:3, :])
        for s_ in range(4):
            r = self.cv_row(s_) - 3
            K.dma(io['cvin'][r:r + 3, :], io['st_c_l%d' % li].v(lambda h: h[s_]))

    def conv_state_out(self, li):
        K, io, L = self.K, self.io, self.L
        r = self.cv_row(-1) + L - 3
        K.dma(io['p_c_l%d' % li][0:3, :], io['cvin'][r:r + 3, :])
        for s_ in range(4):
            r = self.cv_row(s_) + 32 - 3
            K.dma(io['s_c_l%d' % li].v(lambda h: h[s_]), io['cvin'][r:r + 3, :])

    def phase_B_delta(self, li):
        K, io, L = self.K, self.io, self.L
        idf = self.c['idf']
        self.conv_state_out(li)
        with ExitStack() as es:
            cw = [K.sb(es, 'cw', [128, 3072], F32) for _ in range(4)]
            for j in range(4):
                K.dma(cw[j][:], io['conv_l%d' % li].v(lambda h: h[j:j + 1, :].partition_broadcast(128)))
            alB = K.sb(es, 'alB', [128, 8], F32)
            dtB = K.sb(es, 'dtB', [128, 8], F32)
            K.dma(alB[:], io['alog_l%d' % li].v(lambda h: h[0:1, :].partition_broadcast(128)))
            K.dma(dtB[:], io['dtb_l%d' % li].v(lambda h: h[0:1, :].partition_broadcast(128)))
            K.act(alB[:], alB[:], AF.Exp)
            K.ts(alB[:], alB[:], -1.0, ALU.mult)
            sh = [K.sb(es, 'sh', [128, 3072], F32) for _ in range(4)]
            qkv = K.sb(es, 'qkv', [128, 3072], F32)
            sq = K.sb(es, 'sq', [128, 2048], F32)
            ss = K.sb(es, 'ss', [128, 16], F32)
            ab = K.sb(es, 'ab', [128, 16], F32)
            ax = K.sb(es, 'ax', [128, 24], F32)
            for ti in range(self.NT + 1):
                r0 = ti * 128
                self.conv_tile(ti, sh, cw, qkv[:])
                K.act(sq[:], qkv[:, 0:2048], AF.Square)
                K.red(ss[:], sq.v(lambda h: h[:, :].rearrange("p (a b) -> p a b", a=16)), ALU.add)
                self.rsqrt(ss[:], ss[:], 1.0, EPS)
                K.ts(ss[:, 0:8], ss[:, 0:8], 128.0 ** -0.5, ALU.mult)
                K.tt(qkv.v(lambda h: h[:, 0:2048].rearrange("p (a b) -> p a b", a=16)),
                     qkv.v(lambda h: h[:, 0:2048].rearrange("p (a b) -> p a b", a=16)),
                     ss.v(lambda h: h[:, :].unsqueeze(2).to_broadcast([128, 16, 128])), ALU.mult)
                K.dma(io['cvo'][r0:r0 + 128, :], qkv[:])
                K.dma(ab[:], io['proj'][r0:r0 + 128, 4096:4112], q='act')
                K.tt(ax[:, 0:8], ab[:, 0:8], dtB[:], ALU.add)
                K.act(ax[:, 0:8], ax[:, 0:8], AF.Exp)
                K.act(ax[:, 0:8], ax[:, 0:8], AF.Ln, bias=1.0)
                K.tt(ax[:, 0:8], ax[:, 0:8], alB[:], ALU.mult)
                K.act(ax[:, 8:16], ab[:, 8:16], AF.Sigmoid)
                K.ts(ax[:, 16:24], ax[:, 8:16], -1.0, ALU.mult)
                K.dma(io['aux'][r0:r0 + 128, 0:24], ax[:])
        K.S.barrier()
        with ExitStack() as es:
            T = {}
            def t(name, shape, dt=F32):
                T[name] = K.sb(es, name, shape, dt)
            t('qkv', [128, 3072]); t('z', [128, 1024]); t('ax', [128, 24])
            t('gam', [128, 8]); t('eg', [128, 8]); t('beg', [128, 8]); t('w2', [128, 8])
            t('glast', [128, 8]); t('egl', [128, 8])
            t('Qeg4', [128, 4, 128]); t('bV4', [128, 4, 128]); t('bKeg4', [128, 4, 128]); t('Kw4', [128, 4, 128])
            t('QT4', [128, 4, 128]); t('KT4', [128, 4, 128]); t('QegT4', [128, 4, 128])
            t('gbc', [128, 4, 128]); t('d4', [128, 4, 128]); t('m1', [128, 4, 128]); t('m2', [128, 4, 128])
            t('decs4', [128, 4, 128]); t('decT4', [128, 4, 128])
            t('N4', [128, 4, 128]); t('M4', [128, 4, 128]); t('X4', [128, 4, 128])
            t('Pa', [128, 4, 128]); t('Pb', [128, 4, 128]); t('PTa', [128, 4, 128]); t('PTb', [128, 4, 128])
            t('U4', [128, 4, 128]); t('WT4', [128, 4, 128]); t('qkT4', [128, 4, 128]); t('dl4', [128, 4, 128])
            t('S', [128, 8, 128]); t('osb', [128, 8, 128]); t('osq', [128, 8, 128]); t('oss', [128, 8])
            t('ngB', [128, 128]); t('tmp4', [128, 4, 128])
            K.dma(T['ngB'][:], io['ng_l%d' % li].v(lambda h: h[0:1, :].partition_broadcast(128)))

            def V3(nm, c, j0=0, nj=4, w=None):
                w = c if w is None else w
                return T[nm].v(lambda h: h[0:c, j0:j0 + nj, 0:w])

            def chunk(c, row0):
                nl = {128: 7, 32: 5}[c]
                K.dma(T['qkv'][0:c, :], io['cvo'][row0:row0 + c, :])
                K.dma(T['z'][0:c, :], io['proj'][row0:row0 + c, 3072:4096], q='act')
                K.dma(T['ax'][0:c, :], io['aux'][row0:row0 + c, 0:24], q='act')
                bk = self.bank()
                K.mm(bk[0:c, 0:8], self.c['triu'][0:c, 0:c], T['ax'][0:c, 0:8])
                K.copy(T['gam'][0:c, :], bk[0:c, 0:8])
                K.act(T['eg'][0:c, :], T['gam'][0:c, :], AF.Exp)
                K.tt(T['beg'][0:c, :], T['eg'][0:c, :], T['ax'][0:c, 8:16], ALU.mult)
                for hg in range(2):
                    h0 = hg * 4
                    qv = lambda off: T['qkv'].v(lambda h: h[0:c, off + h0 * 128:off + (h0 + 4) * 128].rearrange("p (a b) -> p a b", a=4))
                    bc = lambda nm, lo: T[nm].v(lambda h: h[0:c, lo + h0:lo + h0 + 4].unsqueeze(2).to_broadcast([c, 4, 128]))
                    bcc = lambda nm, lo: T[nm].v(lambda h: h[0:c, lo + h0:lo + h0 + 4].unsqueeze(2).to_broadcast([c, 4, c]))
                    K.tt(V3('Qeg4', c, w=128), qv(0), bc('eg', 0), ALU.mult)
                    K.tt(V3('bV4', c, w=128), qv(2048), bc('ax', 8), ALU.mult, eng='pool')
                    K.tt(V3('bKeg4', c, w=128), qv(1024), bc('beg', 0), ALU.mult)
                    for (src_off, dst, srcT) in ((0, 'QT4', None), (1024, 'KT4', None), (None, 'QegT4', 'Qeg4')):
                        bk = self.bank()
                        for j in range(4):
                            if srcT is None:
                                src = T['qkv'][0:c, src_off + (h0 + j) * 128:src_off + (h0 + j + 1) * 128]
                            else:
                                src = T[srcT][0:c, j, :]
                            K.tr(bk[:, j * c:(j + 1) * c], src, idf[0:c, 0:c])
                        K.copy(T[dst].v(lambda h: h[:, :, 0:c]), bk.v(lambda h: h[:, 0:4 * c].rearrange("p (a b) -> p a b", a=4)),
                               eng='act')
                    bkg = self.bank()
                    for j in range(4):
                        K.ts(T['gbc'][0:c, j, :], self.c['ones'][0:c, :], T['ax'][0:c, h0 + j:h0 + j + 1], ALU.mult)
                        K.mm(bkg[:, j * c:(j + 1) * c], T['gbc'][0:c, j, :], self.c['triu'][0:c, 0:c])
                    gB3 = bkg.v(lambda h: h[0:c, 0:4 * c].rearrange("p (a b) -> p a b", a=4))
                    K.copy(T['glast'].v(lambda h: h[:, h0:h0 + 4]),
                           bkg.v(lambda h: h[:, 0:4 * c].rearrange("p (a b) -> p a b", a=4)[:, :, c - 1]))
                    K.act(T['egl'][:, h0:h0 + 4], T['glast'][:, h0:h0 + 4], AF.Exp)
                    K.tt(T['w2'][0:c, h0:h0 + 4], T['glast'][0:c, h0:h0 + 4], T['gam'][0:c, h0:h0 + 4], ALU.subtract)
                    K.act(T['w2'][0:c, h0:h0 + 4], T['w2'][0:c, h0:h0 + 4], AF.Exp)
                    K.tt(V3('Kw4', c, w=128), qv(1024), bc('w2', 0), ALU.mult, eng='pool')
                    K.tt(V3('d4', c), gB3, bcc('gam', 0), ALU.subtract)
                    K.ts(V3('m1', c), V3('d4', c), 0.0, ALU.max)
                    K.ts(V3('m2', c), V3('d4', c), 0.0, ALU.min, eng='pool')
                    K.act(V3('m1', c), V3('m1', c), AF.Exp, scale=-1.0)
                    K.act(V3('m2', c), V3('m2', c), AF.Exp)
                    K.tt(V3('decs4', c), V3('m1', c), self.c['stril'].v(lambda h: h[0:c, 0:c].unsqueeze(1).to_broadcast([c, 4, c])), ALU.mult)
                    K.tt(V3('decT4', c), V3('m2', c), self.c['triu'].v(lambda h: h[0:c, 0:c].unsqueeze(1).to_broadcast([c, 4, c])), ALU.mult,
                         eng='pool')
                    bk = self.bank()
                    for j in range(4):
                        K.mm(bk[0:c, j * c:(j + 1) * c], T['KT4'][:, j, 0:c], T['KT4'][:, j, 0:c])
                    K.tt(V3('tmp4', c), bk.v(lambda h: h[0:c, 0:4 * c].rearrange("p (a b) -> p a b", a=4)), bcc('ax', 16), ALU.mult)
                    K.tt(V3('N4', c), V3('tmp4', c), V3('decs4', c), ALU.mult)
                    bk = self.bank()
                    for j in range(4):
                        K.tr(bk[0:c, j * c:(j + 1) * c], T['N4'][0:c, j, 0:c], idf[0:c, 0:c])
                    K.copy(V3('M4', c), bk.v(lambda h: h[0:c, 0:4 * c].rearrange("p (a b) -> p a b", a=4)), eng='act')
                    K.tt(V3('X4', c), V3('M4', c), idf.v(lambda h: h[0:c, 0:c].unsqueeze(1).to_broadcast([c, 4, c])), ALU.add)
                    P, PT = 'M4', 'N4'
                    nxt = [('Pa', 'PTa'), ('Pb', 'PTb')]
                    for k in range(1, nl):
                        Pn, PTn = nxt[k % 2]
                        bk = self.bank()
                        for j in range(4):
                            K.mm(bk[0:c, j * c:(j + 1) * c], T[P][0:c, j, 0:c], T[PT][0:c, j, 0:c])
                        K.copy(V3(PTn, c), bk.v(lambda h: h[0:c, 0:4 * c].rearrange("p (a b) -> p a b", a=4)), eng='act')
                        if k < nl - 1:
                            bk2 = self.bank()
                            for j in range(4):
                                K.mm(bk2[0:c, j * c:(j + 1) * c], T[PT][0:c, j, 0:c], T[P][0:c, j, 0:c])
                            K.copy(V3(Pn, c), bk2.v(lambda h: h[0:c, 0:4 * c].rearrange("p (a b) -> p a b", a=4)), eng='pool' if False else 'dve')
                        bk3 = self.bank()
                        for j in range(4):
                            K.mm(bk3[0:c, j * c:(j + 1) * c], T[PTn][0:c, j, 0:c], T['X4'][0:c, j, 0:c])
                        K.tt(V3('X4', c), V3('X4', c), bk3.v(lambda h: h[0:c, 0:4 * c].rearrange("p (a b) -> p a b", a=4)), ALU.add)
                        P, PT = Pn, PTn
                    bk = self.bank()
                    for j in range(4):
                        K.mm(bk[0:c, j * 128:(j + 1) * 128], T['X4'][0:c, j, 0:c], T['bV4'][0:c, j, :])
                    K.copy(V3('U4', c, w=128), bk.v(lambda h: h[0:c, :].rearrange("p (a b) -> p a b", a=4)), eng='act')
                    bk = self.bank()
                    for j in range(4):
                        K.mm(bk[:, j * c:(j + 1) * c], T['bKeg4'][0:c, j, :], T['X4'][0:c, j, 0:c])
                    K.copy(T['WT4'].v(lambda h: h[:, :, 0:c]), bk.v(lambda h: h[:, 0:4 * c].rearrange("p (a b) -> p a b", a=4)), eng='act')
                    bk = self.bank()
                    for j in range(4):
                        K.mm(bk[0:c, j * c:(j + 1) * c], T['KT4'][:, j, 0:c], T['QT4'][:, j, 0:c])
                    K.tt(V3('qkT4', c), bk.v(lambda h: h[0:c, 0:4 * c].rearrange("p (a b) -> p a b", a=4)), V3('decT4', c), ALU.mult)
                    bk = self.bank()
                    for j in range(4):
                        K.mm(bk[0:c, j * 128:(j + 1) * 128], T['WT4'][:, j, 0:c], T['S'][:, h0 + j, :])
                    K.tt(V3('dl4', c, w=128), V3('U4', c, w=128), bk.v(lambda h: h[0:c, :].rearrange("p (a b) -> p a b", a=4)), ALU.subtract)
                    bk = self.bank()
                    for j in range(4):
                        K.mm(bk[0:c, j * 128:(j + 1) * 128], T['QegT4'][:, j, 0:c], T['S'][:, h0 + j, :], start=True, stop=False)
                        K.mm(bk[0:c, j * 128:(j + 1) * 128], T['qkT4'][0:c, j, 0:c], T['dl4'][0:c, j, :], start=False, stop=True)
                    K.copy(T['osb'].v(lambda h: h[0:c, h0:h0 + 4, :]), bk.v(lambda h: h[0:c, :].rearrange("p (a b) -> p a b", a=4)), eng='act')
                    bk = self.bank()
                    for j in range(4):
                        K.mm(bk[:, j * 128:(j + 1) * 128], T['Kw4'][0:c, j, :], T['dl4'][0:c, j, :])
                    Sv = T['S'].v(lambda h: h[:, h0:h0 + 4, :])
                    K.tt(Sv, Sv, T['egl'].v(lambda h: h[:, h0:h0 + 4].unsqueeze(2).to_broadcast([128, 4, 128])), ALU.mult)
                    K.tt(Sv, Sv, bk.v(lambda h: h[:, :].rearrange("p (a b) -> p a b", a=4)), ALU.add)
                ov = T['osb'].v(lambda h: h[0:c, :, :])
                K.act(T['osq'].v(lambda h: h[0:c, :, :]), ov, AF.Square)
                K.red(T['oss'][0:c, :], T['osq'].v(lambda h: h[0:c, :, :]), ALU.add)
                self.rsqrt(T['oss'][0:c, :], T['oss'][0:c, :], 1.0 / 128.0, EPS)
                K.tt(ov, ov, T['oss'].v(lambda h: h[0:c, :].unsqueeze(2).to_broadcast([c, 8, 128])), ALU.mult)
                K.tt(ov, ov, T['ngB'].v(lambda h: h[0:c, :].unsqueeze(1).to_broadcast([c, 8, 128])), ALU.mult, eng='pool')
                K.act(T['z'][0:c, :], T['z'][0:c, :], AF.Silu)
                K.tt(ov, ov, T['z'].v(lambda h: h[0:c, :].rearrange("p (a b) -> p a b", a=8)), ALU.mult)
                K.dma(io['ocat'][row0:row0 + c, 0:1024], T['osb'].v(lambda h: h[0:c, :, :].rearrange("p a b -> p (a b)")))

            K.memset(T['S'][:], 0.0)
            for n in range(self.NT):
                chunk(128, n * 128)
            K.dma(io['p_d_l%d' % li].v(lambda h: h.rearrange("a d e -> d a e")), T['S'][:])
            for s_ in range(4):
                K.dma(T['S'][:], io['st_d_l%d' % li].v(lambda h: h[s_].rearrange("a d e -> d a e")))
                chunk(32, L + 32 * s_)
                K.dma(io['s_d_l%d' % li].v(lambda h: h[s_].rearrange("a d e -> d a e")), T['S'][:])
        K.S.barrier()

    def phase_B_fox(self, li):
        K, io, L, PAST = self.K, self.io, self.L, self.PAST
        idf, idb = self.c['idf'], self.c['idb']
        NT = self.NT
        NP = PAST // 128
        for r0 in range(0, L, 512):
            r1 = min(L, r0 + 512)
            K.dma(io['p_k_l%d' % li][r0:r1, :], io['proj'][r0:r1, 1024:2048])
            K.dma(io['p_v_l%d' % li][r0:r1, :], io['proj'][r0:r1, 2048:3072], q='act')
        K.dma(io['s_k_l%d' % li][:, :], io['proj'][L:L + 128, 1024:2048])
        K.dma(io['s_v_l%d' % li][:, :], io['proj'][L:L + 128, 2048:3072])
        with ExitStack() as es:
            bfB = K.sb(es, 'bfB', [128, 16], F32)
            K.dma(bfB[:], io['bf_l%d' % li].v(lambda h: h[0:1, :].partition_broadcast(128)))
            lf = K.sb(es, 'lf', [128, 16], F32)
            lfT = K.sb(es, 'lfT', [16, max(L, PAST + 32)], F32)
            ncT = K.sb(es, 'ncT', [16, max(L, PAST + 32)], F32)

            def logf_tile(r0, col0, n):
                K.dma(lf[0:n, :], io['proj'][r0:r0 + n, 3072:3088])
                K.tt(lf[0:n, :], lf[0:n, :], bfB[0:n, :], ALU.add)
                K.act(lf[0:n, :], lf[0:n, :], AF.Exp, scale=-1.0)
                K.act(lf[0:n, :], lf[0:n, :], AF.Ln, bias=1.0)
                K.ts(lf[0:n, :], lf[0:n, :], -1.0, ALU.mult)
                K.dma(io['aux'][r0:r0 + n, 0:16], lf[0:n, :])
                bk = self.bank()
                K.tr(bk[0:16, 0:n], lf[0:n, :], idf[0:n, 0:n])
                K.copy(lfT[:, col0:col0 + n], bk[0:16, 0:n])

            def neg_cumsum(n):
                SC = getattr(self, 'scan_chunk', 2048)
                for c0 in range(0, n, SC):
                    c1 = min(n, c0 + SC)
                    init = 0.0 if c0 == 0 else ncT[:, c0 - 1:c0]
                    o, d0, d1 = ncT[:, c0:c1], lfT[:, c0:c1], self.c['zeros'][0:16, 0:c1 - c0]
                    ia = init.ap if isinstance(init, V) else init
                    K.S.op('dve', lambda e, o=o, d0=d0, d1=d1, ia=ia: e.tensor_tensor_scan(
                        out=o.ap, data0=d0.ap, data1=d1.ap, initial=ia, op0=ALU.add, op1=ALU.add),
                        reads=[lfT.name, self.c['zeros'].name, ncT.name], writes=[ncT.name])
                K.ts(lfT[:, 0:n], ncT[:, 0:n], -1.0, ALU.mult)

            for ti in range(NT):
                logf_tile(ti * 128, ti * 128, 128)
            neg_cumsum(L)
            K.dma(io['cumT'][:, 0:L], lfT[:, 0:L])
            for r0 in range(0, L, 512):
                r1 = min(L, r0 + 512)
                K.dma(io['p_f_l%d' % li][r0:r1, :], io['aux'][r0:r1, 0:16])
            for s_ in range(4):
                for t in range(NP):
                    K.dma(lf[:, :], io['c_f_l%d' % li].v(lambda h: h[s_, t * 128:(t + 1) * 128, :]))
                    bk = self.bank()
                    K.tr(bk[0:16, 0:128], lf[:, :], idf[:, :])
                    K.copy(lfT[:, t * 128:(t + 1) * 128], bk[0:16, 0:128])
                logf_tile(L + 32 * s_, PAST, 32)
                neg_cumsum(PAST + 32)
                K.dma(io['cumTs'].v(lambda h: h[s_, :, 0:PAST + 32]), lfT[:, 0:PAST + 32])
            K.dma(io['s_f_l%d' % li][:, :], io['aux'][L:L + 128, 0:16])
        K.S.barrier()
        with ExitStack() as es:
            KMAX = max(L, PAST + 32)
            NKT = max(NT, NP)
            raw = K.sb(es, 'raw', [128, NKT, 128], F32)
            rawb = K.sb(es, 'rawb', [128, NKT, 128], BF16)
            QT = K.sb(es, 'QT', [128, L], BF16)
            KT = K.sb(es, 'KT', [128, KMAX], BF16)
            Vp = K.sb(es, 'Vp', [128, NKT, 128], BF16)
            vnew = K.sb(es, 'vnew', [32, 128], BF16)
            ncr = [K.sb(es, 'ncr', [1, KMAX], F32) for _ in range(2)]
            sc = K.sb(es, 'sc', [128, KMAX], F32)
            pbf = K.sb(es, 'pbf', [128, KMAX], BF16)
            pT = K.sb(es, 'pT', [128, NKT, 128], BF16)
            pTt = K.sb(es, 'pTt', [32, 32], BF16)
            mx = K.sb(es, 'mx', [128, 1], F32)
            rs = K.sb(es, 'rs', [128, 1], F32)
            osb = K.sb(es, 'osb', [128, 128], F32)
            q32 = K.sb(es, 'q32', [32, 128], F32)
            q32b = K.sb(es, 'q32b', [32, 128], BF16)
            cmask = K.sb(es, 'cmask', [128, 128], F32)
            K.memset(cmask[:], 0.0)
            K.aselect(cmask[:], cmask[:], [[-1, 128]], ALU.is_ge, NEG, 0, 1)
            ones1 = self.c['ones']

            def load_cols(dst, src2d, ntile):
                for t0 in range(0, ntile, 8):
                    nb = min(8, ntile - t0)
                    K.dma(dst[:, t0:t0 + nb, :],
                          V(src2d.ap[t0 * 128:(t0 + nb) * 128, :].rearrange("(n p) c -> p n c", p=128), src2d.key),
                          q='sp' if (t0 // 8) % 2 == 0 else 'act')

            def to_T(dst, src_b, ntile, npart=128):
                for t0 in range(0, ntile, 8):
                    nb = min(8, ntile - t0)
                    bk = self.bank()
                    for j in range(nb):
                        K.tr(bk.v(lambda h: h.bitcast(BF16)[:, j * 128:j * 128 + npart]), src_b[0:npart, t0 + j, :], idb[0:npart, 0:npart])
                    K.copy(dst.v(lambda h: h[:, t0 * 128:(t0 + nb) * 128].rearrange("p (a b) -> p a b", a=nb)[:, :, 0:npart]),
                           bk.v(lambda h: h.bitcast(BF16)[:, 0:nb * 128].rearrange("p (a b) -> p a b", a=nb)[:, :, 0:npart]),
                           eng='act' if (t0 // 8) % 2 == 0 else 'dve')

            def attend(nq, qTv, p0, nfull, ntail, ncrow, hh, out_v):
                keys = nfull * 128 + ntail
                dw = ntail if ntail else 128
                d0 = keys - dw
                g = 0
                for c0 in range(0, keys, 512):
                    c1 = min(keys, c0 + 512)
                    bk = self.bank()
                    K.mm(bk[0:nq, 0:c1 - c0], qTv, KT[p0:p0 + 64, c0:c1], start=True, stop=False)
                    K.mm(bk[0:nq, 0:c1 - c0], ones1[0:1, 0:nq], ncrow[0:1, c0:c1], start=False, stop=True)
                    e1 = min(c1, d0)
                    if e1 > c0:
                        K.copy(sc[0:nq, c0:e1], bk[0:nq, 0:e1 - c0], eng='act' if g % 2 == 0 else 'dve')
                    if c1 > d0:
                        K.tt(sc[0:nq, d0:keys], bk[0:nq, d0 - c0:keys - c0], cmask[0:nq, 0:dw], ALU.add)
                    g += 1
                K.red(mx[0:nq, :], sc[0:nq, 0:keys], ALU.max)
                K.ts(mx[0:nq, :], mx[0:nq, :], -1.0, ALU.mult)
                K.act(pbf[0:nq, 0:keys], sc[0:nq, 0:keys], AF.Exp, bias=mx[0:nq, :], accum=rs[0:nq, :])
                K.recip(rs[0:nq, :], rs[0:nq, :])
                for t0 in range(0, nfull, 8):
                    nb = min(8, nfull - t0)
                    bk = self.bank()
                    for j in range(nb):
                        K.tr(bk.v(lambda h: h.bitcast(BF16)[:, j * 128:j * 128 + nq]), pbf[0:nq, (t0 + j) * 128:(t0 + j + 1) * 128], idb[0:nq, 0:nq])
                    K.copy(pT.v(lambda h: h[:, t0:t0 + nb, 0:nq]),
                           bk.v(lambda h: h.bitcast(BF16)[:, 0:nb * 128].rearrange("p (a b) -> p a b", a=nb)[:, :, 0:nq]),
                           eng='act' if (t0 // 8) % 2 == 0 else 'dve')
                if ntail:
                    bk = self.bank()
                    K.tr(bk.v(lambda h: h.bitcast(BF16)[0:ntail, 0:nq]), pbf[0:nq, nfull * 128:keys], idb[0:nq, 0:nq])
                    K.copy(pTt[0:ntail, 0:nq], bk.v(lambda h: h.bitcast(BF16)[0:ntail, 0:nq]))
                bk = self.bank()
                for kt in range(nfull):
                    K.mm(bk[0:nq, 0:64], pT[:, kt, 0:nq], Vp[:, kt, hh * 64:(hh + 1) * 64], start=(kt == 0), stop=(kt == nfull - 1 and not ntail))
                if ntail:
                    K.mm(bk[0:nq, 0:64], pTt[0:ntail, 0:nq], vnew[0:ntail, hh * 64:(hh + 1) * 64], start=(nfull == 0), stop=True)
                K.ts(out_v, bk[0:nq, 0:64], rs[0:nq, :], ALU.mult)

            for m in range(8):
                cq, ck, cv = m * 128, 1024 + m * 128, 2048 + m * 128
                load_cols(raw, io['proj'][0:L, cq:cq + 128], NT)
                K.ts(rawb[:, 0:NT, :], raw[:, 0:NT, :], 0.125, ALU.mult)
                to_T(QT, rawb, NT)
                load_cols(raw, io['proj'][0:L, ck:ck + 128], NT)
                K.copy(rawb[:, 0:NT, :], raw[:, 0:NT, :], eng='pool')
                to_T(KT, rawb, NT)
                load_cols(raw, io['proj'][0:L, cv:cv + 128], NT)
                K.copy(Vp[:, 0:NT, :], raw[:, 0:NT, :], eng='pool')
                for hh in range(2):
                    K.dma(ncr[hh][0:1, 0:L], io['cumT'][2 * m + hh:2 * m + hh + 1, 0:L], q='act')
                for qt in range(NT):
                    for hh in range(2):
                        p0 = 64 * hh
                        attend(128, QT[p0:p0 + 64, qt * 128:(qt + 1) * 128], p0, qt + 1, 0, ncr[hh], hh, osb[:, hh * 64:(hh + 1) * 64])
                    K.dma(io['ocat'][qt * 128:(qt + 1) * 128, cq:cq + 128], osb[:, :])
                for s_ in range(4):
                    r0 = L + 32 * s_
                    K.dma(q32[:, :], io['proj'][r0:r0 + 32, cq:cq + 128])
                    K.ts(rawb[0:32, 0, :], q32[:, :], 0.125, ALU.mult)
                    to_T(QT, rawb, 1, npart=32)
                    load_cols(raw, io['c_k_l%d' % li].v(lambda h: h[s_, :, m * 128:(m + 1) * 128]), NP)
                    K.copy(rawb[:, 0:NP, :], raw[:, 0:NP, :], eng='pool')
                    to_T(KT, rawb, NP)
                    K.dma(q32[:, :], io['proj'][r0:r0 + 32, ck:ck + 128])
                    K.copy(q32b[:, :], q32[:, :])
                    bk = self.bank()
                    K.tr(bk.v(lambda h: h.bitcast(BF16)[:, 0:32]), q32b[:, :], idb[0:32, 0:32])
                    K.copy(KT[:, PAST:PAST + 32], bk.v(lambda h: h.bitcast(BF16)[:, 0:32]))
                    load_cols(raw, io['c_v_l%d' % li].v(lambda h: h[s_, :, m * 128:(m + 1) * 128]), NP)
                    K.copy(Vp[:, 0:NP, :], raw[:, 0:NP, :], eng='pool')
                    K.dma(q32[:, :], io['proj'][r0:r0 + 32, cv:cv + 128])
                    K.copy(vnew[:, :], q32[:, :])
                    for hh in range(2):
                        K.dma(ncr[hh][0:1, 0:PAST + 32], io['cumTs'].v(lambda h: h[s_, 2 * m + hh:2 * m + hh + 1, 0:PAST + 32]), q='act')
                    for hh in range(2):
                        p0 = 64 * hh
                        attend(32, QT[p0:p0 + 64, 0:32], p0, NP, 32, ncr[hh], hh, osb[0:32, hh * 64:(hh + 1) * 64])
                    K.dma(io['ocat'][r0:r0 + 32, cq:cq + 128], osb[0:32, :])
        K.S.barrier()

    def phase_B_ssd(self, li):
        K, io, L = self.K, self.io, self.L
        idf = self.c['idf']
        self.conv_state_out(li)
        with ExitStack() as es:
            cw = [K.sb(es, 'cw', [128, 3072], F32) for _ in range(4)]
            for j in range(4):
                K.dma(cw[j][:], io['conv_l%d' % li].v(lambda h: h[j:j + 1, :].partition_broadcast(128)))
            cb = K.sb(es, 'cb', [128, 3072], F32)
            K.dma(cb[:], io['convb_l%d' % li].v(lambda h: h[0:1, :].partition_broadcast(128)))
            alB = K.sb(es, 'alB', [128, 32], F32)
            dtB = K.sb(es, 'dtB', [128, 32], F32)
            K.dma(alB[:], io['alog_l%d' % li].v(lambda h: h[0:1, :].partition_broadcast(128)))
            K.dma(dtB[:], io['dtb_l%d' % li].v(lambda h: h[0:1, :].partition_broadcast(128)))
            K.act(alB[:], alB[:], AF.Exp)
            K.ts(alB[:], alB[:], -1.0, ALU.mult)
            sh = [K.sb(es, 'sh', [128, 3072], F32) for _ in range(4)]
            xbc = K.sb(es, 'xbc', [128, 3072], F32)
            ab = K.sb(es, 'ab', [128, 32], F32)
            ax = K.sb(es, 'ax', [128, 64], F32)
            for ti in range(self.NT + 1):
                r0 = ti * 128
                self.conv_tile(ti, sh, cw, xbc[:], cbias=cb)
                K.dma(io['cvo'][r0:r0 + 128, :], xbc[:])
                K.dma(ab[:], io['proj'][r0:r0 + 128, 5120:5152], q='act')
                K.tt(ax[:, 0:32], ab[:], dtB[:], ALU.add)
                K.act(ax[:, 0:32], ax[:, 0:32], AF.Exp)
                K.act(ax[:, 0:32], ax[:, 0:32], AF.Ln, bias=1.0)
                K.tt(ax[:, 32:64], ax[:, 0:32], alB[:], ALU.mult)
                K.dma(io['aux'][r0:r0 + 128, 0:64], ax[:])
        K.S.barrier()
        with ExitStack() as es:
            T = {}
            def t(name, shape, dt=F32):
                T[name] = K.sb(es, name, shape, dt)
            t('xbc', [128, 3072]); t('z', [128, 2048]); t('ax', [128, 64])
            t('gam', [128, 32]); t('w2', [128, 32]); t('glast', [128, 32]); t('egl', [128, 32])
            t('xdt', [128, 2048]); t('xdtw', [128, 2048]); t('xd', [128, 2048]); t('ysb', [128, 2048])
            t('BT', [128, 4, 128]); t('CT', [128, 4, 128]); t('cbT', [128, 4, 128])
            t('gbc', [128, 4, 128]); t('d4', [128, 4, 128]); t('MT4', [128, 4, 128]); t('egB4', [128, 4, 128]); t('CegT4', [128, 4, 128])
            t('hT', [128, 32, 64]); t('hO', [64, 32, 128])
            t('dB', [128, 32]); t('ngB', [128, 2048]); t('rstd', [128, 1])
            K.dma(T['dB'][:], io['dsk_l%d' % li].v(lambda h: h[0:1, :].partition_broadcast(128)))
            K.dma(T['ngB'][:], io['ng_l%d' % li].v(lambda h: h[0:1, :].partition_broadcast(128)))

            def chunk(c, row0):
                K.dma(T['xbc'][0:c, :], io['cvo'][row0:row0 + c, :])
                K.dma(T['z'][0:c, :], io['proj'][row0:row0 + c, 0:2048], q='act')
                K.dma(T['ax'][0:c, :], io['aux'][row0:row0 + c, 0:64], q='act')
                bk = self.bank()
                K.mm(bk[0:c, 0:32], self.c['triu'][0:c, 0:c], T['ax'][0:c, 32:64])
                K.copy(T['gam'][0:c, :], bk[0:c, 0:32])
                x3 = T['xbc'].v(lambda h: h[0:c, 0:2048].rearrange("p (a b) -> p a b", a=32))
                K.tt(T['xdt'].v(lambda h: h[0:c, :].rearrange("p (a b) -> p a b", a=32)), x3,
                     T['ax'].v(lambda h: h[0:c, 0:32].unsqueeze(2).to_broadcast([c, 32, 64])), ALU.mult)
                K.tt(T['xd'].v(lambda h: h[0:c, :].rearrange("p (a b) -> p a b", a=32)), x3,
                     T['dB'].v(lambda h: h[0:c, :].unsqueeze(2).to_broadcast([c, 32, 64])), ALU.mult, eng='pool')
                sub = getattr(self, 'sub', 99)
                if sub <= 1:
                    return
                for g in range(4):
                    Bg = T['xbc'][0:c, 2048 + g * 128:2048 + (g + 1) * 128]
                    Cg = T['xbc'][0:c, 2560 + g * 128:2560 + (g + 1) * 128]
                    bk = self.bank()
                    K.tr(bk[:, 0:c], Bg, idf[0:c, 0:c])
                    K.tr(bk[:, c:2 * c], Cg, idf[0:c, 0:c])
                    ea = 'act' if 'A' in getattr(self, 'var', '') else 'dve'
                    K.copy(T['BT'][:, g, 0:c], bk[:, 0:c], eng=ea)
                    K.copy(T['CT'][:, g, 0:c], bk[:, c:2 * c])
                    bk = self.bank()
                    K.mm(bk[0:c, 0:c], T['BT'][:, g, 0:c], T['CT'][:, g, 0:c])
                    K.copy(T['cbT'][0:c, g, 0:c], bk[0:c, 0:c], eng=ea)
                    if sub <= 2:
                        continue
                    bky = self.bank()
                    for half in range(2):
                        h0 = 8 * g + 4 * half
                        bkg = self.bank()
                        for j in range(4):
                            K.ts(T['gbc'][0:c, j, :], self.c['ones'][0:c, :], T['ax'][0:c, 32 + h0 + j:32 + h0 + j + 1], ALU.mult)
                            K.mm(bkg[:, j * c:(j + 1) * c], T['gbc'][0:c, j, :], self.c['triu'][0:c, 0:c])
                        g3 = lambda np_: bkg.v(lambda h: h[0:np_, 0:4 * c].rearrange("p (a b) -> p a b", a=4))
                        K.copy(T['glast'][:, h0:h0 + 4], bkg.v(lambda h: h[:, 0:4 * c].rearrange("p (a b) -> p a b", a=4)[:, :, c - 1]))
                        K.copy(T['egB4'].v(lambda h: h[:, :, 0:c]), g3(128))
                        K.act(T['egB4'].v(lambda h: h[:, :, 0:c]), T['egB4'].v(lambda h: h[:, :, 0:c]), AF.Exp)
                        K.tt(T['CegT4'].v(lambda h: h[:, :, 0:c]), T['egB4'].v(lambda h: h[:, :, 0:c]),
                             T['CT'].v(lambda h: h[:, g, 0:c].unsqueeze(1).to_broadcast([128, 4, c])), ALU.mult, eng='pool')
                        d4 = T['d4'].v(lambda h: h[0:c, :, 0:c])
                        M4 = T['MT4'].v(lambda h: h[0:c, :, 0:c])
                        K.tt(d4, g3(c), T['gam'].v(lambda h: h[0:c, h0:h0 + 4].unsqueeze(2).to_broadcast([c, 4, c])), ALU.subtract)
                        K.ts(d4, d4, 0.0, ALU.min)
                        K.act(d4, d4, AF.Exp)
                        K.tt(M4, d4, self.c['triu'].v(lambda h: h[0:c, 0:c].unsqueeze(1).to_broadcast([c, 4, c])), ALU.mult)
                        K.tt(M4, M4, T['cbT'].v(lambda h: h[0:c, g, 0:c].unsqueeze(1).to_broadcast([c, 4, c])), ALU.mult)
                        for j in range(4):
                            hh = h0 + j
                            oc = (4 * half + j) * 64
                            K.mm(bky[0:c, oc:oc + 64], T['MT4'][0:c, j, 0:c], T['xdt'][0:c, hh * 64:(hh + 1) * 64], start=True, stop=False)
                            K.mm(bky[0:c, oc:oc + 64], T['CegT4'][:, j, 0:c], T['hT'][:, hh, :], start=False, stop=True)
                    K.tt(T['ysb'][0:c, g * 512:(g + 1) * 512], bky[0:c, :], T['xd'][0:c, g * 512:(g + 1) * 512], ALU.add)
                    if sub <= 3:
                        continue
                    gs = slice(8 * g, 8 * g + 8)
                    K.act(T['egl'][:, gs], T['glast'][:, gs], AF.Exp)
                    K.tt(T['w2'][0:c, gs], T['glast'][0:c, gs], T['gam'][0:c, gs], ALU.subtract)
                    K.act(T['w2'][0:c, gs], T['w2'][0:c, gs], AF.Exp)
                    K.tt(T['xdtw'].v(lambda h: h[0:c, g * 512:(g + 1) * 512].rearrange("p (a b) -> p a b", a=8)),
                         T['xdt'].v(lambda h: h[0:c, g * 512:(g + 1) * 512].rearrange("p (a b) -> p a b", a=8)),
                         T['w2'].v(lambda h: h[0:c, gs].unsqueeze(2).to_broadcast([c, 8, 64])), ALU.mult, eng='pool')
                    bkh = self.bank()
                    for j8 in range(8):
                        hh = 8 * g + j8
                        K.mm(bkh[:, j8 * 64:(j8 + 1) * 64], Bg, T['xdtw'][0:c, hh * 64:(hh + 1) * 64])
                    hv = T['hT'].v(lambda h: h[:, gs, :])
                    K.tt(hv, hv, T['egl'].v(lambda h: h[:, gs].unsqueeze(2).to_broadcast([128, 8, 64])), ALU.mult)
                    K.tt(hv, hv, bkh.v(lambda h: h[:, :].rearrange("p (a b) -> p a b", a=8)), ALU.add)
                if sub <= 4:
                    return
                K.act(T['z'][0:c, :], T['z'][0:c, :], AF.Silu)
                K.tt(T['ysb'][0:c, :], T['ysb'][0:c, :], T['z'][0:c, :], ALU.mult)
                self.rmsnorm_rstd(es, T['ysb'][0:c, :], 2048, T['z'][0:c, :], T['rstd'][0:c, :])
                K.stt(T['ysb'][0:c, :], T['ysb'][0:c, :], T['rstd'][0:c, :], T['ngB'][0:c, :], ALU.mult, ALU.mult)
                K.dma(io['ocat'][row0:row0 + c, 0:2048], T['ysb'][0:c, :])

            def state_out(dst):
                for b8 in range(8):
                    bk = self.bank()
                    for j in range(4):
                        K.tr(bk[0:64, j * 128:(j + 1) * 128], T['hT'][:, b8 * 4 + j, :], idf[:, :])
                    K.copy(T['hO'].v(lambda h: h[:, b8 * 4:b8 * 4 + 4, :]), bk.v(lambda h: h[0:64, :].rearrange("p (a b) -> p a b", a=4)))
                for q4 in range(4):
                    K.dma(dst[:, q4 * 8:(q4 + 1) * 8, :], T['hO'][:, q4 * 8:(q4 + 1) * 8, :], q='sp' if q4 % 2 == 0 else 'act')

            def state_in(src):
                for q4 in range(4):
                    K.dma(T['hO'][:, q4 * 8:(q4 + 1) * 8, :], src[:, q4 * 8:(q4 + 1) * 8, :], q='sp' if q4 % 2 == 0 else 'act')
                for b8 in range(4):
                    bk = self.bank()
                    for j in range(8):
                        K.tr(bk[:, j * 64:(j + 1) * 64], T['hO'][:, b8 * 8 + j, :], idf[0:64, 0:64])
                    K.copy(T['hT'].v(lambda h: h[:, b8 * 8:b8 * 8 + 8, :]), bk.v(lambda h: h[:, :].rearrange("p (a b) -> p a b", a=8)))

            cut = getattr(self, 'cut', 99)
            K.memset(T['hT'][:], 0.0)
            if cut >= 2:
                for n in range(self.NT):
                    chunk(128, n * 128)
            if cut >= 3:
                state_out(io['p_s_l%d' % li].v(lambda h: h.rearrange("a p n -> p a n")))
            for s_ in range(4):
                if cut >= 4:
                    state_in(io['st_s_l%d' % li].v(lambda h: h[s_].rearrange("a p n -> p a n")))
                if cut >= 5:
                    chunk(32, L + 32 * s_)
                if cut >= 6:
                    state_out(io['s_s_l%d' % li].v(lambda h: h[s_].rearrange("a p n -> p a n")))
        K.S.barrier()

    def phase_C1(self, li):
        K, io, kd = self.K, self.io, self.kinds[li]
        WO = W_MIX[kd] + 1024
        KO = WO // 128
        with ExitStack() as es:
            wo = K.sb(es, 'wo', [128, KO, D], BF16)
            stg = [K.sb(es, 'stg', [128, 2048], F32) for _ in range(2)]
            self.load_w_bf16(io['w_out_l%d' % li], wo, KO, D, stg)
            oc = [K.sb(es, 'oc', [128, WO], F32) for _ in range(2)]
            ocb = [K.sb(es, 'ocb', [128, WO], BF16) for _ in range(2)]
            ocT = [K.sb(es, 'ocT', [128, KO, 128], BF16) for _ in range(2)]
            xt = [K.sb(es, 'xt', [128, D], F32) for _ in range(2)]
            x1 = [K.sb(es, 'x1', [128, D], F32) for _ in range(2)]
            for ti in range(self.NT + 1):
                b = ti % 2
                r0 = ti * 128
                K.dma(oc[b][:], io['ocat'][r0:r0 + 128, 0:WO])
                if li == 0:
                    src = io['xp'][r0:r0 + 128, :] if ti < self.NT else io['xs'][:, :]
                else:
                    src = io['xres'][r0:r0 + 128, :]
                K.dma(xt[b][:], src, q='act')
                K.copy(ocb[b][:], oc[b][:], eng='pool')
                self.transpose_to(ocT[b], ocb[b], KO, self.c['idb'], BF16)
                for g in range(2):
                    bk = self.bank()
                    for k in range(KO):
                        K.mm(bk[:, :], ocT[b][:, k, :], wo[:, k, g * 512:(g + 1) * 512], start=(k == 0), stop=(k == KO - 1))
                    K.tt(x1[b][:, g * 512:(g + 1) * 512], xt[b][:, g * 512:(g + 1) * 512], bk[:, :], ALU.add)
                K.dma(io['xres'][r0:r0 + 128, :], x1[b][:])
        K.S.barrier()

    def phase_C2(self, li):
        K, io = self.K, self.io
        last = (li == self.DEPTH - 1)
        G = 12
        GS = 4
        self.nrot = 6
        yps = [self.pb[6], self.pb[7]]
        idb = self.c['idb']
        with ExitStack() as es:
            wq = K.sb(es, 'wq', [128, 8, 2048], BF16)
            with ExitStack() as es2:
                stg = [K.sb(es2, 'stg', [128, 2048], F32) for _ in range(2)]
                self.load_w_bf16(io['peer_wq'].v(lambda h: h[li]), wq, 8, 2048, stg)
                K.S.barrier()
            gB = K.sb(es, 'gB', [128, D], F32)
            K.dma(gB[:], io['g_ffn'].v(lambda h: h[li:li + 1, :].partition_broadcast(128)))
            if last:
                gF = K.sb(es, 'gF', [128, D], F32)
                K.dma(gF[:], io['g_final'].v(lambda h: h[0:1, :].partition_broadcast(128)))
            kraw = K.sb(es, 'kraw', [128, 2, 128], F32)
            kT = K.sb(es, 'kT', [128, 2, 128], F32)
            K.dma(kraw[:, 0, :], io['peer_k1'].v(lambda h: h[li]))
            K.dma(kraw[:, 1, :], io['peer_k2'].v(lambda h: h[li]))
            bk = self.bank()
            for hf in range(2):
                K.tr(bk[:, hf * 128:(hf + 1) * 128], kraw[:, hf, :], self.c['idf'][:])
            K.copy(kT.v(lambda h: h[:, :, :]), bk.v(lambda h: h[:, 0:256].rearrange("p (k t) -> p k t", k=2)))
            io16i = K.sb(es, 'io16i', [128, 16], I32)
            io16 = K.sb(es, 'io16', [128, 16], F32)
            K.iota(io16i[:], [[1, 16]], 0, 0)
            K.copy(io16[:], io16i[:])

            xt2 = [K.sb(es, 'xt', [128, D], F32) for _ in range(2)]
            sq = K.sb(es, 'sq', [128, D], F32)
            rstd = K.sb(es, 'rstd', [128, 1], F32)
            xnb2 = [K.sb(es, 'xnb', [128, D], BF16) for _ in range(2)]
            xnT = K.sb(es, 'xnT', [128, 8, 128], BF16)
            qT = K.sb(es, 'qT', [128, 16, 128], F32)
            sc = K.sb(es, 'sc', [128, 16, 128], F32)
            sc2 = K.sb(es, 'sc2', [128, 16, 128], F32)
            v12 = K.sb(es, 'v12', [128, 16, 16], F32)
            i12 = K.sb(es, 'i12', [128, 16, 16], U32)
            i12f = K.sb(es, 'i12f', [128, 16, 16], F32)
            cand = K.sb(es, 'cand', [128, 8, 256], F32)
            cand2 = K.sb(es, 'cand2', [128, 8, 256], F32)
            tsv = K.sb(es, 'tsv', [128, 8, 16], F32)
            pos = K.sb(es, 'pos', [128, 8, 16], U32)
            ai = K.sb(es, 'ai', [128, 8, 16], U32)
            bi = K.sb(es, 'bi', [128, 8, 16], U32)
            af = K.sb(es, 'af', [128, 8, 16], F32)
            bf = K.sb(es, 'bf', [128, 8, 16], F32)
            oh = K.sb(es, 'oh', [128, 16, 16], F32)
            oh2 = K.sb(es, 'oh2', [128, 16, 16], F32)
            i1s = K.sb(es, 'i1s', [128, 8, 16], F32)
            i2s = K.sb(es, 'i2s', [128, 8, 16], F32)
            eif = K.sb(es, 'eif', [128, 128], F32)
            eidx2 = [K.sb(es, 'eidx', [128, 128], I32) for _ in range(2)]
            gate2 = [K.sb(es, 'gate', [128, 8, 16], F32) for _ in range(2)]
            gsum = K.sb(es, 'gsum', [128, 8], F32)
            dots = K.sb(es, 'dots', [128, 128], F32)
            actw = K.sb(es, 'actw', [128, 128], F32)
            gb = [K.sb(es, 'gbuf', [128, 2 * D], BF16) for _ in range(G)]
            scr = K.sb(es, 'scr', [128, D], BF16)
            acc = [K.sb(es, 'acc', [128, D], F32) for _ in range(2)]
            dg = [K.sb(es, 'dg', [128, 128], BF16) for _ in range(8)]
            uvt = io['uvb_l%d' % li]

            def stage1(ti, pb_):
                xt, xnb, eidx, gate = xt2[pb_], xnb2[pb_], eidx2[pb_], gate2[pb_]
                r0 = ti * 128
                K.dma(xt[:], io['xres'][r0:r0 + 128, :])
                self.rmsnorm_rstd(es, xt[:], D, sq[:], rstd[:])
                K.stt(xnb[:], xt[:], rstd[:], gB[:], ALU.mult, ALU.mult)
                yield
                self.transpose_to(xnT, xnb, 8, self.c['idb'], BF16)
                yield
                for c4 in range(4):
                    bk = self.bank()
                    for j in range(4):
                        cc = c4 * 4 + j
                        for k in range(8):
                            K.mm(bk[:, j * 128:(j + 1) * 128], wq[:, k, cc * 128:(cc + 1) * 128], xnT[:, k, :],
                                 start=(k == 0), stop=(k == 7))
                        yield
                    K.copy(qT.v(lambda h: h[:, c4 * 4:c4 * 4 + 4, :]),
                           bk.v(lambda h: h[:, :].rearrange("p (k t) -> p k t", k=4)), eng='act' if c4 % 2 == 0 else 'dve')
                for c4 in range(4):
                    bk = self.bank()
                    for j in range(4):
                        cc = c4 * 4 + j
                        K.mm(bk[:, j * 128:(j + 1) * 128], qT[:, cc, :], kT[:, cc % 2, :])
                    K.copy(sc.v(lambda h: h[:, c4 * 4:c4 * 4 + 4, :]),
                           bk.v(lambda h: h[:, :].rearrange("p (k t) -> p k t", k=4)), eng='act' if c4 % 2 == 0 else 'dve')
                    yield
                for cc in range(16):
                    K.max8(v12[:, cc, 0:8], sc[:, cc, :])
                    K.maxidx(i12[:, cc, 0:8], v12[:, cc, 0:8], sc[:, cc, :])
                    K.mrep(sc2[:, cc, :], v12[:, cc, 0:8], sc[:, cc, :], NEG)
                    K.max8(v12[:, cc, 8:16], sc2[:, cc, :])
                    K.maxidx(i12[:, cc, 8:16], v12[:, cc, 8:16], sc2[:, cc, :])
                    yield
                K.copy(i12f[:], i12[:])
                for h in range(8):
                    K.tt(cand.v(lambda hh: hh[:, h, :].rearrange("p (a b) -> p a b", a=16)),
                         v12.v(lambda hh: hh[:, 2 * h, :].unsqueeze(2).to_broadcast([128, 16, 16])),
                         v12.v(lambda hh: hh[:, 2 * h + 1, :].unsqueeze(1).to_broadcast([128, 16, 16])), ALU.add)
                    K.max8(tsv[:, h, 0:8], cand[:, h, :])
                    K.maxidx(pos[:, h, 0:8], tsv[:, h, 0:8], cand[:, h, :])
                    K.mrep(cand2[:, h, :], tsv[:, h, 0:8], cand[:, h, :], NEG)
                    K.max8(tsv[:, h, 8:16], cand2[:, h, :])
                    K.maxidx(pos[:, h, 8:16], tsv[:, h, 8:16], cand2[:, h, :])
                    yield
                K.ts(ai[:], pos[:], 4, ALU.logical_shift_right)
                K.ts(bi[:], pos[:], 15, ALU.bitwise_and)
                K.copy(af[:], ai[:])
                K.copy(bf[:], bi[:])
                yield
                for h in range(8):
                    for (xf, half, dst) in ((af, 0, i1s), (bf, 1, i2s)):
                        K.tt(oh[:], xf.v(lambda hh: hh[:, h, :].unsqueeze(2).to_broadcast([128, 16, 16])),
                             io16.v(lambda hh: hh[:, :].unsqueeze(1).to_broadcast([128, 16, 16])), ALU.is_equal)
                        K.tt(oh2[:], oh[:], i12f.v(lambda hh: hh[:, 2 * h + half, :].unsqueeze(1).to_broadcast([128, 16, 16])), ALU.mult)
                        K.red(dst[:, h, :], oh2[:], ALU.add)
                    yield
                K.stt(eif.v(lambda hh: hh[:, :].rearrange("p (a b) -> p a b", a=8)), i1s[:], 128.0, i2s[:], ALU.mult, ALU.add)
                K.ts(eif[:], eif[:], 0.0, ALU.max, 16383.0, ALU.min)
                K.copy(eidx[:], eif[:])
                K.tt(gate[:], tsv[:], tsv.v(lambda hh: hh[:, :, 0:1].to_broadcast([128, 8, 16])), ALU.subtract)
                K.act(gate[:], gate[:], AF.Exp)
                K.red(gsum[:], gate[:], ALU.add)
                K.recip(gsum[:], gsum[:])
                K.tt(gate[:], gate[:], gsum.v(lambda hh: hh[:, :].unsqueeze(2).to_broadcast([128, 8, 16])), ALU.mult)
                yield

            def advance(gen, n):
                if gen is None:
                    return None
                for _ in range(n):
                    try:
                        next(gen)
                    except StopIteration:
                        return None
                return gen

            NGRP = 128 // GS
            NTT = self.NT + 1
            advance(stage1(0, 0), 10 ** 6)
            for ti in range(NTT):
                cur = ti % 2
                xt, xnb, eidx, gate = xt2[cur], xnb2[cur], eidx2[cur], gate2[cur]
                gflat = gate.v(lambda hh: hh[:, :, :].rearrange("p a b -> p (a b)"))
                r0 = ti * 128
                nxt = stage1(ti + 1, 1 - cur) if ti + 1 < NTT else None

                def head(gi):
                    for jj in range(GS):
                        j = gi * GS + jj
                        g = gb[j % G]
                        K.gather(g[:], uvt[:, :], eidx[:, j:j + 1])
                        K.ttr(scr[:], g[:, 0:D], xnb[:], dots[:, j:j + 1])
                    sl = slice(gi * GS, (gi + 1) * GS)
                    K.act(actw[:, sl], dots[:, sl], AF.Gelu)

                def tail(gi):
                    sl = slice(gi * GS, (gi + 1) * GS)
                    K.tt(actw[:, sl], actw[:, sl], gflat[:, sl], ALU.mult)
                    for jj in range(GS):
                        j = gi * GS + jj
                        g = gb[j % G]
                        d_ = dg[j % 8]
                        K.act(d_[:], idb[:], AF.Copy, scale=actw[:, j:j + 1])
                        for hf in range(2):
                            K.mm(yps[hf][:, :], d_[:], g[:, D + hf * 512:D + (hf + 1) * 512], start=(j == 0), stop=(j == 127))

                for gi in range(NGRP + 1):
                    if gi < NGRP:
                        head(gi)
                    if gi >= 1:
                        tail(gi - 1)
                    if gi >= 2:
                        nxt = advance(nxt, 3)
                advance(nxt, 10 ** 6)
                for hf in range(2):
                    K.tt(acc[0][:, hf * 512:(hf + 1) * 512], yps[hf][:, :], xt[:, hf * 512:(hf + 1) * 512], ALU.add)
                if not last:
                    K.dma(io['xres'][r0:r0 + 128, :], acc[0][:])
                else:
                    if self.debug:
                        K.dma(io['xres'][r0:r0 + 128, :], acc[0][:])
                    self.rmsnorm_rstd(es, acc[0][:], D, sq[:], rstd[:])
                    K.stt(acc[1][:], acc[0][:], rstd[:], gF[:], ALU.mult, ALU.mult)
                    dst = io['y_p'][r0:r0 + 128, :] if ti < self.NT else io['y_s'][:, :]
                    K.dma(dst, acc[1][:])
        K.S.barrier()
        self.nrot = 8

    def build(self, stop=None):
        self.declare()
        with ExitStack() as es:
            self.consts(es)
            self.K.S.barrier()
            self.build_uv()
            for li in range(self.DEPTH):
                if self.kinds[li] in ('delta', 'ssd'):
                    self.conv_states(li, None, None)
                self.phase_A(li)
                if stop == ('A', li):
                    break
                if stop != ('skipB', li):
                    self.phase_B(li)
                else:
                    WO = W_MIX[self.kinds[li]] + 1024
                    for ti in range(self.NT + 1):
                        self.K.dma(self.io['ocat'][ti * 128:(ti + 1) * 128, 0:WO], self.io['proj'][ti * 128:(ti + 1) * 128, 0:WO])
                    self.K.S.barrier()
                if stop == ('B', li):
                    break
                self.phase_C1(li)
                if stop == ('C1', li):
                    break
                self.phase_C2(li)
                if stop in (('C2', li), ('skipB', li)):
                    break
            self.K.S.emit()
        return self.nc


N_W_IN = {0: 'w_in_l0', 1: 'w_in_l1', 2: 'w_in_l2', 3: 'w_in_l3'}
N_W_OUT = {0: 'w_out_l0', 1: 'w_out_l1', 2: 'w_out_l2', 3: 'w_out_l3'}
N_DN_CONV = {0: 'dn_conv_l0', 3: 'dn_conv_l3'}
N_DN_ALOG = {0: 'dn_alog_l0', 3: 'dn_alog_l3'}
N_DN_DTB = {0: 'dn_dtb_l0', 3: 'dn_dtb_l3'}
N_DN_NG = {0: 'dn_ng_l0', 3: 'dn_ng_l3'}
N_ST_DELTA = {0: 'state_l0_delta', 3: 'state_l3_delta'}
N_ST_CONV = {0: 'state_l0_conv', 2: 'state_l2_conv', 3: 'state_l3_conv'}
N_FOX_BF = {1: 'fox_bf_l1'}
N_C_K = {1: 'cache_l1_k'}
N_C_V = {1: 'cache_l1_v'}
N_C_F = {1: 'cache_l1_logf'}
N_SSD_CONV = {2: 'ssd_conv_l2'}
N_SSD_CONVB = {2: 'ssd_convb_l2'}
N_SSD_ALOG = {2: 'ssd_alog_l2'}
N_SSD_DTB = {2: 'ssd_dtb_l2'}
N_SSD_D = {2: 'ssd_d_l2'}
N_SSD_NG = {2: 'ssd_ng_l2'}
N_ST_SSM = {2: 'state_l2_ssm'}


def make_in_maps(inp, L, PAST, DEPTH):
    kinds = [MIXERS[i % 3] for i in range(DEPTH)]
    f = lambda a: np.ascontiguousarray(np.asarray(a, dtype=np.float32))
    maps = []
    shared = {}
    shared['g_mix'] = f(inp['g_mix'])[:DEPTH]
    shared['g_mem'] = f(inp['g_mem'])[:DEPTH]
    shared['g_ffn'] = f(inp['g_ffn'])[:DEPTH]
    shared['w_mem_k'] = f(inp['w_mem_k'])[:DEPTH]
    shared['w_mem_v'] = f(inp['w_mem_v'])[:DEPTH]
    shared['peer_wq'] = f(inp['peer_wq'])[:DEPTH]
    shared['peer_k1'] = f(inp['peer_k1'])[:DEPTH]
    shared['peer_k2'] = f(inp['peer_k2'])[:DEPTH]
    peer_u, peer_v = inp['peer_u'], inp['peer_v']
    for i in range(DEPTH):
        shared['peer_u_l%d' % i] = f(peer_u[i])
        shared['peer_v_l%d' % i] = f(peer_v[i])
    shared['g_final'] = f(inp['g_final']).reshape(1, D)
    for i, kd in enumerate(kinds):
        shared['w_in_l%d' % i] = f(inp[N_W_IN[i]])
        shared['w_out_l%d' % i] = f(inp[N_W_OUT[i]])
        if kd == 'delta':
            shared['conv_l%d' % i] = f(inp[N_DN_CONV[i]])
            shared['alog_l%d' % i] = f(inp[N_DN_ALOG[i]]).reshape(1, 8)
            shared['dtb_l%d' % i] = f(inp[N_DN_DTB[i]]).reshape(1, 8)
            shared['ng_l%d' % i] = f(inp[N_DN_NG[i]]).reshape(1, 128)
        elif kd == 'fox':
            shared['bf_l%d' % i] = f(inp[N_FOX_BF[i]]).reshape(1, 16)
        else:
            shared['conv_l%d' % i] = f(inp[N_SSD_CONV[i]])
            shared['convb_l%d' % i] = f(inp[N_SSD_CONVB[i]]).reshape(1, 3072)
            shared['alog_l%d' % i] = f(inp[N_SSD_ALOG[i]]).reshape(1, 32)
            shared['dtb_l%d' % i] = f(inp[N_SSD_DTB[i]]).reshape(1, 32)
            shared['dsk_l%d' % i] = f(inp[N_SSD_D[i]]).reshape(1, 32)
            shared['ng_l%d' % i] = f(inp[N_SSD_NG[i]]).reshape(1, 2048)
    x_prompt, x_sample, mem_prompt = inp['x_prompt'], inp['x_sample'], inp['mem_prompt']
    cache_mem_k, cache_mem_v = inp['cache_mem_k'], inp['cache_mem_v']
    for c in range(8):
        b = c % 4
        sl = slice(4 * c, 4 * c + 4)
        m = dict(shared)
        m['xp'] = f(x_prompt[b])
        m['xs'] = f(x_sample[sl]).reshape(128, D)
        m['memp'] = f(mem_prompt[b])
        m['cmk'] = f(cache_mem_k[:DEPTH, sl]).reshape(DEPTH, 4, 256, 1024)
        m['cmv'] = f(cache_mem_v[:DEPTH, sl]).reshape(DEPTH, 4, 256, 1024)
        for i, kd in enumerate(kinds):
            if kd == 'delta':
                m['st_d_l%d' % i] = f(inp[N_ST_DELTA[i]][sl])
                m['st_c_l%d' % i] = f(inp[N_ST_CONV[i]][sl])
            elif kd == 'fox':
                m['c_k_l%d' % i] = f(inp[N_C_K[i]][sl]).reshape(4, PAST, 1024)
                m['c_v_l%d' % i] = f(inp[N_C_V[i]][sl]).reshape(4, PAST, 1024)
                m['c_f_l%d' % i] = f(inp[N_C_F[i]][sl])
            else:
                m['st_s_l%d' % i] = f(inp[N_ST_SSM[i]][sl])
                m['st_c_l%d' % i] = f(inp[N_ST_CONV[i]][sl])
        maps.append(m)
    return maps


def run_config(inp, L, PAST, DEPTH, debug=False):
    kinds = [MIXERS[i % 3] for i in range(DEPTH)]
    mk = MK(L, PAST, DEPTH, debug=debug)
    nc = mk.build()
    maps = make_in_maps(inp, L, PAST, DEPTH)
    res = run_bass_kernel_spmd(nc, maps, core_ids=list(range(8)))
    R = res.results
    NB = 4
    f32 = np.float32

    def pstack(name, shape):
        return np.stack([np.asarray(R[b][name], dtype=f32).reshape(shape) for b in range(NB)], axis=0)

    def scat(name, shape):
        return np.concatenate([np.asarray(R[c][name], dtype=f32).reshape((4,) + tuple(shape)) for c in range(8)], axis=0)

    y_prompt = pstack('y_p', (L, D))
    y_sample = scat('y_s', (32, D))
    p_mem_k = np.stack([np.asarray(R[b]['o_mk'], dtype=f32).reshape(DEPTH, 256, 4, 256) for b in range(NB)], axis=1)
    p_mem_v = np.stack([np.asarray(R[b]['o_mv'], dtype=f32).reshape(DEPTH, 256, 4, 256) for b in range(NB)], axis=1)
    outs = {}
    for i, kd in enumerate(kinds):
        if kd == 'delta':
            outs['p_l%d_delta' % i] = pstack('p_d_l%d' % i, (8, 128, 128))
            outs['p_l%d_conv' % i] = pstack('p_c_l%d' % i, (3, 3072))
            outs['s_l%d_delta' % i] = scat('s_d_l%d' % i, (8, 128, 128))
            outs['s_l%d_conv' % i] = scat('s_c_l%d' % i, (3, 3072))
        elif kd == 'fox':
            outs['p_l%d_k' % i] = pstack('p_k_l%d' % i, (L, 16, 64))
            outs['p_l%d_v' % i] = pstack('p_v_l%d' % i, (L, 16, 64))
            outs['p_l%d_logf' % i] = pstack('p_f_l%d' % i, (L, 16))
            outs['s_l%d_k' % i] = scat('s_k_l%d' % i, (32, 16, 64))
            outs['s_l%d_v' % i] = scat('s_v_l%d' % i, (32, 16, 64))
            outs['s_l%d_logf' % i] = scat('s_f_l%d' % i, (32, 16))
        else:
            outs['p_l%d_ssm' % i] = pstack('p_s_l%d' % i, (32, 64, 128))
            outs['p_l%d_conv' % i] = pstack('p_c_l%d' % i, (3, 3072))
            outs['s_l%d_ssm' % i] = scat('s_s_l%d' % i, (32, 64, 128))
            outs['s_l%d_conv' % i] = scat('s_c_l%d' % i, (3, 3072))
    order = ['p_l0_delta', 'p_l0_conv', 'p_l1_k', 'p_l1_v', 'p_l1_logf', 'p_l2_ssm', 'p_l2_conv', 'p_l3_delta', 'p_l3_conv',
             's_l0_delta', 's_l0_conv', 's_l1_k', 's_l1_v', 's_l1_logf', 's_l2_ssm', 's_l2_conv', 's_l3_delta', 's_l3_conv']
    return (y_prompt, y_sample, p_mem_k, p_mem_v) + tuple(outs.get(k) for k in order)


def kernel(**inputs):
    L = int(np.shape(inputs['x_prompt'])[1])
    PAST = int(np.shape(inputs['cache_l1_k'])[1])
    return run_config(inputs, L, PAST, 4)
```

```python
import numpy as np
from contextlib import ExitStack
import concourse.bass as bass
import concourse.mybir as mybir
from concourse.bass_utils import run_bass_kernel_spmd

F32 = mybir.dt.float32
BF16 = mybir.dt.bfloat16
I32 = mybir.dt.int32
U32 = mybir.dt.uint32
AF = mybir.ActivationFunctionType
ALU = mybir.AluOpType
AX = mybir.AxisListType

D = 1024
EPS = 1e-6
MIXERS = ('delta', 'fox', 'ssd')
W_IN = {'delta': 5136, 'fox': 4112, 'ssd': 6176}
W_MIX = {'delta': 1024, 'fox': 1024, 'ssd': 2048}
WMAX = 6176
NEG = -1.0e30

ENG = ['pe', 'dve', 'act', 'pool', 'sp']
NLANES = {'sp': 12, 'pool': 12, 'act': 6}
ENGOBJ = {'pe': 'tensor', 'dve': 'vector', 'act': 'scalar', 'pool': 'gpsimd', 'sp': 'sync'}


class V:
    __slots__ = ('ap', 'key')

    def __init__(self, ap, key):
        self.ap = ap
        self.key = key

    def __getitem__(self, idx):
        return V(self.ap[idx], self.key)


class Tl:
    def __init__(self, h, name):
        self.h = h
        self.name = name

    def __getitem__(self, idx):
        return V(self.h[idx], self.name)

    def v(self, fn):
        return V(fn(self.h), self.name)


class Sched:
    def __init__(self, nc):
        self.nc = nc
        self.stream = {e: [] for e in ENG}
        self.count = {e: 0 for e in ENG}
        self.known = {e: {} for e in ENG}
        self.res = {}
        self.dma_n = {q: 0 for q in NLANES}
        self.nops = 0
        self.noself = False

    def _deps(self, reads, writes):
        deps = {}
        for r in reads:
            st = self.res.get(r)
            if st and st['w']:
                sk, v = st['w']
                if deps.get(sk, 0) < v:
                    deps[sk] = v
        for w in writes:
            st = self.res.get(w)
            if st:
                if st['w']:
                    sk, v = st['w']
                    if deps.get(sk, 0) < v:
                        deps[sk] = v
                for sk, v in st['r'].items():
                    if deps.get(sk, 0) < v:
                        deps[sk] = v
        return deps

    def _emit_waits(self, eng, deps):
        kn = self.known[eng]
        for sk, v in deps.items():
            if sk == ('c', eng) and (eng == 'pe' or self.noself):
                continue
            if kn.get(sk, 0) >= v:
                continue
            self.stream[eng].append(('wait', sk, v))
            kn[sk] = v

    def _record(self, sk, val, reads, writes):
        for r in reads:
            st = self.res.get(r)
            if st is None:
                st = self.res[r] = {'w': None, 'r': {}}
            if st['r'].get(sk, 0) < val:
                st['r'][sk] = val
        for w in writes:
            self.res[w] = {'w': (sk, val), 'r': {}}

    def op(self, eng, fn, reads=(), writes=()):
        deps = self._deps(reads, writes)
        self._emit_waits(eng, deps)
        self.count[eng] += 1
        sk = ('c', eng)
        self.stream[eng].append(('op', fn, sk, 1))
        self._record(sk, self.count[eng], reads, writes)
        self.nops += 1

    def dma(self, q, fn, reads=(), writes=()):
        n = self.dma_n[q]
        nl = NLANES[q]
        lane = n % nl
        val = 16 * (n // nl + 1)
        sk = ('d', q, lane)
        deps = self._deps(reads, writes)
        if n >= nl and deps.get(sk, 0) < val - 16:
            deps[sk] = val - 16
        self._emit_waits(q, deps)
        self.dma_n[q] = n + 1
        self.stream[q].append(('op', fn, sk, 16))
        self._record(sk, val, reads, writes)
        self.nops += 1

    def _all_outstanding(self):
        fin = {}
        for e in ENG:
            if self.count[e]:
                fin[('c', e)] = self.count[e]
        for q in NLANES:
            n = self.dma_n[q]
            for l in range(NLANES[q]):
                cnt = (n - l + NLANES[q] - 1) // NLANES[q] if n > l else 0
                if cnt:
                    fin[('d', q, l)] = 16 * cnt
        return fin

    def barrier(self):
        fin = self._all_outstanding()
        for e in ENG:
            kn = self.known[e]
            for sk, v in fin.items():
                if sk == ('c', e):
                    continue
                if kn.get(sk, 0) < v:
                    self.stream[e].append(('wait', sk, v))
                    kn[sk] = v
        self.res = {}

    def emit(self):
        nc = self.nc
        self.barrier()
        sks = [('c', e) for e in ENG] + [('d', q, l) for q in NLANES for l in range(NLANES[q])]
        with ExitStack() as es:
            sems = {}
            for sk in sks:
                sems[sk] = es.enter_context(nc.semaphore("s_" + "_".join(str(x) for x in sk)))
            block = es.enter_context(nc.Block())
            for e in ENG:
                items = self.stream[e]

                def body(engine, items=items):
                    for it in items:
                        if it[0] == 'wait':
                            engine.wait_ge(sems[it[1]], it[2])
                        else:
                            it[1](engine).then_inc(sems[it[2]], it[3])
                getattr(block, ENGOBJ[e])(body)


class KB:
    def __init__(self, nc):
        self.nc = nc
        self.S = Sched(nc)
        self.uid = 0

    def _rk(self, *vs):
        return [v.key for v in vs if isinstance(v, V)]

    @staticmethod
    def _a(x):
        return x.ap if isinstance(x, V) else x

    def sb(self, es, name, shape, dt=F32):
        self.uid += 1
        nm = "%s_%d" % (name, self.uid)
        return Tl(es.enter_context(self.nc.sbuf_tensor(nm, list(shape), dt)), nm)

    def ps(self, es, name, shape, dt=F32):
        self.uid += 1
        nm = "%s_%d" % (name, self.uid)
        return Tl(es.enter_context(self.nc.psum_tensor(nm, list(shape), dt)), nm)

    def dma(self, out, in_, q='sp', **kw):
        o, i = out.ap, in_.ap
        self.S.dma(q, lambda e: e.dma_start(out=o, in_=i, **kw), reads=[in_.key], writes=[out.key])

    def gather(self, out, table, idx):
        o, t, ix = out.ap, table.ap, idx.ap
        self.S.dma('pool', lambda e: e.indirect_dma_start(
            out=o, out_offset=None, in_=t,
            in_offset=bass.IndirectOffsetOnAxis(ap=ix, axis=0)),
            reads=[table.key, idx.key], writes=[out.key])

    def mm(self, out, lhsT, rhs, start=True, stop=True):
        o, l, r = out.ap, lhsT.ap, rhs.ap
        self.S.op('pe', lambda e: e.matmul(o, lhsT=l, rhs=r, start=start, stop=stop),
                  reads=[lhsT.key, rhs.key] + ([] if start else [out.key]), writes=[out.key])

    def tr(self, out, in_, ident):
        o, i, d = out.ap, in_.ap, ident.ap
        self.S.op('pe', lambda e: e.transpose(out=o, in_=i, identity=d),
                  reads=[in_.key, ident.key], writes=[out.key])

    def act(self, out, in_, func, bias=None, scale=None, accum=None):
        o, i = out.ap, in_.ap
        kw = {}
        rd = [in_.key]
        if bias is not None:
            kw['bias'] = self._a(bias)
            rd += self._rk(bias)
        if scale is not None:
            kw['scale'] = self._a(scale)
            rd += self._rk(scale)
        wr = [out.key]
        if accum is not None:
            kw['accum_out'] = accum.ap
            wr.append(accum.key)
        self.S.op('act', lambda e: e.activation(out=o, in_=i, func=func, **kw), reads=rd, writes=wr)

    def ts(self, out, in0, s1, op0, s2=None, op1=None, eng='dve', accum=None):
        o, i = out.ap, in0.ap
        a1, a2 = self._a(s1), self._a(s2)
        rd = [in0.key] + self._rk(s1, s2)
        kw = {}
        if op1 is not None:
            kw['op1'] = op1
        wr = [out.key]
        if accum is not None:
            kw['accum_out'] = accum.ap
            wr.append(accum.key)
        self.S.op(eng, lambda e: e.tensor_scalar(out=o, in0=i, scalar1=a1, scalar2=a2, op0=op0, **kw),
                  reads=rd, writes=wr)

    def tt(self, out, in0, in1, op, eng='dve'):
        o, a, b = out.ap, in0.ap, in1.ap
        self.S.op(eng, lambda e: e.tensor_tensor(out=o, in0=a, in1=b, op=op),
                  reads=[in0.key, in1.key], writes=[out.key])

    def stt(self, out, in0, scalar, in1, op0, op1):
        o, a, b = out.ap, in0.ap, in1.ap
        s = self._a(scalar)
        self.S.op('dve', lambda e: e.scalar_tensor_tensor(out=o, in0=a, scalar=s, in1=b, op0=op0, op1=op1),
                  reads=[in0.key, in1.key] + self._rk(scalar), writes=[out.key])

    def ttr(self, out, in0, in1, accum):
        o, a, b, c = out.ap, in0.ap, in1.ap, accum.ap
        self.S.op('dve', lambda e: e.scalar_tensor_tensor(out=o, in0=a, scalar=1.0, in1=b, op0=ALU.mult,
                                                          op1=ALU.mult, accum_out=c),
                  reads=[in0.key, in1.key], writes=[out.key, accum.key])

    def red(self, out, in_, op, axis=AX.X):
        o, i = out.ap, in_.ap
        self.S.op('dve', lambda e: e.tensor_reduce(out=o, in_=i, axis=axis, op=op),
                  reads=[in_.key], writes=[out.key])

    def copy(self, out, in_, eng='dve'):
        o, i = out.ap, in_.ap
        if eng == 'act':
            self.S.op('act', lambda e: e.copy(out=o, in_=i), reads=[in_.key], writes=[out.key])
        else:
            self.S.op(eng, lambda e: e.tensor_copy(out=o, in_=i), reads=[in_.key], writes=[out.key])

    def memset(self, out, val, eng='pool'):
        o = out.ap
        self.S.op(eng, lambda e: e.memset(o, val), writes=[out.key])

    def recip(self, out, in_):
        o, i = out.ap, in_.ap
        self.S.op('dve', lambda e: e.reciprocal(out=o, in_=i), reads=[in_.key], writes=[out.key])

    def aselect(self, out, in_, pattern, cmp, fill, base, cm):
        o, i = out.ap, in_.ap
        self.S.op('pool', lambda e: e.affine_select(out=o, in_=i, pattern=pattern, compare_op=cmp, fill=fill,
                                                    base=base, channel_multiplier=cm),
                  reads=[in_.key], writes=[out.key])

    def iota(self, out, pattern, base=0, cm=0):
        o = out.ap
        self.S.op('pool', lambda e: e.iota(o, pattern=pattern, base=base, channel_multiplier=cm,
                                           allow_small_or_imprecise_dtypes=True), writes=[out.key])

    def max8(self, out, in_):
        o, i = out.ap, in_.ap
        self.S.op('dve', lambda e: e.max(out=o, in_=i), reads=[in_.key], writes=[out.key])

    def maxidx(self, out, in_max, in_values):
        o, m, v = out.ap, in_max.ap, in_values.ap
        self.S.op('dve', lambda e: e.max_index(out=o, in_max=m, in_values=v),
                  reads=[in_max.key, in_values.key], writes=[out.key])

    def mrep(self, out, in_to_replace, in_values, imm):
        o, r, v = out.ap, in_to_replace.ap, in_values.ap
        self.S.op('dve', lambda e: e.match_replace(out=o, in_to_replace=r, in_values=v, imm_value=imm),
                  reads=[in_to_replace.key, in_values.key], writes=[out.key])


class MK:
    def __init__(self, L, PAST, DEPTH, debug=False):
        self.L, self.PAST, self.DEPTH, self.debug = L, PAST, DEPTH, debug
        self.NT = L // 128
        self.NTOK = L + 128
        self.nc = bass.Bass("TRN2", target_bir_lowering=False)
        self.K = KB(self.nc)
        self.kinds = [MIXERS[i % 3] for i in range(DEPTH)]
        self.io = {}
        self.in_shapes = {}

    def din(self, name, shape, dt=F32):
        t = self.nc.dram_tensor(name, list(shape), dt, kind="ExternalInput")
        self.io[name] = Tl(t.ap(), name)
        self.in_shapes[name] = list(shape)
        return self.io[name]

    def dout(self, name, shape, dt=F32):
        t = self.nc.dram_tensor(name, list(shape), dt, kind="ExternalOutput")
        self.io[name] = Tl(t.ap(), name)
        return self.io[name]

    def dscr(self, name, shape, dt=F32):
        kind = "ExternalOutput" if self.debug else "Internal"
        t = self.nc.dram_tensor(name, list(shape), dt, kind=kind)
        self.io[name] = Tl(t.ap(), name)
        return self.io[name]

    def declare(self):
        L, PAST, DP = self.L, self.PAST, self.DEPTH
        d = self.din
        d('xp', [L, D]); d('xs', [128, D]); d('memp', [256, D])
        d('cmk', [DP, 4, 256, 1024]); d('cmv', [DP, 4, 256, 1024])
        d('g_mix', [DP, D]); d('g_mem', [DP, D]); d('g_ffn', [DP, D]); d('g_final', [1, D])
        d('w_mem_k', [DP, D, 1024]); d('w_mem_v', [DP, D, 1024])
        d('peer_wq', [DP, D, 2048]); d('peer_k1', [DP, 128, 128]); d('peer_k2', [DP, 128, 128])
        for i in range(DP):
            d('peer_u_l%d' % i, [16384, D]); d('peer_v_l%d' % i, [16384, D])
        for i, kd in enumerate(self.kinds):
            d('w_in_l%d' % i, [D, W_IN[kd]]); d('w_out_l%d' % i, [W_MIX[kd] + 1024, D])
            if kd == 'delta':
                d('conv_l%d' % i, [4, 3072]); d('alog_l%d' % i, [1, 8]); d('dtb_l%d' % i, [1, 8]); d('ng_l%d' % i, [1, 128])
                d('st_d_l%d' % i, [4, 8, 128, 128]); d('st_c_l%d' % i, [4, 3, 3072])
            elif kd == 'fox':
                d('bf_l%d' % i, [1, 16])
                d('c_k_l%d' % i, [4, PAST, 1024]); d('c_v_l%d' % i, [4, PAST, 1024]); d('c_f_l%d' % i, [4, PAST, 16])
            else:
                d('conv_l%d' % i, [4, 3072]); d('convb_l%d' % i, [1, 3072]); d('alog_l%d' % i, [1, 32]); d('dtb_l%d' % i, [1, 32])
                d('dsk_l%d' % i, [1, 32]); d('ng_l%d' % i, [1, 2048])
                d('st_s_l%d' % i, [4, 32, 64, 128]); d('st_c_l%d' % i, [4, 3, 3072])
        o = self.dout
        o('y_p', [L, D]); o('y_s', [128, D])
        o('o_mk', [DP, 256, 1024]); o('o_mv', [DP, 256, 1024])
        for i, kd in enumerate(self.kinds):
            if kd == 'delta':
                o('p_d_l%d' % i, [8, 128, 128]); o('p_c_l%d' % i, [3, 3072])
                o('s_d_l%d' % i, [4, 8, 128, 128]); o('s_c_l%d' % i, [4, 3, 3072])
            elif kd == 'fox':
                o('p_k_l%d' % i, [L, 1024]); o('p_v_l%d' % i, [L, 1024]); o('p_f_l%d' % i, [L, 16])
                o('s_k_l%d' % i, [128, 1024]); o('s_v_l%d' % i, [128, 1024]); o('s_f_l%d' % i, [128, 16])
            else:
                o('p_s_l%d' % i, [32, 64, 128]); o('p_c_l%d' % i, [3, 3072])
                o('s_s_l%d' % i, [4, 32, 64, 128]); o('s_c_l%d' % i, [4, 3, 3072])
        s = self.dscr
        s('xres', [self.NTOK, D])
        s('proj', [self.NTOK, WMAX])
        s('ocat', [self.NTOK, 3072])
        s('cvin', [3 + L + 4 * 35, 3072])
        s('cvo', [self.NTOK, 3072])
        s('aux', [self.NTOK, 64])
        s('cumT', [16, L])
        for i in range(DP):
            t = self.nc.dram_tensor('uvb_l%d' % i, [16384, 2048], BF16, kind="Internal")
            self.io['uvb_l%d' % i] = Tl(t.ap(), 'uvb_l%d' % i)
        s('cumTs', [4, 16, PAST + 32])

    def cv_row(self, seq):
        return 3 if seq < 0 else 3 + self.L + 35 * seq + 3

    def consts(self, es):
        K = self.K
        c = {}
        c['idf'] = K.sb(es, 'idf', [128, 128], F32)
        K.memset(c['idf'][:], 0.0)
        K.aselect(c['idf'][:], c['idf'][:], [[-1, 128]], ALU.not_equal, 1.0, 0, 1)
        c['idb'] = K.sb(es, 'idb', [128, 128], BF16)
        K.copy(c['idb'][:], c['idf'][:])
        c['triu'] = K.sb(es, 'triu', [128, 128], F32)
        K.memset(c['triu'][:], 1.0)
        K.aselect(c['triu'][:], c['triu'][:], [[1, 128]], ALU.is_ge, 0.0, 0, -1)
        c['striu'] = K.sb(es, 'striu', [128, 128], F32)
        K.memset(c['striu'][:], 1.0)
        K.aselect(c['striu'][:], c['striu'][:], [[1, 128]], ALU.is_ge, 0.0, -1, -1)
        c['tril'] = K.sb(es, 'tril', [128, 128], F32)
        K.memset(c['tril'][:], 1.0)
        K.aselect(c['tril'][:], c['tril'][:], [[-1, 128]], ALU.is_ge, 0.0, 0, 1)
        c['stril'] = K.sb(es, 'stril', [128, 128], F32)
        K.memset(c['stril'][:], 1.0)
        K.aselect(c['stril'][:], c['stril'][:], [[-1, 128]], ALU.is_ge, 0.0, -1, 1)
        c['ones'] = K.sb(es, 'ones', [128, 128], F32)
        K.memset(c['ones'][:], 1.0)
        c['zeros'] = K.sb(es, 'zeros', [128, 3072], F32)
        K.memset(c['zeros'][:], 0.0)
        self.c = c
        self.pb = [K.ps(es, 'pb%d' % i, [128, 512], F32) for i in range(8)]
        self.pbi = 0
        self.nrot = 8

    def bank(self):
        b = self.pb[self.pbi % self.nrot]
        self.pbi += 1
        return b

    def rmsnorm_rstd(self, es_tmp, x, width, tmp, rstd):
        K = self.K
        K.act(tmp, x, AF.Square, accum=rstd)
        self.rsqrt(rstd, rstd, 1.0 / width, EPS)

    def rsqrt(self, out, in_, scale, eps):
        K = self.K
        K.ts(out, in_, scale, ALU.mult, eps, ALU.add)
        K.act(out, out, AF.Sqrt)
        K.recip(out, out)

    def load_w_bf16(self, wsrc, wdst, nk, ncols, stg):
        K = self.K
        i = 0
        for k in range(nk):
            for c0 in range(0, ncols, 2048):
                c1 = min(ncols, c0 + 2048)
                s = stg[i % len(stg)]
                K.dma(s[:, 0:c1 - c0], wsrc[k * 128:(k + 1) * 128, c0:c1], q='sp' if i % 2 == 0 else 'act')
                K.copy(wdst[:, k, c0:c1], s[:, 0:c1 - c0], eng='dve' if i % 2 == 0 else 'pool')
                i += 1

    def transpose_to(self, dst, src, nblk, ident, dt, evac_engs=('act', 'dve')):
        K = self.K
        per = 4 if dt == F32 else 8
        for b0 in range(0, nblk, per):
            nb = min(per, nblk - b0)
            bk = self.bank()
            if dt == F32:
                pv = bk
                for j in range(nb):
                    K.tr(pv[:, j * 128:(j + 1) * 128], src[:, (b0 + j) * 128:(b0 + j + 1) * 128], ident[:])
                K.copy(dst.v(lambda h: h[:, b0:b0 + nb, :]),
                       pv.v(lambda h: h[:, 0:nb * 128].rearrange("p (k t) -> p k t", k=nb)),
                       eng=evac_engs[(b0 // per) % len(evac_engs)])
            else:
                for j in range(nb):
                    K.tr(bk.v(lambda h: h.bitcast(BF16)[:, j * 128:(j + 1) * 128]),
                         src[:, (b0 + j) * 128:(b0 + j + 1) * 128], ident[:])
                K.copy(dst.v(lambda h: h[:, b0:b0 + nb, :]),
                       bk.v(lambda h: h.bitcast(BF16)[:, 0:nb * 128].rearrange("p (k t) -> p k t", k=nb)),
                       eng=evac_engs[(b0 // per) % len(evac_engs)])

    def build_uv(self):
        K, io = self.K, self.io
        RB = 4
        with ExitStack() as es:
            ld = [K.sb(es, 'uvld', [128, RB, 1024], F32) for _ in range(3)]
            cv = [K.sb(es, 'uvcv', [128, RB, 1024], BF16) for _ in range(3)]
            n = 0
            for li in range(self.DEPTH):
                dst = io['uvb_l%d' % li]
                for half, nm in ((0, 'peer_u_l%d' % li), (1, 'peer_v_l%d' % li)):
                    src = io[nm]
                    for r0 in range(0, 128, RB):
                        b = n % 3
                        q = 'sp' if n % 2 == 0 else 'act'
                        K.dma(ld[b][:], src.v(lambda h: h.rearrange("(p r) d -> p r d", p=128)[:, r0:r0 + RB, :]), q=q)
                        K.copy(cv[b][:], ld[b][:], eng=('act', 'pool', 'dve')[n % 3])
                        K.dma(dst.v(lambda h: h.rearrange("(p r) d -> p r d", p=128)[:, r0:r0 + RB, half * 1024:(half + 1) * 1024]),
                              cv[b][:], q=q)
                        n += 1
        K.S.barrier()

    def phase_A(self, li):
        K, io, kd = self.K, self.io, self.kinds[li]
        W = W_IN[kd]
        with ExitStack() as es:
            wbf = K.sb(es, 'wbf', [128, 8, W], BF16)
            stg = [K.sb(es, 'stg', [128, 2048], F32) for _ in range(2)]
            gB = K.sb(es, 'gB', [128, D], F32)
            K.dma(gB[:], io['g_mix'].v(lambda h: h[li:li + 1, :].partition_broadcast(128)))
            self.load_w_bf16(io['w_in_l%d' % li], wbf, 8, W, stg)
            xt = [K.sb(es, 'xt', [128, D], F32) for _ in range(2)]
            sq = K.sb(es, 'sq', [128, D], F32)
            rstd = [K.sb(es, 'rstd', [128, 1], F32) for _ in range(2)]
            hn = [K.sb(es, 'hn', [128, D], BF16) for _ in range(2)]
            hnT = [K.sb(es, 'hnT', [128, 8, 128], BF16) for _ in range(2)]
            pj = [K.sb(es, 'pj', [128, W], F32) for _ in range(2)]
            for ti in range(self.NT + 1):
                b = ti % 2
                r0 = ti * 128
                if li == 0:
                    src = io['xp'][r0:r0 + 128, :] if ti < self.NT else io['xs'][:, :]
                else:
                    src = io['xres'][r0:r0 + 128, :]
                K.dma(xt[b][:], src)
                self.rmsnorm_rstd(es, xt[b][:], D, sq[:], rstd[b][:])
                K.stt(hn[b][:], xt[b][:], rstd[b][:], gB[:], ALU.mult, ALU.mult)
                self.transpose_to(hnT[b], hn[b], 8, self.c['idb'], BF16)
                g = 0
                for c0 in range(0, W, 512):
                    c1 = min(W, c0 + 512)
                    bk = self.bank()
                    for k in range(8):
                        K.mm(bk[:, 0:c1 - c0], hnT[b][:, k, :], wbf[:, k, c0:c1], start=(k == 0), stop=(k == 7))
                    K.copy(pj[b][:, c0:c1], bk[:, 0:c1 - c0], eng='act' if g % 2 == 0 else 'dve')
                    g += 1
                K.dma(io['proj'][r0:r0 + 128, 0:W], pj[b][:], q='sp')
                if kd in ('delta', 'ssd'):
                    coff = 0 if kd == 'delta' else 2048
                    if ti < self.NT:
                        cr = self.cv_row(-1) + r0
                        K.dma(io['cvin'][cr:cr + 128, :], pj[b][:, coff:coff + 3072], q='act')
                    else:
                        for s in range(4):
                            cr = self.cv_row(s)
                            K.dma(io['cvin'][cr:cr + 32, :], pj[b][s * 32:(s + 1) * 32, coff:coff + 3072], q='act')
        K.S.barrier()

    def phase_B_mem(self, li):
        K, io, kd = self.K, self.io, self.kinds[li]
        W = W_IN[kd]
        WM = W_MIX[kd]
        L = self.L
        idb = self.c['idb']
        with ExitStack() as es:
            mkT = [K.sb(es, 'mkT', [128, 8, 256], BF16) for _ in range(2)]
            mvb = [[K.sb(es, 'mvb', [128, 1024], BF16) for _ in range(2)] for _ in range(2)]
            mraw = [K.sb(es, 'mraw', [128, 1024], F32) for _ in range(2)]
            mkb = K.sb(es, 'mkb', [128, 1024], BF16)
            with ExitStack() as es2:
                wk = K.sb(es2, 'wk', [128, 8, 1024], BF16)
                wv = K.sb(es2, 'wv', [128, 8, 1024], BF16)
                stg = [K.sb(es2, 'stg', [128, 2048], F32) for _ in range(2)]
                self.load_w_bf16(io['w_mem_k'].v(lambda h: h[li]), wk, 8, 1024, stg)
                self.load_w_bf16(io['w_mem_v'].v(lambda h: h[li]), wv, 8, 1024, stg)
                gB = K.sb(es2, 'gB', [128, D], F32)
                K.dma(gB[:], io['g_mem'].v(lambda h: h[li:li + 1, :].partition_broadcast(128)))
                sq = K.sb(es2, 'sq', [128, D], F32)
                rstd = K.sb(es2, 'rstd', [128, 1], F32)
                mnb = K.sb(es2, 'mnb', [128, D], BF16)
                mnT = K.sb(es2, 'mnT', [128, 8, 128], BF16)
                for t in range(2):
                    K.dma(mraw[0][:], io['memp'][t * 128:(t + 1) * 128, :])
                    self.rmsnorm_rstd(es2, mraw[0][:], D, sq[:], rstd[:])
                    K.stt(mnb[:], mraw[0][:], rstd[:], gB[:], ALU.mult, ALU.mult)
                    self.transpose_to(mnT, mnb, 8, idb, BF16)
                    for (wsb, oname, isk) in ((wk, 'o_mk', True), (wv, 'o_mv', False)):
                        for g in range(2):
                            bk = self.bank()
                            for k in range(8):
                                K.mm(bk[:, :], mnT[:, k, :], wsb[:, k, g * 512:(g + 1) * 512], start=(k == 0), stop=(k == 7))
                            K.copy(mraw[1][:, g * 512:(g + 1) * 512], bk[:, :], eng='act' if g == 0 else 'dve')
                        K.dma(io[oname].v(lambda h: h[li, t * 128:(t + 1) * 128, :]), mraw[1][:])
                        if isk:
                            K.copy(mkb[:], mraw[1][:], eng='pool')
                            bk = self.bank()
                            for c8 in range(8):
                                K.tr(bk.v(lambda h: h.bitcast(BF16)[:, c8 * 128:(c8 + 1) * 128]), mkb[:, c8 * 128:(c8 + 1) * 128], idb[:])
                            K.copy(mkT[0].v(lambda h: h[:, :, t * 128:(t + 1) * 128]),
                                   bk.v(lambda h: h.bitcast(BF16)[:, :].rearrange("p (k t) -> p k t", k=8)), eng='act')
                        else:
                            K.copy(mvb[0][t][:], mraw[1][:], eng='pool')
            mq = K.sb(es, 'mq', [128, 1024], F32)
            mqb = K.sb(es, 'mqb', [128, 1024], BF16)
            mqT = K.sb(es, 'mqT', [128, 8, 128], BF16)
            mx = K.sb(es, 'mx', [128, 4], F32)
            rs = K.sb(es, 'rs', [128, 4], F32)
            pb = K.sb(es, 'pb', [128, 4, 256], BF16)
            pT = K.sb(es, 'pT', [128, 4, 2, 128], BF16)
            om = K.sb(es, 'om', [128, 1024], F32)

            def attend(np_, row0, kT, vb, tcol0):
                for hp in range(2):
                    bk = self.bank()
                    for hh in range(2):
                        h = hp * 2 + hh
                        for j in range(2):
                            K.mm(bk[0:np_, hh * 256:(hh + 1) * 256], mqT[:, 2 * h + j, tcol0:tcol0 + np_], kT[:, 2 * h + j, :],
                                 start=(j == 0), stop=(j == 1))
                    K.red(mx[0:np_, hp * 2:hp * 2 + 2], bk.v(lambda h_: h_[0:np_, :].rearrange("p (a b) -> p a b", a=2)), ALU.max)
                    K.ts(mx[0:np_, hp * 2:hp * 2 + 2], mx[0:np_, hp * 2:hp * 2 + 2], -1.0 / 16.0, ALU.mult)
                    for hh in range(2):
                        h = hp * 2 + hh
                        K.act(pb[0:np_, h, :], bk[0:np_, hh * 256:(hh + 1) * 256], AF.Exp, bias=mx[0:np_, h:h + 1], scale=1.0 / 16.0,
                              accum=rs[0:np_, h:h + 1])
                K.recip(rs[0:np_, :], rs[0:np_, :])
                bk = self.bank()
                for h in range(4):
                    for mt in range(2):
                        K.tr(bk.v(lambda h_: h_.bitcast(BF16)[:, (h * 2 + mt) * 128:(h * 2 + mt) * 128 + np_]),
                             pb[0:np_, h, mt * 128:(mt + 1) * 128], idb[0:np_, 0:np_])
                K.copy(pT.v(lambda h_: h_[:, :, :, 0:np_]),
                       bk.v(lambda h_: h_.bitcast(BF16)[:, :].rearrange("p (a b t) -> p a b t", a=4, b=2)[:, :, :, 0:np_]), eng='act')
                for hp in range(2):
                    bk = self.bank()
                    for hh in range(2):
                        h = hp * 2 + hh
                        for mt in range(2):
                            K.mm(bk[0:np_, hh * 256:(hh + 1) * 256], pT[:, h, mt, 0:np_], vb[mt][:, h * 256:(h + 1) * 256],
                                 start=(mt == 0), stop=(mt == 1))
                    K.tt(om.v(lambda h_: h_[0:np_, hp * 512:(hp + 1) * 512].rearrange("p (a b) -> p a b", a=2)),
                         bk.v(lambda h_: h_[0:np_, :].rearrange("p (a b) -> p a b", a=2)),
                         rs.v(lambda h_: h_[0:np_, hp * 2:hp * 2 + 2].unsqueeze(2).to_broadcast([np_, 2, 256])), ALU.mult)
                K.dma(io['ocat'][row0:row0 + np_, WM:WM + 1024], om[0:np_, :])

            for ti in range(self.NT):
                K.dma(mq[:], io['proj'][ti * 128:(ti + 1) * 128, W - 1024:W])
                K.copy(mqb[:], mq[:], eng='pool')
                self.transpose_to(mqT, mqb, 8, idb, BF16)
                attend(128, ti * 128, mkT[0], mvb[0], 0)
            K.dma(mq[:], io['proj'][L:L + 128, W - 1024:W])
            K.copy(mqb[:], mq[:], eng='pool')
            self.transpose_to(mqT, mqb, 8, idb, BF16)
            for s in range(4):
                for t in range(2):
                    K.dma(mraw[0][:], io['cmk'].v(lambda h: h[li, s, t * 128:(t + 1) * 128, :]))
                    K.dma(mraw[1][:], io['cmv'].v(lambda h: h[li, s, t * 128:(t + 1) * 128, :]), q='act')
                    K.copy(mkb[:], mraw[0][:], eng='pool')
                    bk = self.bank()
                    for c8 in range(8):
                        K.tr(bk.v(lambda h: h.bitcast(BF16)[:, c8 * 128:(c8 + 1) * 128]), mkb[:, c8 * 128:(c8 + 1) * 128], idb[:])
                    K.copy(mkT[1].v(lambda h: h[:, :, t * 128:(t + 1) * 128]),
                           bk.v(lambda h: h.bitcast(BF16)[:, :].rearrange("p (k t) -> p k t", k=8)), eng='act')
                    K.copy(mvb[1][t][:], mraw[1][:], eng='pool')
                attend(32, L + 32 * s, mkT[1], mvb[1], 32 * s)
        K.S.barrier()

    def phase_B(self, li):
        kd = self.kinds[li]
        self.phase_B_mem(li)
        getattr(self, 'phase_B_' + kd)(li)

    def conv_tile(self, ti, sh, cw, dst, cbias=None):
        K, io, L = self.K, self.io, self.L
        for j in range(4):
            if ti < self.NT:
                r = self.cv_row(-1) + ti * 128 - 3 + j
                K.dma(sh[j][:], io['cvin'][r:r + 128, :], q='sp' if j % 2 == 0 else 'act')
            else:
                for s_ in range(4):
                    r = self.cv_row(s_) - 3 + j
                    K.dma(sh[j][s_ * 32:(s_ + 1) * 32, :], io['cvin'][r:r + 32, :], q='sp' if j % 2 == 0 else 'act')
        for j in range(4):
            K.tt(sh[j][:], sh[j][:], cw[j][:], ALU.mult, eng='dve' if j % 2 == 0 else 'pool')
        K.tt(sh[0][:], sh[0][:], sh[1][:], ALU.add)
        K.tt(sh[2][:], sh[2][:], sh[3][:], ALU.add, eng='pool')
        K.tt(sh[0][:], sh[0][:], sh[2][:], ALU.add)
        if cbias is not None:
            K.tt(sh[0][:], sh[0][:], cbias[:], ALU.add)
        K.act(dst, sh[0][:], AF.Silu)

    def conv_states(self, li, pname, sname):
        K, io, L = self.K, self.io, self.L
        K.dma(io['cvin'][0:3, :], self.c['zeros'][0:3, :])
        for s_ in range(4):
            r = self.cv_row(s_) - 3
            K.dma(io['cvin'][r:r + 3, :], io['st_c_l%d' % li].v(lambda h: h[s_]))

    def conv_state_out(self, li):
        K, io, L = self.K, self.io, self.L
        r = self.cv_row(-1) + L - 3
        K.dma(io['p_c_l%d' % li][0:3, :], io['cvin'][r:r + 3, :])
        for s_ in range(4):
            r = self.cv_row(s_) + 32 - 3
            K.dma(io['s_c_l%d' % li].v(lambda h: h[s_]), io['cvin'][r:r + 3, :])

    def phase_B_delta(self, li):
        K, io, L = self.K, self.io, self.L
        idf = self.c['idf']
        self.conv_state_out(li)
        with ExitStack() as es:
            cw = [K.sb(es, 'cw', [128, 3072], F32) for _ in range(4)]
            for j in range(4):
                K.dma(cw[j][:], io['conv_l%d' % li].v(lambda h: h[j:j + 1, :].partition_broadcast(128)))
            alB = K.sb(es, 'alB', [128, 8], F32)
            dtB = K.sb(es, 'dtB', [128, 8], F32)
            K.dma(alB[:], io['alog_l%d' % li].v(lambda h: h[0:1, :].partition_broadcast(128)))
            K.dma(dtB[:], io['dtb_l%d' % li].v(lambda h: h[0:1, :].partition_broadcast(128)))
            K.act(alB[:], alB[:], AF.Exp)
            K.ts(alB[:], alB[:], -1.0, ALU.mult)
            sh = [K.sb(es, 'sh', [128, 3072], F32) for _ in range(4)]
            qkv = K.sb(es, 'qkv', [128, 3072], F32)
            sq = K.sb(es, 'sq', [128, 2048], F32)
            ss = K.sb(es, 'ss', [128, 16], F32)
            ab = K.sb(es, 'ab', [128, 16], F32)
            ax = K.sb(es, 'ax', [128, 24], F32)
            for ti in range(self.NT + 1):
                r0 = ti * 128
                self.conv_tile(ti, sh, cw, qkv[:])
                K.act(sq[:], qkv[:, 0:2048], AF.Square)
                K.red(ss[:], sq.v(lambda h: h[:, :].rearrange("p (a b) -> p a b", a=16)), ALU.add)
                self.rsqrt(ss[:], ss[:], 1.0, EPS)
                K.ts(ss[:, 0:8], ss[:, 0:8], 128.0 ** -0.5, ALU.mult)
                K.tt(qkv.v(lambda h: h[:, 0:2048].rearrange("p (a b) -> p a b", a=16)),
                     qkv.v(lambda h: h[:, 0:2048].rearrange("p (a b) -> p a b", a=16)),
                     ss.v(lambda h: h[:, :].unsqueeze(2).to_broadcast([128, 16, 128])), ALU.mult)
                K.dma(io['cvo'][r0:r0 + 128, :], qkv[:])
                K.dma(ab[:], io['proj'][r0:r0 + 128, 4096:4112], q='act')
                K.tt(ax[:, 0:8], ab[:, 0:8], dtB[:], ALU.add)
                K.act(ax[:, 0:8], ax[:, 0:8], AF.Exp)
                K.act(ax[:, 0:8], ax[:, 0:8], AF.Ln, bias=1.0)
                K.tt(ax[:, 0:8], ax[:, 0:8], alB[:], ALU.mult)
                K.act(ax[:, 8:16], ab[:, 8:16], AF.Sigmoid)
                K.ts(ax[:, 16:24], ax[:, 8:16], -1.0, ALU.mult)
                K.dma(io['aux'][r0:r0 + 128, 0:24], ax[:])
        K.S.barrier()
        with ExitStack() as es:
            T = {}
            def t(name, shape, dt=F32):
                T[name] = K.sb(es, name, shape, dt)
            t('qkv', [128, 3072]); t('z', [128, 1024]); t('ax', [128, 24])
            t('gam', [128, 8]); t('eg', [128, 8]); t('beg', [128, 8]); t('w2', [128, 8])
            t('glast', [128, 8]); t('egl', [128, 8])
            t('Qeg4', [128, 4, 128]); t('bV4', [128, 4, 128]); t('bKeg4', [128, 4, 128]); t('Kw4', [128, 4, 128])
            t('QT4', [128, 4, 128]); t('KT4', [128, 4, 128]); t('QegT4', [128, 4, 128])
            t('gbc', [128, 4, 128]); t('d4', [128, 4, 128]); t('m1', [128, 4, 128]); t('m2', [128, 4, 128])
            t('decs4', [128, 4, 128]); t('decT4', [128, 4, 128])
            t('N4', [128, 4, 128]); t('M4', [128, 4, 128]); t('X4', [128, 4, 128])
            t('Pa', [128, 4, 128]); t('Pb', [128, 4, 128]); t('PTa', [128, 4, 128]); t('PTb', [128, 4, 128])
            t('U4', [128, 4, 128]); t('WT4', [128, 4, 128]); t('qkT4', [128, 4, 128]); t('dl4', [128, 4, 128])
            t('S', [128, 8, 128]); t('osb', [128, 8, 128]); t('osq', [128, 8, 128]); t('oss', [128, 8])
            t('ngB', [128, 128]); t('tmp4', [128, 4, 128])
            K.dma(T['ngB'][:], io['ng_l%d' % li].v(lambda h: h[0:1, :].partition_broadcast(128)))

            def V3(nm, c, j0=0, nj=4, w=None):
                w = c if w is None else w
                return T[nm].v(lambda h: h[0:c, j0:j0 + nj, 0:w])

            def chunk(c, row0):
                nl = {128: 7, 32: 5}[c]
                K.dma(T['qkv'][0:c, :], io['cvo'][row0:row0 + c, :])
                K.dma(T['z'][0:c, :], io['proj'][row0:row0 + c, 3072:4096], q='act')
                K.dma(T['ax'][0:c, :], io['aux'][row0:row0 + c, 0:24], q='act')
                bk = self.bank()
                K.mm(bk[0:c, 0:8], self.c['triu'][0:c, 0:c], T['ax'][0:c, 0:8])
                K.copy(T['gam'][0:c, :], bk[0:c, 0:8])
                K.act(T['eg'][0:c, :], T['gam'][0:c, :], AF.Exp)
                K.tt(T['beg'][0:c, :], T['eg'][0:c, :], T['ax'][0:c, 8:16], ALU.mult)
                for hg in range(2):
                    h0 = hg * 4
                    qv = lambda off: T['qkv'].v(lambda h: h[0:c, off + h0 * 128:off + (h0 + 4) * 128].rearrange("p (a b) -> p a b", a=4))
                    bc = lambda nm, lo: T[nm].v(lambda h: h[0:c, lo + h0:lo + h0 + 4].unsqueeze(2).to_broadcast([c, 4, 128]))
                    bcc = lambda nm, lo: T[nm].v(lambda h: h[0:c, lo + h0:lo + h0 + 4].unsqueeze(2).to_broadcast([c, 4, c]))
                    K.tt(V3('Qeg4', c, w=128), qv(0), bc('eg', 0), ALU.mult)
                    K.tt(V3('bV4', c, w=128), qv(2048), bc('ax', 8), ALU.mult, eng='pool')
                    K.tt(V3('bKeg4', c, w=128), qv(1024), bc('beg', 0), ALU.mult)
                    for (src_off, dst, srcT) in ((0, 'QT4', None), (1024, 'KT4', None), (None, 'QegT4', 'Qeg4')):
                        bk = self.bank()
                        for j in range(4):
                            if srcT is None:
                                src = T['qkv'][0:c, src_off + (h0 + j) * 128:src_off + (h0 + j + 1) * 128]
                            else:
                                src = T[srcT][0:c, j, :]
                            K.tr(bk[:, j * c:(j + 1) * c], src, idf[0:c, 0:c])
                        K.copy(T[dst].v(lambda h: h[:, :, 0:c]), bk.v(lambda h: h[:, 0:4 * c].rearrange("p (a b) -> p a b", a=4)),
                               eng='act')
                    bkg = self.bank()
                    for j in range(4):
                        K.ts(T['gbc'][0:c, j, :], self.c['ones'][0:c, :], T['ax'][0:c, h0 + j:h0 + j + 1], ALU.mult)
                        K.mm(bkg[:, j * c:(j + 1) * c], T['gbc'][0:c, j, :], self.c['triu'][0:c, 0:c])
                    gB3 = bkg.v(lambda h: h[0:c, 0:4 * c].rearrange("p (a b) -> p a b", a=4))
                    K.copy(T['glast'].v(lambda h: h[:, h0:h0 + 4]),
                           bkg.v(lambda h: h[:, 0:4 * c].rearrange("p (a b) -> p a b", a=4)[:, :, c - 1]))
                    K.act(T['egl'][:, h0:h0 + 4], T['glast'][:, h0:h0 + 4], AF.Exp)
                    K.tt(T['w2'][0:c, h0:h0 + 4], T['glast'][0:c, h0:h0 + 4], T['gam'][0:c, h0:h0 + 4], ALU.subtract)
                    K.act(T['w2'][0:c, h0:h0 + 4], T['w2'][0:c, h0:h0 + 4], AF.Exp)
                    K.tt(V3('Kw4', c, w=128), qv(1024), bc('w2', 0), ALU.mult, eng='pool')
                    K.tt(V3('d4', c), gB3, bcc('gam', 0), ALU.subtract)
                    K.ts(V3('m1', c), V3('d4', c), 0.0, ALU.max)
                    K.ts(V3('m2', c), V3('d4', c), 0.0, ALU.min, eng='pool')
                    K.act(V3('m1', c), V3('m1', c), AF.Exp, scale=-1.0)
                    K.act(V3('m2', c), V3('m2', c), AF.Exp)
                    K.tt(V3('decs4', c), V3('m1', c), self.c['stril'].v(lambda h: h[0:c, 0:c].unsqueeze(1).to_broadcast([c, 4, c])), ALU.mult)
                    K.tt(V3('decT4', c), V3('m2', c), self.c['triu'].v(lambda h: h[0:c, 0:c].unsqueeze(1).to_broadcast([c, 4, c])), ALU.mult,
                         eng='pool')
                    bk = self.bank()
                    for j in range(4):
                        K.mm(bk[0:c, j * c:(j + 1) * c], T['KT4'][:, j, 0:c], T['KT4'][:, j, 0:c])
                    K.tt(V3('tmp4', c), bk.v(lambda h: h[0:c, 0:4 * c].rearrange("p (a b) -> p a b", a=4)), bcc('ax', 16), ALU.mult)
                    K.tt(V3('N4', c), V3('tmp4', c), V3('decs4', c), ALU.mult)
                    bk = self.bank()
                    for j in range(4):
                        K.tr(bk[0:c, j * c:(j + 1) * c], T['N4'][0:c, j, 0:c], idf[0:c, 0:c])
                    K.copy(V3('M4', c), bk.v(lambda h: h[0:c, 0:4 * c].rearrange("p (a b) -> p a b", a=4)), eng='act')
                    K.tt(V3('X4', c), V3('M4', c), idf.v(lambda h: h[0:c, 0:c].unsqueeze(1).to_broadcast([c, 4, c])), ALU.add)
                    P, PT = 'M4', 'N4'
                    nxt = [('Pa', 'PTa'), ('Pb', 'PTb')]
                    for k in range(1, nl):
                        Pn, PTn = nxt[k % 2]
                        bk = self.bank()
                        for j in range(4):
                            K.mm(bk[0:c, j * c:(j + 1) * c], T[P][0:c, j, 0:c], T[PT][0:c, j, 0:c])
                        K.copy(V3(PTn, c), bk.v(lambda h: h[0:c, 0:4 * c].rearrange("p (a b) -> p a b", a=4)), eng='act')
                        if k < nl - 1:
                            bk2 = self.bank()
                            for j in range(4):
                                K.mm(bk2[0:c, j * c:(j + 1) * c], T[PT][0:c, j, 0:c], T[P][0:c, j, 0:c])
                            K.copy(V3(Pn, c), bk2.v(lambda h: h[0:c, 0:4 * c].rearrange("p (a b) -> p a b", a=4)), eng='pool' if False else 'dve')
                        bk3 = self.bank()
                        for j in range(4):
                            K.mm(bk3[0:c, j * c:(j + 1) * c], T[PTn][0:c, j, 0:c], T['X4'][0:c, j, 0:c])
                        K.tt(V3('X4', c), V3('X4', c), bk3.v(lambda h: h[0:c, 0:4 * c].rearrange("p (a b) -> p a b", a=4)), ALU.add)
                        P, PT = Pn, PTn
                    bk = self.bank()
                    for j in range(4):
                        K.mm(bk[0:c, j * 128:(j + 1) * 128], T['X4'][0:c, j, 0:c], T['bV4'][0:c, j, :])
                    K.copy(V3('U4', c, w=128), bk.v(lambda h: h[0:c, :].rearrange("p (a b) -> p a b", a=4)), eng='act')
                    bk = self.bank()
                    for j in range(4):
                        K.mm(bk[:, j * c:(j + 1) * c], T['bKeg4'][0:c, j, :], T['X4'][0:c, j, 0:c])
                    K.copy(T['WT4'].v(lambda h: h[:, :, 0:c]), bk.v(lambda h: h[:, 0:4 * c].rearrange("p (a b) -> p a b", a=4)), eng='act')
                    bk = self.bank()
                    for j in range(4):
                        K.mm(bk[0:c, j * c:(j + 1) * c], T['KT4'][:, j, 0:c], T['QT4'][:, j, 0:c])
                    K.tt(V3('qkT4', c), bk.v(lambda h: h[0:c, 0:4 * c].rearrange("p (a b) -> p a b", a=4)), V3('decT4', c), ALU.mult)
                    bk = self.bank()
                    for j in range(4):
                        K.mm(bk[0:c, j * 128:(j + 1) * 128], T['WT4'][:, j, 0:c], T['S'][:, h0 + j, :])
                    K.tt(V3('dl4', c, w=128), V3('U4', c, w=128), bk.v(lambda h: h[0:c, :].rearrange("p (a b) -> p a b", a=4)), ALU.subtract)
                    bk = self.bank()
                    for j in range(4):
                        K.mm(bk[0:c, j * 128:(j + 1) * 128], T['QegT4'][:, j, 0:c], T['S'][:, h0 + j, :], start=True, stop=False)
                        K.mm(bk[0:c, j * 128:(j + 1) * 128], T['qkT4'][0:c, j, 0:c], T['dl4'][0:c, j, :], start=False, stop=True)
                    K.copy(T['osb'].v(lambda h: h[0:c, h0:h0 + 4, :]), bk.v(lambda h: h[0:c, :].rearrange("p (a b) -> p a b", a=4)), eng='act')
                    bk = self.bank()
                    for j in range(4):
                        K.mm(bk[:, j * 128:(j + 1) * 128], T['Kw4'][0:c, j, :], T['dl4'][0:c, j, :])
                    Sv = T['S'].v(lambda h: h[:, h0:h0 + 4, :])
                    K.tt(Sv, Sv, T['egl'].v(lambda h: h[:, h0:h0 + 4].unsqueeze(2).to_broadcast([128, 4, 128])), ALU.mult)
                    K.tt(Sv, Sv, bk.v(lambda h: h[:, :].rearrange("p (a b) -> p a b", a=4)), ALU.add)
                ov = T['osb'].v(lambda h: h[0:c, :, :])
                K.act(T['osq'].v(lambda h: h[0:c, :, :]), ov, AF.Square)
                K.red(T['oss'][0:c, :], T['osq'].v(lambda h: h[0:c, :, :]), ALU.add)
                self.rsqrt(T['oss'][0:c, :], T['oss'][0:c, :], 1.0 / 128.0, EPS)
                K.tt(ov, ov, T['oss'].v(lambda h: h[0:c, :].unsqueeze(2).to_broadcast([c, 8, 128])), ALU.mult)
                K.tt(ov, ov, T['ngB'].v(lambda h: h[0:c, :].unsqueeze(1).to_broadcast([c, 8, 128])), ALU.mult, eng='pool')
                K.act(T['z'][0:c, :], T['z'][0:c, :], AF.Silu)
                K.tt(ov, ov, T['z'].v(lambda h: h[0:c, :].rearrange("p (a b) -> p a b", a=8)), ALU.mult)
                K.dma(io['ocat'][row0:row0 + c, 0:1024], T['osb'].v(lambda h: h[0:c, :, :].rearrange("p a b -> p (a b)")))

            K.memset(T['S'][:], 0.0)
            for n in range(self.NT):
                chunk(128, n * 128)
            K.dma(io['p_d_l%d' % li].v(lambda h: h.rearrange("a d e -> d a e")), T['S'][:])
            for s_ in range(4):
                K.dma(T['S'][:], io['st_d_l%d' % li].v(lambda h: h[s_].rearrange("a d e -> d a e")))
                chunk(32, L + 32 * s_)
                K.dma(io['s_d_l%d' % li].v(lambda h: h[s_].rearrange("a d e -> d a e")), T['S'][:])
        K.S.barrier()

    def phase_B_fox(self, li):
        K, io, L, PAST = self.K, self.io, self.L, self.PAST
        idf, idb = self.c['idf'], self.c['idb']
        NT = self.NT
        NP = PAST // 128
        for r0 in range(0, L, 512):
            r1 = min(L, r0 + 512)
            K.dma(io['p_k_l%d' % li][r0:r1, :], io['proj'][r0:r1, 1024:2048])
            K.dma(io['p_v_l%d' % li][r0:r1, :], io['proj'][r0:r1, 2048:3072], q='act')
        K.dma(io['s_k_l%d' % li][:, :], io['proj'][L:L + 128, 1024:2048])
        K.dma(io['s_v_l%d' % li][:, :], io['proj'][L:L + 128, 2048:3072])
        with ExitStack() as es:
            bfB = K.sb(es, 'bfB', [128, 16], F32)
            K.dma(bfB[:], io['bf_l%d' % li].v(lambda h: h[0:1, :].partition_broadcast(128)))
            lf = K.sb(es, 'lf', [128, 16], F32)
            lfT = K.sb(es, 'lfT', [16, max(L, PAST + 32)], F32)
            ncT = K.sb(es, 'ncT', [16, max(L, PAST + 32)], F32)

            def logf_tile(r0, col0, n):
                K.dma(lf[0:n, :], io['proj'][r0:r0 + n, 3072:3088])
                K.tt(lf[0:n, :], lf[0:n, :], bfB[0:n, :], ALU.add)
                K.act(lf[0:n, :], lf[0:n, :], AF.Exp, scale=-1.0)
                K.act(lf[0:n, :], lf[0:n, :], AF.Ln, bias=1.0)
                K.ts(lf[0:n, :], lf[0:n, :], -1.0, ALU.mult)
                K.dma(io['aux'][r0:r0 + n, 0:16], lf[0:n, :])
                bk = self.bank()
                K.tr(bk[0:16, 0:n], lf[0:n, :], idf[0:n, 0:n])
                K.copy(lfT[:, col0:col0 + n], bk[0:16, 0:n])

            def neg_cumsum(n):
                SC = getattr(self, 'scan_chunk', 2048)
                for c0 in range(0, n, SC):
                    c1 = min(n, c0 + SC)
                    init = 0.0 if c0 == 0 else ncT[:, c0 - 1:c0]
                    o, d0, d1 = ncT[:, c0:c1], lfT[:, c0:c1], self.c['zeros'][0:16, 0:c1 - c0]
                    ia = init.ap if isinstance(init, V) else init
                    K.S.op('dve', lambda e, o=o, d0=d0, d1=d1, ia=ia: e.tensor_tensor_scan(
                        out=o.ap, data0=d0.ap, data1=d1.ap, initial=ia, op0=ALU.add, op1=ALU.add),
                        reads=[lfT.name, self.c['zeros'].name, ncT.name], writes=[ncT.name])
                K.ts(lfT[:, 0:n], ncT[:, 0:n], -1.0, ALU.mult)

            for ti in range(NT):
                logf_tile(ti * 128, ti * 128, 128)
            neg_cumsum(L)
            K.dma(io['cumT'][:, 0:L], lfT[:, 0:L])
            for r0 in range(0, L, 512):
                r1 = min(L, r0 + 512)
                K.dma(io['p_f_l%d' % li][r0:r1, :], io['aux'][r0:r1, 0:16])
            for s_ in range(4):
                for t in range(NP):
                    K.dma(lf[:, :], io['c_f_l%d' % li].v(lambda h: h[s_, t * 128:(t + 1) * 128, :]))
                    bk = self.bank()
                    K.tr(bk[0:16, 0:128], lf[:, :], idf[:, :])
                    K.copy(lfT[:, t * 128:(t + 1) * 128], bk[0:16, 0:128])
                logf_tile(L + 32 * s_, PAST, 32)
                neg_cumsum(PAST + 32)
                K.dma(io['cumTs'].v(lambda h: h[s_, :, 0:PAST + 32]), lfT[:, 0:PAST + 32])
            K.dma(io['s_f_l%d' % li][:, :], io['aux'][L:L + 128, 0:16])
        K.S.barrier()
        with ExitStack() as es:
            KMAX = max(L, PAST + 32)
            NKT = max(NT, NP)
            raw = K.sb(es, 'raw', [128, NKT, 128], F32)
            rawb = K.sb(es, 'rawb', [128, NKT, 128], BF16)
            QT = K.sb(es, 'QT', [128, L], BF16)
            KT = K.sb(es, 'KT', [128, KMAX], BF16)
            Vp = K.sb(es, 'Vp', [128, NKT, 128], BF16)
            vnew = K.sb(es, 'vnew', [32, 128], BF16)
            ncr = [K.sb(es, 'ncr', [1, KMAX], F32) for _ in range(2)]
            sc = K.sb(es, 'sc', [128, KMAX], F32)
            pbf = K.sb(es, 'pbf', [128, KMAX], BF16)
            pT = K.sb(es, 'pT', [128, NKT, 128], BF16)
            pTt = K.sb(es, 'pTt', [32, 32], BF16)
            mx = K.sb(es, 'mx', [128, 1], F32)
            rs = K.sb(es, 'rs', [128, 1], F32)
            osb = K.sb(es, 'osb', [128, 128], F32)
            q32 = K.sb(es, 'q32', [32, 128], F32)
            q32b = K.sb(es, 'q32b', [32, 128], BF16)
            cmask = K.sb(es, 'cmask', [128, 128], F32)
            K.memset(cmask[:], 0.0)
            K.aselect(cmask[:], cmask[:], [[-1, 128]], ALU.is_ge, NEG, 0, 1)
            ones1 = self.c['ones']

            def load_cols(dst, src2d, ntile):
                for t0 in range(0, ntile, 8):
                    nb = min(8, ntile - t0)
                    K.dma(dst[:, t0:t0 + nb, :],
                          V(src2d.ap[t0 * 128:(t0 + nb) * 128, :].rearrange("(n p) c -> p n c", p=128), src2d.key),
                          q='sp' if (t0 // 8) % 2 == 0 else 'act')

            def to_T(dst, src_b, ntile, npart=128):
                for t0 in range(0, ntile, 8):
                    nb = min(8, ntile - t0)
                    bk = self.bank()
                    for j in range(nb):
                        K.tr(bk.v(lambda h: h.bitcast(BF16)[:, j * 128:j * 128 + npart]), src_b[0:npart, t0 + j, :], idb[0:npart, 0:npart])
                    K.copy(dst.v(lambda h: h[:, t0 * 128:(t0 + nb) * 128].rearrange("p (a b) -> p a b", a=nb)[:, :, 0:npart]),
                           bk.v(lambda h: h.bitcast(BF16)[:, 0:nb * 128].rearrange("p (a b) -> p a b", a=nb)[:, :, 0:npart]),
                           eng='act' if (t0 // 8) % 2 == 0 else 'dve')

            def attend(nq, qTv, p0, nfull, ntail, ncrow, hh, out_v):
                keys = nfull * 128 + ntail
                dw = ntail if ntail else 128
                d0 = keys - dw
                g = 0
                for c0 in range(0, keys, 512):
                    c1 = min(keys, c0 + 512)
                    bk = self.bank()
                    K.mm(bk[0:nq, 0:c1 - c0], qTv, KT[p0:p0 + 64, c0:c1], start=True, stop=False)
                    K.mm(bk[0:nq, 0:c1 - c0], ones1[0:1, 0:nq], ncrow[0:1, c0:c1], start=False, stop=True)
                    e1 = min(c1, d0)
                    if e1 > c0:
                        K.copy(sc[0:nq, c0:e1], bk[0:nq, 0:e1 - c0], eng='act' if g % 2 == 0 else 'dve')
                    if c1 > d0:
                        K.tt(sc[0:nq, d0:keys], bk[0:nq, d0 - c0:keys - c0], cmask[0:nq, 0:dw], ALU.add)
                    g += 1
                K.red(mx[0:nq, :], sc[0:nq, 0:keys], ALU.max)
                K.ts(mx[0:nq, :], mx[0:nq, :], -1.0, ALU.mult)
                K.act(pbf[0:nq, 0:keys], sc[0:nq, 0:keys], AF.Exp, bias=mx[0:nq, :], accum=rs[0:nq, :])
                K.recip(rs[0:nq, :], rs[0:nq, :])
                for t0 in range(0, nfull, 8):
                    nb = min(8, nfull - t0)
                    bk = self.bank()
                    for j in range(nb):
                        K.tr(bk.v(lambda h: h.bitcast(BF16)[:, j * 128:j * 128 + nq]), pbf[0:nq, (t0 + j) * 128:(t0 + j + 1) * 128], idb[0:nq, 0:nq])
                    K.copy(pT.v(lambda h: h[:, t0:t0 + nb, 0:nq]),
                           bk.v(lambda h: h.bitcast(BF16)[:, 0:nb * 128].rearrange("p (a b) -> p a b", a=nb)[:, :, 0:nq]),
                           eng='act' if (t0 // 8) % 2 == 0 else 'dve')
                if ntail:
                    bk = self.bank()
                    K.tr(bk.v(lambda h: h.bitcast(BF16)[0:ntail, 0:nq]), pbf[0:nq, nfull * 128:keys], idb[0:nq, 0:nq])
                    K.copy(pTt[0:ntail, 0:nq], bk.v(lambda h: h.bitcast(BF16)[0:ntail, 0:nq]))
                bk = self.bank()
                for kt in range(nfull):
                    K.mm(bk[0:nq, 0:64], pT[:, kt, 0:nq], Vp[:, kt, hh * 64:(hh + 1) * 64], start=(kt == 0), stop=(kt == nfull - 1 and not ntail))
                if ntail:
                    K.mm(bk[0:nq, 0:64], pTt[0:ntail, 0:nq], vnew[0:ntail, hh * 64:(hh + 1) * 64], start=(nfull == 0), stop=True)
                K.ts(out_v, bk[0:nq, 0:64], rs[0:nq, :], ALU.mult)

            for m in range(8):
                cq, ck, cv = m * 128, 1024 + m * 128, 2048 + m * 128
                load_cols(raw, io['proj'][0:L, cq:cq + 128], NT)
                K.ts(rawb[:, 0:NT, :], raw[:, 0:NT, :], 0.125, ALU.mult)
                to_T(QT, rawb, NT)
                load_cols(raw, io['proj'][0:L, ck:ck + 128], NT)
                K.copy(rawb[:, 0:NT, :], raw[:, 0:NT, :], eng='pool')
                to_T(KT, rawb, NT)
                load_cols(raw, io['proj'][0:L, cv:cv + 128], NT)
                K.copy(Vp[:, 0:NT, :], raw[:, 0:NT, :], eng='pool')
                for hh in range(2):
                    K.dma(ncr[hh][0:1, 0:L], io['cumT'][2 * m + hh:2 * m + hh + 1, 0:L], q='act')
                for qt in range(NT):
                    for hh in range(2):
                        p0 = 64 * hh
                        attend(128, QT[p0:p0 + 64, qt * 128:(qt + 1) * 128], p0, qt + 1, 0, ncr[hh], hh, osb[:, hh * 64:(hh + 1) * 64])
                    K.dma(io['ocat'][qt * 128:(qt + 1) * 128, cq:cq + 128], osb[:, :])
                for s_ in range(4):
                    r0 = L + 32 * s_
                    K.dma(q32[:, :], io['proj'][r0:r0 + 32, cq:cq + 128])
                    K.ts(rawb[0:32, 0, :], q32[:, :], 0.125, ALU.mult)
                    to_T(QT, rawb, 1, npart=32)
                    load_cols(raw, io['c_k_l%d' % li].v(lambda h: h[s_, :, m * 128:(m + 1) * 128]), NP)
                    K.copy(rawb[:, 0:NP, :], raw[:, 0:NP, :], eng='pool')
                    to_T(KT, rawb, NP)
                    K.dma(q32[:, :], io['proj'][r0:r0 + 32, ck:ck + 128])
                    K.copy(q32b[:, :], q32[:, :])
                    bk = self.bank()
                    K.tr(bk.v(lambda h: h.bitcast(BF16)[:, 0:32]), q32b[:, :], idb[0:32, 0:32])
                    K.copy(KT[:, PAST:PAST + 32], bk.v(lambda h: h.bitcast(BF16)[:, 0:32]))
                    load_cols(raw, io['c_v_l%d' % li].v(lambda h: h[s_, :, m * 128:(m + 1) * 128]), NP)
                    K.copy(Vp[:, 0:NP, :], raw[:, 0:NP, :], eng='pool')
                    K.dma(q32[:, :], io['proj'][r0:r0 + 32, cv:cv + 128])
                    K.copy(vnew[:, :], q32[:, :])
                    for hh in range(2):
                        K.dma(ncr[hh][0:1, 0:PAST + 32], io['cumTs'].v(lambda h: h[s_, 2 * m + hh:2 * m + hh + 1, 0:PAST + 32]), q='act')
                    for hh in range(2):
                        p0 = 64 * hh
                        attend(32, QT[p0:p0 + 64, 0:32], p0, NP, 32, ncr[hh], hh, osb[0:32, hh * 64:(hh + 1) * 64])
                    K.dma(io['ocat'][r0:r0 + 32, cq:cq + 128], osb[0:32, :])
        K.S.barrier()

    def phase_B_ssd(self, li):
        K, io, L = self.K, self.io, self.L
        idf = self.c['idf']
        self.conv_state_out(li)
        with ExitStack() as es:
            cw = [K.sb(es, 'cw', [128, 3072], F32) for _ in range(4)]
            for j in range(4):
                K.dma(cw[j][:], io['conv_l%d' % li].v(lambda h: h[j:j + 1, :].partition_broadcast(128)))
            cb = K.sb(es, 'cb', [128, 3072], F32)
            K.dma(cb[:], io['convb_l%d' % li].v(lambda h: h[0:1, :].partition_broadcast(128)))
            alB = K.sb(es, 'alB', [128, 32], F32)
            dtB = K.sb(es, 'dtB', [128, 32], F32)
            K.dma(alB[:], io['alog_l%d' % li].v(lambda h: h[0:1, :].partition_broadcast(128)))
            K.dma(dtB[:], io['dtb_l%d' % li].v(lambda h: h[0:1, :].partition_broadcast(128)))
            K.act(alB[:], alB[:], AF.Exp)
            K.ts(alB[:], alB[:], -1.0, ALU.mult)
            sh = [K.sb(es, 'sh', [128, 3072], F32) for _ in range(4)]
            xbc = K.sb(es, 'xbc', [128, 3072], F32)
            ab = K.sb(es, 'ab', [128, 32], F32)
            ax = K.sb(es, 'ax', [128, 64], F32)
            for ti in range(self.NT + 1):
                r0 = ti * 128
                self.conv_tile(ti, sh, cw, xbc[:], cbias=cb)
                K.dma(io['cvo'][r0:r0 + 128, :], xbc[:])
                K.dma(ab[:], io['proj'][r0:r0 + 128, 5120:5152], q='act')
                K.tt(ax[:, 0:32], ab[:], dtB[:], ALU.add)
                K.act(ax[:, 0:32], ax[:, 0:32], AF.Exp)
                K.act(ax[:, 0:32], ax[:, 0:32], AF.Ln, bias=1.0)
                K.tt(ax[:, 32:64], ax[:, 0:32], alB[:], ALU.mult)
                K.dma(io['aux'][r0:r0 + 128, 0:64], ax[:])
        K.S.barrier()
        with ExitStack() as es:
            T = {}
            def t(name, shape, dt=F32):
                T[name] = K.sb(es, name, shape, dt)
            t('xbc', [128, 3072]); t('z', [128, 2048]); t('ax', [128, 64])
            t('gam', [128, 32]); t('w2', [128, 32]); t('glast', [128, 32]); t('egl', [128, 32])
            t('xdt', [128, 2048]); t('xdtw', [128, 2048]); t('xd', [128, 2048]); t('ysb', [128, 2048])
            t('BT', [128, 4, 128]); t('CT', [128, 4, 128]); t('cbT', [128, 4, 128])
            t('gbc', [128, 4, 128]); t('d4', [128, 4, 128]); t('MT4', [128, 4, 128]); t('egB4', [128, 4, 128]); t('CegT4', [128, 4, 128])
            t('hT', [128, 32, 64]); t('hO', [64, 32, 128])
            t('dB', [128, 32]); t('ngB', [128, 2048]); t('rstd', [128, 1])
            K.dma(T['dB'][:], io['dsk_l%d' % li].v(lambda h: h[0:1, :].partition_broadcast(128)))
            K.dma(T['ngB'][:], io['ng_l%d' % li].v(lambda h: h[0:1, :].partition_broadcast(128)))

            def chunk(c, row0):
                K.dma(T['xbc'][0:c, :], io['cvo'][row0:row0 + c, :])
                K.dma(T['z'][0:c, :], io['proj'][row0:row0 + c, 0:2048], q='act')
                K.dma(T['ax'][0:c, :], io['aux'][row0:row0 + c, 0:64], q='act')
                bk = self.bank()
                K.mm(bk[0:c, 0:32], self.c['triu'][0:c, 0:c], T['ax'][0:c, 32:64])
                K.copy(T['gam'][0:c, :], bk[0:c, 0:32])
                x3 = T['xbc'].v(lambda h: h[0:c, 0:2048].rearrange("p (a b) -> p a b", a=32))
                K.tt(T['xdt'].v(lambda h: h[0:c, :].rearrange("p (a b) -> p a b", a=32)), x3,
                     T['ax'].v(lambda h: h[0:c, 0:32].unsqueeze(2).to_broadcast([c, 32, 64])), ALU.mult)
                K.tt(T['xd'].v(lambda h: h[0:c, :].rearrange("p (a b) -> p a b", a=32)), x3,
                     T['dB'].v(lambda h: h[0:c, :].unsqueeze(2).to_broadcast([c, 32, 64])), ALU.mult, eng='pool')
                sub = getattr(self, 'sub', 99)
                if sub <= 1:
                    return
                for g in range(4):
                    Bg = T['xbc'][0:c, 2048 + g * 128:2048 + (g + 1) * 128]
                    Cg = T['xbc'][0:c, 2560 + g * 128:2560 + (g + 1) * 128]
                    bk = self.bank()
                    K.tr(bk[:, 0:c], Bg, idf[0:c, 0:c])
                    K.tr(bk[:, c:2 * c], Cg, idf[0:c, 0:c])
                    ea = 'act' if 'A' in getattr(self, 'var', '') else 'dve'
                    K.copy(T['BT'][:, g, 0:c], bk[:, 0:c], eng=ea)
                    K.copy(T['CT'][:, g, 0:c], bk[:, c:2 * c])
                    bk = self.bank()
                    K.mm(bk[0:c, 0:c], T['BT'][:, g, 0:c], T['CT'][:, g, 0:c])
                    K.copy(T['cbT'][0:c, g, 0:c], bk[0:c, 0:c], eng=ea)
                    if sub <= 2:
                        continue
                    bky = self.bank()
                    for half in range(2):
                        h0 = 8 * g + 4 * half
                        bkg = self.bank()
                        for j in range(4):
                            K.ts(T['gbc'][0:c, j, :], self.c['ones'][0:c, :], T['ax'][0:c, 32 + h0 + j:32 + h0 + j + 1], ALU.mult)
                            K.mm(bkg[:, j * c:(j + 1) * c], T['gbc'][0:c, j, :], self.c['triu'][0:c, 0:c])
                        g3 = lambda np_: bkg.v(lambda h: h[0:np_, 0:4 * c].rearrange("p (a b) -> p a b", a=4))
                        K.copy(T['glast'][:, h0:h0 + 4], bkg.v(lambda h: h[:, 0:4 * c].rearrange("p (a b) -> p a b", a=4)[:, :, c - 1]))
                        K.copy(T['egB4'].v(lambda h: h[:, :, 0:c]), g3(128))
                        K.act(T['egB4'].v(lambda h: h[:, :, 0:c]), T['egB4'].v(lambda h: h[:, :, 0:c]), AF.Exp)
                        K.tt(T['CegT4'].v(lambda h: h[:, :, 0:c]), T['egB4'].v(lambda h: h[:, :, 0:c]),
                             T['CT'].v(lambda h: h[:, g, 0:c].unsqueeze(1).to_broadcast([128, 4, c])), ALU.mult, eng='pool')
                        d4 = T['d4'].v(lambda h: h[0:c, :, 0:c])
                        M4 = T['MT4'].v(lambda h: h[0:c, :, 0:c])
                        K.tt(d4, g3(c), T['gam'].v(lambda h: h[0:c, h0:h0 + 4].unsqueeze(2).to_broadcast([c, 4, c])), ALU.subtract)
                        K.ts(d4, d4, 0.0, ALU.min)
                        K.act(d4, d4, AF.Exp)
                        K.tt(M4, d4, self.c['triu'].v(lambda h: h[0:c, 0:c].unsqueeze(1).to_broadcast([c, 4, c])), ALU.mult)
                        K.tt(M4, M4, T['cbT'].v(lambda h: h[0:c, g, 0:c].unsqueeze(1).to_broadcast([c, 4, c])), ALU.mult)
                        for j in range(4):
                            hh = h0 + j
                            oc = (4 * half + j) * 64
                            K.mm(bky[0:c, oc:oc + 64], T['MT4'][0:c, j, 0:c], T['xdt'][0:c, hh * 64:(hh + 1) * 64], start=True, stop=False)
                            K.mm(bky[0:c, oc:oc + 64], T['CegT4'][:, j, 0:c], T['hT'][:, hh, :], start=False, stop=True)
                    K.tt(T['ysb'][0:c, g * 512:(g + 1) * 512], bky[0:c, :], T['xd'][0:c, g * 512:(g + 1) * 512], ALU.add)
                    if sub <= 3:
                        continue
                    gs = slice(8 * g, 8 * g + 8)
                    K.act(T['egl'][:, gs], T['glast'][:, gs], AF.Exp)
                    K.tt(T['w2'][0:c, gs], T['glast'][0:c, gs], T['gam'][0:c, gs], ALU.subtract)
                    K.act(T['w2'][0:c, gs], T['w2'][0:c, gs], AF.Exp)
                    K.tt(T['xdtw'].v(lambda h: h[0:c, g * 512:(g + 1) * 512].rearrange("p (a b) -> p a b", a=8)),
                         T['xdt'].v(lambda h: h[0:c, g * 512:(g + 1) * 512].rearrange("p (a b) -> p a b", a=8)),
                         T['w2'].v(lambda h: h[0:c, gs].unsqueeze(2).to_broadcast([c, 8, 64])), ALU.mult, eng='pool')
                    bkh = self.bank()
                    for j8 in range(8):
                        hh = 8 * g + j8
                        K.mm(bkh[:, j8 * 64:(j8 + 1) * 64], Bg, T['xdtw'][0:c, hh * 64:(hh + 1) * 64])
                    hv = T['hT'].v(lambda h: h[:, gs, :])
                    K.tt(hv, hv, T['egl'].v(lambda h: h[:, gs].unsqueeze(2).to_broadcast([128, 8, 64])), ALU.mult)
                    K.tt(hv, hv, bkh.v(lambda h: h[:, :].rearrange("p (a b) -> p a b", a=8)), ALU.add)
                if sub <= 4:
                    return
                K.act(T['z'][0:c, :], T['z'][0:c, :], AF.Silu)
                K.tt(T['ysb'][0:c, :], T['ysb'][0:c, :], T['z'][0:c, :], ALU.mult)
                self.rmsnorm_rstd(es, T['ysb'][0:c, :], 2048, T['z'][0:c, :], T['rstd'][0:c, :])
                K.stt(T['ysb'][0:c, :], T['ysb'][0:c, :], T['rstd'][0:c, :], T['ngB'][0:c, :], ALU.mult, ALU.mult)
                K.dma(io['ocat'][row0:row0 + c, 0:2048], T['ysb'][0:c, :])

            def state_out(dst):
                for b8 in range(8):
                    bk = self.bank()
                    for j in range(4):
                        K.tr(bk[0:64, j * 128:(j + 1) * 128], T['hT'][:, b8 * 4 + j, :], idf[:, :])
                    K.copy(T['hO'].v(lambda h: h[:, b8 * 4:b8 * 4 + 4, :]), bk.v(lambda h: h[0:64, :].rearrange("p (a b) -> p a b", a=4)))
                for q4 in range(4):
                    K.dma(dst[:, q4 * 8:(q4 + 1) * 8, :], T['hO'][:, q4 * 8:(q4 + 1) * 8, :], q='sp' if q4 % 2 == 0 else 'act')

            def state_in(src):
                for q4 in range(4):
                    K.dma(T['hO'][:, q4 * 8:(q4 + 1) * 8, :], src[:, q4 * 8:(q4 + 1) * 8, :], q='sp' if q4 % 2 == 0 else 'act')
                for b8 in range(4):
                    bk = self.bank()
                    for j in range(8):
                        K.tr(bk[:, j * 64:(j + 1) * 64], T['hO'][:, b8 * 8 + j, :], idf[0:64, 0:64])
                    K.copy(T['hT'].v(lambda h: h[:, b8 * 8:b8 * 8 + 8, :]), bk.v(lambda h: h[:, :].rearrange("p (a b) -> p a b", a=8)))

            cut = getattr(self, 'cut', 99)
            K.memset(T['hT'][:], 0.0)
            if cut >= 2:
                for n in range(self.NT):
                    chunk(128, n * 128)
            if cut >= 3:
                state_out(io['p_s_l%d' % li].v(lambda h: h.rearrange("a p n -> p a n")))
            for s_ in range(4):
                if cut >= 4:
                    state_in(io['st_s_l%d' % li].v(lambda h: h[s_].rearrange("a p n -> p a n")))
                if cut >= 5:
                    chunk(32, L + 32 * s_)
                if cut >= 6:
                    state_out(io['s_s_l%d' % li].v(lambda h: h[s_].rearrange("a p n -> p a n")))
        K.S.barrier()

    def phase_C1(self, li):
        K, io, kd = self.K, self.io, self.kinds[li]
        WO = W_MIX[kd] + 1024
        KO = WO // 128
        with ExitStack() as es:
            wo = K.sb(es, 'wo', [128, KO, D], BF16)
            stg = [K.sb(es, 'stg', [128, 2048], F32) for _ in range(2)]
            self.load_w_bf16(io['w_out_l%d' % li], wo, KO, D, stg)
            oc = [K.sb(es, 'oc', [128, WO], F32) for _ in range(2)]
            ocb = [K.sb(es, 'ocb', [128, WO], BF16) for _ in range(2)]
            ocT = [K.sb(es, 'ocT', [128, KO, 128], BF16) for _ in range(2)]
            xt = [K.sb(es, 'xt', [128, D], F32) for _ in range(2)]
            x1 = [K.sb(es, 'x1', [128, D], F32) for _ in range(2)]
            for ti in range(self.NT + 1):
                b = ti % 2
                r0 = ti * 128
                K.dma(oc[b][:], io['ocat'][r0:r0 + 128, 0:WO])
                if li == 0:
                    src = io['xp'][r0:r0 + 128, :] if ti < self.NT else io['xs'][:, :]
                else:
                    src = io['xres'][r0:r0 + 128, :]
                K.dma(xt[b][:], src, q='act')
                K.copy(ocb[b][:], oc[b][:], eng='pool')
                self.transpose_to(ocT[b], ocb[b], KO, self.c['idb'], BF16)
                for g in range(2):
                    bk = self.bank()
                    for k in range(KO):
                        K.mm(bk[:, :], ocT[b][:, k, :], wo[:, k, g * 512:(g + 1) * 512], start=(k == 0), stop=(k == KO - 1))
                    K.tt(x1[b][:, g * 512:(g + 1) * 512], xt[b][:, g * 512:(g + 1) * 512], bk[:, :], ALU.add)
                K.dma(io['xres'][r0:r0 + 128, :], x1[b][:])
        K.S.barrier()

    def phase_C2(self, li):
        K, io = self.K, self.io
        last = (li == self.DEPTH - 1)
        G = 16
        GS = 4
        self.nrot = 6
        yps = [self.pb[6], self.pb[7]]
        idb = self.c['idb']
        with ExitStack() as es:
            wq = K.sb(es, 'wq', [128, 8, 2048], BF16)
            with ExitStack() as es2:
                stg = [K.sb(es2, 'stg', [128, 2048], F32) for _ in range(2)]
                self.load_w_bf16(io['peer_wq'].v(lambda h: h[li]), wq, 8, 2048, stg)
                K.S.barrier()
            gB = K.sb(es, 'gB', [128, D], F32)
            K.dma(gB[:], io['g_ffn'].v(lambda h: h[li:li + 1, :].partition_broadcast(128)))
            if last:
                gF = K.sb(es, 'gF', [128, D], F32)
                K.dma(gF[:], io['g_final'].v(lambda h: h[0:1, :].partition_broadcast(128)))
            kraw = K.sb(es, 'kraw', [128, 2, 128], F32)
            kT = K.sb(es, 'kT', [128, 2, 128], F32)
            K.dma(kraw[:, 0, :], io['peer_k1'].v(lambda h: h[li]))
            K.dma(kraw[:, 1, :], io['peer_k2'].v(lambda h: h[li]))
            bk = self.bank()
            for hf in range(2):
                K.tr(bk[:, hf * 128:(hf + 1) * 128], kraw[:, hf, :], self.c['idf'][:])
            K.copy(kT.v(lambda h: h[:, :, :]), bk.v(lambda h: h[:, 0:256].rearrange("p (k t) -> p k t", k=2)))
            io16i = K.sb(es, 'io16i', [128, 16], I32)
            io16 = K.sb(es, 'io16', [128, 16], F32)
            K.iota(io16i[:], [[1, 16]], 0, 0)
            K.copy(io16[:], io16i[:])

            xt2 = [K.sb(es, 'xt', [128, D], F32) for _ in range(2)]
            sq = K.sb(es, 'sq', [128, D], F32)
            rstd = K.sb(es, 'rstd', [128, 1], F32)
            xnb2 = [K.sb(es, 'xnb', [128, D], BF16) for _ in range(2)]
            xnT = K.sb(es, 'xnT', [128, 8, 128], BF16)
            qT = K.sb(es, 'qT', [128, 16, 128], F32)
            sc = K.sb(es, 'sc', [128, 16, 128], F32)
            sc2 = K.sb(es, 'sc2', [128, 16, 128], F32)
            v12 = K.sb(es, 'v12', [128, 16, 16], F32)
            i12 = K.sb(es, 'i12', [128, 16, 16], U32)
            i12f = K.sb(es, 'i12f', [128, 16, 16], F32)
            cand = K.sb(es, 'cand', [128, 8, 256], F32)
            cand2 = K.sb(es, 'cand2', [128, 8, 256], F32)
            tsv = K.sb(es, 'tsv', [128, 8, 16], F32)
            pos = K.sb(es, 'pos', [128, 8, 16], U32)
            ai = K.sb(es, 'ai', [128, 8, 16], U32)
            bi = K.sb(es, 'bi', [128, 8, 16], U32)
            af = K.sb(es, 'af', [128, 8, 16], F32)
            bf = K.sb(es, 'bf', [128, 8, 16], F32)
            oh = K.sb(es, 'oh', [128, 16, 16], F32)
            oh2 = K.sb(es, 'oh2', [128, 16, 16], F32)
            i1s = K.sb(es, 'i1s', [128, 8, 16], F32)
            i2s = K.sb(es, 'i2s', [128, 8, 16], F32)
            eif = K.sb(es, 'eif', [128, 128], F32)
            eidx2 = [K.sb(es, 'eidx', [128, 128], I32) for _ in range(2)]
            gate2 = [K.sb(es, 'gate', [128, 8, 16], F32) for _ in range(2)]
            gsum = K.sb(es, 'gsum', [128, 8], F32)
            dots = K.sb(es, 'dots', [128, 128], F32)
            actw = K.sb(es, 'actw', [128, 128], F32)
            gb = [K.sb(es, 'gbuf', [128, 2 * D], BF16) for _ in range(G)]
            scr = K.sb(es, 'scr', [128, D], BF16)
            acc = [K.sb(es, 'acc', [128, D], F32) for _ in range(2)]
            dg = [K.sb(es, 'dg', [128, 128], BF16) for _ in range(8)]
            uvt = io['uvb_l%d' % li]

            def stage1(ti, pb_):
                xt, xnb, eidx, gate = xt2[pb_], xnb2[pb_], eidx2[pb_], gate2[pb_]
                r0 = ti * 128
                K.dma(xt[:], io['xres'][r0:r0 + 128, :])
                self.rmsnorm_rstd(es, xt[:], D, sq[:], rstd[:])
                K.stt(xnb[:], xt[:], rstd[:], gB[:], ALU.mult, ALU.mult)
                yield
                self.transpose_to(xnT, xnb, 8, self.c['idb'], BF16)
                yield
                for c4 in range(4):
                    bk = self.bank()
                    for j in range(4):
                        cc = c4 * 4 + j
                        for k in range(8):
                            K.mm(bk[:, j * 128:(j + 1) * 128], wq[:, k, cc * 128:(cc + 1) * 128], xnT[:, k, :],
                                 start=(k == 0), stop=(k == 7))
                        yield
                    K.copy(qT.v(lambda h: h[:, c4 * 4:c4 * 4 + 4, :]),
                           bk.v(lambda h: h[:, :].rearrange("p (k t) -> p k t", k=4)), eng='act' if c4 % 2 == 0 else 'dve')
                for c4 in range(4):
                    bk = self.bank()
                    for j in range(4):
                        cc = c4 * 4 + j
                        K.mm(bk[:, j * 128:(j + 1) * 128], qT[:, cc, :], kT[:, cc % 2, :])
                    K.copy(sc.v(lambda h: h[:, c4 * 4:c4 * 4 + 4, :]),
                           bk.v(lambda h: h[:, :].rearrange("p (k t) -> p k t", k=4)), eng='act' if c4 % 2 == 0 else 'dve')
                    yield
                for cc in range(16):
                    K.max8(v12[:, cc, 0:8], sc[:, cc, :])
                    K.maxidx(i12[:, cc, 0:8], v12[:, cc, 0:8], sc[:, cc, :])
                    K.mrep(sc2[:, cc, :], v12[:, cc, 0:8], sc[:, cc, :], NEG)
                    K.max8(v12[:, cc, 8:16], sc2[:, cc, :])
                    K.maxidx(i12[:, cc, 8:16], v12[:, cc, 8:16], sc2[:, cc, :])
                    yield
                K.copy(i12f[:], i12[:])
                for h in range(8):
                    K.tt(cand.v(lambda hh: hh[:, h, :].rearrange("p (a b) -> p a b", a=16)),
                         v12.v(lambda hh: hh[:, 2 * h, :].unsqueeze(2).to_broadcast([128, 16, 16])),
                         v12.v(lambda hh: hh[:, 2 * h + 1, :].unsqueeze(1).to_broadcast([128, 16, 16])), ALU.add)
                    K.max8(tsv[:, h, 0:8], cand[:, h, :])
                    K.maxidx(pos[:, h, 0:8], tsv[:, h, 0:8], cand[:, h, :])
                    K.mrep(cand2[:, h, :], tsv[:, h, 0:8], cand[:, h, :], NEG)
                    K.max8(tsv[:, h, 8:16], cand2[:, h, :])
                    K.maxidx(pos[:, h, 8:16], tsv[:, h, 8:16], cand2[:, h, :])
                    yield
                K.ts(ai[:], pos[:], 4, ALU.logical_shift_right)
                K.ts(bi[:], pos[:], 15, ALU.bitwise_and)
                K.copy(af[:], ai[:])
                K.copy(bf[:], bi[:])
                yield
                for h in range(8):
                    for (xf, half, dst) in ((af, 0, i1s), (bf, 1, i2s)):
                        K.tt(oh[:], xf.v(lambda hh: hh[:, h, :].unsqueeze(2).to_broadcast([128, 16, 16])),
                             io16.v(lambda hh: hh[:, :].unsqueeze(1).to_broadcast([128, 16, 16])), ALU.is_equal)
                        K.tt(oh2[:], oh[:], i12f.v(lambda hh: hh[:, 2 * h + half, :].unsqueeze(1).to_broadcast([128, 16, 16])), ALU.mult)
                        K.red(dst[:, h, :], oh2[:], ALU.add)
                    yield
                K.stt(eif.v(lambda hh: hh[:, :].rearrange("p (a b) -> p a b", a=8)), i1s[:], 128.0, i2s[:], ALU.mult, ALU.add)
                K.ts(eif[:], eif[:], 0.0, ALU.max, 16383.0, ALU.min)
                K.copy(eidx[:], eif[:])
                K.tt(gate[:], tsv[:], tsv.v(lambda hh: hh[:, :, 0:1].to_broadcast([128, 8, 16])), ALU.subtract)
                K.act(gate[:], gate[:], AF.Exp)
                K.red(gsum[:], gate[:], ALU.add)
                K.recip(gsum[:], gsum[:])
                K.tt(gate[:], gate[:], gsum.v(lambda hh: hh[:, :].unsqueeze(2).to_broadcast([128, 8, 16])), ALU.mult)
                yield

            def advance(gen, n):
                if gen is None:
                    return None
                for _ in range(n):
                    try:
                        next(gen)
                    except StopIteration:
                        return None
                return gen

            NGRP = 128 // GS
            NTT = self.NT + 1
            advance(stage1(0, 0), 10 ** 6)
            for ti in range(NTT):
                cur = ti % 2
                xt, xnb, eidx, gate = xt2[cur], xnb2[cur], eidx2[cur], gate2[cur]
                gflat = gate.v(lambda hh: hh[:, :, :].rearrange("p a b -> p (a b)"))
                r0 = ti * 128
                nxt = stage1(ti + 1, 1 - cur) if ti + 1 < NTT else None

                def head(gi):
                    for jj in range(GS):
                        j = gi * GS + jj
                        g = gb[j % G]
                        K.gather(g[:], uvt[:, :], eidx[:, j:j + 1])
                        K.ttr(scr[:], g[:, 0:D], xnb[:], dots[:, j:j + 1])
                    sl = slice(gi * GS, (gi + 1) * GS)
                    K.act(actw[:, sl], dots[:, sl], AF.Gelu)

                def tail(gi):
                    sl = slice(gi * GS, (gi + 1) * GS)
                    K.tt(actw[:, sl], actw[:, sl], gflat[:, sl], ALU.mult)
                    for jj in range(GS):
                        j = gi * GS + jj
                        g = gb[j % G]
                        d_ = dg[j % 8]
                        K.act(d_[:], idb[:], AF.Copy, scale=actw[:, j:j + 1])
                        for hf in range(2):
                            K.mm(yps[hf][:, :], d_[:], g[:, D + hf * 512:D + (hf + 1) * 512], start=(j == 0), stop=(j == 127))

                for gi in range(NGRP + 1):
                    if gi < NGRP:
                        head(gi)
                    if gi >= 1:
                        tail(gi - 1)
                    if gi >= 2:
                        nxt = advance(nxt, 3)
                advance(nxt, 10 ** 6)
                for hf in range(2):
                    K.tt(acc[0][:, hf * 512:(hf + 1) * 512], yps[hf][:, :], xt[:, hf * 512:(hf + 1) * 512], ALU.add)
                if not last:
                    K.dma(io['xres'][r0:r0 + 128, :], acc[0][:])
                else:
                    if self.debug:
                        K.dma(io['xres'][r0:r0 + 128, :], acc[0][:])
                    self.rmsnorm_rstd(es, acc[0][:], D, sq[:], rstd[:])
                    K.stt(acc[1][:], acc[0][:], rstd[:], gF[:], ALU.mult, ALU.mult)
                    dst = io['y_p'][r0:r0 + 128, :] if ti < self.NT else io['y_s'][:, :]
                    K.dma(dst, acc[1][:])
        K.S.barrier()
        self.nrot = 8

    def build(self, stop=None):
        self.declare()
        with ExitStack() as es:
            self.consts(es)
            self.K.S.barrier()
            self.build_uv()
            for li in range(self.DEPTH):
                if self.kinds[li] in ('delta', 'ssd'):
                    self.conv_states(li, None, None)
                self.phase_A(li)
                if stop == ('A', li):
                    break
                if stop != ('skipB', li):
                    self.phase_B(li)
                else:
                    WO = W_MIX[self.kinds[li]] + 1024
                    for ti in range(self.NT + 1):
                        self.K.dma(self.io['ocat'][ti * 128:(ti + 1) * 128, 0:WO], self.io['proj'][ti * 128:(ti + 1) * 128, 0:WO])
                    self.K.S.barrier()
                if stop == ('B', li):
                    break
                self.phase_C1(li)
                if stop == ('C1', li):
                    break
                self.phase_C2(li)
                if stop in (('C2', li), ('skipB', li)):
                    break
            self.K.S.emit()
        return self.nc


N_W_IN = {0: 'w_in_l0', 1: 'w_in_l1', 2: 'w_in_l2', 3: 'w_in_l3'}
N_W_OUT = {0: 'w_out_l0', 1: 'w_out_l1', 2: 'w_out_l2', 3: 'w_out_l3'}
N_DN_CONV = {0: 'dn_conv_l0', 3: 'dn_conv_l3'}
N_DN_ALOG = {0: 'dn_alog_l0', 3: 'dn_alog_l3'}
N_DN_DTB = {0: 'dn_dtb_l0', 3: 'dn_dtb_l3'}
N_DN_NG = {0: 'dn_ng_l0', 3: 'dn_ng_l3'}
N_ST_DELTA = {0: 'state_l0_delta', 3: 'state_l3_delta'}
N_ST_CONV = {0: 'state_l0_conv', 2: 'state_l2_conv', 3: 'state_l3_conv'}
N_FOX_BF = {1: 'fox_bf_l1'}
N_C_K = {1: 'cache_l1_k'}
N_C_V = {1: 'cache_l1_v'}
N_C_F = {1: 'cache_l1_logf'}
N_SSD_CONV = {2: 'ssd_conv_l2'}
N_SSD_CONVB = {2: 'ssd_convb_l2'}
N_SSD_ALOG = {2: 'ssd_alog_l2'}
N_SSD_DTB = {2: 'ssd_dtb_l2'}
N_SSD_D = {2: 'ssd_d_l2'}
N_SSD_NG = {2: 'ssd_ng_l2'}
N_ST_SSM = {2: 'state_l2_ssm'}


def make_in_maps(inp, L, PAST, DEPTH):
    kinds = [MIXERS[i % 3] for i in range(DEPTH)]
    f = lambda a: np.ascontiguousarray(np.asarray(a, dtype=np.float32))
    maps = []
    shared = {}
    shared['g_mix'] = f(inp['g_mix'])[:DEPTH]
    shared['g_mem'] = f(inp['g_mem'])[:DEPTH]
    shared['g_ffn'] = f(inp['g_ffn'])[:DEPTH]
    shared['w_mem_k'] = f(inp['w_mem_k'])[:DEPTH]
    shared['w_mem_v'] = f(inp['w_mem_v'])[:DEPTH]
    shared['peer_wq'] = f(inp['peer_wq'])[:DEPTH]
    shared['peer_k1'] = f(inp['peer_k1'])[:DEPTH]
    shared['peer_k2'] = f(inp['peer_k2'])[:DEPTH]
    peer_u, peer_v = inp['peer_u'], inp['peer_v']
    for i in range(DEPTH):
        shared['peer_u_l%d' % i] = f(peer_u[i])
        shared['peer_v_l%d' % i] = f(peer_v[i])
    shared['g_final'] = f(inp['g_final']).reshape(1, D)
    for i, kd in enumerate(kinds):
        shared['w_in_l%d' % i] = f(inp[N_W_IN[i]])
        shared['w_out_l%d' % i] = f(inp[N_W_OUT[i]])
        if kd == 'delta':
            shared['conv_l%d' % i] = f(inp[N_DN_CONV[i]])
            shared['alog_l%d' % i] = f(inp[N_DN_ALOG[i]]).reshape(1, 8)
            shared['dtb_l%d' % i] = f(inp[N_DN_DTB[i]]).reshape(1, 8)
            shared['ng_l%d' % i] = f(inp[N_DN_NG[i]]).reshape(1, 128)
        elif kd == 'fox':
            shared['bf_l%d' % i] = f(inp[N_FOX_BF[i]]).reshape(1, 16)
        else:
            shared['conv_l%d' % i] = f(inp[N_SSD_CONV[i]])
            shared['convb_l%d' % i] = f(inp[N_SSD_CONVB[i]]).reshape(1, 3072)
            shared['alog_l%d' % i] = f(inp[N_SSD_ALOG[i]]).reshape(1, 32)
            shared['dtb_l%d' % i] = f(inp[N_SSD_DTB[i]]).reshape(1, 32)
            shared['dsk_l%d' % i] = f(inp[N_SSD_D[i]]).reshape(1, 32)
            shared['ng_l%d' % i] = f(inp[N_SSD_NG[i]]).reshape(1, 2048)
    x_prompt, x_sample, mem_prompt = inp['x_prompt'], inp['x_sample'], inp['mem_prompt']
    cache_mem_k, cache_mem_v = inp['cache_mem_k'], inp['cache_mem_v']
    for c in range(8):
        b = c % 4
        sl = slice(4 * c, 4 * c + 4)
        m = dict(shared)
        m['xp'] = f(x_prompt[b])
        m['xs'] = f(x_sample[sl]).reshape(128, D)
        m['memp'] = f(mem_prompt[b])
        m['cmk'] = f(cache_mem_k[:DEPTH, sl]).reshape(DEPTH, 4, 256, 1024)
        m['cmv'] = f(cache_mem_v[:DEPTH, sl]).reshape(DEPTH, 4, 256, 1024)
        for i, kd in enumerate(kinds):
            if kd == 'delta':
                m['st_d_l%d' % i] = f(inp[N_ST_DELTA[i]][sl])
                m['st_c_l%d' % i] = f(inp[N_ST_CONV[i]][sl])
            elif kd == 'fox':
                m['c_k_l%d' % i] = f(inp[N_C_K[i]][sl]).reshape(4, PAST, 1024)
                m['c_v_l%d' % i] = f(inp[N_C_V[i]][sl]).reshape(4, PAST, 1024)
                m['c_f_l%d' % i] = f(inp[N_C_F[i]][sl])
            else:
                m['st_s_l%d' % i] = f(inp[N_ST_SSM[i]][sl])
                m['st_c_l%d' % i] = f(inp[N_ST_CONV[i]][sl])
        maps.append(m)
    return maps


def run_config(inp, L, PAST, DEPTH, debug=False):
    kinds = [MIXERS[i % 3] for i in range(DEPTH)]
    mk = MK(L, PAST, DEPTH, debug=debug)
    nc = mk.build()
    maps = make_in_maps(inp, L, PAST, DEPTH)
    res = run_bass_kernel_spmd(nc, maps, core_ids=list(range(8)))
    R = res.results
    NB = 4
    f32 = np.float32

    def pstack(name, shape):
        return np.stack([np.asarray(R[b][name], dtype=f32).reshape(shape) for b in range(NB)], axis=0)

    def scat(name, shape):
        return np.concatenate([np.asarray(R[c][name], dtype=f32).reshape((4,) + tuple(shape)) for c in range(8)], axis=0)

    y_prompt = pstack('y_p', (L, D))
    y_sample = scat('y_s', (32, D))
    p_mem_k = np.stack([np.asarray(R[b]['o_mk'], dtype=f32).reshape(DEPTH, 256, 4, 256) for b in range(NB)], axis=1)
    p_mem_v = np.stack([np.asarray(R[b]['o_mv'], dtype=f32).reshape(DEPTH, 256, 4, 256) for b in range(NB)], axis=1)
    outs = {}
    for i, kd in enumerate(kinds):
        if kd == 'delta':
            outs['p_l%d_delta' % i] = pstack('p_d_l%d' % i, (8, 128, 128))
            outs['p_l%d_conv' % i] = pstack('p_c_l%d' % i, (3, 3072))
            outs['s_l%d_delta' % i] = scat('s_d_l%d' % i, (8, 128, 128))
            outs['s_l%d_conv' % i] = scat('s_c_l%d' % i, (3, 3072))
        elif kd == 'fox':
            outs['p_l%d_k' % i] = pstack('p_k_l%d' % i, (L, 16, 64))
            outs['p_l%d_v' % i] = pstack('p_v_l%d' % i, (L, 16, 64))
            outs['p_l%d_logf' % i] = pstack('p_f_l%d' % i, (L, 16))
            outs['s_l%d_k' % i] = scat('s_k_l%d' % i, (32, 16, 64))
            outs['s_l%d_v' % i] = scat('s_v_l%d' % i, (32, 16, 64))
            outs['s_l%d_logf' % i] = scat('s_f_l%d' % i, (32, 16))
        else:
            outs['p_l%d_ssm' % i] = pstack('p_s_l%d' % i, (32, 64, 128))
            outs['p_l%d_conv' % i] = pstack('p_c_l%d' % i, (3, 3072))
            outs['s_l%d_ssm' % i] = scat('s_s_l%d' % i, (32, 64, 128))
            outs['s_l%d_conv' % i] = scat('s_c_l%d' % i, (3, 3072))
    order = ['p_l0_delta', 'p_l0_conv', 'p_l1_k', 'p_l1_v', 'p_l1_logf', 'p_l2_ssm', 'p_l2_conv', 'p_l3_delta', 'p_l3_conv',
             's_l0_delta', 's_l0_conv', 's_l1_k', 's_l1_v', 's_l1_logf', 's_l2_ssm', 's_l2_conv', 's_l3_delta', 's_l3_conv']
    return (y_prompt, y_sample, p_mem_k, p_mem_v) + tuple(outs.get(k) for k in order)


def kernel(**inputs):
    L = int(np.shape(inputs['x_prompt'])[1])
    PAST = int(np.shape(inputs['cache_l1_k'])[1])
    return run_config(inputs, L, PAST, 4)
```
